# Optimizing a Trainium2 kernel written in Bass

```python
import math
import jax, jax.numpy as jnp
from jax import lax
import numpy as np

D_MODEL = 1024
BATCH = 4
SEQ = 4096
DEPTH = 2
DEC_BATCH = 128
DEC_SEQ = 1
PAST_LEN = 2048
PAGE_SIZE = 128

HEAD_DIM = 64
N_NSA_HEADS = (D_MODEL // 2) // HEAD_DIM
N_NSA_KV = N_NSA_HEADS // 4
NSA_GROUP = N_NSA_HEADS // N_NSA_KV
N_GDN_HEADS = (D_MODEL // 2) // HEAD_DIM
NSA_WIDTH = N_NSA_HEADS * HEAD_DIM
GDN_WIDTH = N_GDN_HEADS * HEAD_DIM
MIX_WIDTH = NSA_WIDTH + GDN_WIDTH
KV_WIDTH = N_NSA_KV * HEAD_DIM
GDN_QKV = 3 * GDN_WIDTH
CMP_BLOCK = 32
SLC_BLOCK = 64
TOP_N = 16
N_LOCAL = 2
WINDOW = 512
Q_BLOCK = 128
GDN_CHUNK = 64
GDN_CONV = 4
FFN_CONV = 3
D_FF = ((8 * D_MODEL // 3 + 127) // 128) * 128
PLE_DIM = 256
LN_EPS = 1e-5
RMS_EPS = 1e-6
DN_ALPHA = (2 * DEPTH) ** 0.25
DN_BETA = (8 * DEPTH) ** -0.25
NEG = -1e30

C_Q = 0
C_KV = C_Q + NSA_WIDTH
C_WIN = C_KV + 4 * KV_WIDTH
C_GATE = C_WIN + 2 * KV_WIDTH
C_GQKV = C_GATE + 3 * N_NSA_HEADS
C_GA = C_GQKV + GDN_QKV
C_GB = C_GA + N_GDN_HEADS
C_GZ = C_GB + N_GDN_HEADS
IN_WIDTH = C_GZ + GDN_WIDTH
V_COLS = ((C_KV + KV_WIDTH, C_KV + 2 * KV_WIDTH), (C_KV + 3 * KV_WIDTH, C_WIN),
          (C_WIN + KV_WIDTH, C_GATE), (C_GQKV + 2 * GDN_WIDTH, C_GA))

kernel_name = "nsa_gdn_hybrid_decoder_step"


def alibi_slopes():
    return 2.0 ** (-8.0 * jnp.arange(1, N_NSA_HEADS + 1, dtype=jnp.float32) / N_NSA_HEADS)


def layer_norm(x, g, b):
    xf = x.astype(jnp.float32)
    mu = xf.mean(-1, keepdims=True)
    var = jnp.square(xf - mu).mean(-1, keepdims=True)
    return ((xf - mu) * lax.rsqrt(var + LN_EPS) * g.astype(jnp.float32) + b.astype(jnp.float32)).astype(x.dtype)


def causal_dwconv(x, buf, w):
    xp = jnp.concatenate([buf.astype(x.dtype), x], axis=1)
    y = lax.conv_general_dilated(xp, w[:, None, :].astype(x.dtype), window_strides=(1,), padding='VALID',
                                 dimension_numbers=('NWC', 'WIO', 'NWC'), feature_group_count=x.shape[-1])
    return y, xp[:, -(w.shape[0] - 1):]


def compress_blocks(rows, pe, phi):
    B, Lp, Hk, D = rows.shape
    blk = rows.reshape(B, Lp // CMP_BLOCK, CMP_BLOCK, Hk, D) + pe[None, None, :, None, :]
    return jnp.einsum('bnhd,de->bnhe', blk.mean(axis=2), phi)


def nsa_block(q, q_pos, kc, vc, ks_blk, vs_blk):
    B, Tq, H, D = q.shape
    Nc, Ns = kc.shape[1], ks_blk.shape[2]
    qg = q.reshape(B, Tq, N_NSA_KV, NSA_GROUP, D)
    sl = alibi_slopes().reshape(N_NSA_KV, NSA_GROUP)
    scale = D ** -0.5
    qpf = q_pos.astype(jnp.float32)
    c_end = jnp.arange(Nc) * CMP_BLOCK + (CMP_BLOCK - 1)
    c_mid = jnp.arange(Nc, dtype=jnp.float32) * CMP_BLOCK + (CMP_BLOCK - 1) / 2
    c_mask = c_end[None, :] <= q_pos[:, None]
    s = jnp.einsum('btngd,bcnd->bngtc', qg, kc).astype(jnp.float32) * scale
    s = s - sl[None, :, :, None, None] * (qpf[:, None] - c_mid[None, :])[None, None, None]
    p = jax.nn.softmax(jnp.where(c_mask, s, NEG), axis=-1) * c_mask
    o_cmp = jnp.einsum('bngtc,bcnd->btngd', p.astype(vc.dtype), vc)
    imp = p.sum(axis=2).reshape(B, N_NSA_KV, Tq, Ns, SLC_BLOCK // CMP_BLOCK).sum(-1)
    blk = jnp.arange(Ns)
    cur = q_pos // SLC_BLOCK
    future = blk[None, :] > cur[:, None]
    forced = (blk[None, :] == 0) | (((cur[:, None] - blk[None, :]) < N_LOCAL) & ~future)
    score = jnp.where(future, -jnp.inf, jnp.where(forced, jnp.inf, imp))
    k_eff = min(TOP_N, Ns)
    _, idx = lax.top_k(score, k_eff)
    b_i = jnp.arange(B)[:, None, None, None]
    h_i = jnp.arange(N_NSA_KV)[None, :, None, None]
    kg = ks_blk[b_i, h_i, idx]
    vg = vs_blk[b_i, h_i, idx]
    tok = idx[..., None] * SLC_BLOCK + jnp.arange(SLC_BLOCK)
    dist = q_pos[None, None, :, None, None] - tok
    valid = dist >= 0
    s2 = jnp.einsum('btngd,bntksd->bngtks', qg, kg).astype(jnp.float32) * scale
    s2 = s2 - sl[None, :, :, None, None, None] * dist.astype(jnp.float32)[:, :, None]
    s2 = jnp.where(valid[:, :, None], s2, NEG).reshape(B, N_NSA_KV, NSA_GROUP, Tq, k_eff * SLC_BLOCK)
    p2 = jax.nn.softmax(s2, axis=-1).reshape(B, N_NSA_KV, NSA_GROUP, Tq, k_eff, SLC_BLOCK)
    o_slc = jnp.einsum('bngtks,bntksd->btngd', p2.astype(vg.dtype), vg)
    return o_cmp.reshape(B, Tq, H, D), o_slc.reshape(B, Tq, H, D)


def window_attend(q, q_pos, k, v, k_pos):
    B, NB, Tb, H, D = q.shape
    qg = q.reshape(B, NB, Tb, N_NSA_KV, NSA_GROUP, D)
    sl = alibi_slopes().reshape(N_NSA_KV, NSA_GROUP)
    s = jnp.einsum('bjtngd,bjsnd->bjngts', qg, k).astype(jnp.float32) * (D ** -0.5)
    dist = q_pos[:, :, None] - k_pos[:, None, :]
    mask = (dist >= 0) & (dist < WINDOW) & (k_pos[:, None, :] >= 0)
    s = s - sl[None, None, :, :, None, None] * dist.astype(jnp.float32)[None, :, None, None]
    p = jax.nn.softmax(jnp.where(mask[None, :, None, None], s, NEG), axis=-1)
    o = jnp.einsum('bjngts,bjsnd->bjtngd', p.astype(v.dtype), v)
    return o.reshape(B, NB, Tb, H, D)


def gated_delta_chunked(q, k, v, g, beta, S0):
    B, T, H, D = q.shape
    C = GDN_CHUNK
    Tp = -(-T // C) * C

    def prep(a):
        a = jnp.pad(a, [(0, 0), (0, Tp - T)] + [(0, 0)] * (a.ndim - 2))
        a = a.reshape((B, Tp // C, C) + a.shape[2:])
        return jnp.moveaxis(jnp.moveaxis(a, 3, 2), 1, 0)

    q, k, v, g, beta = prep(q), prep(k), prep(v), prep(g), prep(beta)
    gc = jnp.cumsum(g, axis=-1)
    ar = jnp.arange(C)
    diff = gc[..., :, None] - gc[..., None, :]
    dec_strict = jnp.exp(jnp.where(ar[:, None] > ar[None, :], diff, -jnp.inf))
    dec_incl = jnp.exp(jnp.where(ar[:, None] >= ar[None, :], diff, -jnp.inf))
    kb = k * beta[..., None]
    A = jnp.einsum('nbhid,nbhjd->nbhij', kb, k) * dec_strict + jnp.eye(C, dtype=jnp.float32)
    rhs = jnp.concatenate([v * beta[..., None], kb * jnp.exp(gc)[..., None]], axis=-1)
    X = lax.linalg.triangular_solve(A, rhs, left_side=True, lower=True)
    val, kcd = X[..., :D], X[..., D:]
    inner = jnp.einsum('nbhid,nbhjd->nbhij', q, k) * dec_incl
    qg = q * jnp.exp(gc)[..., None]
    glast = gc[..., -1]
    kend = k * jnp.exp(glast[..., None] - gc)[..., None]

    def step(S, xs):
        val_c, kcd_c, inner_c, qg_c, kend_c, gl = xs
        vn = val_c - jnp.einsum('bhcd,bhde->bhce', kcd_c, S)
        o = jnp.einsum('bhcd,bhde->bhce', qg_c, S) + jnp.einsum('bhij,bhje->bhie', inner_c, vn)
        S = S * jnp.exp(gl)[..., None, None] + jnp.einsum('bhcd,bhce->bhde', kend_c, vn)
        return S, o

    S, o = lax.scan(step, S0, (val, kcd, inner, qg, kend, glast))
    o = jnp.moveaxis(jnp.moveaxis(o, 0, 1), 2, 3).reshape(B, Tp, H, D)[:, :T]
    return o, S


def l2norm(x):
    return x * lax.rsqrt(jnp.sum(x * x, axis=-1, keepdims=True) + RMS_EPS)


def trunk_layer(x, p, kv_past, win_buf, gdn_state, gdn_buf, ffn_buf,
                w_in, nsa_pe, nsa_phi, gdn_conv_w, gdn_A_log, gdn_dt_bias, gdn_norm_w, w_out,
                ln_g, ln_b, ffn_w_up, ffn_conv_w, ffn_w_down, ple_w_proj, ple_w_gate):
    B, T, _ = x.shape
    H, Hk, D = N_NSA_HEADS, N_NSA_KV, HEAD_DIM
    f32 = jnp.float32
    past = 0 if kv_past is None else kv_past.shape[1]
    q_pos = past + jnp.arange(T, dtype=jnp.int32)
    h = x @ w_in
    q = h[..., C_Q:C_KV].reshape(B, T, H, D)
    kv_new = h[..., C_KV:C_WIN].reshape(B, T, 4, Hk, D)
    win_new = h[..., C_WIN:C_GATE].reshape(B, T, 2, Hk, D)
    gates = jax.nn.sigmoid(h[..., C_GATE:C_GQKV]).reshape(B, T, H, 3)
    kv_all = kv_new if kv_past is None else jnp.concatenate([kv_past.astype(kv_new.dtype), kv_new], axis=1)
    L = kv_all.shape[1]
    Lp = -(-L // SLC_BLOCK) * SLC_BLOCK
    kv_all = jnp.pad(kv_all, ((0, 0), (0, Lp - L), (0, 0), (0, 0), (0, 0)))
    kc = compress_blocks(kv_all[:, :, 0], nsa_pe[0], nsa_phi[0])
    vc = compress_blocks(kv_all[:, :, 1], nsa_pe[1], nsa_phi[1])
    sel = kv_all[:, :, 2:4].reshape(B, Lp // SLC_BLOCK, SLC_BLOCK, 2, Hk, D).transpose(3, 0, 4, 1, 2, 5)
    ks_blk, vs_blk = sel[0], sel[1]
    if T > Q_BLOCK and T % Q_BLOCK == 0:
        nb = T // Q_BLOCK
        o_cmp, o_slc = lax.map(lambda a: nsa_block(a[0], a[1], kc, vc, ks_blk, vs_blk),
                               (q.reshape(B, nb, Q_BLOCK, H, D).swapaxes(0, 1), q_pos.reshape(nb, Q_BLOCK)))
        o_cmp = o_cmp.swapaxes(0, 1).reshape(B, T, H, D)
        o_slc = o_slc.swapaxes(0, 1).reshape(B, T, H, D)
    else:
        o_cmp, o_slc = nsa_block(q, q_pos, kc, vc, ks_blk, vs_blk)
    if win_buf is None:
        nb = T // Q_BLOCK
        nw = WINDOW // Q_BLOCK + 1
        wp = jnp.pad(win_new, ((0, 0), (WINDOW, 0), (0, 0), (0, 0), (0, 0))).reshape(B, nb + nw - 1, Q_BLOCK, 2, Hk, D)
        wk = jnp.stack([wp[:, i:i + nb] for i in range(nw)], axis=2).reshape(B, nb, nw * Q_BLOCK, 2, Hk, D)
        k_pos = jnp.arange(nb)[:, None] * Q_BLOCK - WINDOW + jnp.arange(nw * Q_BLOCK)[None, :]
        o_win = window_attend(q.reshape(B, nb, Q_BLOCK, H, D), q_pos.reshape(nb, Q_BLOCK),
                              wk[:, :, :, 0], wk[:, :, :, 1], k_pos)
        new_win = win_new[:, -min(WINDOW, T):]
    else:
        wb = win_buf.shape[1]
        wk = jnp.concatenate([win_buf.astype(win_new.dtype), win_new], axis=1)
        k_pos = past - wb + jnp.arange(wb + T)
        o_win = window_attend(q[:, None], q_pos[None], wk[:, None, :, 0], wk[:, None, :, 1], k_pos[None])
        new_win = wk[:, -wb:]
    o_win = o_win.reshape(B, T, H, D)
    o_nsa = gates[..., 0:1] * o_cmp + gates[..., 1:2] * o_slc + gates[..., 2:3] * o_win
    if gdn_buf is None:
        gdn_buf = jnp.zeros((B, GDN_CONV - 1, GDN_QKV), x.dtype)
        gdn_state = jnp.zeros((B, N_GDN_HEADS, HEAD_DIM, HEAD_DIM), f32)
    qkv, new_gbuf = causal_dwconv(h[..., C_GQKV:C_GA], gdn_buf, gdn_conv_w)
    qkv = jax.nn.silu(qkv.astype(f32)).reshape(B, T, 3, N_GDN_HEADS, HEAD_DIM)
    gq = l2norm(qkv[:, :, 0]) * (HEAD_DIM ** -0.5)
    gk = l2norm(qkv[:, :, 1])
    gv = qkv[:, :, 2]
    g = -jnp.exp(gdn_A_log.astype(f32)) * jax.nn.softplus(h[..., C_GA:C_GB].astype(f32) + gdn_dt_bias.astype(f32))
    beta = jax.nn.sigmoid(h[..., C_GB:C_GZ].astype(f32))
    o_g, S = gated_delta_chunked(gq, gk, gv, g, beta, gdn_state.astype(f32))
    z = h[..., C_GZ:IN_WIDTH].astype(f32).reshape(B, T, N_GDN_HEADS, HEAD_DIM)
    o_g = o_g * lax.rsqrt(jnp.mean(o_g * o_g, axis=-1, keepdims=True) + RMS_EPS) * gdn_norm_w.astype(f32) * jax.nn.silu(z)
    mix = jnp.concatenate([o_nsa.reshape(B, T, NSA_WIDTH), o_g.reshape(B, T, GDN_WIDTH).astype(x.dtype)], axis=-1)
    x = layer_norm(DN_ALPHA * x + mix @ w_out, ln_g[0], ln_b[0])
    if ffn_buf is None:
        ffn_buf = jnp.zeros((B, FFN_CONV - 1, D_FF), x.dtype)
    up = x @ ffn_w_up
    hg, new_fbuf = causal_dwconv(up[..., :D_FF], ffn_buf, ffn_conv_w)
    ffn = (jax.nn.gelu(hg, approximate=False) * up[..., D_FF:]) @ ffn_w_down
    x = layer_norm(DN_ALPHA * x + ffn, ln_g[1], ln_b[1])
    ple = jax.nn.sigmoid(x @ ple_w_gate) * (p.astype(x.dtype) @ ple_w_proj)
    x = layer_norm(DN_ALPHA * x + ple, ln_g[2], ln_b[2])
    return x, (kv_new, new_win, S, new_gbuf, new_fbuf)


def setup_inputs(seed: int = 0) -> dict:
    key = jax.random.key(seed)
    ks = jax.random.split(key, 26)
    f32 = jnp.float32

    def nrm(k, shape, s=1.0):
        return jax.random.normal(k, shape, f32) * s

    n_pages = PAST_LEN // PAGE_SIZE
    n_used = DEC_BATCH * n_pages
    n_pool = (5 * n_used + 3) // 4
    wb = min(WINDOW, PAST_LEN)
    col_scale = jnp.ones((IN_WIDTH,), f32)
    for lo, hi in V_COLS:
        col_scale = col_scale.at[lo:hi].set(DN_BETA)
    page_table = jax.random.permutation(ks[5], n_pool)[:n_used].reshape(DEC_BATCH, n_pages).astype(jnp.int32)
    dt = jnp.exp(jax.random.uniform(ks[12], (DEPTH, N_GDN_HEADS), f32, math.log(1e-3), math.log(1e-1)))
    return {
        "x_prompt": nrm(ks[0], (BATCH, SEQ, D_MODEL)),
        "x_sample": nrm(ks[1], (DEC_BATCH, DEC_SEQ, D_MODEL)),
        "cache_nsa_kv": nrm(ks[2], (DEPTH, n_pool, PAGE_SIZE, 4, N_NSA_KV, HEAD_DIM)),
        "state_nsa_win": nrm(ks[3], (DEPTH, DEC_BATCH, wb, 2, N_NSA_KV, HEAD_DIM)),
        "state_gdn": nrm(ks[4], (DEPTH, DEC_BATCH, N_GDN_HEADS, HEAD_DIM, HEAD_DIM), 0.1),
        "state_gdn_conv": nrm(ks[6], (DEPTH, DEC_BATCH, GDN_CONV - 1, GDN_QKV)),
        "state_ffn_conv": nrm(ks[7], (DEPTH, DEC_BATCH, FFN_CONV - 1, D_FF), 0.5),
        "page_table": page_table,
        "p_prompt": nrm(ks[8], (DEPTH, BATCH, SEQ, PLE_DIM)),
        "p_sample": nrm(ks[9], (DEPTH, DEC_BATCH, DEC_SEQ, PLE_DIM)),
        "w_in": nrm(ks[10], (DEPTH, D_MODEL, IN_WIDTH), D_MODEL ** -0.5) * col_scale,
        "nsa_pe": nrm(ks[11], (DEPTH, 2, CMP_BLOCK, HEAD_DIM), 0.02),
        "nsa_phi": nrm(ks[13], (DEPTH, 2, HEAD_DIM, HEAD_DIM), (CMP_BLOCK / HEAD_DIM) ** 0.5),
        "gdn_conv_w": nrm(ks[14], (DEPTH, GDN_CONV, GDN_QKV), GDN_CONV ** -0.5),
        "gdn_A_log": jnp.log(jax.random.uniform(ks[15], (DEPTH, N_GDN_HEADS), f32, 1.0, 16.0)),
        "gdn_dt_bias": dt + jnp.log(-jnp.expm1(-dt)),
        "gdn_norm_w": 1.0 + nrm(ks[16], (DEPTH, HEAD_DIM), 0.02),
        "w_out": nrm(ks[17], (DEPTH, MIX_WIDTH, D_MODEL), MIX_WIDTH ** -0.5 * DN_BETA),
        "ln_g": 1.0 + nrm(ks[18], (DEPTH, 3, D_MODEL), 0.02),
        "ln_b": nrm(ks[19], (DEPTH, 3, D_MODEL), 0.02),
        "ffn_w_up": nrm(ks[20], (DEPTH, D_MODEL, 2 * D_FF), D_MODEL ** -0.5 * DN_BETA),
        "ffn_conv_w": nrm(ks[21], (DEPTH, FFN_CONV, D_FF), FFN_CONV ** -0.5),
        "ffn_w_down": nrm(ks[22], (DEPTH, D_FF, D_MODEL), D_FF ** -0.5 * DN_BETA),
        "ple_w_proj": nrm(ks[23], (DEPTH, PLE_DIM, D_MODEL), PLE_DIM ** -0.5 * DN_BETA),
        "ple_w_gate": nrm(ks[24], (DEPTH, D_MODEL, D_MODEL), D_MODEL ** -0.5),
    }


def reference(x_prompt, x_sample, cache_nsa_kv, state_nsa_win, state_gdn, state_gdn_conv, state_ffn_conv,
              page_table, p_prompt, p_sample, w_in, nsa_pe, nsa_phi, gdn_conv_w, gdn_A_log, gdn_dt_bias,
              gdn_norm_w, w_out, ln_g, ln_b, ffn_w_up, ffn_conv_w, ffn_w_down, ple_w_proj, ple_w_gate):
    dec_b, n_pages = page_table.shape
    past_len = n_pages * cache_nsa_kv.shape[2]
    xp, xs = x_prompt, x_sample
    st_p, st_s = [], []
    for l in range(DEPTH):
        prm = (w_in[l], nsa_pe[l], nsa_phi[l], gdn_conv_w[l], gdn_A_log[l], gdn_dt_bias[l], gdn_norm_w[l],
               w_out[l], ln_g[l], ln_b[l], ffn_w_up[l], ffn_conv_w[l], ffn_w_down[l], ple_w_proj[l], ple_w_gate[l])
        xp, sp = trunk_layer(xp, p_prompt[l], None, None, None, None, None, *prm)
        kv_past = cache_nsa_kv[l][page_table].reshape(dec_b, past_len, 4, N_NSA_KV, HEAD_DIM)
        xs, ss = trunk_layer(xs, p_sample[l], kv_past, state_nsa_win[l], state_gdn[l], state_gdn_conv[l],
                             state_ffn_conv[l], *prm)
        st_p.append(sp)
        st_s.append(ss)
    kv_rows_prompt = jnp.stack([s[0] for s in st_p])
    kv_rows_sample = jnp.stack([s[0] for s in st_s])
    win_prompt = jnp.stack([s[1] for s in st_p])
    win_sample = jnp.stack([s[1] for s in st_s])
    gdn_state_prompt = jnp.stack([s[2] for s in st_p])
    gdn_state_sample = jnp.stack([s[2] for s in st_s])
    gdn_conv_prompt = jnp.stack([s[3] for s in st_p])
    gdn_conv_sample = jnp.stack([s[3] for s in st_s])
    ffn_conv_prompt = jnp.stack([s[4] for s in st_p])
    ffn_conv_sample = jnp.stack([s[4] for s in st_s])
    return (xp, xs, kv_rows_prompt, kv_rows_sample, win_prompt, win_sample, gdn_state_prompt, gdn_state_sample,
            gdn_conv_prompt, gdn_conv_sample, ffn_conv_prompt, ffn_conv_sample)
```

```python
import os
import numpy as np
import ml_dtypes
from contextlib import ExitStack
import concourse.bass as bass
import concourse.mybir as mybir
from concourse.bass_utils import run_bass_kernel_spmd

F32 = mybir.dt.float32
BF16 = mybir.dt.bfloat16
I32 = mybir.dt.int32
AF = mybir.ActivationFunctionType
ALU = mybir.AluOpType
AX = mybir.AxisListType
bf = ml_dtypes.bfloat16

NEG = -60000.0
DM = 1024
INW = 3368
DFF = 2816
PLED = 256
C_KV, C_WIN, C_GATE, C_GQKV, C_GA, C_GB, C_GZ = 512, 1024, 1280, 1304, 2840, 2848, 2856
DEPTH = 2
ALPHA = float((2 * DEPTH) ** 0.25)
LN_EPS = 1e-5
RMS_EPS = 1e-6
NCORES = 8


class Buf:
    __slots__ = ("t", "w", "r", "name", "excl")

    def __init__(self, t, name="", excl=False):
        self.t = t
        self.w = {}
        self.r = {}
        self.name = name
        self.excl = excl

    def __getitem__(self, idx):
        return self.t[idx]


class K:
    NDMA = 32

    def __init__(self, nc):
        self.nc = nc
        self.eng = {"pe": nc.tensor, "act": nc.scalar, "dve": nc.vector, "pool": nc.gpsimd, "sp": nc.sync}
        self.sem = {}
        self.cnt = {}
        for e in self.eng:
            self.sem[e] = nc.alloc_semaphore("sem_" + e)
            self.cnt[e] = 0
        for j in range(self.NDMA):
            key = ("dma", j)
            self.sem[key] = nc.alloc_semaphore("sem_dma%d" % j)
            self.cnt[key] = 0
        self.dma_rr = 0
        self.seen = {e: {} for e in self.eng}
        self.nins = 0
        self.uid = 0

    def name(self, s):
        self.uid += 1
        return "%s_%d" % (s, self.uid)

    def sb(self, es, name, shape, dt=F32):
        t = es.enter_context(self.nc.sbuf_tensor(self.name(name), list(shape), dt))
        return Buf(t, name)

    def ps(self, name, shape, dt=F32):
        return Buf(self.nc.alloc_psum_tensor(self.name(name), list(shape), dt), name, excl=True)

    def dram(self, name, shape, dt=F32, kind="Internal"):
        return Buf(self.nc.dram_tensor(name, list(shape), dt, kind=kind).ap(), name)

    def _wait(self, e, deps):
        eng = self.eng[e]
        seen = self.seen[e]
        for key, v in deps.items():
            if v <= 0 or seen.get(key, 0) >= v:
                continue
            eng.wait_ge(self.sem[key], v)
            self.nins += 1
            seen[key] = v

    @staticmethod
    def _deps(reads, writes):
        deps = {}
        for b in reads:
            for key, v in b.w.items():
                if deps.get(key, 0) < v:
                    deps[key] = v
        for b in writes:
            for key, v in b.w.items():
                if deps.get(key, 0) < v:
                    deps[key] = v
            for key, v in b.r.items():
                if deps.get(key, 0) < v:
                    deps[key] = v
        return deps

    @staticmethod
    def _record(key, val, reads, writes):
        for b in reads:
            if b.r.get(key, 0) < val:
                b.r[key] = val
        for b in writes:
            b.w.clear()
            b.w[key] = val
            b.r.clear()

    def op(self, e, fn, reads=(), writes=(), inc=True):
        ex = [b for b in reads if b.excl]
        if ex:
            writes = list(writes) + ex
        deps = self._deps(reads, writes)
        if e == "pe":
            deps.pop("pe", None)
        if e in deps and deps[e] > self.cnt[e]:
            deps[e] = self.cnt[e]
        self._wait(e, deps)
        ins = fn(self.eng[e])
        self.nins += 1
        if inc:
            self.cnt[e] += 1
            ins.then_inc(self.sem[e], 1)
            val = self.cnt[e]
        else:
            val = self.cnt[e] + 1
        self._record(e, val, reads, writes)
        return ins

    def _dma_issue(self, q, reads, writes, fn):
        deps = self._deps(reads, writes)
        j = self.dma_rr
        self.dma_rr = (self.dma_rr + 1) % self.NDMA
        key = ("dma", j)
        if self.cnt[key] > 0 and deps.get(key, 0) < self.cnt[key]:
            deps[key] = self.cnt[key]
        if q in deps and deps[q] > self.cnt[q]:
            deps[q] = self.cnt[q]
        self._wait(q, deps)
        ins = fn(self.eng[q])
        self.nins += 1
        self.cnt[key] += 16
        ins.then_inc(self.sem[key], 16)
        self._record(key, self.cnt[key], reads, writes)
        return ins

    def dma(self, q, out, in_, reads=(), writes=(), **kw):
        return self._dma_issue(q, reads, writes, lambda e: e.dma_start(out=out, in_=in_, **kw))

    def gather(self, out, table, idx, reads=(), writes=()):
        return self._dma_issue(
            "pool", reads, writes,
            lambda e: e.indirect_dma_start(out=out, out_offset=None, in_=table,
                                           in_offset=bass.IndirectOffsetOnAxis(ap=idx, axis=0)))

    def barrier(self):
        full = dict(self.cnt)
        for e in self.eng:
            deps = {key: v for key, v in full.items() if key != e}
            self._wait(e, deps)

    def finish(self):
        deps = {key: v for key, v in self.cnt.items() if key != "sp"}
        self._wait("sp", deps)


def make_consts(T, NS=16, NPG=16):
    NJ = T // 128
    c = {}
    p = np.arange(128)
    ident = np.eye(128, dtype=np.float32)
    U = (p[:, None] <= p[None, :]).astype(np.float32)
    ones = np.ones((128, 128), np.float32)
    mask_incl = np.where(p[:, None] >= p[None, :], 0.0, -1e4).astype(np.float32)
    maskT_incl = np.where(p[None, :] >= p[:, None], 0.0, -1e4).astype(np.float32)
    strict01 = (p[:, None] > p[None, :]).astype(np.float32)
    tk = np.zeros((128, NJ, 64), np.float32)
    blk = np.arange(64)
    for j in range(NJ):
        cur = (128 * j + p) // 64
        fut = blk[None, :] > cur[:, None]
        forced = (blk[None, :] == 0) | (((cur[:, None] - blk[None, :]) < 2) & ~fut)
        tk[:, j, :] = np.where(fut, -1e4, np.where(forced, 1e4, 0.0))
    c["c32"] = np.concatenate([ident, U, ones, mask_incl, maskT_incl, strict01, tk.reshape(128, NJ * 64)], axis=1)
    causal = np.where(p[:, None] <= p[None, :], 0.0, NEG)
    winedge = np.where(p[:, None] > p[None, :], 0.0, NEG)
    pool2 = (p[:, None] // 2 == np.arange(64)[None, :]).astype(np.float32)
    avg = (p[:, None] // 32 == np.arange(4)[None, :]).astype(np.float32) / 32.0
    c["cb128"] = np.concatenate([np.tile(causal, (1, 4)), np.tile(winedge, (1, 4)), ident, pool2,
                                 avg, np.ones((128, 4))], axis=1).astype(bf)
    kp = np.arange(T)
    Eall = (kp[None, :] // 64 == np.arange(64)[:, None]).astype(np.float32)
    slopes = 2.0 ** (-np.arange(1, 9, dtype=np.float64))
    kauxrel = np.zeros((128, NJ, 128), np.float32)
    for d in range(NJ):
        kauxrel[0, d, :] = -d
        kauxrel[1, d, :] = p
        kauxrel[2, d, :] = 1.0
    cc = np.arange(128)
    kcauxrel = np.zeros((128, NJ, 128), np.float32)
    for j in range(NJ):
        kcauxrel[0, j, :] = cc / 4.0 - j
        kcauxrel[2, j, :] = 1.0
    qaux = np.zeros((128, 2, 4, 128), np.float32)
    for n in range(2):
        for g in range(4):
            sl = slopes[4 * n + g]
            qaux[0, n, g, :] = sl * 8 * 128
            qaux[1, n, g, :] = sl * 8
            qaux[2, n, g, :] = -8.0 * sl * p
    cmpsel = np.zeros((128, NJ, 128), np.float32)
    for j in range(NJ):
        for r in range(4):
            if 4 * j + r < 128:
                cmpsel[r, j, 4 * j + r] = 1.0
        cmpsel[4, j, 4 * j + 4:] = 1.0
    vis = np.zeros((128, 4, 128), np.float32)
    for r in range(4):
        vis[r, :, :] = np.where(32 * r + 31 <= p, 0.0, NEG)[None, :]
    vis[4] = NEG
    Epad = np.zeros((128, T), np.float32)
    Epad[0:64] = Eall
    c["cbp"] = np.concatenate([kauxrel.reshape(128, -1), kcauxrel.reshape(128, -1), qaux.reshape(128, 1024),
                               cmpsel.reshape(128, -1), vis.reshape(128, 512), Epad], axis=1).astype(bf)
    PAST = NPG * 128
    NWT = 4
    al_s = np.zeros((128, 2, NPG, 4), np.float32)
    al_w = np.zeros((128, 2, NWT, 4), np.float32)
    al_c = np.zeros((128, NS, 2, 4), np.float32)
    for n in range(2):
        for g in range(4):
            sl = slopes[4 * n + g]
            for t in range(NPG):
                al_s[:, n, t, g] = -sl * (PAST - (t * 128 + p))
            for t in range(NWT):
                dist = NWT * 128 - (t * 128 + p)
                al_w[:, n, t, g] = np.where(dist < NWT * 128, -sl * dist, -1e4)
            al_c[:, :, n, g] = (-sl * (PAST - (32 * p + 15.5)))[:, None]
    GS = np.zeros((128, 2 * NS), np.float32)
    GS[np.arange(NS * 8), np.arange(NS * 8) // 4] = 1.0
    NB33 = NPG * 2 + 1
    tkb = np.zeros((128, 40), np.float32)
    tkb[:, 0] = 1e4
    tkb[:, NB33 - 2] = 1e4
    tkb[:, NB33 - 1] = 1e4
    piota = np.stack([2.0 * p, 2.0 * p], axis=1).astype(np.float32)
    c["cs32"] = np.concatenate([al_s.reshape(128, -1), al_w.reshape(128, -1), al_c.reshape(128, -1), GS, tkb, piota], axis=1).astype(np.float32)
    OH = np.zeros((128, 2 * NS, 128), np.float32)
    for r in range(2 * NS):
        OH[r, r, :] = 1.0
    pool33 = np.zeros((128, NB33 + 1), np.float32)
    for cidx in range(NPG * 4):
        pool33[cidx, cidx // 2] = 1.0
    pool33[:, NB33] = 1.0
    c["csb"] = np.concatenate([OH.reshape(128, -1), pool33], axis=1).astype(bf)
    return c


def build(T, NS, NPOOL, NPG=16, WB=512, parts=("prompt", "sample")):
    NJ = T // 128
    PAST = NPG * 128
    NWT = WB // 128
    nc = bass.Bass("TRN2", target_bir_lowering=False)
    k = K(nc)
    do_prompt = "prompt" in parts
    do_sample = "sample" in parts

    def din(name, shape, dt=F32):
        return k.dram(name, shape, dt, kind="ExternalInput")

    def dout(name, shape, dt=F32):
        return k.dram(name, shape, dt, kind="ExternalOutput")

    xp = din("xp", [T, DM])
    pp = din("pp", [DEPTH, T, PLED])
    xs = din("xs", [NS, DM])
    pps = din("pps", [DEPTH, NS, PLED])
    pool = din("pool", [DEPTH * NPOOL * 128, 512])
    ptab = din("ptab", [1, NS * NPG], I32)
    st_win = din("st_win", [DEPTH, NS, WB, 256])
    st_gdn = din("st_gdn", [DEPTH, NS, 8, 64, 64])
    st_gconv = din("st_gconv", [DEPTH, NS, 3, 1536])
    st_fconv = din("st_fconv", [DEPTH, NS, 2, DFF])
    w_in = din("w_in", [DEPTH, DM, INW])
    nsa_pe = din("nsa_pe", [DEPTH, 2, 32, 64])
    nsa_phi = din("nsa_phi", [DEPTH, 2, 64, 64])
    gconv_w = din("gconv_w", [DEPTH, 4, 1536])
    A_log = din("A_log", [DEPTH, 8])
    dt_bias = din("dt_bias", [DEPTH, 8])
    gnorm_w = din("gnorm_w", [DEPTH, 64])
    w_out = din("w_out", [DEPTH, DM, DM])
    ln_g = din("ln_g", [DEPTH, 3, DM])
    ln_b = din("ln_b", [DEPTH, 3, DM])
    w_up = din("w_up", [DEPTH, DM, 2 * DFF])
    fconv_w = din("fconv_w", [DEPTH, 3, DFF])
    w_down = din("w_down", [DEPTH, DFF, DM])
    w_proj = din("w_proj", [DEPTH, PLED, DM])
    w_gate = din("w_gate", [DEPTH, DM, DM])
    NC32 = 6 * 128 + NJ * 64
    c32_d = din("c32", [128, NC32])
    NCB128 = 512 + 512 + 128 + 64 + 4 + 4
    cb128_d = din("cb128", [128, NCB128], BF16)
    NCS32 = 2 * NPG * 4 + 2 * NWT * 4 + NS * 8 + 2 * NS + 40 + 2
    cs32_d = din("cs32", [128, NCS32])
    NCSB = 2 * NS * 128 + NPG * 2 + 2
    csb_d = din("csb", [128, NCSB], BF16)
    NCBP = 3 * NJ * 128 + 1024 + 512 + T
    cbp_d = din("cbp", [128, NCBP], BF16)
    y_p = dout("y_p", [T, DM])
    y_s = dout("y_s", [NS, DM])
    kv_p = dout("kv_p", [DEPTH, T, 512])
    kv_s = dout("kv_s", [DEPTH, NS, 512])
    win_p = dout("win_p", [DEPTH, WB, 256])
    win_s = dout("win_s", [DEPTH, NS, WB, 256])
    gst_p = dout("gst_p", [DEPTH, 8, 64, 64])
    gst_s = dout("gst_s", [DEPTH, NS, 8, 64, 64])
    gcv_p = dout("gcv_p", [DEPTH, 3, 1536])
    gcv_s = dout("gcv_s", [DEPTH, NS, 3, 1536])
    fcv_p = dout("fcv_p", [DEPTH, 2, DFF])
    fcv_s = dout("fcv_s", [DEPTH, NS, 2, DFF])
    wb_in = k.dram("wb_in", [DEPTH, DM, INW], BF16)
    wb_out = k.dram("wb_out", [DEPTH, DM, DM], BF16)
    wb_up = k.dram("wb_up", [DEPTH, DM, 2 * DFF], BF16)
    wb_down = k.dram("wb_down", [DEPTH, DFF, DM], BF16)
    wb_gate = k.dram("wb_gate", [DEPTH, DM, DM], BF16)
    wb_proj = k.dram("wb_proj", [DEPTH, PLED, DM], BF16)
    xT_d = k.dram("xT_d", [DM, T], BF16)
    mixT_d = k.dram("mixT_d", [DM, T], BF16)
    x1_d = k.dram("x1_d", [T, DM], F32)
    xsT_d = k.dram("xsT_d", [DM, NS], BF16)
    mixsT_d = k.dram("mixsT_d", [DM, NS], BF16)
    xs1_d = k.dram("xs1_d", [NS, DM], F32)

    pb = [k.ps("pb%d" % i, [128, 512]) for i in range(8)]

    with ExitStack() as gs:
        c32 = k.sb(gs, "c32", [128, NC32])
        k.dma("sp", c32[:], c32_d[:], writes=[c32])
        cb128 = k.sb(gs, "cb128", [128, NCB128], BF16)
        k.dma("sp", cb128[:], cb128_d[:], writes=[cb128])
        ident = c32.t[:, 0:128]
        Umat = c32.t[:, 128:256]
        ones32 = c32.t[:, 256:384]
        mask_incl = c32.t[:, 384:512]
        maskT_incl = c32.t[:, 512:640]
        strict01 = c32.t[:, 640:768]
        tkb = c32.t[:, 768:768 + NJ * 64]
        causal4 = cb128.t[:, 0:512]
        winedge4 = cb128.t[:, 512:1024]
        identb = cb128.t[:, 1024:1152]
        pool2 = cb128.t[:, 1152:1216]
        avg4 = cb128.t[:, 1216:1220]
        epsc = k.sb(gs, "epsc", [128, 2])
        k.op("pool", lambda e: e.memset(epsc[:, 0:1], RMS_EPS), writes=[epsc])
        k.op("pool", lambda e: e.memset(epsc[:, 1:2], LN_EPS), writes=[epsc])

        cast_rr = [0]

        def cast(out, in_, reads, writes, psum=False):
            e = ("dve", "act")[cast_rr[0] % 2] if psum else ("dve", "act", "pool")[cast_rr[0] % 3]
            cast_rr[0] += 1
            if e == "act":
                k.op("act", lambda en: en.copy(out, in_), reads=reads, writes=writes)
            else:
                k.op(e, lambda en: en.tensor_copy(out, in_), reads=reads, writes=writes)

        with ExitStack() as es:
            stg = [k.sb(es, "wstg%d" % i, [128, 2 * DFF]) for i in range(2)]
            stgb = [k.sb(es, "wstgb%d" % i, [128, 2 * DFF], BF16) for i in range(2)]
            it = 0
            for l in range(DEPTH):
                for (src, dst, rows, cols) in ((w_in, wb_in, DM, INW), (w_out, wb_out, DM, DM), (w_up, wb_up, DM, 2 * DFF),
                                               (w_down, wb_down, DFF, DM), (w_gate, wb_gate, DM, DM), (w_proj, wb_proj, PLED, DM)):
                    for r0 in range(0, rows, 128):
                        s, sb_ = stg[it % 2], stgb[it % 2]
                        it += 1
                        k.dma("sp", s[:, 0:cols], src[l, r0:r0 + 128, :], writes=[s])
                        cast(sb_[:, 0:cols], s[:, 0:cols], [s], [sb_])
                        k.dma("pool", dst[l, r0:r0 + 128, :], sb_[:, 0:cols], reads=[sb_], writes=[dst])
        k.barrier()

        def transpose_to_xT(es_name, src_tile, ntok, dstT, col0, trp, trs):
            for half in range(2):
                for c4 in range(4):
                    c = half * 4 + c4
                    k.op("pe", lambda e, c=c, c4=c4: e.transpose(trp[:, c4 * 128:c4 * 128 + ntok],
                                                               src_tile[0:ntok, c * 128:(c + 1) * 128], ident[0:ntok, 0:ntok]),
                         reads=[src_tile, c32], writes=[trp])
                cast(trs[:, half * 4:(half + 1) * 4, 0:ntok],
                     trp[:, :].rearrange("p (c t) -> p c t", c=4)[:, :, 0:ntok], [trp], [trs], psum=True)
            k.dma("pool", dstT.t.rearrange("(c p) t -> p c t", p=128)[:, :, col0:col0 + ntok], trs[:, :, 0:ntok],
                  reads=[trs], writes=[dstT])

        with ExitStack() as es:
            xt_in = [k.sb(es, "xt_in%d" % i, [128, DM]) for i in range(2)]
            trs = [k.sb(es, "trs%d" % i, [128, 8, 128], BF16) for i in range(2)]
            if do_prompt:
                for j in range(NJ):
                    xt = xt_in[j % 2]
                    k.dma("sp", xt[:], xp[j * 128:(j + 1) * 128, :], writes=[xt])
                    transpose_to_xT("x0", xt, 128, xT_d, j * 128, pb[j % 2], trs[j % 2])
            if do_sample:
                xt = xt_in[0]
                k.dma("sp", xt[0:NS, :], xs[:, :], writes=[xt])
                transpose_to_xT("xs0", xt, NS, xsT_d, 0, pb[2], trs[0])
        k.barrier()

        def gdn_prompt(l):
            with ExitStack() as es:
                wsrc = wb_in.t[l].rearrange("(c p) n -> p c n", p=128)
                wg = k.sb(es, "wg", [128, 8, 8, 192], BF16)
                for blk in range(3):
                    for kc in range(8):
                        k.dma("sp", wg[:, kc, :, blk * 64:(blk + 1) * 64],
                              wb_in.t[l, kc * 128:(kc + 1) * 128, C_GQKV + blk * 512:C_GQKV + (blk + 1) * 512].rearrange("p (h d) -> p h d", h=8),
                              reads=[wb_in], writes=[wg])
                wab = k.sb(es, "wab", [128, 8, 16], BF16)
                k.dma("sp", wab[:], wsrc[:, :, C_GA:C_GA + 16], reads=[wb_in], writes=[wab])
                wz = k.sb(es, "wz", [128, 8, 512], BF16)
                k.dma("sp", wz[:], wsrc[:, :, C_GZ:C_GZ + 512], reads=[wb_in], writes=[wz])
                cw = k.sb(es, "cw", [64, 8, 3, 4])
                for h in range(8):
                    for blk in range(3):
                        c0 = blk * 512 + h * 64
                        k.dma("sp", cw[:, h, blk, :], gconv_w[l][:, c0:c0 + 64].rearrange("w d -> d w"), writes=[cw],
                              allow_slow_non_contiguous=True)
                dtb = k.sb(es, "dtb", [128, 8])
                k.dma("sp", dtb[:], dt_bias[l:l + 1, :].partition_broadcast(128), writes=[dtb])
                negA = k.sb(es, "negA", [128, 8])
                k.dma("sp", negA[:], A_log[l:l + 1, :].partition_broadcast(128), writes=[negA])
                k.op("act", lambda e: e.activation(negA[:], negA[:], AF.Exp), reads=[negA], writes=[negA])
                k.op("dve", lambda e: e.tensor_scalar(negA[:], negA[:], -1.0, None, ALU.mult), reads=[negA], writes=[negA])
                nw = k.sb(es, "nw", [128, 64])
                k.dma("sp", nw[:], gnorm_w[l:l + 1, :].partition_broadcast(128), writes=[nw])
                S = [k.sb(es, "S%d" % h, [128, 64]) for h in range(8)]
                carry = [k.sb(es, "carry%d" % h, [64, 3, 3]) for h in range(8)]
                for h in range(8):
                    k.op("pool", lambda e: e.memset(S[h][:], 0.0), writes=[S[h]])
                    k.op("pool", lambda e: e.memset(carry[h][:], 0.0), writes=[carry[h]])
                xTt = [k.sb(es, "gxTt%d" % i, [128, 8, 128], BF16) for i in range(2)]
                tmp8 = k.sb(es, "tmp8", [128, 8])
                gtok = k.sb(es, "gtok", [128, 8])
                btok = k.sb(es, "btok", [128, 8])
                gctok = k.sb(es, "gctok", [128, 8])
                ngctok = k.sb(es, "ngctok", [128, 8])
                egctok = k.sb(es, "egctok", [128, 8])
                nz = k.sb(es, "nz", [128, 8, 64])
                og = k.sb(es, "og", [128, 512])
                ogT = k.sb(es, "ogT", [128, 4, 128], BF16)
                NSLOT = 4
                RG = []
                for s_ in range(NSLOT):
                    row = []
                    for r in range(4):
                        rb = Buf(pb[s_].t[:, r * 128:(r + 1) * 128], "R%d_%d" % (s_, r), excl=True)
                        rb.w = pb[s_].w
                        rb.r = pb[s_].r
                        row.append(rb)
                    RG.append(row)
                PAB, PZ, PTR = pb[4], pb[5], pb[6]

                def mk(s):
                    d = {}
                    for nm, shp in (("ext", [64, 3, 131]), ("y", [64, 3, 128]), ("ys", [64, 3, 128]), ("sq", [64, 256]), ("rs", [64, 256]),
                                    ("qTf", [64, 128]), ("kTf", [64, 128]), ("qgT", [128, 128]), ("rhsX", [128, 128]), ("kend", [128, 64]),
                                    ("Ug", [128, 128]), ("X1", [128, 128]), ("dec", [128, 128]), ("X2", [128, 128]), ("decT", [128, 128]),
                                    ("egcB", [64, 128]), ("glc", [128, 4]), ("tN", [128, 128]), ("N", [128, 128]), ("NT", [128, 128]),
                                    ("P0", [128, 128]), ("P1", [128, 128]), ("Q1", [128, 128]), ("W0", [128, 128]), ("W1", [128, 128]),
                                    ("innerT", [128, 128]), ("val", [128, 64]), ("kcdT", [64, 128]), ("vn", [128, 64]), ("junk", [128, 64]),
                                    ("rstd", [128, 2])):
                        d[nm] = k.sb(es, "%s_%d" % (nm, s), shp)
                    k.op("pool", lambda e: e.memset(d["qgT"][:], 0.0), writes=[d["qgT"]])
                    return d
                WK = [mk(s) for s in range(NSLOT)]
                id64 = ident[0:64, 0:64]
                ones64 = ones32[0:64, 0:64]

                def head_chain(h, i, slot, xt):
                    R = RG[slot]
                    w = WK[slot]
                    bank = pb[slot].t
                    ext, y, ys = w["ext"], w["y"], w["ys"]
                    for blk in range(3):
                        for kc in range(8):
                            k.op("pe", lambda e: e.matmul(R[blk][0:64, :], wg[:, kc, h, blk * 64:(blk + 1) * 64], xt[:, kc, :],
                                                          start=(kc == 0), stop=(kc == 7)), reads=[wg, xt], writes=[R[blk]], inc=(kc == 7))
                    yield
                    k.op("pool", lambda e: e.tensor_copy(ext[:, :, 0:3], carry[h][:]), reads=[carry[h]], writes=[ext])
                    k.op("act", lambda e: e.copy(ext[:, :, 3:131], bank[0:64, 0:384].rearrange("p (b t) -> p b t", b=3)),
                         reads=[R[0], R[1], R[2]], writes=[ext])
                    k.op("pool", lambda e: e.tensor_copy(carry[h][:], ext[:, :, 128:131]), reads=[ext], writes=[carry[h]])
                    yield
                    for blk in range(3):
                        eng = "dve"
                        k.op(eng, lambda e: e.tensor_scalar(y[:, blk, :], ext[:, blk, 0:128], cw[:, h, blk, 0:1], None, ALU.mult),
                             reads=[ext, cw], writes=[y])
                        for tap in range(1, 4):
                            k.op(eng, lambda e: e.scalar_tensor_tensor(y[:, blk, :], ext[:, blk, tap:tap + 128], cw[:, h, blk, tap:tap + 1], y[:, blk, :],
                                                                       ALU.mult, ALU.add), reads=[ext, cw, y], writes=[y])
                    yield
                    k.op("act", lambda e: e.activation(ys[:], y[:], AF.Silu), reads=[y], writes=[ys])
                    k.op("pool", lambda e: e.tensor_tensor(w["sq"][:], ys[:, 0:2, :].rearrange("p b t -> p (b t)"), ys[:, 0:2, :].rearrange("p b t -> p (b t)"), ALU.mult),
                         reads=[ys], writes=[w["sq"]])
                    yield
                    k.op("pe", lambda e: e.matmul(R[0][0:64, :], ones64, w["sq"][:, 0:128], start=True, stop=True), reads=[c32, w["sq"]], writes=[R[0]])
                    k.op("pe", lambda e: e.matmul(R[1][0:64, :], ones64, w["sq"][:, 128:256], start=True, stop=True), reads=[c32, w["sq"]], writes=[R[1]])
                    yield
                    sub = int(os.environ.get("DEV_SUB", "9"))
                    if sub >= 1:
                        k.op("act", lambda e: e.activation(w["rs"][:], bank[0:64, 0:256], AF.Sqrt, bias=epsc[0:64, 0:1], scale=1.0),
                             reads=[R[0], R[1], epsc], writes=[w["rs"]])
                    if sub >= 2:
                        k.op("dve", lambda e: e.reciprocal(w["rs"][:], w["rs"][:]), reads=[w["rs"]], writes=[w["rs"]])
                    if sub >= 3:
                        k.op("dve", lambda e: e.scalar_tensor_tensor(w["qTf"][:], ys[:, 0, :], 0.125, w["rs"][:, 0:128], ALU.mult, ALU.mult),
                             reads=[ys, w["rs"]], writes=[w["qTf"]])
                    if sub >= 4:
                        k.op("dve", lambda e: e.tensor_tensor(w["kTf"][:], ys[:, 1, :], w["rs"][:, 128:256], ALU.mult), reads=[ys, w["rs"]], writes=[w["kTf"]])
                    yield
                    k.op("dve", lambda e: e.tensor_scalar(w["Ug"][:], Umat, gtok[:, h:h + 1], None, ALU.mult), reads=[c32, gtok], writes=[w["Ug"]])
                    k.op("pe", lambda e: e.matmul(R[3][:, :], ones32, w["Ug"][:], start=True, stop=True), reads=[c32, w["Ug"]], writes=[R[3]])
                    k.op("pe", lambda e: e.transpose(R[2][:, 0:64], w["kTf"][:], id64), reads=[w["kTf"], c32], writes=[R[2]])
                    k.op("pe", lambda e: e.transpose(R[2][:, 64:128], ys[:, 2, :], id64), reads=[ys, c32], writes=[R[2]])
                    yield
                    sub8 = int(os.environ.get("DEV_SUB8", "99"))
                    if sub8 >= 1:
                        k.op("dve", lambda e: e.scalar_tensor_tensor(w["X1"][:], R[3][:, :], -1.0, mask_incl, ALU.mult, ALU.add), reads=[R[3], c32], writes=[w["X1"]])
                    if sub8 >= 2:
                        k.op("dve", lambda e: e.tensor_tensor(w["X2"][:], R[3][:, :], maskT_incl, ALU.add), reads=[R[3], c32], writes=[w["X2"]])
                    if sub8 >= 3:
                        k.op("act", lambda e: e.activation(w["egcB"][:], R[3][0:64, :], AF.Exp), reads=[R[3]], writes=[w["egcB"]])
                    if sub8 >= 4:
                        k.op("act", lambda e: e.copy(w["glc"][:, 0:1], R[3][:, 127:128]), reads=[R[3]], writes=[w["glc"]])
                    if sub8 >= 5:
                        k.op("act", lambda e: e.activation(w["dec"][:], w["X1"][:], AF.Exp, bias=gctok[:, h:h + 1], scale=1.0), reads=[w["X1"], gctok], writes=[w["dec"]])
                    if sub8 >= 6:
                        k.op("act", lambda e: e.activation(w["decT"][:], w["X2"][:], AF.Exp, bias=ngctok[:, h:h + 1], scale=1.0), reads=[w["X2"], ngctok], writes=[w["decT"]])
                    if sub8 >= 7:
                        k.op("act", lambda e: e.activation(w["glc"][:, 1:2], w["glc"][:, 0:1], AF.Exp), reads=[w["glc"]], writes=[w["glc"]])
                    if sub8 >= 8:
                        k.op("act", lambda e: e.activation(w["glc"][:, 2:3], ngctok[:, h:h + 1], AF.Exp, bias=w["glc"][:, 0:1], scale=1.0),
                             reads=[w["glc"], ngctok], writes=[w["glc"]])
                    yield
                    k.op("dve", lambda e: e.tensor_tensor(w["qgT"][0:64, :], w["qTf"][:], w["egcB"][:], ALU.mult), reads=[w["qTf"], w["egcB"]], writes=[w["qgT"]])
                    k.op("dve", lambda e: e.tensor_scalar(w["rhsX"][:, 0:64], R[2][:, 64:128], btok[:, h:h + 1], None, ALU.mult),
                         reads=[R[2], btok], writes=[w["rhsX"]])
                    k.op("dve", lambda e: e.tensor_scalar(w["rhsX"][:, 64:128], R[2][:, 0:64], btok[:, h:h + 1], egctok[:, h:h + 1], ALU.mult, ALU.mult),
                         reads=[R[2], btok, egctok], writes=[w["rhsX"]])
                    k.op("dve", lambda e: e.tensor_scalar(w["kend"][:], R[2][:, 0:64], w["glc"][:, 2:3], None, ALU.mult), reads=[R[2], w["glc"]], writes=[w["kend"]])
                    yield
                    k.op("pe", lambda e: e.matmul(R[0][:, :], w["kTf"][:], w["kTf"][:], start=True, stop=True), reads=[w["kTf"]], writes=[R[0]])
                    k.op("pe", lambda e: e.matmul(R[1][:, :], w["kTf"][:], w["qTf"][:], start=True, stop=True), reads=[w["kTf"], w["qTf"]], writes=[R[1]])
                    yield
                    k.op("dve", lambda e: e.tensor_tensor(w["tN"][:], R[0][:, :], w["dec"][:], ALU.mult), reads=[R[0], w["dec"]], writes=[w["tN"]])
                    k.op("dve", lambda e: e.scalar_tensor_tensor(w["N"][:], w["tN"][:], btok[:, h:h + 1], strict01, ALU.mult, ALU.mult),
                         reads=[w["tN"], btok, c32], writes=[w["N"]])
                    k.op("dve", lambda e: e.tensor_tensor(w["innerT"][:], R[1][:, :], w["decT"][:], ALU.mult), reads=[R[1], w["decT"]], writes=[w["innerT"]])
                    k.op("pe", lambda e: e.transpose(R[2][:, :], w["N"][:], ident), reads=[w["N"], c32], writes=[R[2]])
                    yield
                    k.op("act", lambda e: e.copy(w["NT"][:], R[2][:, :]), reads=[R[2]], writes=[w["NT"]])
                    k.op("dve", lambda e: e.tensor_tensor(w["W0"][:], ident, R[2][:, :], ALU.subtract), reads=[c32, R[2]], writes=[w["W0"]])
                    yield
                    P, Q, Wc = w["N"], w["NT"], w["W0"]
                    Pn_l = [w["P0"], w["P1"]]
                    Qn_l = [w["Q1"], w["NT"]]
                    Wn_l = [w["W1"], w["W0"]]
                    for kk in range(1, 7):
                        Pn = Pn_l[kk % 2]
                        k.op("pe", lambda e: e.matmul(R[0][:, :], Q[:], P[:], start=True, stop=True), reads=[Q, P], writes=[R[0]])
                        if kk <= 5:
                            Qn = (w["NT"], w["Q1"])[kk % 2]
                            k.op("pe", lambda e: e.matmul(R[1][:, :], P[:], Q[:], start=True, stop=True), reads=[Q, P], writes=[R[1]])
                        yield
                        k.op("act", lambda e: e.copy(Pn[:], R[0][:, :]), reads=[R[0]], writes=[Pn])
                        if kk <= 5:
                            k.op("dve", lambda e: e.tensor_copy(Qn[:], R[1][:, :]), reads=[R[1]], writes=[Qn])
                        yield
                        Wn = Wn_l[(kk - 1) % 2]
                        k.op("pe", lambda e: e.matmul(R[2][:, :], Pn[:], Wc[:], start=True, stop=True), reads=[Pn, Wc], writes=[R[2]])
                        yield
                        k.op("dve", lambda e: e.tensor_tensor(Wn[:], Wc[:], R[2][:, :], ALU.add), reads=[Wc, R[2]], writes=[Wn])
                        P, Wc = Pn, Wn
                        if kk <= 5:
                            Q = Qn
                        yield
                    k.op("pe", lambda e: e.matmul(R[3][:, 0:64], Wc[:], w["rhsX"][:, 0:64], start=True, stop=True), reads=[Wc, w["rhsX"]], writes=[R[3]])
                    k.op("pe", lambda e: e.matmul(R[0][0:64, :], w["rhsX"][:, 64:128], Wc[:], start=True, stop=True), reads=[Wc, w["rhsX"]], writes=[R[0]])
                    yield
                    k.op("act", lambda e: e.copy(w["val"][:], R[3][:, 0:64]), reads=[R[3]], writes=[w["val"]])
                    k.op("act", lambda e: e.copy(w["kcdT"][:], R[0][0:64, :]), reads=[R[0]], writes=[w["kcdT"]])
                    yield
                    k.op("pe", lambda e: e.matmul(R[1][:, 0:64], w["kcdT"][:], S[h][0:64, :], start=True, stop=True), reads=[w["kcdT"], S[h]], writes=[R[1]])
                    yield
                    k.op("dve", lambda e: e.tensor_tensor(w["vn"][:], w["val"][:], R[1][:, 0:64], ALU.subtract), reads=[w["val"], R[1]], writes=[w["vn"]])
                    yield
                    k.op("pe", lambda e: e.matmul(R[2][:, 0:64], w["qgT"][:], S[h][:], start=True, stop=False), reads=[w["qgT"], S[h]], writes=[R[2]], inc=False)
                    k.op("pe", lambda e: e.matmul(R[2][:, 0:64], w["innerT"][:], w["vn"][:], start=False, stop=True), reads=[w["innerT"], w["vn"]], writes=[R[2]])
                    k.op("pe", lambda e: e.matmul(R[3][0:64, 0:64], w["kend"][:], w["vn"][:], start=True, stop=True), reads=[w["kend"], w["vn"]], writes=[R[3]])
                    yield
                    k.op("dve", lambda e: e.scalar_tensor_tensor(S[h][0:64, :], S[h][0:64, :], w["glc"][0:64, 1:2], R[3][0:64, 0:64], ALU.mult, ALU.add),
                         reads=[S[h], w["glc"], R[3]], writes=[S[h]])
                    k.op("pool", lambda e: e.memset(w["rstd"][:, 0:1], 0.0), writes=[w["rstd"]])
                    k.op("act", lambda e: e.activation(w["junk"][:], R[2][:, 0:64], AF.Square, accum_out=w["rstd"][:, 0:1]), reads=[R[2]], writes=[w["junk"], w["rstd"]])
                    yield
                    k.op("act", lambda e: e.activation(w["rstd"][:, 1:2], w["rstd"][:, 0:1], AF.Sqrt, bias=epsc[:, 0:1], scale=1.0 / 64), reads=[w["rstd"], epsc], writes=[w["rstd"]])
                    k.op("dve", lambda e: e.reciprocal(w["rstd"][:, 1:2], w["rstd"][:, 1:2]), reads=[w["rstd"]], writes=[w["rstd"]])
                    k.op("dve", lambda e: e.scalar_tensor_tensor(og[:, h * 64:(h + 1) * 64], R[2][:, 0:64], w["rstd"][:, 1:2], nz[:, h, :], ALU.mult, ALU.mult),
                         reads=[R[2], w["rstd"], nz], writes=[og])
                    yield

                for i in range(NJ):
                    xt = xTt[i % 2]
                    k.dma("sp", xt[:], xT_d.t.rearrange("(c p) t -> p c t", p=128)[:, :, i * 128:(i + 1) * 128], reads=[xT_d], writes=[xt])
                    for kc in range(8):
                        k.op("pe", lambda e: e.matmul(PAB[:, 0:16], xt[:, kc, :], wab[:, kc, :], start=(kc == 0), stop=(kc == 7)),
                             reads=[xt, wab], writes=[PAB], inc=(kc == 7))
                    for kc in range(8):
                        k.op("pe", lambda e: e.matmul(PZ[:, :], xt[:, kc, :], wz[:, kc, :], start=(kc == 0), stop=(kc == 7)),
                             reads=[xt, wz], writes=[PZ], inc=(kc == 7))
                    k.op("dve", lambda e: e.tensor_tensor(tmp8[:], PAB[:, 0:8], dtb[:], ALU.add), reads=[PAB, dtb], writes=[tmp8])
                    k.op("act", lambda e: e.activation(tmp8[:], tmp8[:], AF.Exp), reads=[tmp8], writes=[tmp8])
                    k.op("act", lambda e: e.activation(tmp8[:], tmp8[:], AF.Ln, bias=1.0, scale=1.0), reads=[tmp8], writes=[tmp8])
                    k.op("dve", lambda e: e.tensor_tensor(gtok[:], tmp8[:], negA[:], ALU.mult), reads=[tmp8, negA], writes=[gtok])
                    k.op("act", lambda e: e.activation(btok[:], PAB[:, 8:16], AF.Sigmoid), reads=[PAB], writes=[btok])
                    k.op("pe", lambda e: e.matmul(PAB[:, 16:24], Umat, gtok[:], start=True, stop=True), reads=[c32, gtok], writes=[PAB])
                    k.op("act", lambda e: e.copy(gctok[:], PAB[:, 16:24]), reads=[PAB], writes=[gctok])
                    k.op("dve", lambda e: e.tensor_scalar(ngctok[:], PAB[:, 16:24], -1.0, None, ALU.mult), reads=[PAB], writes=[ngctok])
                    k.op("act", lambda e: e.activation(egctok[:], gctok[:], AF.Exp), reads=[gctok], writes=[egctok])
                    k.op("act", lambda e: e.activation(nz[:].rearrange("p h d -> p (h d)"), PZ[:, :], AF.Silu), reads=[PZ], writes=[nz])
                    k.op("pool", lambda e: e.tensor_tensor(nz[:], nz[:], nw[:].unsqueeze(1).to_broadcast([128, 8, 64]), ALU.mult), reads=[nz, nw], writes=[nz])
                    for wave in range(2):
                        gens = [head_chain(wave * 4 + s, i, s, xt) for s in range(NSLOT)]
                        alive = list(gens)
                        gsteps = int(os.environ.get("DEV_GSTEPS", "1000"))
                        nst = 0
                        while alive and nst < gsteps:
                            nst += 1
                            nxt = []
                            for gdef in alive:
                                try:
                                    next(gdef)
                                    nxt.append(gdef)
                                except StopIteration:
                                    pass
                            alive = nxt
                    for c in range(4):
                        k.op("pe", lambda e: e.transpose(PTR[:, c * 128:(c + 1) * 128], og[:, c * 128:(c + 1) * 128], ident), reads=[og, c32], writes=[PTR])
                    k.op("act", lambda e: e.copy(ogT[:], PTR[:, :].rearrange("p (c t) -> p c t", c=4)), reads=[PTR], writes=[ogT])
                    k.dma("pool", mixT_d.t[512:1024, :].rearrange("(c p) t -> p c t", p=128)[:, :, i * 128:(i + 1) * 128], ogT[:],
                          reads=[ogT], writes=[mixT_d])
                for h in range(8):
                    k.dma("pool", gst_p[l, h], S[h][0:64, :], reads=[S[h]], writes=[gst_p])
                    for blk in range(3):
                        c0 = blk * 512 + h * 64
                        k.dma("pool", gcv_p[l][:, c0:c0 + 64].rearrange("w d -> d w"), carry[h][:, blk, :], reads=[carry[h]], writes=[gcv_p],
                              allow_slow_non_contiguous=True)
            k.barrier()

        def chain(l, xres_p, xres_s, yout_p, yout_s, last):
            TS = 256
            with ExitStack() as es:
                wo = k.sb(es, "wo", [128, 8, DM], BF16)
                k.dma("sp", wo[:], wb_out.t[l].rearrange("(c p) n -> p c n", p=128), reads=[wb_out], writes=[wo])
                wgt = k.sb(es, "wgt", [128, 8, DM], BF16)
                k.dma("sp", wgt[:], wb_gate.t[l].rearrange("(c p) n -> p c n", p=128), reads=[wb_gate], writes=[wgt])
                wpj = k.sb(es, "wpj", [128, 2, DM], BF16)
                k.dma("sp", wpj[:], wb_proj.t[l].rearrange("(c p) n -> p c n", p=128), reads=[wb_proj], writes=[wpj])
                wdn = k.sb(es, "wdn", [128, 22, DM], BF16)
                k.dma("sp", wdn[:], wb_down.t[l].rearrange("(c p) n -> p c n", p=128), reads=[wb_down], writes=[wdn])
                lng = k.sb(es, "lng", [128, 3, DM])
                lnb = k.sb(es, "lnb", [128, 3, DM])
                for i in range(3):
                    k.dma("sp", lng[:, i, :], ln_g[l, i:i + 1, :].partition_broadcast(128), writes=[lng])
                    k.dma("sp", lnb[:, i, :], ln_b[l, i:i + 1, :].partition_broadcast(128), writes=[lnb])
                fcw = k.sb(es, "fcw", [128, 22, 3])
                for tap in range(3):
                    k.dma("sp", fcw[:, :, tap], fconv_w[l, tap].rearrange("(c p) -> p c", p=128), writes=[fcw], allow_slow_non_contiguous=True)
                wup = [k.sb(es, "wup%d" % i, [128, 8, 256], BF16) for i in range(3)]
                mixT = k.sb(es, "mixT", [128, 8, TS], BF16)
                x1T = k.sb(es, "x1T", [128, 8, TS], BF16)
                x2T = k.sb(es, "x2T", [128, 8, 128], BF16)
                hid = k.sb(es, "hid", [128, 22, TS], BF16)
                ext = [k.sb(es, "fext%d" % i, [128, TS + 2]) for i in range(2)]
                hg = [k.sb(es, "hg%d" % i, [128, TS]) for i in range(2)]
                carry = k.sb(es, "fcarry", [128, 22, 2])
                k.op("pool", lambda e: e.memset(carry[:], 0.0), writes=[carry])
                xres = k.sb(es, "xres", [128, DM])
                x1 = [k.sb(es, "x1_%d" % i, [128, DM]) for i in range(2)]
                x2 = k.sb(es, "x2", [128, DM])
                x3 = xres
                rr = k.sb(es, "rr", [128, DM])
                sig = k.sb(es, "sig", [128, DM])
                junk = sig
                st = k.sb(es, "lnst", [128, 8])
                ptk = k.sb(es, "ptk", [128, PLED])
                pT = k.sb(es, "pT", [128, 2, 128], BF16)
                trs = k.sb(es, "ctrs", [128, 8, 128], BF16)
                sgo = k.sb(es, "sgo", [NS, DFF])
                sbT = k.sb(es, "sbT", [128, 22, 2, NS])
                sgT = k.sb(es, "sgT", [128, 22, NS])
                PY = (pb[0], pb[1])
                PU = (pb[2], pb[3], pb[4], pb[5])
                PT_, PM = pb[6], pb[7]

                def layer_norm(src, nt, idx, dst):
                    k.op("pool", lambda e: e.memset(st[:, 0:4], 0.0), writes=[st])
                    k.op("act", lambda e: e.activation(junk[0:nt, :], src[0:nt, :], AF.Identity, accum_out=st[0:nt, 0:1]), reads=[src], writes=[junk, st])
                    k.op("dve", lambda e: e.tensor_scalar(st[0:nt, 1:2], st[0:nt, 0:1], -1.0 / DM, None, ALU.mult), reads=[st], writes=[st])
                    k.op("act", lambda e: e.activation(junk[0:nt, :], src[0:nt, :], AF.Square, bias=st[0:nt, 1:2], scale=1.0, accum_out=st[0:nt, 2:3]),
                         reads=[src, st], writes=[junk, st])
                    k.op("act", lambda e: e.activation(st[0:nt, 3:4], st[0:nt, 2:3], AF.Sqrt, bias=epsc[0:nt, 1:2], scale=1.0 / DM), reads=[st, epsc], writes=[st])
                    k.op("dve", lambda e: e.reciprocal(st[0:nt, 3:4], st[0:nt, 3:4]), reads=[st], writes=[st])
                    k.op("dve", lambda e: e.tensor_scalar(dst[0:nt, :], src[0:nt, :], st[0:nt, 1:2], st[0:nt, 3:4], ALU.add, ALU.mult), reads=[src, st], writes=[dst])
                    k.op("dve", lambda e: e.tensor_tensor(dst[0:nt, :], dst[0:nt, :], lng[0:nt, idx, :], ALU.mult), reads=[dst, lng], writes=[dst])
                    k.op("pool", lambda e: e.tensor_tensor(dst[0:nt, :], dst[0:nt, :], lnb[0:nt, idx, :], ALU.add), reads=[dst, lnb], writes=[dst])

                def to_T(src, nt, dstT, c0):
                    for half in range(2):
                        for c4 in range(4):
                            c = half * 4 + c4
                            k.op("pe", lambda e: e.transpose(PT_[:, c4 * 128:c4 * 128 + nt], src[0:nt, c * 128:(c + 1) * 128], ident[0:nt, 0:nt]),
                                 reads=[src, c32], writes=[PT_])
                        k.op("act", lambda e: e.copy(dstT[:, half * 4:(half + 1) * 4, c0:c0 + nt],
                                                     PT_[:, :].rearrange("p (c t) -> p c t", c=4)[:, :, 0:nt]), reads=[PT_], writes=[dstT])

                def supertile(kind, tok0, tiles):
                    ntot = sum(nt for _, nt in tiles)
                    if kind == "p":
                        srcT, xr_d, p_d, y_d, nxtT = mixT_d, xres_p, pp, yout_p, xT_d
                    else:
                        srcT, xr_d, p_d, y_d, nxtT = mixsT_d, xres_s, pps, yout_s, xsT_d
                    k.dma("sp", mixT[:, :, 0:ntot], srcT.t.rearrange("(c p) t -> p c t", p=128)[:, :, tok0:tok0 + ntot], reads=[srcT], writes=[mixT])
                    for ti, (o0, nt) in enumerate(tiles):
                        k.dma("sp", xres[0:nt, :], xr_d[tok0 + o0:tok0 + o0 + nt, :], reads=[xr_d], writes=[xres])
                        for hb in range(2):
                            for kc in range(8):
                                k.op("pe", lambda e: e.matmul(PY[hb][0:nt, :], mixT[:, kc, o0:o0 + nt], wo[:, kc, hb * 512:(hb + 1) * 512],
                                                              start=(kc == 0), stop=(kc == 7)), reads=[mixT, wo], writes=[PY[hb]], inc=(kc == 7))
                        for hb in range(2):
                            k.op("dve", lambda e: e.scalar_tensor_tensor(rr[0:nt, hb * 512:(hb + 1) * 512], xres[0:nt, hb * 512:(hb + 1) * 512], ALPHA,
                                                                         PY[hb][0:nt, :], ALU.mult, ALU.add), reads=[xres, PY[hb]], writes=[rr])
                        layer_norm(rr, nt, 0, x1[ti])
                        to_T(x1[ti], nt, x1T, o0)
                    if kind == "s":
                        for r in range(2):
                            k.dma("sp", sgo[:], st_fconv[l, :, r, :], writes=[sgo])
                            for c in range(22):
                                k.op("pe", lambda e: e.matmul(PM[:, (c % 16) * NS:(c % 16 + 1) * NS], sgo[0:NS, c * 128:(c + 1) * 128], ident[0:NS, 0:NS], start=True, stop=True),
                                     reads=[sgo, c32], writes=[PM])
                                if c % 16 == 15 or c == 21:
                                    cs = (c // 16) * 16
                                    k.op("act", lambda e: e.copy(sbT[:, cs:c + 1, r, :], PM[:, 0:(c - cs + 1) * NS].rearrange("p (c s) -> p c s", s=NS)),
                                         reads=[PM], writes=[sbT])
                        k.dma("pool", fcv_s[l, :, 0, :], st_fconv[l, :, 1, :], reads=[st_fconv], writes=[fcv_s])
                    for c in range(22):
                        wu = wup[c % 3]
                        wsrc = wb_up.t[l].rearrange("(kc p) n -> p kc n", p=128)
                        k.dma("sp", wu[:, :, 0:128], wsrc[:, :, c * 128:(c + 1) * 128], reads=[wb_up], writes=[wu])
                        k.dma("sp", wu[:, :, 128:256], wsrc[:, :, DFF + c * 128:DFF + (c + 1) * 128], reads=[wb_up], writes=[wu])
                        PG, PV = PU[(c % 2) * 2], PU[(c % 2) * 2 + 1]
                        for kc in range(8):
                            k.op("pe", lambda e: e.matmul(PG[:, 0:ntot], wu[:, kc, 0:128], x1T[:, kc, 0:ntot], start=(kc == 0), stop=(kc == 7)),
                                 reads=[wu, x1T], writes=[PG], inc=(kc == 7))
                        for kc in range(8):
                            k.op("pe", lambda e: e.matmul(PV[:, 0:ntot], wu[:, kc, 128:256], x1T[:, kc, 0:ntot], start=(kc == 0), stop=(kc == 7)),
                                 reads=[wu, x1T], writes=[PV], inc=(kc == 7))
                        h_ = hg[c % 2]
                        if kind == "p":
                            ex = ext[c % 2]
                            k.op("pool", lambda e: e.tensor_copy(ex[:, 0:2], carry[:, c, :]), reads=[carry], writes=[ex])
                            k.op("act", lambda e: e.copy(ex[:, 2:2 + ntot], PG[:, 0:ntot]), reads=[PG], writes=[ex])
                            k.op("pool", lambda e: e.tensor_copy(carry[:, c, :], ex[:, ntot:ntot + 2]), reads=[ex], writes=[carry])
                            k.op("dve", lambda e: e.tensor_scalar(h_[:, 0:ntot], ex[:, 0:ntot], fcw[:, c, 0:1], None, ALU.mult), reads=[ex, fcw], writes=[h_])
                            for tap in (1, 2):
                                k.op("dve", lambda e: e.scalar_tensor_tensor(h_[:, 0:ntot], ex[:, tap:tap + ntot], fcw[:, c, tap:tap + 1], h_[:, 0:ntot],
                                                                             ALU.mult, ALU.add), reads=[ex, fcw, h_], writes=[h_])
                        else:
                            k.op("act", lambda e: e.copy(sgT[:, c, :], PG[:, 0:NS]), reads=[PG], writes=[sgT])
                            k.op("dve", lambda e: e.tensor_scalar(h_[:, 0:NS], sbT[:, c, 0, :], fcw[:, c, 0:1], None, ALU.mult), reads=[sbT, fcw], writes=[h_])
                            k.op("dve", lambda e: e.scalar_tensor_tensor(h_[:, 0:NS], sbT[:, c, 1, :], fcw[:, c, 1:2], h_[:, 0:NS], ALU.mult, ALU.add),
                                 reads=[sbT, fcw, h_], writes=[h_])
                            k.op("dve", lambda e: e.scalar_tensor_tensor(h_[:, 0:NS], sgT[:, c, :], fcw[:, c, 2:3], h_[:, 0:NS], ALU.mult, ALU.add),
                                 reads=[sgT, fcw, h_], writes=[h_])
                        k.op("act", lambda e: e.activation(h_[:, 0:ntot], h_[:, 0:ntot], AF.Gelu), reads=[h_], writes=[h_])
                        k.op("dve", lambda e: e.tensor_tensor(hid[:, c, 0:ntot], h_[:, 0:ntot], PV[:, 0:ntot], ALU.mult), reads=[h_, PV], writes=[hid])
                    if kind == "s":
                        for c in range(22):
                            k.op("pe", lambda e: e.transpose(PM[0:NS, (c % 4) * 128:(c % 4 + 1) * 128], sgT[:, c, :], ident), reads=[sgT, c32], writes=[PM])
                            if c % 4 == 3 or c == 21:
                                cs = (c // 4) * 4
                                k.op("act", lambda e: e.copy(sgo[:, cs * 128:(c + 1) * 128], PM[0:NS, 0:(c - cs + 1) * 128]), reads=[PM], writes=[sgo])
                        k.dma("pool", fcv_s[l, :, 1, :], sgo[:], reads=[sgo], writes=[fcv_s])
                    for ti, (o0, nt) in enumerate(tiles):
                        for c in range(22):
                            for hb in range(2):
                                k.op("pe", lambda e: e.matmul(PY[hb][0:nt, :], hid[:, c, o0:o0 + nt], wdn[:, c, hb * 512:(hb + 1) * 512],
                                                              start=(c == 0), stop=(c == 21)), reads=[hid, wdn], writes=[PY[hb]], inc=(c == 21))
                        for hb in range(2):
                            k.op("dve", lambda e: e.scalar_tensor_tensor(rr[0:nt, hb * 512:(hb + 1) * 512], x1[ti][0:nt, hb * 512:(hb + 1) * 512], ALPHA,
                                                                         PY[hb][0:nt, :], ALU.mult, ALU.add), reads=[x1[ti], PY[hb]], writes=[rr])
                        layer_norm(rr, nt, 1, x2)
                        to_T(x2, nt, x2T, 0)
                        k.dma("sp", ptk[0:nt, :], p_d[l, tok0 + o0:tok0 + o0 + nt, :], reads=[p_d], writes=[ptk])
                        for c in range(2):
                            k.op("pe", lambda e: e.transpose(PM[:, c * 128:c * 128 + nt], ptk[0:nt, c * 128:(c + 1) * 128], ident[0:nt, 0:nt]), reads=[ptk, c32], writes=[PM])
                        k.op("act", lambda e: e.copy(pT[:, :, 0:nt], PM[:, 0:256].rearrange("p (c t) -> p c t", c=2)[:, :, 0:nt]), reads=[PM], writes=[pT])
                        for hb in range(2):
                            for kc in range(8):
                                k.op("pe", lambda e: e.matmul(PY[hb][0:nt, :], x2T[:, kc, 0:nt], wgt[:, kc, hb * 512:(hb + 1) * 512],
                                                              start=(kc == 0), stop=(kc == 7)), reads=[x2T, wgt], writes=[PY[hb]], inc=(kc == 7))
                            k.op("act", lambda e: e.activation(sig[0:nt, hb * 512:(hb + 1) * 512], PY[hb][0:nt, :], AF.Sigmoid), reads=[PY[hb]], writes=[sig])
                        for hb in range(2):
                            for c in range(2):
                                k.op("pe", lambda e: e.matmul(PY[hb][0:nt, :], pT[:, c, 0:nt], wpj[:, c, hb * 512:(hb + 1) * 512],
                                                              start=(c == 0), stop=(c == 1)), reads=[pT, wpj], writes=[PY[hb]], inc=(c == 1))
                            k.op("dve", lambda e: e.tensor_tensor(sig[0:nt, hb * 512:(hb + 1) * 512], sig[0:nt, hb * 512:(hb + 1) * 512], PY[hb][0:nt, :], ALU.mult),
                                 reads=[sig, PY[hb]], writes=[sig])
                        k.op("dve", lambda e: e.scalar_tensor_tensor(rr[0:nt, :], x2[0:nt, :], ALPHA, sig[0:nt, :], ALU.mult, ALU.add), reads=[x2, sig], writes=[rr])
                        layer_norm(rr, nt, 2, x3)
                        k.dma("pool", y_d[tok0 + o0:tok0 + o0 + nt, :], x3[0:nt, :], reads=[x3], writes=[y_d])
                        if not last:
                            to_T(x3, nt, trs, 0)
                            k.dma("pool", nxtT.t.rearrange("(c p) t -> p c t", p=128)[:, :, tok0 + o0:tok0 + o0 + nt], trs[:, :, 0:nt], reads=[trs], writes=[nxtT])

                if do_prompt:
                    for s0 in range(0, T, TS):
                        supertile("p", s0, [(o, 128) for o in range(0, TS, 128)])
                    for tap in range(2):
                        k.dma("pool", fcv_p[l, tap].rearrange("(c p) -> p c", p=128), carry[:, :, tap], reads=[carry], writes=[fcv_p],
                              allow_slow_non_contiguous=True)
                if do_sample:
                    supertile("s", 0, [(0, NS)])
            k.barrier()

        def sample_mixers(l):
            NQ = NS * 8
            with ExitStack() as es:
                cs32 = k.sb(es, "cs32", [128, NCS32])
                k.dma("sp", cs32[:], cs32_d[:], writes=[cs32])
                csb = k.sb(es, "csb", [128, NCSB], BF16)
                k.dma("sp", csb[:], csb_d[:], writes=[csb])
                o_ = 0
                alibi_s = cs32.t[:, o_:o_ + 2 * NPG * 4]
                o_ += 2 * NPG * 4
                alibi_w = cs32.t[:, o_:o_ + 2 * NWT * 4]
                o_ += 2 * NWT * 4
                alibi_c = cs32.t[:, o_:o_ + NQ]
                o_ += NQ
                GS = cs32.t[:, o_:o_ + 2 * NS]
                o_ += 2 * NS
                tkb_s = cs32.t[:, o_:o_ + 40]
                o_ += 40
                piota = cs32.t[:, o_:o_ + 2]
                NB33 = NPG * 2 + 1
                OH = csb.t[0:2 * NS, 0:2 * NS * 128]
                pool33 = csb.t[0:NPG * 4, 2 * NS * 128:2 * NS * 128 + NB33 + 1]
                B0, B1, B2, B3, B4, B5, B6, B7 = pb
                xsT = k.sb(es, "xsT", [128, 8, NS], BF16)
                k.dma("sp", xsT[:], xsT_d.t.rearrange("(c p) s -> p c s", p=128), reads=[xsT_d], writes=[xsT])
                hs = k.sb(es, "hs", [NS, INW])
                wch = [k.sb(es, "wch%d" % i, [128, 8, 512], BF16) for i in range(2)]
                wsrc = wb_in.t[l].rearrange("(c p) n -> p c n", p=128)
                for ch in range(7):
                    c0 = ch * 512
                    cw_ = min(512, INW - c0)
                    wc = wch[ch % 2]
                    k.dma("sp", wc[:, :, 0:cw_], wsrc[:, :, c0:c0 + cw_], reads=[wb_in], writes=[wc])
                    for kc in range(8):
                        k.op("pe", lambda e: e.matmul(B0[0:NS, 0:cw_], xsT[:, kc, :], wc[:, kc, 0:cw_], start=(kc == 0), stop=(kc == 7)),
                             reads=[xsT, wc], writes=[B0], inc=(kc == 7))
                    k.op("act", lambda e: e.copy(hs[:, c0:c0 + cw_], B0[0:NS, 0:cw_]), reads=[B0], writes=[hs])
                k.dma("pool", kv_s[l], hs[:, C_KV:C_WIN], reads=[hs], writes=[kv_s])
                k.dma("pool", win_s[l, :, WB - 1, :], hs[:, C_WIN:C_GATE], reads=[hs], writes=[win_s])
                for s in range(NS):
                    k.dma("pool", win_s[l, s, 0:WB - 1, :], st_win[l, s, 1:WB, :], reads=[st_win], writes=[win_s])
                k.dma("pool", gcv_s[l, :, 2, :], hs[:, C_GQKV:C_GA], reads=[hs], writes=[gcv_s])
                for r in range(2):
                    k.dma("pool", gcv_s[l, :, r, :], st_gconv[l, :, r + 1, :], reads=[st_gconv], writes=[gcv_s])
                with ExitStack() as e2:
                    gbuf = k.sb(e2, "gbuf", [NS, 3, 1536])
                    k.dma("sp", gbuf[:], st_gconv[l], writes=[gbuf])
                    cwt = k.sb(e2, "cwt", [NS, 4, 1536])
                    for tap in range(4):
                        k.dma("sp", cwt[:, tap, :], gconv_w[l, tap:tap + 1, :].partition_broadcast(NS), writes=[cwt])
                    qkv = k.sb(e2, "qkv", [NS, 1536])
                    tq = k.sb(e2, "tq", [NS, 1536])
                    k.op("dve", lambda e: e.tensor_tensor(qkv[:], cwt[:, 3, :], hs[:, C_GQKV:C_GA], ALU.mult), reads=[cwt, hs], writes=[qkv])
                    for tap in range(3):
                        k.op("dve", lambda e: e.tensor_tensor(tq[:], cwt[:, tap, :], gbuf[:, tap, :], ALU.mult), reads=[cwt, gbuf], writes=[tq])
                        k.op("dve", lambda e: e.tensor_tensor(qkv[:], qkv[:], tq[:], ALU.add), reads=[qkv, tq], writes=[qkv])
                    k.op("act", lambda e: e.activation(qkv[:], qkv[:], AF.Silu), reads=[qkv], writes=[qkv])
                    ss = k.sb(e2, "ss", [NS, 16])
                    k.op("dve", lambda e: e.tensor_tensor(tq[:, 0:1024], qkv[:, 0:1024], qkv[:, 0:1024], ALU.mult), reads=[qkv], writes=[tq])
                    k.op("dve", lambda e: e.tensor_reduce(ss[:], tq[:, 0:1024].rearrange("s (a d) -> s a d", d=64), AX.X, ALU.add), reads=[tq], writes=[ss])
                    k.op("act", lambda e: e.activation(ss[:], ss[:], AF.Sqrt, bias=epsc[0:NS, 0:1], scale=1.0), reads=[ss, epsc], writes=[ss])
                    k.op("dve", lambda e: e.reciprocal(ss[:], ss[:]), reads=[ss], writes=[ss])
                    qkn = k.sb(e2, "qkn", [NS, 16, 64])
                    k.op("dve", lambda e: e.tensor_tensor(qkn[:], qkv[:, 0:1024].rearrange("s (a d) -> s a d", d=64), ss[:].unsqueeze(2).to_broadcast([NS, 16, 64]), ALU.mult),
                         reads=[qkv, ss], writes=[qkn])
                    k.op("dve", lambda e: e.tensor_scalar(qkn[:, 0:8, :], qkn[:, 0:8, :], 0.125, None, ALU.mult), reads=[qkn], writes=[qkn])
                    dtb = k.sb(e2, "sdtb", [NS, 8])
                    k.dma("sp", dtb[:], dt_bias[l:l + 1, :].partition_broadcast(NS), writes=[dtb])
                    negA = k.sb(e2, "snegA", [NS, 8])
                    k.dma("sp", negA[:], A_log[l:l + 1, :].partition_broadcast(NS), writes=[negA])
                    k.op("act", lambda e: e.activation(negA[:], negA[:], AF.Exp), reads=[negA], writes=[negA])
                    k.op("dve", lambda e: e.tensor_scalar(negA[:], negA[:], -1.0, None, ALU.mult), reads=[negA], writes=[negA])
                    nw = k.sb(e2, "snw", [NS, 64])
                    k.dma("sp", nw[:], gnorm_w[l:l + 1, :].partition_broadcast(NS), writes=[nw])
                    gea = k.sb(e2, "gea", [NS, 8])
                    gbe = k.sb(e2, "gbe", [NS, 8])
                    k.op("dve", lambda e: e.tensor_tensor(gea[:], hs[:, C_GA:C_GB], dtb[:], ALU.add), reads=[hs, dtb], writes=[gea])
                    k.op("act", lambda e: e.activation(gea[:], gea[:], AF.Exp), reads=[gea], writes=[gea])
                    k.op("act", lambda e: e.activation(gea[:], gea[:], AF.Ln, bias=1.0, scale=1.0), reads=[gea], writes=[gea])
                    k.op("dve", lambda e: e.tensor_tensor(gea[:], gea[:], negA[:], ALU.mult), reads=[gea, negA], writes=[gea])
                    k.op("act", lambda e: e.activation(gea[:], gea[:], AF.Exp), reads=[gea], writes=[gea])
                    k.op("act", lambda e: e.activation(gbe[:], hs[:, C_GB:C_GZ], AF.Sigmoid), reads=[hs], writes=[gbe])
                    kqT = k.sb(e2, "kqT", [64, 2, NS, 8])
                    for a in range(2):
                        for h in range(8):
                            k.op("pe", lambda e: e.matmul(B7[0:64, h * NS:(h + 1) * NS], qkn[:, (1 - a) * 8 + h, :], ident[0:NS, 0:NS], start=True, stop=True), reads=[qkn, c32], writes=[B7])
                        k.op("act", lambda e: e.copy(kqT[:, a, :, :].rearrange("p s h -> p h s"), B7[0:64, 0:8 * NS].rearrange("p (h s) -> p h s", h=8)),
                             reads=[B7], writes=[kqT])
                    bd = k.sb(e2, "bd", [NS, 2, NS, 8])
                    k.op("dve", lambda e: e.tensor_tensor(bd[:, 0, :, :], gea[:].unsqueeze(1).to_broadcast([NS, NS, 8]),
                                                          ident[0:NS, 0:NS].unsqueeze(2).to_broadcast([NS, NS, 8]), ALU.mult), reads=[gea, c32], writes=[bd])
                    k.op("dve", lambda e: e.tensor_tensor(bd[:, 1, :, :], gbe[:].unsqueeze(1).to_broadcast([NS, NS, 8]),
                                                          ident[0:NS, 0:NS].unsqueeze(2).to_broadcast([NS, NS, 8]), ALU.mult), reads=[gbe, c32], writes=[bd])
                    k.op("pe", lambda e: e.matmul(B7[0:64, 0:2 * NQ], ones32[0:NS, 0:64], bd[:].rearrange("p a s h -> p (a s h)"), start=True, stop=True),
                         reads=[bd, c32], writes=[B7])
                    eab = k.sb(e2, "eab", [64, 2, NS, 8])
                    k.op("act", lambda e: e.copy(eab[:].rearrange("p a s h -> p (a s h)"), B7[0:64, 0:2 * NQ]), reads=[B7], writes=[eab])
                    bk = k.sb(e2, "bk", [64, NS, 8])
                    k.op("dve", lambda e: e.tensor_tensor(bk[:], kqT[:, 0, :, :], eab[:, 1, :, :], ALU.mult), reads=[kqT, eab], writes=[bk])
                    if os.environ.get("DEV_DBG", "") == "1" and l == 0:
                        k.dma("pool", y_p[0:64, 0:256], kqT[:].rearrange("p a s h -> p (a s h)"), reads=[kqT], writes=[y_p])
                        k.dma("pool", y_p[64:128, 0:256], eab[:].rearrange("p a s h -> p (a s h)"), reads=[eab], writes=[y_p])
                        k.dma("pool", y_p[128:192, 0:128], bk[:].rearrange("p s h -> p (s h)"), reads=[bk], writes=[y_p])
                        k.dma("pool", y_p[192:208, 0:1024], qkn[:].rearrange("p a d -> p (a d)"), reads=[qkn], writes=[y_p])
                        k.dma("pool", y_p[208:224, 0:1024], qkv[:, 0:1024], reads=[qkv], writes=[y_p])
                        k.dma("pool", y_p[224:240, 0:16], ss[:], reads=[ss], writes=[y_p])
                    S0 = k.sb(e2, "S0", [64, NQ, 64])
                    k.dma("sp", S0[:], st_gdn[l].rearrange("s h a b -> a (s h) b"), writes=[S0])
                    otok = k.sb(e2, "otok", [NS, 512])
                    k.op("pool", lambda e: e.memset(otok[:], 0.0), writes=[otok])
                    tmpS = k.sb(e2, "tmpS", [64, 8, 64])
                    t1 = k.sb(e2, "t1", [64, 8, 64])
                    bdv = k.sb(e2, "bdv", [NS, 512])
                    for s in range(NS):
                        S0s = S0[:, s * 8:(s + 1) * 8, :]
                        k.op("dve", lambda e: e.tensor_tensor(tmpS[:], S0s, kqT[:, 0, s, :].unsqueeze(2).to_broadcast([64, 8, 64]), ALU.mult),
                             reads=[S0, kqT], writes=[tmpS])
                        k.op("pe", lambda e: e.matmul(B7[0:64, :], ones32[0:64, 0:64], tmpS[:].rearrange("p h d -> p (h d)"), start=True, stop=True),
                             reads=[tmpS, c32], writes=[B7])
                        k.op("dve", lambda e: e.tensor_tensor(t1[:], B7[0:64, :].rearrange("p (h d) -> p h d", h=8), bk[:, s, :].unsqueeze(2).to_broadcast([64, 8, 64]), ALU.mult),
                             reads=[B7, bk], writes=[t1])
                        k.op("dve", lambda e: e.tensor_tensor(t1[:], S0s, t1[:], ALU.subtract), reads=[S0, t1], writes=[t1])
                        k.op("dve", lambda e: e.tensor_tensor(t1[:], t1[:], eab[:, 0, s, :].unsqueeze(2).to_broadcast([64, 8, 64]), ALU.mult), reads=[t1, eab], writes=[t1])
                        k.op("dve", lambda e: e.tensor_scalar(bdv[:], qkv[:, 1024:1536], ident[0:NS, s:s + 1], None, ALU.mult), reads=[qkv, c32], writes=[bdv])
                        k.op("pe", lambda e: e.matmul(B7[0:64, :], ones32[0:NS, 0:64], bdv[:], start=True, stop=True), reads=[bdv, c32], writes=[B7])
                        k.op("dve", lambda e: e.tensor_tensor(tmpS[:], B7[0:64, :].rearrange("p (h d) -> p h d", h=8), bk[:, s, :].unsqueeze(2).to_broadcast([64, 8, 64]), ALU.mult),
                             reads=[B7, bk], writes=[tmpS])
                        k.op("dve", lambda e: e.tensor_tensor(S0s, t1[:], tmpS[:], ALU.add), reads=[t1, tmpS], writes=[S0])
                        k.op("dve", lambda e: e.tensor_tensor(tmpS[:], S0s, kqT[:, 1, s, :].unsqueeze(2).to_broadcast([64, 8, 64]), ALU.mult),
                             reads=[S0, kqT], writes=[tmpS])
                        k.op("pe", lambda e: e.matmul(B7[0:NS, :], ones32[0:64, 0:NS], tmpS[:].rearrange("p h d -> p (h d)"), start=True, stop=True),
                             reads=[tmpS, c32], writes=[B7])
                        k.op("dve", lambda e: e.scalar_tensor_tensor(otok[:], B7[0:NS, :], ident[0:NS, s:s + 1], otok[:], ALU.mult, ALU.add),
                             reads=[B7, c32, otok], writes=[otok])
                    k.dma("pool", gst_s[l].rearrange("s h a b -> a (s h) b"), S0[:], reads=[S0], writes=[gst_s])
                    ms = k.sb(e2, "gms", [NS, 8])
                    k.op("dve", lambda e: e.tensor_tensor(tq[:, 0:512], otok[:], otok[:], ALU.mult), reads=[otok], writes=[tq])
                    k.op("dve", lambda e: e.tensor_reduce(ms[:], tq[:, 0:512].rearrange("s (h d) -> s h d", d=64), AX.X, ALU.add), reads=[tq], writes=[ms])
                    k.op("act", lambda e: e.activation(ms[:], ms[:], AF.Sqrt, bias=epsc[0:NS, 0:1], scale=1.0 / 64), reads=[ms, epsc], writes=[ms])
                    k.op("dve", lambda e: e.reciprocal(ms[:], ms[:]), reads=[ms], writes=[ms])
                    zs = k.sb(e2, "szs", [NS, 8, 64])
                    k.op("act", lambda e: e.activation(zs[:].rearrange("s h d -> s (h d)"), hs[:, C_GZ:INW], AF.Silu), reads=[hs], writes=[zs])
                    k.op("dve", lambda e: e.tensor_tensor(zs[:], zs[:], nw[:].unsqueeze(1).to_broadcast([NS, 8, 64]), ALU.mult), reads=[zs, nw], writes=[zs])
                    k.op("dve", lambda e: e.tensor_tensor(zs[:], zs[:], ms[:].unsqueeze(2).to_broadcast([NS, 8, 64]), ALU.mult), reads=[zs, ms], writes=[zs])
                    k.op("dve", lambda e: e.tensor_tensor(otok[:], otok[:], zs[:].rearrange("s h d -> s (h d)"), ALU.mult), reads=[otok, zs], writes=[otok])
                    ogT = k.sb(e2, "sogT", [128, 4, NS], BF16)
                    for c in range(4):
                        k.op("pe", lambda e: e.matmul(B7[:, c * NS:(c + 1) * NS], otok[:, c * 128:(c + 1) * 128], ident[0:NS, 0:NS], start=True, stop=True), reads=[otok, c32], writes=[B7])
                    k.op("act", lambda e: e.copy(ogT[:], B7[:, 0:4 * NS].rearrange("p (c s) -> p c s", c=4)), reads=[B7], writes=[ogT])
                    k.dma("pool", mixsT_d.t[512:1024, :].rearrange("(c p) s -> p c s", p=128), ogT[:], reads=[ogT], writes=[mixsT_d])
                with ExitStack() as e3:
                    gts = k.sb(e3, "gts", [NS, 24])
                    k.op("act", lambda e: e.activation(gts[:], hs[:, C_GATE:C_GQKV], AF.Sigmoid), reads=[hs], writes=[gts])
                    qperm = k.sb(e3, "qperm", [NS, 4, 2, 64])
                    for n in range(2):
                        k.op("dve", lambda e: e.tensor_copy(qperm[:, :, n, :], hs[:, n * 256:(n + 1) * 256].rearrange("s (g d) -> s g d", g=4)), reads=[hs], writes=[qperm])
                    qz = [k.sb(e3, "qz%d" % n, [128, NS, 4], BF16) for n in range(2)]
                    for n in range(2):
                        k.op("pool", lambda e: e.memset(qz[n][:], 0.0), writes=[qz[n]])
                    for g in range(4):
                        k.op("pe", lambda e: e.matmul(B0[:, g * NS:(g + 1) * NS], qperm[:, g, :, :].rearrange("s n d -> s (n d)"), ident[0:NS, 0:NS], start=True, stop=True), reads=[qperm, c32], writes=[B0])
                    k.op("act", lambda e: e.copy(qz[0][0:64, :, :].rearrange("p s g -> p g s"), B0[0:64, 0:4 * NS].rearrange("p (g s) -> p g s", g=4)), reads=[B0], writes=[qz[0]])
                    k.op("act", lambda e: e.copy(qz[1][64:128, :, :].rearrange("p s g -> p g s"), B0[64:128, 0:4 * NS].rearrange("p (g s) -> p g s", g=4)), reads=[B0], writes=[qz[1]])
                    nK = k.sb(e3, "nK", [128, 2, NS], BF16)
                    k.op("pe", lambda e: e.matmul(B0[:, 0:NS], hs[:, C_KV + 256:C_KV + 384], ident[0:NS, 0:NS], start=True, stop=True), reads=[hs, c32], writes=[B0])
                    k.op("pe", lambda e: e.matmul(B0[:, NS:2 * NS], hs[:, C_WIN:C_WIN + 128], ident[0:NS, 0:NS], start=True, stop=True), reads=[hs, c32], writes=[B0])
                    k.op("act", lambda e: e.copy(nK[:].rearrange("p a s -> p (a s)"), B0[:, 0:2 * NS]), reads=[B0], writes=[nK])
                    pti = k.sb(e3, "pti", [128, NS * NPG], I32)
                    k.dma("sp", pti[:], ptab[:].partition_broadcast(128), writes=[pti])
                    ptf = k.sb(e3, "ptf", [128, NS * NPG])
                    k.op("dve", lambda e: e.tensor_copy(ptf[:], pti[:]), reads=[pti], writes=[ptf])
                    k.op("dve", lambda e: e.tensor_scalar(ptf[:], ptf[:], 256.0, piota[:, 0:1], ALU.mult, ALU.add), reads=[ptf, cs32], writes=[ptf])
                    k.op("dve", lambda e: e.tensor_scalar(ptf[:], ptf[:], float(2 * l * NPOOL * 128), None, ALU.add), reads=[ptf], writes=[ptf])
                    idxc = k.sb(e3, "idxc", [128, NS * NPG], I32)
                    idxs = k.sb(e3, "idxs", [128, NS * NPG], I32)
                    k.op("dve", lambda e: e.tensor_copy(idxc[:], ptf[:]), reads=[ptf], writes=[idxc])
                    k.op("dve", lambda e: e.tensor_scalar(ptf[:], ptf[:], 1.0, None, ALU.add), reads=[ptf], writes=[ptf])
                    k.op("dve", lambda e: e.tensor_copy(idxs[:], ptf[:]), reads=[ptf], writes=[idxs])
                    pool2v = pool.t.rearrange("r (two c) -> (r two) c", two=2)
                    phis = k.sb(e3, "sphis", [128, 2, 128])
                    k.op("pool", lambda e: e.memset(phis[:], 0.0), writes=[phis])
                    for a in range(2):
                        for n in range(2):
                            k.dma("sp", phis[64 * n:64 * n + 64, a, 64 * n:64 * n + 64], nsa_phi[l, a], writes=[phis])
                    phib = k.sb(e3, "sphib", [128, 2, 128], BF16)
                    k.op("dve", lambda e: e.tensor_copy(phib[:], phis[:]), reads=[phis], writes=[phib])
                    pet = k.sb(e3, "spet", [128, 2, 32])
                    for a in range(2):
                        for n in range(2):
                            k.dma("sp", pet[64 * n:64 * n + 64, a, :], nsa_pe[l, a].rearrange("r d -> d r"), writes=[pet], allow_slow_non_contiguous=True)
                    pem = k.sb(e3, "spem", [128, 2])
                    k.op("dve", lambda e: e.tensor_reduce(pem[:], pet[:], AX.X, ALU.add), reads=[pet], writes=[pem])
                    k.op("dve", lambda e: e.tensor_scalar(pem[:], pem[:], 1.0 / 32, None, ALU.mult), reads=[pem], writes=[pem])
                    NCB = NPG * 4
                    kcT_all = k.sb(e3, "kcT_all", [128, NS, NCB], BF16)
                    vc_all = k.sb(e3, "vc_all", [NCB, NS, 2, 68], BF16)
                    k.op("pool", lambda e: e.memset(vc_all[:], 1.0), writes=[vc_all])
                    pgt = [k.sb(e3, "pgt%d" % i, [128, 256]) for i in range(3)]
                    cmpkv = [k.sb(e3, "scmpkv%d" % i, [128, 256], BF16) for i in range(2)]
                    kvm = k.sb(e3, "kvm", [128, 2, NCB], BF16)
                    it = 0
                    for s in range(NS):
                        for pg in range(NPG):
                            pt_ = pgt[it % 3]
                            cb_ = cmpkv[it % 2]
                            it += 1
                            k.gather(pt_[:], pool2v, idxc[:, s * NPG + pg:s * NPG + pg + 1], reads=[idxc], writes=[pt_])
                            cast(cb_[:], pt_[:], [pt_], [cb_])
                            for a in range(2):
                                k.op("pe", lambda e: e.matmul(B1[:, a * NCB + pg * 4:a * NCB + pg * 4 + 4], cb_[:, a * 128:(a + 1) * 128], avg4, start=True, stop=True),
                                     reads=[cb_, cb128], writes=[B1])
                        for a in range(2):
                            k.op("dve", lambda e: e.tensor_scalar(kvm[:, a, :], B1[:, a * NCB:(a + 1) * NCB], pem[:, a:a + 1], None, ALU.add), reads=[B1, pem], writes=[kvm])
                        k.op("pe", lambda e: e.matmul(B2[:, 0:NCB], phib[:, 0, :], kvm[:, 0, :], start=True, stop=True), reads=[phib, kvm], writes=[B2])
                        k.op("act", lambda e: e.copy(kcT_all[:, s, :], B2[:, 0:NCB]), reads=[B2], writes=[kcT_all])
                        k.op("pe", lambda e: e.matmul(B2[0:NCB, 128:256], kvm[:, 1, :], phib[:, 1, :], start=True, stop=True), reads=[phib, kvm], writes=[B2])
                        k.op("act", lambda e: e.copy(vc_all[:, s, :, 0:64], B2[0:NCB, 128:256].rearrange("p (n d) -> p n d", n=2)), reads=[B2], writes=[vc_all])
                        for n in range(2):
                            k.op("pe", lambda e: e.matmul(B3[0:NCB, (s * 2 + n) * 4:(s * 2 + n) * 4 + 4], kcT_all[:, s, :], qz[n][:, s, :], start=True, stop=True),
                                 reads=[kcT_all, qz[n]], writes=[B3])
                    tmpc = k.sb(e3, "tmpc", [NCB, NQ])
                    PTc = k.sb(e3, "sPTc", [NCB, NQ], BF16)
                    k.op("dve", lambda e: e.scalar_tensor_tensor(tmpc[:], B3[0:NCB, 0:NQ], 0.125, alibi_c[0:NCB, :], ALU.mult, ALU.add), reads=[B3, cs32], writes=[tmpc])
                    k.op("act", lambda e: e.activation(PTc[:], tmpc[:], AF.Exp), reads=[tmpc], writes=[PTc])
                    k.op("pe", lambda e: e.matmul(B3[:, 256:256 + NB33 + 1], PTc[:], pool33, start=True, stop=True), reads=[PTc, csb], writes=[B3])
                    ul = k.sb(e3, "ul", [128, NB33 + 1])
                    k.op("act", lambda e: e.copy(ul[:], B3[:, 256:256 + NB33 + 1]), reads=[B3], writes=[ul])
                    k.op("dve", lambda e: e.tensor_scalar(ul[:, NB33:NB33 + 1], ul[:, NB33:NB33 + 1], 1e-30, None, ALU.max), reads=[ul], writes=[ul])
                    k.op("dve", lambda e: e.reciprocal(ul[:, NB33:NB33 + 1], ul[:, NB33:NB33 + 1]), reads=[ul], writes=[ul])
                    k.op("dve", lambda e: e.tensor_scalar(ul[:, 0:NB33], ul[:, 0:NB33], ul[:, NB33:NB33 + 1], None, ALU.mult), reads=[ul], writes=[ul])
                    k.op("pe", lambda e: e.matmul(B3[0:2 * NS, 320:320 + NB33], GS, ul[:, 0:NB33], start=True, stop=True), reads=[ul, cs32], writes=[B3])
                    sc = k.sb(e3, "ssc", [2 * NS, 40])
                    k.op("dve", lambda e: e.tensor_tensor(sc[:, 0:NB33], B3[0:2 * NS, 320:320 + NB33], tkb_s[0:2 * NS, 0:NB33], ALU.add), reads=[B3, cs32], writes=[sc])
                    m8 = k.sb(e3, "sm8", [2 * NS, 16])
                    tkw = k.sb(e3, "stkw", [2 * NS, NB33])
                    k.op("dve", lambda e: e.max(out=m8[:, 0:8], in_=sc[:, 0:NB33]), reads=[sc], writes=[m8])
                    k.op("dve", lambda e: e.match_replace(out=tkw[:], in_to_replace=m8[:, 0:8], in_values=sc[:, 0:NB33], imm_value=-1e9), reads=[sc, m8], writes=[tkw])
                    k.op("dve", lambda e: e.max(out=m8[:, 8:16], in_=tkw[:]), reads=[tkw], writes=[m8])
                    k.op("dve", lambda e: e.tensor_scalar(sc[:, 0:NB33], sc[:, 0:NB33], m8[:, 15:16], None, ALU.is_ge), reads=[sc, m8], writes=[sc])
                    k.op("dve", lambda e: e.tensor_scalar(sc[:, 0:NB33], sc[:, 0:NB33], -1.0, -NEG, ALU.add, ALU.mult), reads=[sc], writes=[sc])
                    selb16 = k.sb(e3, "selb16", [2 * NS, 40], BF16)
                    k.op("dve", lambda e: e.tensor_copy(selb16[:, 0:NB33], sc[:, 0:NB33]), reads=[sc], writes=[selb16])
                    for s in range(NS):
                        for n in range(2):
                            c0 = (s * 2 + n) * 4
                            k.op("pe", lambda e: e.matmul(B6[0:65, c0:c0 + 4], vc_all[:, s, n, 0:65], PTc[:, c0:c0 + 4], start=True, stop=True),
                                 reads=[vc_all, PTc], writes=[B6])
                    KT_s = [k.sb(e3, "KT_s%d" % i, [128, (NPG + 1) * 128], BF16) for i in range(2)]
                    Vs_s = [k.sb(e3, "Vs_s%d" % i, [128, NPG + 1, 2, 68], BF16) for i in range(2)]
                    KTw_s = [k.sb(e3, "KTw_s%d" % i, [128, (NWT + 1) * 128], BF16) for i in range(2)]
                    Vw_s = [k.sb(e3, "Vw_s%d" % i, [128, NWT + 1, 2, 68], BF16) for i in range(2)]
                    for i in range(2):
                        k.op("pool", lambda e: e.memset(Vs_s[i][:], 1.0), writes=[Vs_s[i]])
                        k.op("pool", lambda e: e.memset(Vw_s[i][:], 1.0), writes=[Vw_s[i]])
                        k.op("pool", lambda e: e.memset(Vs_s[i][:, NPG, :, :], 0.0), writes=[Vs_s[i]])
                        k.op("pool", lambda e: e.memset(Vw_s[i][:, NWT, :, :], 0.0), writes=[Vw_s[i]])
                    wt = [k.sb(e3, "wt%d" % i, [128, NWT, 256]) for i in range(2)]
                    vstg = k.sb(e3, "vstg", [1, 2, 128])
                    ones1 = k.sb(e3, "ones1", [1, 2, 2])
                    k.op("pool", lambda e: e.memset(ones1[:], 1.0), writes=[ones1])
                    selm = k.sb(e3, "selm", [128, NPG])
                    tmps = k.sb(e3, "tmps", [128, NPG + 1, 4])
                    PTs = k.sb(e3, "sPTs", [128, NPG + 1, 4], BF16)
                    PTw = k.sb(e3, "sPTw", [128, NWT + 1, 4], BF16)
                    k.op("pool", lambda e: e.memset(PTs[:], 0.0), writes=[PTs])
                    k.op("pool", lambda e: e.memset(PTw[:], 0.0), writes=[PTw])
                    for s in range(NS):
                        KT, V, KTw, Vw, wts = KT_s[s % 2], Vs_s[s % 2], KTw_s[s % 2], Vw_s[s % 2], wt[s % 2]
                        for pg in range(NPG):
                            pt_ = pgt[it % 3]
                            it += 1
                            k.gather(pt_[:], pool2v, idxs[:, s * NPG + pg:s * NPG + pg + 1], reads=[idxs], writes=[pt_])
                            k.op("pe", lambda e: e.matmul(B2[:, 0:128], pt_[:, 0:128], ident, start=True, stop=True), reads=[pt_, c32], writes=[B2])
                            cast(KT[:, pg * 128:(pg + 1) * 128], B2[:, 0:128], [B2], [KT], psum=True)
                            k.op("pool", lambda e: e.tensor_copy(V[:, pg, :, 0:64], pt_[:, 128:256].rearrange("p (n d) -> p n d", n=2)), reads=[pt_], writes=[V])
                        k.dma("sp", wts[:], st_win[l, s].rearrange("(t p) c -> p t c", p=128), writes=[wts])
                        for t in range(NWT):
                            k.op("pe", lambda e: e.matmul(B2[:, 128:256], wts[:, t, 0:128], ident, start=True, stop=True), reads=[wts, c32], writes=[B2])
                            cast(KTw[:, t * 128:(t + 1) * 128], B2[:, 128:256], [B2], [KTw], psum=True)
                            k.op("pool", lambda e: e.tensor_copy(Vw[:, t, :, 0:64], wts[:, t, 128:256].rearrange("p (n d) -> p n d", n=2)), reads=[wts], writes=[Vw])
                        k.op("dve", lambda e: e.tensor_copy(KT[:, NPG * 128:NPG * 128 + 1], nK[:, 0, s:s + 1]), reads=[nK], writes=[KT])
                        k.op("dve", lambda e: e.tensor_copy(KTw[:, NWT * 128:NWT * 128 + 1], nK[:, 1, s:s + 1]), reads=[nK], writes=[KTw])
                        k.dma("sp", vstg[0:1, 0, :], hs[s:s + 1, C_KV + 384:C_KV + 512], reads=[hs], writes=[vstg])
                        k.dma("sp", vstg[0:1, 1, :], hs[s:s + 1, C_WIN + 128:C_WIN + 256], reads=[hs], writes=[vstg])
                        k.op("dve", lambda e: e.tensor_copy(V[0:1, NPG, :, 0:64], vstg[0:1, 0, :].rearrange("p (n d) -> p n d", n=2)), reads=[vstg], writes=[V])
                        k.op("dve", lambda e: e.tensor_copy(Vw[0:1, NWT, :, 0:64], vstg[0:1, 1, :].rearrange("p (n d) -> p n d", n=2)), reads=[vstg], writes=[Vw])
                        k.op("dve", lambda e: e.tensor_copy(V[0:1, NPG, :, 64:65], ones1[0:1, :, 0:1]), reads=[ones1], writes=[V])
                        k.op("dve", lambda e: e.tensor_copy(Vw[0:1, NWT, :, 64:65], ones1[0:1, :, 0:1]), reads=[ones1], writes=[Vw])
                        for n in range(2):
                            r = s * 2 + n
                            c0 = r * 4
                            k.op("pe", lambda e: e.matmul(B2[:, 256:256 + NB33], OH[:, r * 128:(r + 1) * 128], selb16[:, 0:NB33], start=True, stop=True),
                                 reads=[csb, selb16], writes=[B2])
                            k.op("dve", lambda e: e.tensor_copy(selm[0:64, :], B2[0:64, 256:256 + 2 * NPG].rearrange("p (t two) -> p t two", two=2)[:, :, 0]), reads=[B2], writes=[selm])
                            k.op("dve", lambda e: e.tensor_copy(selm[64:128, :], B2[64:128, 256:256 + 2 * NPG].rearrange("p (t two) -> p t two", two=2)[:, :, 1]), reads=[B2], writes=[selm])
                            for (BS, K_, nt_, PTx, Vx, ali, col6) in ((B4, KT, NPG, PTs, V, alibi_s, 1), (B5, KTw, NWT, PTw, Vw, alibi_w, 2)):
                                for t in range(nt_):
                                    k.op("pe", lambda e: e.matmul(BS[:, t * 4:(t + 1) * 4], K_[:, t * 128:(t + 1) * 128], qz[n][:, s, :], start=True, stop=True),
                                         reads=[K_, qz[n]], writes=[BS])
                                k.op("pe", lambda e: e.matmul(BS[0:1, 128:132], K_[:, nt_ * 128:nt_ * 128 + 1], qz[n][:, s, :], start=True, stop=True),
                                     reads=[K_, qz[n]], writes=[BS])
                                tv = tmps[:, 0:nt_, :]
                                k.op("dve", lambda e: e.scalar_tensor_tensor(tv, BS[:, 0:nt_ * 4].rearrange("p (t g) -> p t g", g=4), 0.125,
                                                                             ali[:, n * nt_ * 4:(n + 1) * nt_ * 4].rearrange("p (t g) -> p t g", g=4), ALU.mult, ALU.add),
                                     reads=[BS, cs32], writes=[tmps])
                                if col6 == 1:
                                    k.op("dve", lambda e: e.tensor_tensor(tv, tv, selm[:].unsqueeze(2).to_broadcast([128, NPG, 4]), ALU.add), reads=[tmps, selm], writes=[tmps])
                                k.op("act", lambda e: e.activation(PTx[:, 0:nt_, :], tv, AF.Exp), reads=[tmps], writes=[PTx])
                                k.op("act", lambda e: e.activation(PTx[0:1, nt_, :], BS[0:1, 128:132], AF.Exp, scale=0.125), reads=[BS], writes=[PTx])
                                oc = col6 * NQ + c0
                                for t in range(nt_ + 1):
                                    k.op("pe", lambda e: e.matmul(B6[0:65, oc:oc + 4], Vx[:, t, n, 0:65], PTx[:, t, :], start=(t == 0), stop=(t == nt_)),
                                         reads=[Vx, PTx], writes=[B6], inc=(t == nt_))
                    ot = k.sb(e3, "ot", [65, 3, NQ])
                    k.op("act", lambda e: e.copy(ot[:].rearrange("p a q -> p (a q)"), B6[0:65, 0:3 * NQ]), reads=[B6], writes=[ot])
                    k.op("dve", lambda e: e.tensor_scalar(ot[64:65, :, :], ot[64:65, :, :], 1e-30, None, ALU.max), reads=[ot], writes=[ot])
                    k.op("dve", lambda e: e.reciprocal(ot[64:65, :, :], ot[64:65, :, :]), reads=[ot], writes=[ot])
                    k.op("pe", lambda e: e.matmul(B0[0:64, 0:3 * NQ], ones32[64:65, 0:64], ot[64:65, :, :].rearrange("p a q -> p (a q)"), start=True, stop=True),
                         reads=[ot, c32], writes=[B0])
                    k.op("dve", lambda e: e.tensor_tensor(ot[0:64, :, :], ot[0:64, :, :], B0[0:64, 0:3 * NQ].rearrange("p (a q) -> p a q", a=3), ALU.mult), reads=[ot, B0], writes=[ot])
                    gbd = k.sb(e3, "gbd", [NS, 3, NS, 8])
                    for br in range(3):
                        k.op("dve", lambda e: e.tensor_tensor(gbd[:, br, :, :], gts[:].rearrange("s (h b) -> s h b", b=3)[:, :, br].unsqueeze(1).to_broadcast([NS, NS, 8]),
                                                              ident[0:NS, 0:NS].unsqueeze(2).to_broadcast([NS, NS, 8]), ALU.mult), reads=[gts, c32], writes=[gbd])
                    k.op("pe", lambda e: e.matmul(B0[0:64, 0:3 * NQ], ones32[0:NS, 0:64], gbd[:].rearrange("p a s h -> p (a s h)"), start=True, stop=True),
                         reads=[gbd, c32], writes=[B0])
                    k.op("dve", lambda e: e.tensor_tensor(ot[0:64, :, :], ot[0:64, :, :], B0[0:64, 0:3 * NQ].rearrange("p (a q) -> p a q", a=3), ALU.mult), reads=[ot, B0], writes=[ot])
                    k.op("dve", lambda e: e.tensor_tensor(ot[0:64, 0, :], ot[0:64, 0, :], ot[0:64, 1, :], ALU.add), reads=[ot], writes=[ot])
                    k.op("dve", lambda e: e.tensor_tensor(ot[0:64, 0, :], ot[0:64, 0, :], ot[0:64, 2, :], ALU.add), reads=[ot], writes=[ot])
                    onb = k.sb(e3, "onb", [64, 8, NS], BF16)
                    k.op("dve", lambda e: e.tensor_copy(onb[:], ot[0:64, 0, :].rearrange("p (s h) -> p h s", h=8)), reads=[ot], writes=[onb])
                    k.dma("pool", mixsT_d.t[0:512, :].rearrange("(h d) s -> d h s", d=64), onb[:], reads=[onb], writes=[mixsT_d])
            k.barrier()

        STOP = os.environ.get("DEV_STOP", "")
        for l in range(DEPTH):
            if STOP in ("W", "X0"):
                break
            xres_p = xp if l == 0 else x1_d
            xres_s = xs if l == 0 else xs1_d
            yout_p = x1_d if l == 0 else y_p
            yout_s = xs1_d if l == 0 else y_s
            last = (l == DEPTH - 1)
            if do_prompt:
                with ExitStack() as es:
                    cbp = k.sb(es, "cbp", [128, NCBP], BF16)
                    k.dma("sp", cbp[:], cbp_d[:], writes=[cbp])
                    cb64 = cb3 = cb8 = cbp
                    wq = k.sb(es, "wq", [128, 8, 512], BF16)
                    wkT = k.sb(es, "wkT", [128, 8, 256], BF16)
                    wtok = k.sb(es, "wtok", [128, 8, 792], BF16)
                    wsrc = wb_in.t[l].rearrange("(c p) n -> p c n", p=128)
                    for kc in range(8):
                        for n in range(2):
                            k.dma("sp", wq[:, kc, :].rearrange("p (c n d) -> p c n d", c=4, n=2)[:, :, n, :],
                                  wb_in.t[l, kc * 128:(kc + 1) * 128, n * 256:(n + 1) * 256].rearrange("p (c d) -> p c d", c=4),
                                  reads=[wb_in], writes=[wq])
                    k.dma("sp", wkT[:, :, 0:128], wsrc[:, :, C_KV + 256:C_KV + 384], reads=[wb_in], writes=[wkT])
                    k.dma("sp", wkT[:, :, 128:256], wsrc[:, :, C_WIN:C_WIN + 128], reads=[wb_in], writes=[wkT])
                    k.dma("sp", wtok[:], wsrc[:, :, C_KV:C_KV + 792], reads=[wb_in], writes=[wtok])
                    phis = k.sb(es, "phis", [128, 2, 128])
                    k.op("pool", lambda e: e.memset(phis[:], 0.0), writes=[phis])
                    for a in range(2):
                        for n in range(2):
                            k.dma("sp", phis[64 * n:64 * n + 64, a, 64 * n:64 * n + 64], nsa_phi[l, a], writes=[phis])
                    phib = k.sb(es, "phib", [128, 2, 128], BF16)
                    k.op("dve", lambda e: e.tensor_copy(phib[:], phis[:]), reads=[phis], writes=[phib])
                    pet = k.sb(es, "pet", [128, 2, 32])
                    for a in range(2):
                        for n in range(2):
                            k.dma("sp", pet[64 * n:64 * n + 64, a, :], nsa_pe[l, a].rearrange("r d -> d r"), writes=[pet],
                                  allow_slow_non_contiguous=True)
                    pem = k.sb(es, "pem", [128, 2])
                    k.op("dve", lambda e: e.tensor_reduce(pem[:], pet[:], AX.X, ALU.add), reads=[pet], writes=[pem])
                    k.op("dve", lambda e: e.tensor_scalar(pem[:], pem[:], 1.0 / 32, None, ALU.mult), reads=[pem], writes=[pem])
                    KTs = k.sb(es, "KTs", [128, T], BF16)
                    KTw = k.sb(es, "KTw", [128, T], BF16)
                    Vs = k.sb(es, "Vs", [128, NJ, 2, 68], BF16)
                    Vw = k.sb(es, "Vw", [128, NJ, 2, 68], BF16)
                    k.op("pool", lambda e: e.memset(Vs[:], 1.0), writes=[Vs])
                    k.op("pool", lambda e: e.memset(Vw[:], 1.0), writes=[Vw])
                    kcmT = k.sb(es, "kcmT", [128, 128], BF16)
                    vcmT = k.sb(es, "vcmT", [128, 128], BF16)
                    kcT = k.sb(es, "kcT", [128, 128], BF16)
                    k.op("pool", lambda e: e.memset(kcmT[:], 0.0), writes=[kcmT])
                    k.op("pool", lambda e: e.memset(vcmT[:], 0.0), writes=[vcmT])
                    k.op("pool", lambda e: e.memset(kcT[:], 0.0), writes=[kcT])
                    cmprhs = k.sb(es, "cmprhs", [128, 2, 68], BF16)
                    k.op("pool", lambda e: e.memset(cmprhs[:], 1.0), writes=[cmprhs])
                    PT0 = k.sb(es, "PT0", [128, NJ, 512], BF16)
                    PT = [PT0, PT0]
                    PTc = k.sb(es, "PTc", [128, 512], BF16)
                    xTt = [k.sb(es, "xTt%d" % i, [128, 8, 128], BF16) for i in range(2)]
                    kvf = [k.sb(es, "kvf%d" % i, [128, 792]) for i in range(2)]
                    cmpkv = k.sb(es, "cmpkv", [128, 256], BF16)
                    gates = k.sb(es, "gates", [128, 24])
                    qTz = [k.sb(es, "qTz%d" % n, [128, 4, 128], BF16) for n in range(2)]
                    for n in range(2):
                        k.op("pool", lambda e: e.memset(qTz[n][:], 0.0), writes=[qTz[n]])
                    selbT = [k.sb(es, "selbT%d" % n, [128, 4, 128], BF16) for n in range(2)]
                    for n in range(2):
                        k.op("pool", lambda e: e.memset(selbT[n][:], 0.0), writes=[selbT[n]])
                    sm = k.sb(es, "sm", [128, 64])
                    imp = k.sb(es, "imp", [128, 64])
                    tkw = k.sb(es, "tkw", [128, 64])
                    m8 = k.sb(es, "m8", [128, 16])
                    selb = k.sb(es, "selb", [128, 64])
                    ocmp = k.sb(es, "ocmp", [128, 8, 65])
                    rl = k.sb(es, "rl", [128, 8, 3])
                    fgt = k.sb(es, "fgt", [128, 8, 3])
                    onsa = k.sb(es, "onsa", [128, 512])
                    onT = k.sb(es, "onT", [128, 4, 128], BF16)
                    o_ = 0
                    kaux = cbp.t[:, o_:o_ + NJ * 128]
                    o_ += NJ * 128
                    kcaux = cbp.t[:, o_:o_ + NJ * 128]
                    o_ += NJ * 128
                    qaux = cbp.t[:, o_:o_ + 1024]
                    o_ += 1024
                    cmpsel = cbp.t[:, o_:o_ + NJ * 128]
                    o_ += NJ * 128
                    vispat = cbp.t[:, o_:o_ + 512]
                    o_ += 512
                    Epad = cbp.t[:, o_:o_ + T]
                    SC = (pb[0], pb[1])
                    ACC = {("s", 0): pb[2], ("s", 1): pb[3], ("w", 0): pb[4], ("w", 1): pb[5]}
                    MA, MB = pb[6], pb[7]
                    sc_i = [0]

                    def scores(n, lhs_aux, lhsK, mask, out_pt):
                        S = SC[sc_i[0] % 2]
                        sc_i[0] += 1
                        k.op("pe", lambda e: e.matmul(S[:, :], lhs_aux, qaux[:, n * 512:(n + 1) * 512],
                                                      start=True, stop=False), reads=[cb3], writes=[S], inc=False)
                        if mask is not None:
                            ml, mr, mrd = mask
                            k.op("pe", lambda e: e.matmul(S[:, :], ml, mr, start=False, stop=False), reads=mrd, writes=[S], inc=False)
                        for g in range(4):
                            k.op("pe", lambda e, g=g: e.matmul(S[:, g * 128:(g + 1) * 128], lhsK, qTz[n][:, g, :],
                                                             start=False, stop=(g == 3)),
                                 reads=[qTz[n], KTs, KTw, kcT], writes=[S], inc=(g == 3))
                        k.op("act", lambda e: e.activation(out_pt, S[:, :], AF.Exp, scale=0.125), reads=[S], writes=[PT[n], PTc])

                    for j in range(NJ):
                        xt = xTt[j % 2]
                        kv = kvf[j % 2]
                        k.dma("sp", xt[:], xT_d.t.rearrange("(c p) t -> p c t", p=128)[:, :, j * 128:(j + 1) * 128],
                              reads=[xT_d], writes=[xt])
                        for kc in range(8):
                            k.op("pe", lambda e, kc=kc: e.matmul(MA[:, :], xt[:, kc, :], wtok[:, kc, 0:512], start=(kc == 0), stop=(kc == 7)),
                                 reads=[xt, wtok], writes=[MA], inc=(kc == 7))
                        for kc in range(8):
                            k.op("pe", lambda e, kc=kc: e.matmul(MB[:, 0:280], xt[:, kc, :], wtok[:, kc, 512:792], start=(kc == 0), stop=(kc == 7)),
                                 reads=[xt, wtok], writes=[MB], inc=(kc == 7))
                        k.op("act", lambda e: e.copy(kv[:, 0:512], MA[:, :]), reads=[MA], writes=[kv])
                        k.op("dve", lambda e: e.tensor_copy(kv[:, 512:792], MB[:, 0:280]), reads=[MB], writes=[kv])
                        k.dma("pool", kv_p[l, j * 128:(j + 1) * 128, :], kv[:, 0:512], reads=[kv], writes=[kv_p])
                        if j >= NJ - NWT:
                            jj = j - (NJ - NWT)
                            k.dma("pool", win_p[l, jj * 128:(jj + 1) * 128, :], kv[:, 512:768], reads=[kv], writes=[win_p])
                        k.op("act", lambda e: e.activation(gates[:], kv[:, 768:792], AF.Sigmoid), reads=[kv], writes=[gates])
                        k.op("dve", lambda e: e.tensor_copy(Vs[:, j, :, 0:64], kv[:, 384:512].rearrange("p (n d) -> p n d", n=2)),
                             reads=[kv], writes=[Vs])
                        k.op("pool", lambda e: e.tensor_copy(Vw[:, j, :, 0:64], kv[:, 640:768].rearrange("p (n d) -> p n d", n=2)),
                             reads=[kv], writes=[Vw])
                        k.op("dve", lambda e: e.tensor_copy(cmpkv[:], kv[:, 0:256]), reads=[kv], writes=[cmpkv])
                        k.op("pe", lambda e: e.matmul(MA[:, 0:4], cmpkv[:, 0:128], avg4, start=True, stop=True), reads=[cmpkv, cb128], writes=[MA])
                        k.op("pe", lambda e: e.matmul(MA[:, 4:8], cmpkv[:, 128:256], avg4, start=True, stop=True), reads=[cmpkv, cb128], writes=[MA])
                        k.op("dve", lambda e: e.tensor_scalar(kcmT[:, 4 * j:4 * j + 4], MA[:, 0:4], pem[:, 0:1], None, ALU.add),
                             reads=[MA, pem], writes=[kcmT])
                        k.op("dve", lambda e: e.tensor_scalar(vcmT[:, 4 * j:4 * j + 4], MA[:, 4:8], pem[:, 1:2], None, ALU.add),
                             reads=[MA, pem], writes=[vcmT])
                        k.op("pe", lambda e: e.matmul(MB[:, 0:4], phib[:, 0, :], kcmT[:, 4 * j:4 * j + 4], start=True, stop=True),
                             reads=[phib, kcmT], writes=[MB])
                        k.op("dve", lambda e: e.tensor_copy(kcT[:, 4 * j:4 * j + 4], MB[:, 0:4]), reads=[MB], writes=[kcT])
                        k.op("pe", lambda e: e.matmul(MB[:, 128:256], vcmT[:, :], phib[:, 1, :], start=True, stop=True),
                             reads=[phib, vcmT], writes=[MB])
                        k.op("dve", lambda e: e.tensor_copy(cmprhs[:, :, 0:64], MB[:, 128:256].rearrange("p (n d) -> p n d", n=2)),
                             reads=[MB], writes=[cmprhs])
                        for c in range(4):
                            for kc in range(8):
                                k.op("pe", lambda e, c=c, kc=kc: e.matmul(MA[:, c * 128:(c + 1) * 128], wq[:, kc, c * 128:(c + 1) * 128], xt[:, kc, :],
                                                                       start=(kc == 0), stop=(kc == 7)),
                                     reads=[xt, wq], writes=[MA], inc=(kc == 7))
                        k.op("act", lambda e: e.copy(qTz[0][0:64, :, :], MA[0:64, :].rearrange("p (c t) -> p c t", c=4)), reads=[MA], writes=[qTz[0]])
                        k.op("dve", lambda e: e.tensor_copy(qTz[1][64:128, :, :], MA[64:128, :].rearrange("p (c t) -> p c t", c=4)), reads=[MA], writes=[qTz[1]])
                        for a, KT in ((0, KTs), (1, KTw)):
                            for kc in range(8):
                                k.op("pe", lambda e, a=a, kc=kc: e.matmul(MB[:, a * 128:(a + 1) * 128], wkT[:, kc, a * 128:(a + 1) * 128], xt[:, kc, :],
                                                                       start=(kc == 0), stop=(kc == 7)),
                                     reads=[xt, wkT], writes=[MB], inc=(kc == 7))
                        k.op("dve", lambda e: e.tensor_copy(KTs[:, j * 128:(j + 1) * 128], MB[:, 0:128]), reads=[MB], writes=[KTs])
                        k.op("dve", lambda e: e.tensor_copy(KTw[:, j * 128:(j + 1) * 128], MB[:, 128:256]), reads=[MB], writes=[KTw])
                        if os.environ.get("DEV_P1", "") == "proj":
                            continue
                        for n in range(2):
                            scores(n, kcaux[:, j * 128:(j + 1) * 128], kcT[:, :],
                                   (cmpsel[:, j * 128:(j + 1) * 128], vispat, [cb8]), PTc[:, :])
                            for g in range(4):
                                k.op("pe", lambda e, g=g: e.matmul(MA[:, g * 65:(g + 1) * 65], PTc[:, g * 128:(g + 1) * 128], cmprhs[:, n, 0:65],
                                                                 start=True, stop=True), reads=[PTc, cmprhs], writes=[MA])
                            for g in range(4):
                                k.op("pe", lambda e, g=g: e.matmul(MB[:, g * 64:(g + 1) * 64], PTc[:, g * 128:(g + 1) * 128], pool2,
                                                                 start=True, stop=True), reads=[PTc, cb128], writes=[MB])
                            k.op("act", lambda e: e.copy(ocmp[:, 4 * n:4 * n + 4, :], MA[:, 0:260].rearrange("p (g d) -> p g d", g=4)),
                                 reads=[MA], writes=[ocmp])
                            k.op("dve", lambda e: e.tensor_scalar(rl[:, 4 * n:4 * n + 4, 0], ocmp[:, 4 * n:4 * n + 4, 64], 1e-30, None, ALU.max),
                                 reads=[ocmp], writes=[rl])
                            k.op("dve", lambda e: e.reciprocal(rl[:, 4 * n:4 * n + 4, 0], rl[:, 4 * n:4 * n + 4, 0]), reads=[rl], writes=[rl])
                            if j >= 8:
                                k.op("dve", lambda e: e.tensor_scalar(imp[:], MB[:, 0:64], rl[:, 4 * n, 0:1], None, ALU.mult),
                                     reads=[MB, rl], writes=[imp])
                                for g in range(1, 4):
                                    k.op("dve", lambda e, g=g: e.scalar_tensor_tensor(imp[:], MB[:, g * 64:(g + 1) * 64], rl[:, 4 * n + g, 0:1], imp[:],
                                                                                     ALU.mult, ALU.add), reads=[MB, rl, imp], writes=[imp])
                                k.op("dve", lambda e: e.tensor_tensor(imp[:], imp[:], tkb[:, j * 64:(j + 1) * 64], ALU.add), reads=[imp, c32], writes=[imp])
                                k.op("dve", lambda e: e.max(out=m8[:, 0:8], in_=imp[:]), reads=[imp], writes=[m8])
                                k.op("dve", lambda e: e.match_replace(out=tkw[:], in_to_replace=m8[:, 0:8], in_values=imp[:], imm_value=-1e9),
                                     reads=[imp, m8], writes=[tkw])
                                k.op("dve", lambda e: e.max(out=m8[:, 8:16], in_=tkw[:]), reads=[tkw], writes=[m8])
                                k.op("dve", lambda e: e.tensor_scalar(selb[:], imp[:], m8[:, 15:16], None, ALU.is_ge), reads=[imp, m8], writes=[selb])
                                k.op("dve", lambda e: e.tensor_scalar(selb[:], selb[:], -1.0, -NEG, ALU.add, ALU.mult), reads=[selb], writes=[selb])
                                k.op("pe", lambda e: e.transpose(MB[0:64, 256:384], selb[:, :], ident), reads=[selb, c32], writes=[MB])
                                k.op("dve", lambda e: e.tensor_copy(selbT[n][0:64, :, :], MB[0:64, 256:384].unsqueeze(1).to_broadcast([64, 4, 128])),
                                     reads=[MB], writes=[selbT[n]])
                        if os.environ.get("DEV_P1", "") == "cmp":
                            continue
                        for br, KT, V, tiles in (("s", KTs, Vs, list(range(0, j + 1))), ("w", KTw, Vw, list(range(max(0, j - 4), j + 1)))):
                            if os.environ.get("DEV_P1", "") == "sonly" and br == "w":
                                continue
                            for n in range(2):
                                for t in tiles:
                                    if t == j:
                                        mask = (identb, causal4, [cb128])
                                    elif br == "w" and t == j - 4:
                                        mask = (identb, winedge4, [cb128])
                                    elif br == "s" and j >= 8:
                                        mask = (Epad[:, t * 128:(t + 1) * 128], selbT[n][:].rearrange("p g q -> p (g q)"), [cb64, selbT[n]])
                                    else:
                                        mask = None
                                    nm_ = os.environ.get("DEV_NOMASK", "")
                                    if nm_ == "1" or (nm_ == "2" and mask is not None and mask[2][0] is cb128) or (nm_ == "3" and mask is not None and mask[2][0] is cb64):
                                        mask = None
                                    scores(n, kaux[:, (j - t) * 128:(j - t + 1) * 128], KT[:, t * 128:(t + 1) * 128], mask, PT[n][:, t, :])
                                if os.environ.get("DEV_P1", "") == "sc":
                                    continue
                                A = ACC[(br, n)]
                                for g in range(4):
                                    for ti, t in enumerate(tiles):
                                        k.op("pe", lambda e, g=g, t=t, ti=ti: e.matmul(A[:, g * 65:(g + 1) * 65], PT[n][:, t, g * 128:(g + 1) * 128], V[:, t, n, 0:65],
                                                                                  start=(ti == 0), stop=(ti == len(tiles) - 1)),
                                             reads=[PT[n], V], writes=[A], inc=(ti == len(tiles) - 1))
                                if os.environ.get("DEV_P1", "") == "pv":
                                    continue
                                col = 1 if br == "s" else 2
                                k.op("dve", lambda e: e.reciprocal(rl[:, 4 * n:4 * n + 4, col],
                                                                   A[:, 0:260].rearrange("p (g d) -> p g d", g=4)[:, :, 64]),
                                     reads=[A], writes=[rl])
                        if os.environ.get("DEV_P1", "") in ("sc", "pv", "rc"):
                            continue
                        k.op("dve", lambda e: e.tensor_tensor(fgt[:], rl[:], gates[:].rearrange("p (h b) -> p h b", b=3), ALU.mult),
                             reads=[rl, gates], writes=[fgt])
                        for h in range(8):
                            n, g = h // 4, h % 4
                            o_ = onsa[:, h * 64:(h + 1) * 64]
                            k.op("dve", lambda e: e.tensor_scalar(o_, ocmp[:, h, 0:64], fgt[:, h, 0:1], None, ALU.mult), reads=[ocmp, fgt], writes=[onsa])
                            k.op("dve", lambda e: e.scalar_tensor_tensor(o_, ACC[("s", n)][:, g * 65:g * 65 + 64], fgt[:, h, 1:2], o_, ALU.mult, ALU.add),
                                 reads=[ACC[("s", n)], fgt, onsa], writes=[onsa])
                            k.op("dve", lambda e: e.scalar_tensor_tensor(o_, ACC[("w", n)][:, g * 65:g * 65 + 64], fgt[:, h, 2:3], o_, ALU.mult, ALU.add),
                                 reads=[ACC[("w", n)], fgt, onsa], writes=[onsa])
                        for c in range(4):
                            k.op("pe", lambda e, c=c: e.transpose(MA[:, c * 128:(c + 1) * 128], onsa[:, c * 128:(c + 1) * 128], ident),
                                 reads=[onsa, c32], writes=[MA])
                        k.op("act", lambda e: e.copy(onT[:], MA[:, :].rearrange("p (c t) -> p c t", c=4)), reads=[MA], writes=[onT])
                        k.dma("pool", mixT_d.t[0:512, :].rearrange("(c p) t -> p c t", p=128)[:, :, j * 128:(j + 1) * 128], onT[:],
                              reads=[onT], writes=[mixT_d])
                k.barrier()
            if STOP == "P1":
                break
            if do_prompt:
                gdn_prompt(l)
            if STOP == "P2":
                break
            if do_sample:
                sample_mixers(l)
            chain(l, xres_p, xres_s, yout_p, yout_s, last)
            if STOP == "P3":
                break
        k.finish()
    return nc


_W_NAMES = {"w_in": "w_in", "nsa_pe": "nsa_pe", "nsa_phi": "nsa_phi", "gdn_conv_w": "gconv_w", "gdn_A_log": "A_log",
            "gdn_dt_bias": "dt_bias", "gdn_norm_w": "gnorm_w", "w_out": "w_out", "ln_g": "ln_g", "ln_b": "ln_b",
            "ffn_w_up": "w_up", "ffn_conv_w": "fconv_w", "ffn_w_down": "w_down", "ple_w_proj": "w_proj", "ple_w_gate": "w_gate"}


def run_cores(inp, ncores, parts=("prompt", "sample")):
    f32 = np.float32
    x_prompt = np.asarray(inp["x_prompt"], f32)
    B, T, _ = x_prompt.shape
    x_sample = np.asarray(inp["x_sample"], f32)
    NSTOT = x_sample.shape[0]
    cache = np.asarray(inp["cache_nsa_kv"], f32)
    NPOOL = cache.shape[1]
    page_table = np.asarray(inp["page_table"], np.int32)
    NPG = page_table.shape[1]
    NS = NSTOT // ncores
    swin = np.asarray(inp["state_nsa_win"], f32)
    WB = swin.shape[2]
    nc = build(T, NS, NPOOL, NPG, WB, parts)
    consts = make_consts(T, NS, NPG)
    common = {v: np.ascontiguousarray(np.asarray(inp[kk], f32)) for kk, v in _W_NAMES.items()}
    common.update(consts)
    pool = np.ascontiguousarray(cache.reshape(DEPTH * NPOOL * 128, 512))
    p_prompt = np.asarray(inp["p_prompt"], f32)
    p_sample = np.asarray(inp["p_sample"], f32)
    sgdn = np.asarray(inp["state_gdn"], f32)
    sgc = np.asarray(inp["state_gdn_conv"], f32)
    sfc = np.asarray(inp["state_ffn_conv"], f32)
    per = ncores // B if ncores >= B else 1
    in_maps = []
    for c in range(ncores):
        b = min(c // per, B - 1)
        sl = slice(c * NS, (c + 1) * NS)
        m = dict(common)
        m.update({
            "xp": np.ascontiguousarray(x_prompt[b]),
            "pp": np.ascontiguousarray(p_prompt[:, b]),
            "xs": np.ascontiguousarray(x_sample[sl, 0]),
            "pps": np.ascontiguousarray(p_sample[:, sl, 0]),
            "pool": pool,
            "ptab": np.ascontiguousarray(page_table[sl].reshape(1, NS * NPG)),
            "st_win": np.ascontiguousarray(swin[:, sl].reshape(DEPTH, NS, WB, 256)),
            "st_gdn": np.ascontiguousarray(sgdn[:, sl]),
            "st_gconv": np.ascontiguousarray(sgc[:, sl]),
            "st_fconv": np.ascontiguousarray(sfc[:, sl]),
        })
        in_maps.append(m)
    res = run_bass_kernel_spmd(nc, in_maps, core_ids=list(range(ncores)))
    R = res.results
    pc = [min(b * per, ncores - 1) for b in range(B)]
    y_prompt = np.stack([R[c]["y_p"] for c in pc])
    y_sample = np.concatenate([R[c]["y_s"] for c in range(ncores)])[:, None, :]
    kv_rows_prompt = np.stack([R[c]["kv_p"] for c in pc], axis=1).reshape(DEPTH, B, T, 4, 2, 64)
    kv_rows_sample = np.concatenate([R[c]["kv_s"] for c in range(ncores)], axis=1).reshape(DEPTH, NSTOT, 1, 4, 2, 64)
    win_prompt = np.stack([R[c]["win_p"] for c in pc], axis=1).reshape(DEPTH, B, -1, 2, 2, 64)
    win_sample = np.concatenate([R[c]["win_s"] for c in range(ncores)], axis=1).reshape(DEPTH, NSTOT, WB, 2, 2, 64)
    gdn_state_prompt = np.stack([R[c]["gst_p"] for c in pc], axis=1)
    gdn_state_sample = np.concatenate([R[c]["gst_s"] for c in range(ncores)], axis=1)
    gdn_conv_prompt = np.stack([R[c]["gcv_p"] for c in pc], axis=1)
    gdn_conv_sample = np.concatenate([R[c]["gcv_s"] for c in range(ncores)], axis=1)
    ffn_conv_prompt = np.stack([R[c]["fcv_p"] for c in pc], axis=1)
    ffn_conv_sample = np.concatenate([R[c]["fcv_s"] for c in range(ncores)], axis=1)
    return (y_prompt, y_sample, kv_rows_prompt, kv_rows_sample, win_prompt, win_sample, gdn_state_prompt, gdn_state_sample,
            gdn_conv_prompt, gdn_conv_sample, ffn_conv_prompt, ffn_conv_sample)


def kernel(**inputs):
    outs = run_cores(inputs, NCORES)
    return tuple(np.ascontiguousarray(o, dtype=np.float32) for o in outs)
```

```python
import os
import numpy as np
import ml_dtypes
from contextlib import ExitStack
import concourse.bass as bass
import concourse.mybir as mybir
from concourse.bass_utils import run_bass_kernel_spmd

F32 = mybir.dt.float32
BF16 = mybir.dt.bfloat16
I32 = mybir.dt.int32
F32R = mybir.dt.float32r
FASTF32 = os.environ.get("DEV_F32R", "0") == "1"
NOSELF = os.environ.get("DEV_NOSELF", "0") == "1"


def fr(ap):
    return ap


def f32v(ap):
    return ap.bitcast(F32) if FASTF32 else ap
AF = mybir.ActivationFunctionType
ALU = mybir.AluOpType
AX = mybir.AxisListType
bf = ml_dtypes.bfloat16

NEG = -60000.0
DM = 1024
INW = 3368
DFF = 2816
PLED = 256
C_KV, C_WIN, C_GATE, C_GQKV, C_GA, C_GB, C_GZ = 512, 1024, 1280, 1304, 2840, 2848, 2856
DEPTH = 2
ALPHA = float((2 * DEPTH) ** 0.25)
LN_EPS = 1e-5
RMS_EPS = 1e-6
NCORES = 8


class Buf:
    __slots__ = ("t", "w", "r", "name", "excl")

    def __init__(self, t, name="", excl=False):
        self.t = t
        self.w = {}
        self.r = {}
        self.name = name
        self.excl = excl

    def __getitem__(self, idx):
        return self.t[idx]


class K:
    NDMA = 32

    def __init__(self, nc):
        self.nc = nc
        self.eng = {"pe": nc.tensor, "act": nc.scalar, "dve": nc.vector, "pool": nc.gpsimd, "sp": nc.sync}
        self.sem = {}
        self.cnt = {}
        for e in self.eng:
            self.sem[e] = nc.alloc_semaphore("sem_" + e)
            self.cnt[e] = 0
        for j in range(self.NDMA):
            key = ("dma", j)
            self.sem[key] = nc.alloc_semaphore("sem_dma%d" % j)
            self.cnt[key] = 0
        self.dma_rr = 0
        self.seen = {e: {} for e in self.eng}
        self.nins = 0
        self.uid = 0

    def name(self, s):
        self.uid += 1
        return "%s_%d" % (s, self.uid)

    def sb(self, es, name, shape, dt=F32):
        t = es.enter_context(self.nc.sbuf_tensor(self.name(name), list(shape), dt))
        return Buf(t, name)

    def ps(self, name, shape, dt=F32):
        return Buf(self.nc.alloc_psum_tensor(self.name(name), list(shape), dt), name, excl=True)

    def dram(self, name, shape, dt=F32, kind="Internal"):
        return Buf(self.nc.dram_tensor(name, list(shape), dt, kind=kind).ap(), name)

    def _wait(self, e, deps):
        eng = self.eng[e]
        seen = self.seen[e]
        for key, v in deps.items():
            if v <= 0 or seen.get(key, 0) >= v:
                continue
            eng.wait_ge(self.sem[key], v)
            self.nins += 1
            seen[key] = v

    @staticmethod
    def _deps(reads, writes):
        deps = {}
        for b in reads:
            for key, v in b.w.items():
                if deps.get(key, 0) < v:
                    deps[key] = v
        for b in writes:
            for key, v in b.w.items():
                if deps.get(key, 0) < v:
                    deps[key] = v
            for key, v in b.r.items():
                if deps.get(key, 0) < v:
                    deps[key] = v
        return deps

    @staticmethod
    def _record(key, val, reads, writes):
        for b in reads:
            if b.r.get(key, 0) < val:
                b.r[key] = val
        for b in writes:
            b.w.clear()
            b.w[key] = val
            b.r.clear()

    def op(self, e, fn, reads=(), writes=(), inc=True):
        ex = [b for b in reads if b.excl]
        if ex:
            writes = list(writes) + ex
        deps = self._deps(reads, writes)
        if e == "pe" or NOSELF:
            deps.pop(e, None)
        if e in deps and deps[e] > self.cnt[e]:
            deps[e] = self.cnt[e]
        self._wait(e, deps)
        ins = fn(self.eng[e])
        self.nins += 1
        if inc:
            self.cnt[e] += 1
            ins.then_inc(self.sem[e], 1)
            val = self.cnt[e]
        else:
            val = self.cnt[e] + 1
        self._record(e, val, reads, writes)
        return ins

    def _dma_issue(self, q, reads, writes, fn):
        deps = self._deps(reads, writes)
        j = self.dma_rr
        self.dma_rr = (self.dma_rr + 1) % self.NDMA
        key = ("dma", j)
        if self.cnt[key] > 0 and deps.get(key, 0) < self.cnt[key]:
            deps[key] = self.cnt[key]
        if q in deps and deps[q] > self.cnt[q]:
            deps[q] = self.cnt[q]
        self._wait(q, deps)
        ins = fn(self.eng[q])
        self.nins += 1
        self.cnt[key] += 16
        ins.then_inc(self.sem[key], 16)
        self._record(key, self.cnt[key], reads, writes)
        return ins

    def dma(self, q, out, in_, reads=(), writes=(), **kw):
        return self._dma_issue(q, reads, writes, lambda e: e.dma_start(out=out, in_=in_, **kw))

    def gather(self, out, table, idx, reads=(), writes=()):
        return self._dma_issue(
            "pool", reads, writes,
            lambda e: e.indirect_dma_start(out=out, out_offset=None, in_=table,
                                           in_offset=bass.IndirectOffsetOnAxis(ap=idx, axis=0)))

    def barrier(self):
        full = dict(self.cnt)
        for e in self.eng:
            deps = {key: v for key, v in full.items() if key != e}
            self._wait(e, deps)

    def finish(self):
        deps = {key: v for key, v in self.cnt.items() if key != "sp"}
        self._wait("sp", deps)


def make_consts(T, NS=16, NPG=16):
    NJ = T // 128
    c = {}
    p = np.arange(128)
    ident = np.eye(128, dtype=np.float32)
    U = (p[:, None] <= p[None, :]).astype(np.float32)
    ones = np.ones((128, 128), np.float32)
    mask_incl = np.where(p[:, None] >= p[None, :], 0.0, -1e4).astype(np.float32)
    maskT_incl = np.where(p[None, :] >= p[:, None], 0.0, -1e4).astype(np.float32)
    strict01 = (p[:, None] > p[None, :]).astype(np.float32)
    tk = np.zeros((128, NJ, 64), np.float32)
    blk = np.arange(64)
    for j in range(NJ):
        cur = (128 * j + p) // 64
        fut = blk[None, :] > cur[:, None]
        forced = (blk[None, :] == 0) | (((cur[:, None] - blk[None, :]) < 2) & ~fut)
        tk[:, j, :] = np.where(fut, -1e4, np.where(forced, 1e4, 0.0))
    c["c32"] = np.concatenate([ident, U, ones, mask_incl, maskT_incl, strict01, tk.reshape(128, NJ * 64)], axis=1)
    causal = np.where(p[:, None] <= p[None, :], 0.0, NEG)
    winedge = np.where(p[:, None] > p[None, :], 0.0, NEG)
    pool2 = (p[:, None] // 2 == np.arange(64)[None, :]).astype(np.float32)
    avg = (p[:, None] // 32 == np.arange(4)[None, :]).astype(np.float32) / 32.0
    c["cb128"] = np.concatenate([np.tile(causal, (1, 4)), np.tile(winedge, (1, 4)), ident, pool2,
                                 avg, np.ones((128, 4))], axis=1).astype(bf)
    kp = np.arange(T)
    Eall = (kp[None, :] // 64 == np.arange(64)[:, None]).astype(np.float32)
    slopes = 2.0 ** (-np.arange(1, 9, dtype=np.float64))
    kauxrel = np.zeros((128, NJ, 128), np.float32)
    for d in range(NJ):
        kauxrel[0, d, :] = -d
        kauxrel[1, d, :] = p
        kauxrel[2, d, :] = 1.0
    cc = np.arange(128)
    kcauxrel = np.zeros((128, NJ, 128), np.float32)
    for j in range(NJ):
        kcauxrel[0, j, :] = cc / 4.0 - j
        kcauxrel[2, j, :] = 1.0
    qaux = np.zeros((128, 2, 4, 128), np.float32)
    for n in range(2):
        for g in range(4):
            sl = slopes[4 * n + g]
            qaux[0, n, g, :] = sl * 8 * 128
            qaux[1, n, g, :] = sl * 8
            qaux[2, n, g, :] = -8.0 * sl * p
    cmpsel = np.zeros((128, NJ, 128), np.float32)
    for j in range(NJ):
        for r in range(4):
            if 4 * j + r < 128:
                cmpsel[r, j, 4 * j + r] = 1.0
        cmpsel[4, j, 4 * j + 4:] = 1.0
    vis = np.zeros((128, 4, 128), np.float32)
    for r in range(4):
        vis[r, :, :] = np.where(32 * r + 31 <= p, 0.0, NEG)[None, :]
    vis[4] = NEG
    Epad = np.zeros((128, T), np.float32)
    Epad[0:64] = Eall
    c["cbp"] = np.concatenate([kauxrel.reshape(128, -1), kcauxrel.reshape(128, -1), qaux.reshape(128, 1024),
                               cmpsel.reshape(128, -1), vis.reshape(128, 512), Epad], axis=1).astype(bf)
    PAST = NPG * 128
    NWT = 4
    al_s = np.zeros((128, 2, NPG, 4), np.float32)
    al_w = np.zeros((128, 2, NWT, 4), np.float32)
    al_c = np.zeros((128, NS, 2, 4), np.float32)
    for n in range(2):
        for g in range(4):
            sl = slopes[4 * n + g]
            for t in range(NPG):
                al_s[:, n, t, g] = -sl * (PAST - (t * 128 + p))
            for t in range(NWT):
                dist = NWT * 128 - (t * 128 + p)
                al_w[:, n, t, g] = np.where(dist < NWT * 128, -sl * dist, -1e4)
            al_c[:, :, n, g] = (-sl * (PAST - (32 * p + 15.5)))[:, None]
    GS = np.zeros((128, 2 * NS), np.float32)
    GS[np.arange(NS * 8), np.arange(NS * 8) // 4] = 1.0
    NB33 = NPG * 2 + 1
    tkb = np.zeros((128, 40), np.float32)
    tkb[:, 0] = 1e4
    tkb[:, NB33 - 2] = 1e4
    tkb[:, NB33 - 1] = 1e4
    piota = np.stack([2.0 * p, 2.0 * p], axis=1).astype(np.float32)
    c["cs32"] = np.concatenate([al_s.reshape(128, -1), al_w.reshape(128, -1), al_c.reshape(128, -1), GS, tkb, piota], axis=1).astype(np.float32)
    OH = np.zeros((128, 2 * NS, 128), np.float32)
    for r in range(2 * NS):
        OH[r, r, :] = 1.0
    pool33 = np.zeros((128, NB33 + 1), np.float32)
    for cidx in range(NPG * 4):
        pool33[cidx, cidx // 2] = 1.0
    pool33[:, NB33] = 1.0
    c["csb"] = np.concatenate([OH.reshape(128, -1), pool33], axis=1).astype(bf)
    return c


def build(T, NS, NPOOL, NPG=16, WB=512, parts=("prompt", "sample")):
    NJ = T // 128
    PAST = NPG * 128
    NWT = WB // 128
    nc = bass.Bass("TRN2", target_bir_lowering=False)
    k = K(nc)
    do_prompt = "prompt" in parts
    do_sample = "sample" in parts

    def din(name, shape, dt=F32):
        return k.dram(name, shape, dt, kind="ExternalInput")

    def dout(name, shape, dt=F32):
        return k.dram(name, shape, dt, kind="ExternalOutput")

    xp = din("xp", [T, DM])
    pp = din("pp", [DEPTH, T, PLED])
    xs = din("xs", [NS, DM])
    pps = din("pps", [DEPTH, NS, PLED])
    pool = din("pool", [DEPTH * NPOOL * 128, 512])
    ptab = din("ptab", [1, NS * NPG], I32)
    st_win = din("st_win", [DEPTH, NS, WB, 256])
    st_gdn = din("st_gdn", [DEPTH, NS, 8, 64, 64])
    st_gconv = din("st_gconv", [DEPTH, NS, 3, 1536])
    st_fconv = din("st_fconv", [DEPTH, NS, 2, DFF])
    w_in = din("w_in", [DEPTH, DM, INW])
    nsa_pe = din("nsa_pe", [DEPTH, 2, 32, 64])
    nsa_phi = din("nsa_phi", [DEPTH, 2, 64, 64])
    gconv_w = din("gconv_w", [DEPTH, 4, 1536])
    A_log = din("A_log", [DEPTH, 8])
    dt_bias = din("dt_bias", [DEPTH, 8])
    gnorm_w = din("gnorm_w", [DEPTH, 64])
    w_out = din("w_out", [DEPTH, DM, DM])
    ln_g = din("ln_g", [DEPTH, 3, DM])
    ln_b = din("ln_b", [DEPTH, 3, DM])
    w_up = din("w_up", [DEPTH, DM, 2 * DFF])
    fconv_w = din("fconv_w", [DEPTH, 3, DFF])
    w_down = din("w_down", [DEPTH, DFF, DM])
    w_proj = din("w_proj", [DEPTH, PLED, DM])
    w_gate = din("w_gate", [DEPTH, DM, DM])
    NC32 = 6 * 128 + NJ * 64
    c32_d = din("c32", [128, NC32])
    NCB128 = 512 + 512 + 128 + 64 + 4 + 4
    cb128_d = din("cb128", [128, NCB128], BF16)
    NCS32 = 2 * NPG * 4 + 2 * NWT * 4 + NS * 8 + 2 * NS + 40 + 2
    cs32_d = din("cs32", [128, NCS32])
    NCSB = 2 * NS * 128 + NPG * 2 + 2
    csb_d = din("csb", [128, NCSB], BF16)
    NCBP = 3 * NJ * 128 + 1024 + 512 + T
    cbp_d = din("cbp", [128, NCBP], BF16)
    y_p = dout("y_p", [T, DM])
    y_s = dout("y_s", [NS, DM])
    kv_p = dout("kv_p", [DEPTH, T, 512])
    kv_s = dout("kv_s", [DEPTH, NS, 512])
    win_p = dout("win_p", [DEPTH, WB, 256])
    win_s = dout("win_s", [DEPTH, NS, WB, 256])
    gst_p = dout("gst_p", [DEPTH, 8, 64, 64])
    gst_s = dout("gst_s", [DEPTH, NS, 8, 64, 64])
    gcv_p = dout("gcv_p", [DEPTH, 3, 1536])
    gcv_s = dout("gcv_s", [DEPTH, NS, 3, 1536])
    fcv_p = dout("fcv_p", [DEPTH, 2, DFF])
    fcv_s = dout("fcv_s", [DEPTH, NS, 2, DFF])
    wb_in = k.dram("wb_in", [DEPTH, DM, INW], BF16)
    wb_out = k.dram("wb_out", [DEPTH, DM, DM], BF16)
    wb_up = k.dram("wb_up", [DEPTH, DM, 2 * DFF], BF16)
    wb_down = k.dram("wb_down", [DEPTH, DFF, DM], BF16)
    wb_gate = k.dram("wb_gate", [DEPTH, DM, DM], BF16)
    wb_proj = k.dram("wb_proj", [DEPTH, PLED, DM], BF16)
    xT_d = k.dram("xT_d", [DM, T], BF16)
    mixT_d = k.dram("mixT_d", [DM, T], BF16)
    x1_d = k.dram("x1_d", [T, DM], F32)
    xsT_d = k.dram("xsT_d", [DM, NS], BF16)
    mixsT_d = k.dram("mixsT_d", [DM, NS], BF16)
    xs1_d = k.dram("xs1_d", [NS, DM], F32)

    pb = [k.ps("pb%d" % i, [128, 512]) for i in range(8)]

    with ExitStack() as gs:
        c32 = k.sb(gs, "c32", [128, NC32])
        k.dma("sp", c32[:], c32_d[:], writes=[c32])
        cb128 = k.sb(gs, "cb128", [128, NCB128], BF16)
        k.dma("sp", cb128[:], cb128_d[:], writes=[cb128])
        ident = c32.t[:, 0:128]
        Umat = c32.t[:, 128:256]
        ones32 = c32.t[:, 256:384]
        mask_incl = c32.t[:, 384:512]
        maskT_incl = c32.t[:, 512:640]
        strict01 = c32.t[:, 640:768]
        tkb = c32.t[:, 768:768 + NJ * 64]
        causal4 = cb128.t[:, 0:512]
        winedge4 = cb128.t[:, 512:1024]
        identb = cb128.t[:, 1024:1152]
        pool2 = cb128.t[:, 1152:1216]
        avg4 = cb128.t[:, 1216:1220]
        epsc = k.sb(gs, "epsc", [128, 2])
        k.op("pool", lambda e: e.memset(epsc[:, 0:1], RMS_EPS), writes=[epsc])
        k.op("pool", lambda e: e.memset(epsc[:, 1:2], LN_EPS), writes=[epsc])

        cast_rr = [0]

        def cast(out, in_, reads, writes, psum=False):
            e = ("dve", "act")[cast_rr[0] % 2] if psum else ("dve", "act", "pool")[cast_rr[0] % 3]
            cast_rr[0] += 1
            if e == "act":
                k.op("act", lambda en: en.copy(out, in_), reads=reads, writes=writes)
            else:
                k.op(e, lambda en: en.tensor_copy(out, in_), reads=reads, writes=writes)

        with ExitStack() as es:
            stg = [k.sb(es, "wstg%d" % i, [128, 2 * DFF]) for i in range(2)]
            stgb = [k.sb(es, "wstgb%d" % i, [128, 2 * DFF], BF16) for i in range(2)]
            it = 0
            for l in range(DEPTH):
                for (src, dst, rows, cols) in ((w_in, wb_in, DM, INW), (w_out, wb_out, DM, DM), (w_up, wb_up, DM, 2 * DFF),
                                               (w_down, wb_down, DFF, DM), (w_gate, wb_gate, DM, DM), (w_proj, wb_proj, PLED, DM)):
                    for r0 in range(0, rows, 128):
                        s, sb_ = stg[it % 2], stgb[it % 2]
                        it += 1
                        k.dma("sp", s[:, 0:cols], src[l, r0:r0 + 128, :], writes=[s])
                        cast(sb_[:, 0:cols], s[:, 0:cols], [s], [sb_])
                        k.dma("pool", dst[l, r0:r0 + 128, :], sb_[:, 0:cols], reads=[sb_], writes=[dst])
        k.barrier()

        def transpose_to_xT(es_name, src_tile, ntok, dstT, col0, trp, trs):
            for half in range(2):
                for c4 in range(4):
                    c = half * 4 + c4
                    k.op("pe", lambda e, c=c, c4=c4: e.transpose(trp[:, c4 * 128:c4 * 128 + ntok],
                                                               src_tile[0:ntok, c * 128:(c + 1) * 128], ident[0:ntok, 0:ntok]),
                         reads=[src_tile, c32], writes=[trp])
                cast(trs[:, half * 4:(half + 1) * 4, 0:ntok],
                     trp[:, :].rearrange("p (c t) -> p c t", c=4)[:, :, 0:ntok], [trp], [trs], psum=True)
            k.dma("pool", dstT.t.rearrange("(c p) t -> p c t", p=128)[:, :, col0:col0 + ntok], trs[:, :, 0:ntok],
                  reads=[trs], writes=[dstT])

        with ExitStack() as es:
            xt_in = [k.sb(es, "xt_in%d" % i, [128, DM]) for i in range(2)]
            trs = [k.sb(es, "trs%d" % i, [128, 8, 128], BF16) for i in range(2)]
            if do_prompt:
                for j in range(NJ):
                    xt = xt_in[j % 2]
                    k.dma("sp", xt[:], xp[j * 128:(j + 1) * 128, :], writes=[xt])
                    transpose_to_xT("x0", xt, 128, xT_d, j * 128, pb[j % 2], trs[j % 2])
            if do_sample:
                xt = xt_in[0]
                k.dma("sp", xt[0:NS, :], xs[:, :], writes=[xt])
                transpose_to_xT("xs0", xt, NS, xsT_d, 0, pb[2], trs[0])
        k.barrier()

        def gdn_prompt(l):
            with ExitStack() as es:
                wsrc = wb_in.t[l].rearrange("(c p) n -> p c n", p=128)
                wg = k.sb(es, "wg", [128, 8, 8, 192], BF16)
                for blk in range(3):
                    for kc in range(8):
                        k.dma("sp", wg[:, kc, :, blk * 64:(blk + 1) * 64],
                              wb_in.t[l, kc * 128:(kc + 1) * 128, C_GQKV + blk * 512:C_GQKV + (blk + 1) * 512].rearrange("p (h d) -> p h d", h=8),
                              reads=[wb_in], writes=[wg])
                wab = k.sb(es, "wab", [128, 8, 16], BF16)
                k.dma("sp", wab[:], wsrc[:, :, C_GA:C_GA + 16], reads=[wb_in], writes=[wab])
                wz = k.sb(es, "wz", [128, 8, 512], BF16)
                k.dma("sp", wz[:], wsrc[:, :, C_GZ:C_GZ + 512], reads=[wb_in], writes=[wz])
                cw = k.sb(es, "cw", [64, 8, 3, 4])
                for h in range(8):
                    for blk in range(3):
                        c0 = blk * 512 + h * 64
                        k.dma("sp", cw[:, h, blk, :], gconv_w[l][:, c0:c0 + 64].rearrange("w d -> d w"), writes=[cw],
                              allow_slow_non_contiguous=True)
                dtb = k.sb(es, "dtb", [128, 8])
                k.dma("sp", dtb[:], dt_bias[l:l + 1, :].partition_broadcast(128), writes=[dtb])
                negA = k.sb(es, "negA", [128, 8])
                k.dma("sp", negA[:], A_log[l:l + 1, :].partition_broadcast(128), writes=[negA])
                k.op("act", lambda e: e.activation(negA[:], negA[:], AF.Exp), reads=[negA], writes=[negA])
                k.op("dve", lambda e: e.tensor_scalar(negA[:], negA[:], -1.0, None, ALU.mult), reads=[negA], writes=[negA])
                nw = k.sb(es, "nw", [128, 64])
                k.dma("sp", nw[:], gnorm_w[l:l + 1, :].partition_broadcast(128), writes=[nw])
                S = [k.sb(es, "S%d" % h, [128, 64]) for h in range(8)]
                carry = [k.sb(es, "carry%d" % h, [64, 3, 3]) for h in range(8)]
                for h in range(8):
                    k.op("pool", lambda e: e.memset(S[h][:], 0.0), writes=[S[h]])
                    k.op("pool", lambda e: e.memset(carry[h][:], 0.0), writes=[carry[h]])
                xTt = [k.sb(es, "gxTt%d" % i, [128, 8, 128], BF16) for i in range(2)]
                tmp8 = k.sb(es, "tmp8", [128, 8])
                gtok = k.sb(es, "gtok", [128, 8])
                btok = k.sb(es, "btok", [128, 8])
                gctok = k.sb(es, "gctok", [128, 8])
                ngctok = k.sb(es, "ngctok", [128, 8])
                egctok = k.sb(es, "egctok", [128, 8])
                nz = k.sb(es, "nz", [128, 8, 64])
                og = k.sb(es, "og", [128, 512])
                ogT = k.sb(es, "ogT", [128, 4, 128], BF16)
                NSLOT = int(os.environ.get("DEV_NSLOT", "8"))
                RG = []
                for s_ in range(NSLOT):
                    row = []
                    for r in range(4):
                        rb = Buf(pb[s_].t[:, r * 128:(r + 1) * 128], "R%d_%d" % (s_, r), excl=True)
                        rb.w = pb[s_].w
                        rb.r = pb[s_].r
                        row.append(rb)
                    RG.append(row)
                PAB, PZ, PTR = (pb[4], pb[5], pb[6]) if NSLOT == 4 else (pb[7], pb[6], pb[5])

                def mk(s):
                    d = {}
                    for nm, shp in (("ext", [64, 3, 131]), ("y", [64, 3, 128]), ("ys", [64, 3, 128]), ("sq", [64, 256]), ("rs", [64, 256]),
                                    ("qTf", [64, 128]), ("kTf", [64, 128]), ("qgT", [128, 128]), ("rhsX", [128, 128]), ("kend", [128, 64]),
                                    ("Ug", [128, 128]), ("X1", [128, 128]), ("dec", [128, 128]), ("X2", [128, 128]), ("decT", [128, 128]),
                                    ("egcB", [64, 128]), ("glc", [128, 4]), ("tN", [128, 128]), ("N", [128, 128]), ("NT", [128, 128]),
                                    ("P0", [128, 128]), ("P1", [128, 128]), ("Q1", [128, 128]), ("W0", [128, 128]), ("W1", [128, 128]),
                                    ("innerT", [128, 128]), ("val", [128, 64]), ("kcdT", [64, 128]), ("vn", [128, 64]), ("junk", [128, 64]),
                                    ("rstd", [128, 2])):
                        d[nm] = k.sb(es, "%s_%d" % (nm, s), shp, F32R if (FASTF32 and nm in ("N", "NT", "P0", "P1", "Q1", "W0", "W1")) else F32)
                    k.op("pool", lambda e: e.memset(d["qgT"][:], 0.0), writes=[d["qgT"]])
                    return d
                WK = [mk(s) for s in range(NSLOT)]
                id64 = ident[0:64, 0:64]
                ones64 = ones32[0:64, 0:64]

                def head_chain(h, i, slot, xt):
                    R = RG[slot]
                    w = WK[slot]
                    bank = pb[slot].t
                    ext, y, ys = w["ext"], w["y"], w["ys"]
                    for blk in range(3):
                        for kc in range(8):
                            k.op("pe", lambda e: e.matmul(R[blk][0:64, :], wg[:, kc, h, blk * 64:(blk + 1) * 64], xt[:, kc, :],
                                                          start=(kc == 0), stop=(kc == 7)), reads=[wg, xt], writes=[R[blk]], inc=(kc == 7))
                    yield
                    k.op("pool", lambda e: e.tensor_copy(ext[:, :, 0:3], carry[h][:]), reads=[carry[h]], writes=[ext])
                    k.op("act", lambda e: e.copy(ext[:, :, 3:131], bank[0:64, 0:384].rearrange("p (b t) -> p b t", b=3)),
                         reads=[R[0], R[1], R[2]], writes=[ext])
                    k.op("pool", lambda e: e.tensor_copy(carry[h][:], ext[:, :, 128:131]), reads=[ext], writes=[carry[h]])
                    yield
                    for blk in range(3):
                        eng = "dve"
                        k.op(eng, lambda e: e.tensor_scalar(y[:, blk, :], ext[:, blk, 0:128], cw[:, h, blk, 0:1], None, ALU.mult),
                             reads=[ext, cw], writes=[y])
                        for tap in range(1, 4):
                            k.op(eng, lambda e: e.scalar_tensor_tensor(y[:, blk, :], ext[:, blk, tap:tap + 128], cw[:, h, blk, tap:tap + 1], y[:, blk, :],
                                                                       ALU.mult, ALU.add), reads=[ext, cw, y], writes=[y])
                    yield
                    k.op("act", lambda e: e.activation(ys[:], y[:], AF.Silu), reads=[y], writes=[ys])
                    k.op("pool", lambda e: e.tensor_tensor(w["sq"][:], ys[:, 0:2, :].rearrange("p b t -> p (b t)"), ys[:, 0:2, :].rearrange("p b t -> p (b t)"), ALU.mult),
                         reads=[ys], writes=[w["sq"]])
                    yield
                    k.op("pe", lambda e: e.matmul(R[0][0:64, :], ones64, w["sq"][:, 0:128], start=True, stop=True), reads=[c32, w["sq"]], writes=[R[0]])
                    k.op("pe", lambda e: e.matmul(R[1][0:64, :], ones64, w["sq"][:, 128:256], start=True, stop=True), reads=[c32, w["sq"]], writes=[R[1]])
                    yield
                    sub = int(os.environ.get("DEV_SUB", "9"))
                    if sub >= 1:
                        k.op("act", lambda e: e.activation(w["rs"][:], bank[0:64, 0:256], AF.Sqrt, bias=epsc[0:64, 0:1], scale=1.0),
                             reads=[R[0], R[1], epsc], writes=[w["rs"]])
                    if sub >= 2:
                        k.op("dve", lambda e: e.reciprocal(w["rs"][:], w["rs"][:]), reads=[w["rs"]], writes=[w["rs"]])
                    if sub >= 3:
                        k.op("dve", lambda e: e.scalar_tensor_tensor(w["qTf"][:], ys[:, 0, :], 0.125, w["rs"][:, 0:128], ALU.mult, ALU.mult),
                             reads=[ys, w["rs"]], writes=[w["qTf"]])
                    if sub >= 4:
                        k.op("dve", lambda e: e.tensor_tensor(w["kTf"][:], ys[:, 1, :], w["rs"][:, 128:256], ALU.mult), reads=[ys, w["rs"]], writes=[w["kTf"]])
                    yield
                    k.op("dve", lambda e: e.tensor_scalar(w["Ug"][:], Umat, gtok[:, h:h + 1], None, ALU.mult), reads=[c32, gtok], writes=[w["Ug"]])
                    k.op("pe", lambda e: e.matmul(R[3][:, :], ones32, w["Ug"][:], start=True, stop=True), reads=[c32, w["Ug"]], writes=[R[3]])
                    k.op("pe", lambda e: e.transpose(R[2][:, 0:64], w["kTf"][:], id64), reads=[w["kTf"], c32], writes=[R[2]])
                    k.op("pe", lambda e: e.transpose(R[2][:, 64:128], ys[:, 2, :], id64), reads=[ys, c32], writes=[R[2]])
                    yield
                    sub8 = int(os.environ.get("DEV_SUB8", "99"))
                    if sub8 >= 1:
                        k.op("dve", lambda e: e.scalar_tensor_tensor(w["X1"][:], R[3][:, :], -1.0, mask_incl, ALU.mult, ALU.add), reads=[R[3], c32], writes=[w["X1"]])
                    if sub8 >= 2:
                        k.op("dve", lambda e: e.tensor_tensor(w["X2"][:], R[3][:, :], maskT_incl, ALU.add), reads=[R[3], c32], writes=[w["X2"]])
                    if sub8 >= 3:
                        k.op("act", lambda e: e.activation(w["egcB"][:], R[3][0:64, :], AF.Exp), reads=[R[3]], writes=[w["egcB"]])
                    if sub8 >= 4:
                        k.op("act", lambda e: e.copy(w["glc"][:, 0:1], R[3][:, 127:128]), reads=[R[3]], writes=[w["glc"]])
                    if sub8 >= 5:
                        k.op("act", lambda e: e.activation(w["dec"][:], w["X1"][:], AF.Exp, bias=gctok[:, h:h + 1], scale=1.0), reads=[w["X1"], gctok], writes=[w["dec"]])
                    if sub8 >= 6:
                        k.op("act", lambda e: e.activation(w["decT"][:], w["X2"][:], AF.Exp, bias=ngctok[:, h:h + 1], scale=1.0), reads=[w["X2"], ngctok], writes=[w["decT"]])
                    if sub8 >= 7:
                        k.op("act", lambda e: e.activation(w["glc"][:, 1:2], w["glc"][:, 0:1], AF.Exp), reads=[w["glc"]], writes=[w["glc"]])
                    if sub8 >= 8:
                        k.op("act", lambda e: e.activation(w["glc"][:, 2:3], ngctok[:, h:h + 1], AF.Exp, bias=w["glc"][:, 0:1], scale=1.0),
                             reads=[w["glc"], ngctok], writes=[w["glc"]])
                    yield
                    k.op("dve", lambda e: e.tensor_tensor(w["qgT"][0:64, :], w["qTf"][:], w["egcB"][:], ALU.mult), reads=[w["qTf"], w["egcB"]], writes=[w["qgT"]])
                    k.op("dve", lambda e: e.tensor_scalar(w["rhsX"][:, 0:64], R[2][:, 64:128], btok[:, h:h + 1], None, ALU.mult),
                         reads=[R[2], btok], writes=[w["rhsX"]])
                    k.op("dve", lambda e: e.tensor_scalar(w["rhsX"][:, 64:128], R[2][:, 0:64], btok[:, h:h + 1], egctok[:, h:h + 1], ALU.mult, ALU.mult),
                         reads=[R[2], btok, egctok], writes=[w["rhsX"]])
                    k.op("dve", lambda e: e.tensor_scalar(w["kend"][:], R[2][:, 0:64], w["glc"][:, 2:3], None, ALU.mult), reads=[R[2], w["glc"]], writes=[w["kend"]])
                    yield
                    k.op("pe", lambda e: e.matmul(R[0][:, :], w["kTf"][:], w["kTf"][:], start=True, stop=True), reads=[w["kTf"]], writes=[R[0]])
                    k.op("pe", lambda e: e.matmul(R[1][:, :], w["kTf"][:], w["qTf"][:], start=True, stop=True), reads=[w["kTf"], w["qTf"]], writes=[R[1]])
                    yield
                    k.op("dve", lambda e: e.tensor_tensor(w["tN"][:], R[0][:, :], w["dec"][:], ALU.mult), reads=[R[0], w["dec"]], writes=[w["tN"]])
                    k.op("dve", lambda e: e.scalar_tensor_tensor(w["N"][:], w["tN"][:], btok[:, h:h + 1], strict01, ALU.mult, ALU.mult),
                         reads=[w["tN"], btok, c32], writes=[w["N"]])
                    k.op("dve", lambda e: e.tensor_tensor(w["innerT"][:], R[1][:, :], w["decT"][:], ALU.mult), reads=[R[1], w["decT"]], writes=[w["innerT"]])
                    k.op("pe", lambda e: e.transpose(R[2][:, :], f32v(w["N"][:]), ident), reads=[w["N"], c32], writes=[R[2]])
                    yield
                    k.op("act", lambda e: e.copy(w["NT"][:], R[2][:, :]), reads=[R[2]], writes=[w["NT"]])
                    k.op("dve", lambda e: e.tensor_tensor(w["W0"][:], ident, R[2][:, :], ALU.subtract), reads=[c32, R[2]], writes=[w["W0"]])
                    yield
                    P, Q, Wc = w["N"], w["NT"], w["W0"]
                    Pn_l = [w["P0"], w["P1"]]
                    Qn_l = [w["Q1"], w["NT"]]
                    Wn_l = [w["W1"], w["W0"]]
                    for kk in range(1, 7):
                        Pn = Pn_l[kk % 2]
                        k.op("pe", lambda e: e.matmul(R[0][:, :], fr(Q[:]), fr(P[:]), start=True, stop=True), reads=[Q, P], writes=[R[0]])
                        if kk <= 5:
                            Qn = (w["NT"], w["Q1"])[kk % 2]
                            k.op("pe", lambda e: e.matmul(R[1][:, :], fr(P[:]), fr(Q[:]), start=True, stop=True), reads=[Q, P], writes=[R[1]])
                        yield
                        k.op("act", lambda e: e.copy(Pn[:], R[0][:, :]), reads=[R[0]], writes=[Pn])
                        if kk <= 5:
                            k.op("dve", lambda e: e.tensor_copy(Qn[:], R[1][:, :]), reads=[R[1]], writes=[Qn])
                        yield
                        Wn = Wn_l[(kk - 1) % 2]
                        k.op("pe", lambda e: e.matmul(R[2][:, :], fr(Pn[:]), fr(Wc[:]), start=True, stop=True), reads=[Pn, Wc], writes=[R[2]])
                        yield
                        k.op("dve", lambda e: e.tensor_tensor(Wn[:], f32v(Wc[:]), R[2][:, :], ALU.add), reads=[Wc, R[2]], writes=[Wn])
                        P, Wc = Pn, Wn
                        if kk <= 5:
                            Q = Qn
                        yield
                    k.op("pe", lambda e: e.matmul(R[3][:, 0:64], f32v(Wc[:]), w["rhsX"][:, 0:64], start=True, stop=True), reads=[Wc, w["rhsX"]], writes=[R[3]])
                    k.op("pe", lambda e: e.matmul(R[0][0:64, :], w["rhsX"][:, 64:128], f32v(Wc[:]), start=True, stop=True), reads=[Wc, w["rhsX"]], writes=[R[0]])
                    yield
                    k.op("act", lambda e: e.copy(w["val"][:], R[3][:, 0:64]), reads=[R[3]], writes=[w["val"]])
                    k.op("act", lambda e: e.copy(w["kcdT"][:], R[0][0:64, :]), reads=[R[0]], writes=[w["kcdT"]])
                    yield
                    k.op("pe", lambda e: e.matmul(R[1][:, 0:64], w["kcdT"][:], S[h][0:64, :], start=True, stop=True), reads=[w["kcdT"], S[h]], writes=[R[1]])
                    yield
                    k.op("dve", lambda e: e.tensor_tensor(w["vn"][:], w["val"][:], R[1][:, 0:64], ALU.subtract), reads=[w["val"], R[1]], writes=[w["vn"]])
                    yield
                    k.op("pe", lambda e: e.matmul(R[2][:, 0:64], w["qgT"][:], S[h][:], start=True, stop=False), reads=[w["qgT"], S[h]], writes=[R[2]], inc=False)
                    k.op("pe", lambda e: e.matmul(R[2][:, 0:64], w["innerT"][:], w["vn"][:], start=False, stop=True), reads=[w["innerT"], w["vn"]], writes=[R[2]])
                    k.op("pe", lambda e: e.matmul(R[3][0:64, 0:64], w["kend"][:], w["vn"][:], start=True, stop=True), reads=[w["kend"], w["vn"]], writes=[R[3]])
                    yield
                    k.op("dve", lambda e: e.scalar_tensor_tensor(S[h][0:64, :], S[h][0:64, :], w["glc"][0:64, 1:2], R[3][0:64, 0:64], ALU.mult, ALU.add),
                         reads=[S[h], w["glc"], R[3]], writes=[S[h]])
                    k.op("pool", lambda e: e.memset(w["rstd"][:, 0:1], 0.0), writes=[w["rstd"]])
                    k.op("act", lambda e: e.activation(w["junk"][:], R[2][:, 0:64], AF.Square, accum_out=w["rstd"][:, 0:1]), reads=[R[2]], writes=[w["junk"], w["rstd"]])
                    yield
                    k.op("act", lambda e: e.activation(w["rstd"][:, 1:2], w["rstd"][:, 0:1], AF.Sqrt, bias=epsc[:, 0:1], scale=1.0 / 64), reads=[w["rstd"], epsc], writes=[w["rstd"]])
                    k.op("dve", lambda e: e.reciprocal(w["rstd"][:, 1:2], w["rstd"][:, 1:2]), reads=[w["rstd"]], writes=[w["rstd"]])
                    k.op("dve", lambda e: e.scalar_tensor_tensor(og[:, h * 64:(h + 1) * 64], R[2][:, 0:64], w["rstd"][:, 1:2], nz[:, h, :], ALU.mult, ALU.mult),
                         reads=[R[2], w["rstd"], nz], writes=[og])
                    yield

                for i in range(NJ):
                    xt = xTt[i % 2]
                    k.dma("sp", xt[:], xT_d.t.rearrange("(c p) t -> p c t", p=128)[:, :, i * 128:(i + 1) * 128], reads=[xT_d], writes=[xt])
                    for kc in range(8):
                        k.op("pe", lambda e: e.matmul(PAB[:, 0:16], xt[:, kc, :], wab[:, kc, :], start=(kc == 0), stop=(kc == 7)),
                             reads=[xt, wab], writes=[PAB], inc=(kc == 7))
                    for kc in range(8):
                        k.op("pe", lambda e: e.matmul(PZ[:, :], xt[:, kc, :], wz[:, kc, :], start=(kc == 0), stop=(kc == 7)),
                             reads=[xt, wz], writes=[PZ], inc=(kc == 7))
                    k.op("dve", lambda e: e.tensor_tensor(tmp8[:], PAB[:, 0:8], dtb[:], ALU.add), reads=[PAB, dtb], writes=[tmp8])
                    k.op("act", lambda e: e.activation(tmp8[:], tmp8[:], AF.Exp), reads=[tmp8], writes=[tmp8])
                    k.op("act", lambda e: e.activation(tmp8[:], tmp8[:], AF.Ln, bias=1.0, scale=1.0), reads=[tmp8], writes=[tmp8])
                    k.op("dve", lambda e: e.tensor_tensor(gtok[:], tmp8[:], negA[:], ALU.mult), reads=[tmp8, negA], writes=[gtok])
                    k.op("act", lambda e: e.activation(btok[:], PAB[:, 8:16], AF.Sigmoid), reads=[PAB], writes=[btok])
                    k.op("pe", lambda e: e.matmul(PAB[:, 16:24], Umat, gtok[:], start=True, stop=True), reads=[c32, gtok], writes=[PAB])
                    k.op("act", lambda e: e.copy(gctok[:], PAB[:, 16:24]), reads=[PAB], writes=[gctok])
                    k.op("dve", lambda e: e.tensor_scalar(ngctok[:], PAB[:, 16:24], -1.0, None, ALU.mult), reads=[PAB], writes=[ngctok])
                    k.op("act", lambda e: e.activation(egctok[:], gctok[:], AF.Exp), reads=[gctok], writes=[egctok])
                    k.op("act", lambda e: e.activation(nz[:].rearrange("p h d -> p (h d)"), PZ[:, :], AF.Silu), reads=[PZ], writes=[nz])
                    k.op("pool", lambda e: e.tensor_tensor(nz[:], nz[:], nw[:].unsqueeze(1).to_broadcast([128, 8, 64]), ALU.mult), reads=[nz, nw], writes=[nz])
                    for wave in range(8 // NSLOT):
                        gens = [head_chain(wave * NSLOT + s, i, s, xt) for s in range(NSLOT)]
                        alive = list(gens)
                        gsteps = int(os.environ.get("DEV_GSTEPS", "1000"))
                        nst = 0
                        while alive and nst < gsteps:
                            nst += 1
                            nxt = []
                            for gdef in alive:
                                try:
                                    next(gdef)
                                    nxt.append(gdef)
                                except StopIteration:
                                    pass
                            alive = nxt
                    for c in range(4):
                        k.op("pe", lambda e: e.transpose(PTR[:, c * 128:(c + 1) * 128], og[:, c * 128:(c + 1) * 128], ident), reads=[og, c32], writes=[PTR])
                    k.op("act", lambda e: e.copy(ogT[:], PTR[:, :].rearrange("p (c t) -> p c t", c=4)), reads=[PTR], writes=[ogT])
                    k.dma("pool", mixT_d.t[512:1024, :].rearrange("(c p) t -> p c t", p=128)[:, :, i * 128:(i + 1) * 128], ogT[:],
                          reads=[ogT], writes=[mixT_d])
                for h in range(8):
                    k.dma("pool", gst_p[l, h], S[h][0:64, :], reads=[S[h]], writes=[gst_p])
                    for blk in range(3):
                        c0 = blk * 512 + h * 64
                        k.dma("pool", gcv_p[l][:, c0:c0 + 64].rearrange("w d -> d w"), carry[h][:, blk, :], reads=[carry[h]], writes=[gcv_p],
                              allow_slow_non_contiguous=True)
            k.barrier()

        def chain(l, xres_p, xres_s, yout_p, yout_s, last):
            TS = 256
            with ExitStack() as es:
                wo = k.sb(es, "wo", [128, 8, DM], BF16)
                k.dma("sp", wo[:], wb_out.t[l].rearrange("(c p) n -> p c n", p=128), reads=[wb_out], writes=[wo])
                wgt = k.sb(es, "wgt", [128, 8, DM], BF16)
                k.dma("sp", wgt[:], wb_gate.t[l].rearrange("(c p) n -> p c n", p=128), reads=[wb_gate], writes=[wgt])
                wpj = k.sb(es, "wpj", [128, 2, DM], BF16)
                k.dma("sp", wpj[:], wb_proj.t[l].rearrange("(c p) n -> p c n", p=128), reads=[wb_proj], writes=[wpj])
                wdn = k.sb(es, "wdn", [128, 22, DM], BF16)
                k.dma("sp", wdn[:], wb_down.t[l].rearrange("(c p) n -> p c n", p=128), reads=[wb_down], writes=[wdn])
                lng = k.sb(es, "lng", [128, 3, DM])
                lnb = k.sb(es, "lnb", [128, 3, DM])
                for i in range(3):
                    k.dma("sp", lng[:, i, :], ln_g[l, i:i + 1, :].partition_broadcast(128), writes=[lng])
                    k.dma("sp", lnb[:, i, :], ln_b[l, i:i + 1, :].partition_broadcast(128), writes=[lnb])
                fcw = k.sb(es, "fcw", [128, 22, 3])
                for tap in range(3):
                    k.dma("sp", fcw[:, :, tap], fconv_w[l, tap].rearrange("(c p) -> p c", p=128), writes=[fcw], allow_slow_non_contiguous=True)
                wup = [k.sb(es, "wup%d" % i, [128, 8, 256], BF16) for i in range(3)]
                mixT = k.sb(es, "mixT", [128, 8, TS], BF16)
                x1T = k.sb(es, "x1T", [128, 8, TS], BF16)
                x2T = k.sb(es, "x2T", [128, 8, 128], BF16)
                hid = k.sb(es, "hid", [128, 22, TS], BF16)
                ext = [k.sb(es, "fext%d" % i, [128, TS + 2]) for i in range(2)]
                hg = [k.sb(es, "hg%d" % i, [128, TS]) for i in range(2)]
                carry = k.sb(es, "fcarry", [128, 22, 2])
                k.op("pool", lambda e: e.memset(carry[:], 0.0), writes=[carry])
                xres = k.sb(es, "xres", [128, DM])
                x1 = [k.sb(es, "x1_%d" % i, [128, DM]) for i in range(2)]
                x2 = k.sb(es, "x2", [128, DM])
                x3 = xres
                rr = k.sb(es, "rr", [128, DM])
                sig = k.sb(es, "sig", [128, DM])
                junk = sig
                st = k.sb(es, "lnst", [128, 8])
                ptk = k.sb(es, "ptk", [128, PLED])
                pT = k.sb(es, "pT", [128, 2, 128], BF16)
                trs = k.sb(es, "ctrs", [128, 8, 128], BF16)
                sgo = k.sb(es, "sgo", [NS, DFF])
                sbT = k.sb(es, "sbT", [128, 22, 2, NS])
                sgT = k.sb(es, "sgT", [128, 22, NS])
                PY = (pb[0], pb[1])
                PU = (pb[2], pb[3], pb[4], pb[5])
                PT_, PM = pb[6], pb[7]

                def layer_norm(src, nt, idx, dst):
                    k.op("pool", lambda e: e.memset(st[:, 0:4], 0.0), writes=[st])
                    k.op("act", lambda e: e.activation(junk[0:nt, :], src[0:nt, :], AF.Identity, accum_out=st[0:nt, 0:1]), reads=[src], writes=[junk, st])
                    k.op("dve", lambda e: e.tensor_scalar(st[0:nt, 1:2], st[0:nt, 0:1], -1.0 / DM, None, ALU.mult), reads=[st], writes=[st])
                    k.op("act", lambda e: e.activation(junk[0:nt, :], src[0:nt, :], AF.Square, bias=st[0:nt, 1:2], scale=1.0, accum_out=st[0:nt, 2:3]),
                         reads=[src, st], writes=[junk, st])
                    k.op("act", lambda e: e.activation(st[0:nt, 3:4], st[0:nt, 2:3], AF.Sqrt, bias=epsc[0:nt, 1:2], scale=1.0 / DM), reads=[st, epsc], writes=[st])
                    k.op("dve", lambda e: e.reciprocal(st[0:nt, 3:4], st[0:nt, 3:4]), reads=[st], writes=[st])
                    k.op("dve", lambda e: e.tensor_scalar(dst[0:nt, :], src[0:nt, :], st[0:nt, 1:2], st[0:nt, 3:4], ALU.add, ALU.mult), reads=[src, st], writes=[dst])
                    k.op("dve", lambda e: e.tensor_tensor(dst[0:nt, :], dst[0:nt, :], lng[0:nt, idx, :], ALU.mult), reads=[dst, lng], writes=[dst])
                    k.op("pool", lambda e: e.tensor_tensor(dst[0:nt, :], dst[0:nt, :], lnb[0:nt, idx, :], ALU.add), reads=[dst, lnb], writes=[dst])

                def to_T(src, nt, dstT, c0):
                    for half in range(2):
                        for c4 in range(4):
                            c = half * 4 + c4
                            k.op("pe", lambda e: e.transpose(PT_[:, c4 * 128:c4 * 128 + nt], src[0:nt, c * 128:(c + 1) * 128], ident[0:nt, 0:nt]),
                                 reads=[src, c32], writes=[PT_])
                        k.op("act", lambda e: e.copy(dstT[:, half * 4:(half + 1) * 4, c0:c0 + nt],
                                                     PT_[:, :].rearrange("p (c t) -> p c t", c=4)[:, :, 0:nt]), reads=[PT_], writes=[dstT])

                def supertile(kind, tok0, tiles):
                    ntot = sum(nt for _, nt in tiles)
                    if kind == "p":
                        srcT, xr_d, p_d, y_d, nxtT = mixT_d, xres_p, pp, yout_p, xT_d
                    else:
                        srcT, xr_d, p_d, y_d, nxtT = mixsT_d, xres_s, pps, yout_s, xsT_d
                    k.dma("sp", mixT[:, :, 0:ntot], srcT.t.rearrange("(c p) t -> p c t", p=128)[:, :, tok0:tok0 + ntot], reads=[srcT], writes=[mixT])
                    for ti, (o0, nt) in enumerate(tiles):
                        k.dma("sp", xres[0:nt, :], xr_d[tok0 + o0:tok0 + o0 + nt, :], reads=[xr_d], writes=[xres])
                        for hb in range(2):
                            for kc in range(8):
                                k.op("pe", lambda e: e.matmul(PY[hb][0:nt, :], mixT[:, kc, o0:o0 + nt], wo[:, kc, hb * 512:(hb + 1) * 512],
                                                              start=(kc == 0), stop=(kc == 7)), reads=[mixT, wo], writes=[PY[hb]], inc=(kc == 7))
                        for hb in range(2):
                            k.op("dve", lambda e: e.scalar_tensor_tensor(rr[0:nt, hb * 512:(hb + 1) * 512], xres[0:nt, hb * 512:(hb + 1) * 512], ALPHA,
                                                                         PY[hb][0:nt, :], ALU.mult, ALU.add), reads=[xres, PY[hb]], writes=[rr])
                        layer_norm(rr, nt, 0, x1[ti])
                        to_T(x1[ti], nt, x1T, o0)
                    if kind == "s":
                        for r in range(2):
                            k.dma("sp", sgo[:], st_fconv[l, :, r, :], writes=[sgo])
                            for c in range(22):
                                k.op("pe", lambda e: e.matmul(PM[:, (c % 16) * NS:(c % 16 + 1) * NS], sgo[0:NS, c * 128:(c + 1) * 128], ident[0:NS, 0:NS], start=True, stop=True),
                                     reads=[sgo, c32], writes=[PM])
                                if c % 16 == 15 or c == 21:
                                    cs = (c // 16) * 16
                                    k.op("act", lambda e: e.copy(sbT[:, cs:c + 1, r, :], PM[:, 0:(c - cs + 1) * NS].rearrange("p (c s) -> p c s", s=NS)),
                                         reads=[PM], writes=[sbT])
                        k.dma("pool", fcv_s[l, :, 0, :], st_fconv[l, :, 1, :], reads=[st_fconv], writes=[fcv_s])
                    for c in range(22):
                        wu = wup[c % 3]
                        wsrc = wb_up.t[l].rearrange("(kc p) n -> p kc n", p=128)
                        k.dma("sp", wu[:, :, 0:128], wsrc[:, :, c * 128:(c + 1) * 128], reads=[wb_up], writes=[wu])
                        k.dma("sp", wu[:, :, 128:256], wsrc[:, :, DFF + c * 128:DFF + (c + 1) * 128], reads=[wb_up], writes=[wu])
                        PG, PV = PU[(c % 2) * 2], PU[(c % 2) * 2 + 1]
                        for kc in range(8):
                            k.op("pe", lambda e: e.matmul(PG[:, 0:ntot], wu[:, kc, 0:128], x1T[:, kc, 0:ntot], start=(kc == 0), stop=(kc == 7)),
                                 reads=[wu, x1T], writes=[PG], inc=(kc == 7))
                        for kc in range(8):
                            k.op("pe", lambda e: e.matmul(PV[:, 0:ntot], wu[:, kc, 128:256], x1T[:, kc, 0:ntot], start=(kc == 0), stop=(kc == 7)),
                                 reads=[wu, x1T], writes=[PV], inc=(kc == 7))
                        h_ = hg[c % 2]
                        if kind == "p":
                            ex = ext[c % 2]
                            k.op("pool", lambda e: e.tensor_copy(ex[:, 0:2], carry[:, c, :]), reads=[carry], writes=[ex])
                            k.op("act", lambda e: e.copy(ex[:, 2:2 + ntot], PG[:, 0:ntot]), reads=[PG], writes=[ex])
                            k.op("pool", lambda e: e.tensor_copy(carry[:, c, :], ex[:, ntot:ntot + 2]), reads=[ex], writes=[carry])
                            k.op("dve", lambda e: e.tensor_scalar(h_[:, 0:ntot], ex[:, 0:ntot], fcw[:, c, 0:1], None, ALU.mult), reads=[ex, fcw], writes=[h_])
                            for tap in (1, 2):
                                k.op("dve", lambda e: e.scalar_tensor_tensor(h_[:, 0:ntot], ex[:, tap:tap + ntot], fcw[:, c, tap:tap + 1], h_[:, 0:ntot],
                                                                             ALU.mult, ALU.add), reads=[ex, fcw, h_], writes=[h_])
                        else:
                            k.op("act", lambda e: e.copy(sgT[:, c, :], PG[:, 0:NS]), reads=[PG], writes=[sgT])
                            k.op("dve", lambda e: e.tensor_scalar(h_[:, 0:NS], sbT[:, c, 0, :], fcw[:, c, 0:1], None, ALU.mult), reads=[sbT, fcw], writes=[h_])
                            k.op("dve", lambda e: e.scalar_tensor_tensor(h_[:, 0:NS], sbT[:, c, 1, :], fcw[:, c, 1:2], h_[:, 0:NS], ALU.mult, ALU.add),
                                 reads=[sbT, fcw, h_], writes=[h_])
                            k.op("dve", lambda e: e.scalar_tensor_tensor(h_[:, 0:NS], sgT[:, c, :], fcw[:, c, 2:3], h_[:, 0:NS], ALU.mult, ALU.add),
                                 reads=[sgT, fcw, h_], writes=[h_])
                        k.op("act", lambda e: e.activation(h_[:, 0:ntot], h_[:, 0:ntot], AF.Gelu), reads=[h_], writes=[h_])
                        k.op("dve", lambda e: e.tensor_tensor(hid[:, c, 0:ntot], h_[:, 0:ntot], PV[:, 0:ntot], ALU.mult), reads=[h_, PV], writes=[hid])
                    if kind == "s":
                        for c in range(22):
                            k.op("pe", lambda e: e.transpose(PM[0:NS, (c % 4) * 128:(c % 4 + 1) * 128], sgT[:, c, :], ident), reads=[sgT, c32], writes=[PM])
                            if c % 4 == 3 or c == 21:
                                cs = (c // 4) * 4
                                k.op("act", lambda e: e.copy(sgo[:, cs * 128:(c + 1) * 128], PM[0:NS, 0:(c - cs + 1) * 128]), reads=[PM], writes=[sgo])
                        k.dma("pool", fcv_s[l, :, 1, :], sgo[:], reads=[sgo], writes=[fcv_s])
                    for ti, (o0, nt) in enumerate(tiles):
                        for c in range(22):
                            for hb in range(2):
                                k.op("pe", lambda e: e.matmul(PY[hb][0:nt, :], hid[:, c, o0:o0 + nt], wdn[:, c, hb * 512:(hb + 1) * 512],
                                                              start=(c == 0), stop=(c == 21)), reads=[hid, wdn], writes=[PY[hb]], inc=(c == 21))
                        for hb in range(2):
                            k.op("dve", lambda e: e.scalar_tensor_tensor(rr[0:nt, hb * 512:(hb + 1) * 512], x1[ti][0:nt, hb * 512:(hb + 1) * 512], ALPHA,
                                                                         PY[hb][0:nt, :], ALU.mult, ALU.add), reads=[x1[ti], PY[hb]], writes=[rr])
                        layer_norm(rr, nt, 1, x2)
                        to_T(x2, nt, x2T, 0)
                        k.dma("sp", ptk[0:nt, :], p_d[l, tok0 + o0:tok0 + o0 + nt, :], reads=[p_d], writes=[ptk])
                        for c in range(2):
                            k.op("pe", lambda e: e.transpose(PM[:, c * 128:c * 128 + nt], ptk[0:nt, c * 128:(c + 1) * 128], ident[0:nt, 0:nt]), reads=[ptk, c32], writes=[PM])
                        k.op("act", lambda e: e.copy(pT[:, :, 0:nt], PM[:, 0:256].rearrange("p (c t) -> p c t", c=2)[:, :, 0:nt]), reads=[PM], writes=[pT])
                        for hb in range(2):
                            for kc in range(8):
                                k.op("pe", lambda e: e.matmul(PY[hb][0:nt, :], x2T[:, kc, 0:nt], wgt[:, kc, hb * 512:(hb + 1) * 512],
                                                              start=(kc == 0), stop=(kc == 7)), reads=[x2T, wgt], writes=[PY[hb]], inc=(kc == 7))
                            k.op("act", lambda e: e.activation(sig[0:nt, hb * 512:(hb + 1) * 512], PY[hb][0:nt, :], AF.Sigmoid), reads=[PY[hb]], writes=[sig])
                        for hb in range(2):
                            for c in range(2):
                                k.op("pe", lambda e: e.matmul(PY[hb][0:nt, :], pT[:, c, 0:nt], wpj[:, c, hb * 512:(hb + 1) * 512],
                                                              start=(c == 0), stop=(c == 1)), reads=[pT, wpj], writes=[PY[hb]], inc=(c == 1))
                            k.op("dve", lambda e: e.tensor_tensor(sig[0:nt, hb * 512:(hb + 1) * 512], sig[0:nt, hb * 512:(hb + 1) * 512], PY[hb][0:nt, :], ALU.mult),
                                 reads=[sig, PY[hb]], writes=[sig])
                        k.op("dve", lambda e: e.scalar_tensor_tensor(rr[0:nt, :], x2[0:nt, :], ALPHA, sig[0:nt, :], ALU.mult, ALU.add), reads=[x2, sig], writes=[rr])
                        layer_norm(rr, nt, 2, x3)
                        k.dma("pool", y_d[tok0 + o0:tok0 + o0 + nt, :], x3[0:nt, :], reads=[x3], writes=[y_d])
                        if not last:
                            to_T(x3, nt, trs, 0)
                            k.dma("pool", nxtT.t.rearrange("(c p) t -> p c t", p=128)[:, :, tok0 + o0:tok0 + o0 + nt], trs[:, :, 0:nt], reads=[trs], writes=[nxtT])

                if do_prompt:
                    for s0 in range(0, T, TS):
                        supertile("p", s0, [(o, 128) for o in range(0, TS, 128)])
                    for tap in range(2):
                        k.dma("pool", fcv_p[l, tap].rearrange("(c p) -> p c", p=128), carry[:, :, tap], reads=[carry], writes=[fcv_p],
                              allow_slow_non_contiguous=True)
                if do_sample:
                    supertile("s", 0, [(0, NS)])
            k.barrier()

        def sample_mixers(l):
            NQ = NS * 8
            with ExitStack() as es:
                cs32 = k.sb(es, "cs32", [128, NCS32])
                k.dma("sp", cs32[:], cs32_d[:], writes=[cs32])
                csb = k.sb(es, "csb", [128, NCSB], BF16)
                k.dma("sp", csb[:], csb_d[:], writes=[csb])
                o_ = 0
                alibi_s = cs32.t[:, o_:o_ + 2 * NPG * 4]
                o_ += 2 * NPG * 4
                alibi_w = cs32.t[:, o_:o_ + 2 * NWT * 4]
                o_ += 2 * NWT * 4
                alibi_c = cs32.t[:, o_:o_ + NQ]
                o_ += NQ
                GS = cs32.t[:, o_:o_ + 2 * NS]
                o_ += 2 * NS
                tkb_s = cs32.t[:, o_:o_ + 40]
                o_ += 40
                piota = cs32.t[:, o_:o_ + 2]
                NB33 = NPG * 2 + 1
                OH = csb.t[0:2 * NS, 0:2 * NS * 128]
                pool33 = csb.t[0:NPG * 4, 2 * NS * 128:2 * NS * 128 + NB33 + 1]
                B0, B1, B2, B3, B4, B5, B6, B7 = pb
                xsT = k.sb(es, "xsT", [128, 8, NS], BF16)
                k.dma("sp", xsT[:], xsT_d.t.rearrange("(c p) s -> p c s", p=128), reads=[xsT_d], writes=[xsT])
                hs = k.sb(es, "hs", [NS, INW])
                wch = [k.sb(es, "wch%d" % i, [128, 8, 512], BF16) for i in range(2)]
                wsrc = wb_in.t[l].rearrange("(c p) n -> p c n", p=128)
                for ch in range(7):
                    c0 = ch * 512
                    cw_ = min(512, INW - c0)
                    wc = wch[ch % 2]
                    k.dma("sp", wc[:, :, 0:cw_], wsrc[:, :, c0:c0 + cw_], reads=[wb_in], writes=[wc])
                    for kc in range(8):
                        k.op("pe", lambda e: e.matmul(B0[0:NS, 0:cw_], xsT[:, kc, :], wc[:, kc, 0:cw_], start=(kc == 0), stop=(kc == 7)),
                             reads=[xsT, wc], writes=[B0], inc=(kc == 7))
                    k.op("act", lambda e: e.copy(hs[:, c0:c0 + cw_], B0[0:NS, 0:cw_]), reads=[B0], writes=[hs])
                k.dma("pool", kv_s[l], hs[:, C_KV:C_WIN], reads=[hs], writes=[kv_s])
                k.dma("pool", win_s[l, :, WB - 1, :], hs[:, C_WIN:C_GATE], reads=[hs], writes=[win_s])
                for s in range(NS):
                    k.dma("pool", win_s[l, s, 0:WB - 1, :], st_win[l, s, 1:WB, :], reads=[st_win], writes=[win_s])
                k.dma("pool", gcv_s[l, :, 2, :], hs[:, C_GQKV:C_GA], reads=[hs], writes=[gcv_s])
                for r in range(2):
                    k.dma("pool", gcv_s[l, :, r, :], st_gconv[l, :, r + 1, :], reads=[st_gconv], writes=[gcv_s])
                with ExitStack() as e2:
                    gbuf = k.sb(e2, "gbuf", [NS, 3, 1536])
                    k.dma("sp", gbuf[:], st_gconv[l], writes=[gbuf])
                    cwt = k.sb(e2, "cwt", [NS, 4, 1536])
                    for tap in range(4):
                        k.dma("sp", cwt[:, tap, :], gconv_w[l, tap:tap + 1, :].partition_broadcast(NS), writes=[cwt])
                    qkv = k.sb(e2, "qkv", [NS, 1536])
                    tq = k.sb(e2, "tq", [NS, 1536])
                    k.op("dve", lambda e: e.tensor_tensor(qkv[:], cwt[:, 3, :], hs[:, C_GQKV:C_GA], ALU.mult), reads=[cwt, hs], writes=[qkv])
                    for tap in range(3):
                        k.op("dve", lambda e: e.tensor_tensor(tq[:], cwt[:, tap, :], gbuf[:, tap, :], ALU.mult), reads=[cwt, gbuf], writes=[tq])
                        k.op("dve", lambda e: e.tensor_tensor(qkv[:], qkv[:], tq[:], ALU.add), reads=[qkv, tq], writes=[qkv])
                    k.op("act", lambda e: e.activation(qkv[:], qkv[:], AF.Silu), reads=[qkv], writes=[qkv])
                    ss = k.sb(e2, "ss", [NS, 16])
                    k.op("dve", lambda e: e.tensor_tensor(tq[:, 0:1024], qkv[:, 0:1024], qkv[:, 0:1024], ALU.mult), reads=[qkv], writes=[tq])
                    k.op("dve", lambda e: e.tensor_reduce(ss[:], tq[:, 0:1024].rearrange("s (a d) -> s a d", d=64), AX.X, ALU.add), reads=[tq], writes=[ss])
                    k.op("act", lambda e: e.activation(ss[:], ss[:], AF.Sqrt, bias=epsc[0:NS, 0:1], scale=1.0), reads=[ss, epsc], writes=[ss])
                    k.op("dve", lambda e: e.reciprocal(ss[:], ss[:]), reads=[ss], writes=[ss])
                    qkn = k.sb(e2, "qkn", [NS, 16, 64])
                    k.op("dve", lambda e: e.tensor_tensor(qkn[:], qkv[:, 0:1024].rearrange("s (a d) -> s a d", d=64), ss[:].unsqueeze(2).to_broadcast([NS, 16, 64]), ALU.mult),
                         reads=[qkv, ss], writes=[qkn])
                    k.op("dve", lambda e: e.tensor_scalar(qkn[:, 0:8, :], qkn[:, 0:8, :], 0.125, None, ALU.mult), reads=[qkn], writes=[qkn])
                    dtb = k.sb(e2, "sdtb", [NS, 8])
                    k.dma("sp", dtb[:], dt_bias[l:l + 1, :].partition_broadcast(NS), writes=[dtb])
                    negA = k.sb(e2, "snegA", [NS, 8])
                    k.dma("sp", negA[:], A_log[l:l + 1, :].partition_broadcast(NS), writes=[negA])
                    k.op("act", lambda e: e.activation(negA[:], negA[:], AF.Exp), reads=[negA], writes=[negA])
                    k.op("dve", lambda e: e.tensor_scalar(negA[:], negA[:], -1.0, None, ALU.mult), reads=[negA], writes=[negA])
                    nw = k.sb(e2, "snw", [NS, 64])
                    k.dma("sp", nw[:], gnorm_w[l:l + 1, :].partition_broadcast(NS), writes=[nw])
                    gea = k.sb(e2, "gea", [NS, 8])
                    gbe = k.sb(e2, "gbe", [NS, 8])
                    k.op("dve", lambda e: e.tensor_tensor(gea[:], hs[:, C_GA:C_GB], dtb[:], ALU.add), reads=[hs, dtb], writes=[gea])
                    k.op("act", lambda e: e.activation(gea[:], gea[:], AF.Exp), reads=[gea], writes=[gea])
                    k.op("act", lambda e: e.activation(gea[:], gea[:], AF.Ln, bias=1.0, scale=1.0), reads=[gea], writes=[gea])
                    k.op("dve", lambda e: e.tensor_tensor(gea[:], gea[:], negA[:], ALU.mult), reads=[gea, negA], writes=[gea])
                    k.op("act", lambda e: e.activation(gea[:], gea[:], AF.Exp), reads=[gea], writes=[gea])
                    k.op("act", lambda e: e.activation(gbe[:], hs[:, C_GB:C_GZ], AF.Sigmoid), reads=[hs], writes=[gbe])
                    kqT = k.sb(e2, "kqT", [64, 2, NS, 8])
                    for a in range(2):
                        for h in range(8):
                            k.op("pe", lambda e: e.matmul(B7[0:64, h * NS:(h + 1) * NS], qkn[:, (1 - a) * 8 + h, :], ident[0:NS, 0:NS], start=True, stop=True), reads=[qkn, c32], writes=[B7])
                        k.op("act", lambda e: e.copy(kqT[:, a, :, :].rearrange("p s h -> p h s"), B7[0:64, 0:8 * NS].rearrange("p (h s) -> p h s", h=8)),
                             reads=[B7], writes=[kqT])
                    bd = k.sb(e2, "bd", [NS, 2, NS, 8])
                    k.op("dve", lambda e: e.tensor_tensor(bd[:, 0, :, :], gea[:].unsqueeze(1).to_broadcast([NS, NS, 8]),
                                                          ident[0:NS, 0:NS].unsqueeze(2).to_broadcast([NS, NS, 8]), ALU.mult), reads=[gea, c32], writes=[bd])
                    k.op("dve", lambda e: e.tensor_tensor(bd[:, 1, :, :], gbe[:].unsqueeze(1).to_broadcast([NS, NS, 8]),
                                                          ident[0:NS, 0:NS].unsqueeze(2).to_broadcast([NS, NS, 8]), ALU.mult), reads=[gbe, c32], writes=[bd])
                    k.op("pe", lambda e: e.matmul(B7[0:64, 0:2 * NQ], ones32[0:NS, 0:64], bd[:].rearrange("p a s h -> p (a s h)"), start=True, stop=True),
                         reads=[bd, c32], writes=[B7])
                    eab = k.sb(e2, "eab", [64, 2, NS, 8])
                    k.op("act", lambda e: e.copy(eab[:].rearrange("p a s h -> p (a s h)"), B7[0:64, 0:2 * NQ]), reads=[B7], writes=[eab])
                    bk = k.sb(e2, "bk", [64, NS, 8])
                    k.op("dve", lambda e: e.tensor_tensor(bk[:], kqT[:, 0, :, :], eab[:, 1, :, :], ALU.mult), reads=[kqT, eab], writes=[bk])
                    if os.environ.get("DEV_DBG", "") == "1" and l == 0:
                        k.dma("pool", y_p[0:64, 0:256], kqT[:].rearrange("p a s h -> p (a s h)"), reads=[kqT], writes=[y_p])
                        k.dma("pool", y_p[64:128, 0:256], eab[:].rearrange("p a s h -> p (a s h)"), reads=[eab], writes=[y_p])
                        k.dma("pool", y_p[128:192, 0:128], bk[:].rearrange("p s h -> p (s h)"), reads=[bk], writes=[y_p])
                        k.dma("pool", y_p[192:208, 0:1024], qkn[:].rearrange("p a d -> p (a d)"), reads=[qkn], writes=[y_p])
                        k.dma("pool", y_p[208:224, 0:1024], qkv[:, 0:1024], reads=[qkv], writes=[y_p])
                        k.dma("pool", y_p[224:240, 0:16], ss[:], reads=[ss], writes=[y_p])
                    S0 = k.sb(e2, "S0", [64, NQ, 64])
                    k.dma("sp", S0[:], st_gdn[l].rearrange("s h a b -> a (s h) b"), writes=[S0])
                    otok = k.sb(e2, "otok", [NS, 512])
                    k.op("pool", lambda e: e.memset(otok[:], 0.0), writes=[otok])
                    tmpS = k.sb(e2, "tmpS", [64, 8, 64])
                    t1 = k.sb(e2, "t1", [64, 8, 64])
                    bdv = k.sb(e2, "bdv", [NS, 512])
                    for s in range(NS):
                        S0s = S0[:, s * 8:(s + 1) * 8, :]
                        k.op("dve", lambda e: e.tensor_tensor(tmpS[:], S0s, kqT[:, 0, s, :].unsqueeze(2).to_broadcast([64, 8, 64]), ALU.mult),
                             reads=[S0, kqT], writes=[tmpS])
                        k.op("pe", lambda e: e.matmul(B7[0:64, :], ones32[0:64, 0:64], tmpS[:].rearrange("p h d -> p (h d)"), start=True, stop=True),
                             reads=[tmpS, c32], writes=[B7])
                        k.op("dve", lambda e: e.tensor_tensor(t1[:], B7[0:64, :].rearrange("p (h d) -> p h d", h=8), bk[:, s, :].unsqueeze(2).to_broadcast([64, 8, 64]), ALU.mult),
                             reads=[B7, bk], writes=[t1])
                        k.op("dve", lambda e: e.tensor_tensor(t1[:], S0s, t1[:], ALU.subtract), reads=[S0, t1], writes=[t1])
                        k.op("dve", lambda e: e.tensor_tensor(t1[:], t1[:], eab[:, 0, s, :].unsqueeze(2).to_broadcast([64, 8, 64]), ALU.mult), reads=[t1, eab], writes=[t1])
                        k.op("dve", lambda e: e.tensor_scalar(bdv[:], qkv[:, 1024:1536], ident[0:NS, s:s + 1], None, ALU.mult), reads=[qkv, c32], writes=[bdv])
                        k.op("pe", lambda e: e.matmul(B7[0:64, :], ones32[0:NS, 0:64], bdv[:], start=True, stop=True), reads=[bdv, c32], writes=[B7])
                        k.op("dve", lambda e: e.tensor_tensor(tmpS[:], B7[0:64, :].rearrange("p (h d) -> p h d", h=8), bk[:, s, :].unsqueeze(2).to_broadcast([64, 8, 64]), ALU.mult),
                             reads=[B7, bk], writes=[tmpS])
                        k.op("dve", lambda e: e.tensor_tensor(S0s, t1[:], tmpS[:], ALU.add), reads=[t1, tmpS], writes=[S0])
                        k.op("dve", lambda e: e.tensor_tensor(tmpS[:], S0s, kqT[:, 1, s, :].unsqueeze(2).to_broadcast([64, 8, 64]), ALU.mult),
                             reads=[S0, kqT], writes=[tmpS])
                        k.op("pe", lambda e: e.matmul(B7[0:NS, :], ones32[0:64, 0:NS], tmpS[:].rearrange("p h d -> p (h d)"), start=True, stop=True),
                             reads=[tmpS, c32], writes=[B7])
                        k.op("dve", lambda e: e.scalar_tensor_tensor(otok[:], B7[0:NS, :], ident[0:NS, s:s + 1], otok[:], ALU.mult, ALU.add),
                             reads=[B7, c32, otok], writes=[otok])
                    k.dma("pool", gst_s[l].rearrange("s h a b -> a (s h) b"), S0[:], reads=[S0], writes=[gst_s])
                    ms = k.sb(e2, "gms", [NS, 8])
                    k.op("dve", lambda e: e.tensor_tensor(tq[:, 0:512], otok[:], otok[:], ALU.mult), reads=[otok], writes=[tq])
                    k.op("dve", lambda e: e.tensor_reduce(ms[:], tq[:, 0:512].rearrange("s (h d) -> s h d", d=64), AX.X, ALU.add), reads=[tq], writes=[ms])
                    k.op("act", lambda e: e.activation(ms[:], ms[:], AF.Sqrt, bias=epsc[0:NS, 0:1], scale=1.0 / 64), reads=[ms, epsc], writes=[ms])
                    k.op("dve", lambda e: e.reciprocal(ms[:], ms[:]), reads=[ms], writes=[ms])
                    zs = k.sb(e2, "szs", [NS, 8, 64])
                    k.op("act", lambda e: e.activation(zs[:].rearrange("s h d -> s (h d)"), hs[:, C_GZ:INW], AF.Silu), reads=[hs], writes=[zs])
                    k.op("dve", lambda e: e.tensor_tensor(zs[:], zs[:], nw[:].unsqueeze(1).to_broadcast([NS, 8, 64]), ALU.mult), reads=[zs, nw], writes=[zs])
                    k.op("dve", lambda e: e.tensor_tensor(zs[:], zs[:], ms[:].unsqueeze(2).to_broadcast([NS, 8, 64]), ALU.mult), reads=[zs, ms], writes=[zs])
                    k.op("dve", lambda e: e.tensor_tensor(otok[:], otok[:], zs[:].rearrange("s h d -> s (h d)"), ALU.mult), reads=[otok, zs], writes=[otok])
                    ogT = k.sb(e2, "sogT", [128, 4, NS], BF16)
                    for c in range(4):
                        k.op("pe", lambda e: e.matmul(B7[:, c * NS:(c + 1) * NS], otok[:, c * 128:(c + 1) * 128], ident[0:NS, 0:NS], start=True, stop=True), reads=[otok, c32], writes=[B7])
                    k.op("act", lambda e: e.copy(ogT[:], B7[:, 0:4 * NS].rearrange("p (c s) -> p c s", c=4)), reads=[B7], writes=[ogT])
                    k.dma("pool", mixsT_d.t[512:1024, :].rearrange("(c p) s -> p c s", p=128), ogT[:], reads=[ogT], writes=[mixsT_d])
                with ExitStack() as e3:
                    gts = k.sb(e3, "gts", [NS, 24])
                    k.op("act", lambda e: e.activation(gts[:], hs[:, C_GATE:C_GQKV], AF.Sigmoid), reads=[hs], writes=[gts])
                    qperm = k.sb(e3, "qperm", [NS, 4, 2, 64])
                    for n in range(2):
                        k.op("dve", lambda e: e.tensor_copy(qperm[:, :, n, :], hs[:, n * 256:(n + 1) * 256].rearrange("s (g d) -> s g d", g=4)), reads=[hs], writes=[qperm])
                    qz = [k.sb(e3, "qz%d" % n, [128, NS, 4], BF16) for n in range(2)]
                    for n in range(2):
                        k.op("pool", lambda e: e.memset(qz[n][:], 0.0), writes=[qz[n]])
                    for g in range(4):
                        k.op("pe", lambda e: e.matmul(B0[:, g * NS:(g + 1) * NS], qperm[:, g, :, :].rearrange("s n d -> s (n d)"), ident[0:NS, 0:NS], start=True, stop=True), reads=[qperm, c32], writes=[B0])
                    k.op("act", lambda e: e.copy(qz[0][0:64, :, :].rearrange("p s g -> p g s"), B0[0:64, 0:4 * NS].rearrange("p (g s) -> p g s", g=4)), reads=[B0], writes=[qz[0]])
                    k.op("act", lambda e: e.copy(qz[1][64:128, :, :].rearrange("p s g -> p g s"), B0[64:128, 0:4 * NS].rearrange("p (g s) -> p g s", g=4)), reads=[B0], writes=[qz[1]])
                    nK = k.sb(e3, "nK", [128, 2, NS], BF16)
                    k.op("pe", lambda e: e.matmul(B0[:, 0:NS], hs[:, C_KV + 256:C_KV + 384], ident[0:NS, 0:NS], start=True, stop=True), reads=[hs, c32], writes=[B0])
                    k.op("pe", lambda e: e.matmul(B0[:, NS:2 * NS], hs[:, C_WIN:C_WIN + 128], ident[0:NS, 0:NS], start=True, stop=True), reads=[hs, c32], writes=[B0])
                    k.op("act", lambda e: e.copy(nK[:].rearrange("p a s -> p (a s)"), B0[:, 0:2 * NS]), reads=[B0], writes=[nK])
                    pti = k.sb(e3, "pti", [128, NS * NPG], I32)
                    k.dma("sp", pti[:], ptab[:].partition_broadcast(128), writes=[pti])
                    ptf = k.sb(e3, "ptf", [128, NS * NPG])
                    k.op("dve", lambda e: e.tensor_copy(ptf[:], pti[:]), reads=[pti], writes=[ptf])
                    k.op("dve", lambda e: e.tensor_scalar(ptf[:], ptf[:], 256.0, piota[:, 0:1], ALU.mult, ALU.add), reads=[ptf, cs32], writes=[ptf])
                    k.op("dve", lambda e: e.tensor_scalar(ptf[:], ptf[:], float(2 * l * NPOOL * 128), None, ALU.add), reads=[ptf], writes=[ptf])
                    idxc = k.sb(e3, "idxc", [128, NS * NPG], I32)
                    idxs = k.sb(e3, "idxs", [128, NS * NPG], I32)
                    k.op("dve", lambda e: e.tensor_copy(idxc[:], ptf[:]), reads=[ptf], writes=[idxc])
                    k.op("dve", lambda e: e.tensor_scalar(ptf[:], ptf[:], 1.0, None, ALU.add), reads=[ptf], writes=[ptf])
                    k.op("dve", lambda e: e.tensor_copy(idxs[:], ptf[:]), reads=[ptf], writes=[idxs])
                    pool2v = pool.t.rearrange("r (two c) -> (r two) c", two=2)
                    phis = k.sb(e3, "sphis", [128, 2, 128])
                    k.op("pool", lambda e: e.memset(phis[:], 0.0), writes=[phis])
                    for a in range(2):
                        for n in range(2):
                            k.dma("sp", phis[64 * n:64 * n + 64, a, 64 * n:64 * n + 64], nsa_phi[l, a], writes=[phis])
                    phib = k.sb(e3, "sphib", [128, 2, 128], BF16)
                    k.op("dve", lambda e: e.tensor_copy(phib[:], phis[:]), reads=[phis], writes=[phib])
                    pet = k.sb(e3, "spet", [128, 2, 32])
                    for a in range(2):
                        for n in range(2):
                            k.dma("sp", pet[64 * n:64 * n + 64, a, :], nsa_pe[l, a].rearrange("r d -> d r"), writes=[pet], allow_slow_non_contiguous=True)
                    pem = k.sb(e3, "spem", [128, 2])
                    k.op("dve", lambda e: e.tensor_reduce(pem[:], pet[:], AX.X, ALU.add), reads=[pet], writes=[pem])
                    k.op("dve", lambda e: e.tensor_scalar(pem[:], pem[:], 1.0 / 32, None, ALU.mult), reads=[pem], writes=[pem])
                    NCB = NPG * 4
                    kcT_all = k.sb(e3, "kcT_all", [128, NS, NCB], BF16)
                    vc_all = k.sb(e3, "vc_all", [NCB, NS, 2, 68], BF16)
                    k.op("pool", lambda e: e.memset(vc_all[:], 1.0), writes=[vc_all])
                    pgt = [k.sb(e3, "pgt%d" % i, [128, 256]) for i in range(3)]
                    cmpkv = [k.sb(e3, "scmpkv%d" % i, [128, 256], BF16) for i in range(2)]
                    kvm = k.sb(e3, "kvm", [128, 2, NCB], BF16)
                    it = 0
                    MG = os.environ.get("DEV_MG", "0") == "1"
                    pgm = [k.sb(e3, "pgm%d" % i, [128, NPG, 256]) for i in range(2)] if MG else None
                    for s in range(NS):
                        if MG:
                            pm_ = pgm[s % 2]
                            k.gather(pm_[:], pool2v, idxc[:, s * NPG:(s + 1) * NPG], reads=[idxc], writes=[pm_])
                        for pg in range(NPG):
                            cb_ = cmpkv[it % 2]
                            if MG:
                                src_, srcb_ = pm_[:, pg, :], pm_
                            else:
                                pt_ = pgt[it % 3]
                                k.gather(pt_[:], pool2v, idxc[:, s * NPG + pg:s * NPG + pg + 1], reads=[idxc], writes=[pt_])
                                src_, srcb_ = pt_[:], pt_
                            it += 1
                            cast(cb_[:], src_, [srcb_], [cb_])
                            for a in range(2):
                                k.op("pe", lambda e: e.matmul(B1[:, a * NCB + pg * 4:a * NCB + pg * 4 + 4], cb_[:, a * 128:(a + 1) * 128], avg4, start=True, stop=True),
                                     reads=[cb_, cb128], writes=[B1])
                        for a in range(2):
                            k.op("dve", lambda e: e.tensor_scalar(kvm[:, a, :], B1[:, a * NCB:(a + 1) * NCB], pem[:, a:a + 1], None, ALU.add), reads=[B1, pem], writes=[kvm])
                        k.op("pe", lambda e: e.matmul(B2[:, 0:NCB], phib[:, 0, :], kvm[:, 0, :], start=True, stop=True), reads=[phib, kvm], writes=[B2])
                        k.op("act", lambda e: e.copy(kcT_all[:, s, :], B2[:, 0:NCB]), reads=[B2], writes=[kcT_all])
                        k.op("pe", lambda e: e.matmul(B2[0:NCB, 128:256], kvm[:, 1, :], phib[:, 1, :], start=True, stop=True), reads=[phib, kvm], writes=[B2])
                        k.op("act", lambda e: e.copy(vc_all[:, s, :, 0:64], B2[0:NCB, 128:256].rearrange("p (n d) -> p n d", n=2)), reads=[B2], writes=[vc_all])
                        for n in range(2):
                            k.op("pe", lambda e: e.matmul(B3[0:NCB, (s * 2 + n) * 4:(s * 2 + n) * 4 + 4], kcT_all[:, s, :], qz[n][:, s, :], start=True, stop=True),
                                 reads=[kcT_all, qz[n]], writes=[B3])
                    tmpc = k.sb(e3, "tmpc", [NCB, NQ])
                    PTc = k.sb(e3, "sPTc", [NCB, NQ], BF16)
                    k.op("dve", lambda e: e.scalar_tensor_tensor(tmpc[:], B3[0:NCB, 0:NQ], 0.125, alibi_c[0:NCB, :], ALU.mult, ALU.add), reads=[B3, cs32], writes=[tmpc])
                    k.op("act", lambda e: e.activation(PTc[:], tmpc[:], AF.Exp), reads=[tmpc], writes=[PTc])
                    k.op("pe", lambda e: e.matmul(B3[:, 256:256 + NB33 + 1], PTc[:], pool33, start=True, stop=True), reads=[PTc, csb], writes=[B3])
                    ul = k.sb(e3, "ul", [128, NB33 + 1])
                    k.op("act", lambda e: e.copy(ul[:], B3[:, 256:256 + NB33 + 1]), reads=[B3], writes=[ul])
                    k.op("dve", lambda e: e.tensor_scalar(ul[:, NB33:NB33 + 1], ul[:, NB33:NB33 + 1], 1e-30, None, ALU.max), reads=[ul], writes=[ul])
                    k.op("dve", lambda e: e.reciprocal(ul[:, NB33:NB33 + 1], ul[:, NB33:NB33 + 1]), reads=[ul], writes=[ul])
                    k.op("dve", lambda e: e.tensor_scalar(ul[:, 0:NB33], ul[:, 0:NB33], ul[:, NB33:NB33 + 1], None, ALU.mult), reads=[ul], writes=[ul])
                    k.op("pe", lambda e: e.matmul(B3[0:2 * NS, 320:320 + NB33], GS, ul[:, 0:NB33], start=True, stop=True), reads=[ul, cs32], writes=[B3])
                    sc = k.sb(e3, "ssc", [2 * NS, 40])
                    k.op("dve", lambda e: e.tensor_tensor(sc[:, 0:NB33], B3[0:2 * NS, 320:320 + NB33], tkb_s[0:2 * NS, 0:NB33], ALU.add), reads=[B3, cs32], writes=[sc])
                    m8 = k.sb(e3, "sm8", [2 * NS, 16])
                    tkw = k.sb(e3, "stkw", [2 * NS, NB33])
                    k.op("dve", lambda e: e.max(out=m8[:, 0:8], in_=sc[:, 0:NB33]), reads=[sc], writes=[m8])
                    k.op("dve", lambda e: e.match_replace(out=tkw[:], in_to_replace=m8[:, 0:8], in_values=sc[:, 0:NB33], imm_value=-1e9), reads=[sc, m8], writes=[tkw])
                    k.op("dve", lambda e: e.max(out=m8[:, 8:16], in_=tkw[:]), reads=[tkw], writes=[m8])
                    k.op("dve", lambda e: e.tensor_scalar(sc[:, 0:NB33], sc[:, 0:NB33], m8[:, 15:16], None, ALU.is_ge), reads=[sc, m8], writes=[sc])
                    k.op("dve", lambda e: e.tensor_scalar(sc[:, 0:NB33], sc[:, 0:NB33], -1.0, -NEG, ALU.add, ALU.mult), reads=[sc], writes=[sc])
                    selb16 = k.sb(e3, "selb16", [2 * NS, 40], BF16)
                    k.op("dve", lambda e: e.tensor_copy(selb16[:, 0:NB33], sc[:, 0:NB33]), reads=[sc], writes=[selb16])
                    for s in range(NS):
                        for n in range(2):
                            c0 = (s * 2 + n) * 4
                            k.op("pe", lambda e: e.matmul(B6[0:65, c0:c0 + 4], vc_all[:, s, n, 0:65], PTc[:, c0:c0 + 4], start=True, stop=True),
                                 reads=[vc_all, PTc], writes=[B6])
                    KT_s = [k.sb(e3, "KT_s%d" % i, [128, (NPG + 1) * 128], BF16) for i in range(2)]
                    Vs_s = [k.sb(e3, "Vs_s%d" % i, [128, NPG + 1, 2, 68], BF16) for i in range(2)]
                    KTw_s = [k.sb(e3, "KTw_s%d" % i, [128, (NWT + 1) * 128], BF16) for i in range(2)]
                    Vw_s = [k.sb(e3, "Vw_s%d" % i, [128, NWT + 1, 2, 68], BF16) for i in range(2)]
                    for i in range(2):
                        k.op("pool", lambda e: e.memset(Vs_s[i][:], 1.0), writes=[Vs_s[i]])
                        k.op("pool", lambda e: e.memset(Vw_s[i][:], 1.0), writes=[Vw_s[i]])
                        k.op("pool", lambda e: e.memset(Vs_s[i][:, NPG, :, :], 0.0), writes=[Vs_s[i]])
                        k.op("pool", lambda e: e.memset(Vw_s[i][:, NWT, :, :], 0.0), writes=[Vw_s[i]])
                    wt = [k.sb(e3, "wt%d" % i, [128, NWT, 256]) for i in range(2)]
                    vstg = k.sb(e3, "vstg", [1, 2, 128])
                    ones1 = k.sb(e3, "ones1", [1, 2, 2])
                    k.op("pool", lambda e: e.memset(ones1[:], 1.0), writes=[ones1])
                    selm = k.sb(e3, "selm", [128, NPG])
                    tmps = k.sb(e3, "tmps", [128, NPG + 1, 4])
                    PTs = k.sb(e3, "sPTs", [128, NPG + 1, 4], BF16)
                    PTw = k.sb(e3, "sPTw", [128, NWT + 1, 4], BF16)
                    k.op("pool", lambda e: e.memset(PTs[:], 0.0), writes=[PTs])
                    k.op("pool", lambda e: e.memset(PTw[:], 0.0), writes=[PTw])
                    for s in range(NS):
                        KT, V, KTw, Vw, wts = KT_s[s % 2], Vs_s[s % 2], KTw_s[s % 2], Vw_s[s % 2], wt[s % 2]
                        for pg in range(NPG):
                            pt_ = pgt[it % 3]
                            it += 1
                            k.gather(pt_[:], pool2v, idxs[:, s * NPG + pg:s * NPG + pg + 1], reads=[idxs], writes=[pt_])
                            k.op("pe", lambda e: e.matmul(B2[:, 0:128], pt_[:, 0:128], ident, start=True, stop=True), reads=[pt_, c32], writes=[B2])
                            cast(KT[:, pg * 128:(pg + 1) * 128], B2[:, 0:128], [B2], [KT], psum=True)
                            k.op("pool", lambda e: e.tensor_copy(V[:, pg, :, 0:64], pt_[:, 128:256].rearrange("p (n d) -> p n d", n=2)), reads=[pt_], writes=[V])
                        k.dma("sp", wts[:], st_win[l, s].rearrange("(t p) c -> p t c", p=128), writes=[wts])
                        for t in range(NWT):
                            k.op("pe", lambda e: e.matmul(B2[:, 128:256], wts[:, t, 0:128], ident, start=True, stop=True), reads=[wts, c32], writes=[B2])
                            cast(KTw[:, t * 128:(t + 1) * 128], B2[:, 128:256], [B2], [KTw], psum=True)
                            k.op("pool", lambda e: e.tensor_copy(Vw[:, t, :, 0:64], wts[:, t, 128:256].rearrange("p (n d) -> p n d", n=2)), reads=[wts], writes=[Vw])
                        k.op("dve", lambda e: e.tensor_copy(KT[:, NPG * 128:NPG * 128 + 1], nK[:, 0, s:s + 1]), reads=[nK], writes=[KT])
                        k.op("dve", lambda e: e.tensor_copy(KTw[:, NWT * 128:NWT * 128 + 1], nK[:, 1, s:s + 1]), reads=[nK], writes=[KTw])
                        k.dma("sp", vstg[0:1, 0, :], hs[s:s + 1, C_KV + 384:C_KV + 512], reads=[hs], writes=[vstg])
                        k.dma("sp", vstg[0:1, 1, :], hs[s:s + 1, C_WIN + 128:C_WIN + 256], reads=[hs], writes=[vstg])
                        k.op("dve", lambda e: e.tensor_copy(V[0:1, NPG, :, 0:64], vstg[0:1, 0, :].rearrange("p (n d) -> p n d", n=2)), reads=[vstg], writes=[V])
                        k.op("dve", lambda e: e.tensor_copy(Vw[0:1, NWT, :, 0:64], vstg[0:1, 1, :].rearrange("p (n d) -> p n d", n=2)), reads=[vstg], writes=[Vw])
                        k.op("dve", lambda e: e.tensor_copy(V[0:1, NPG, :, 64:65], ones1[0:1, :, 0:1]), reads=[ones1], writes=[V])
                        k.op("dve", lambda e: e.tensor_copy(Vw[0:1, NWT, :, 64:65], ones1[0:1, :, 0:1]), reads=[ones1], writes=[Vw])
                        for n in range(2):
                            r = s * 2 + n
                            c0 = r * 4
                            k.op("pe", lambda e: e.matmul(B2[:, 256:256 + NB33], OH[:, r * 128:(r + 1) * 128], selb16[:, 0:NB33], start=True, stop=True),
                                 reads=[csb, selb16], writes=[B2])
                            k.op("dve", lambda e: e.tensor_copy(selm[0:64, :], B2[0:64, 256:256 + 2 * NPG].rearrange("p (t two) -> p t two", two=2)[:, :, 0]), reads=[B2], writes=[selm])
                            k.op("dve", lambda e: e.tensor_copy(selm[64:128, :], B2[64:128, 256:256 + 2 * NPG].rearrange("p (t two) -> p t two", two=2)[:, :, 1]), reads=[B2], writes=[selm])
                            for (BS, K_, nt_, PTx, Vx, ali, col6) in ((B4, KT, NPG, PTs, V, alibi_s, 1), (B5, KTw, NWT, PTw, Vw, alibi_w, 2)):
                                for t in range(nt_):
                                    k.op("pe", lambda e: e.matmul(BS[:, t * 4:(t + 1) * 4], K_[:, t * 128:(t + 1) * 128], qz[n][:, s, :], start=True, stop=True),
                                         reads=[K_, qz[n]], writes=[BS])
                                k.op("pe", lambda e: e.matmul(BS[0:1, 128:132], K_[:, nt_ * 128:nt_ * 128 + 1], qz[n][:, s, :], start=True, stop=True),
                                     reads=[K_, qz[n]], writes=[BS])
                                tv = tmps[:, 0:nt_, :]
                                k.op("dve", lambda e: e.scalar_tensor_tensor(tv, BS[:, 0:nt_ * 4].rearrange("p (t g) -> p t g", g=4), 0.125,
                                                                             ali[:, n * nt_ * 4:(n + 1) * nt_ * 4].rearrange("p (t g) -> p t g", g=4), ALU.mult, ALU.add),
                                     reads=[BS, cs32], writes=[tmps])
                                if col6 == 1:
                                    k.op("dve", lambda e: e.tensor_tensor(tv, tv, selm[:].unsqueeze(2).to_broadcast([128, NPG, 4]), ALU.add), reads=[tmps, selm], writes=[tmps])
                                k.op("act", lambda e: e.activation(PTx[:, 0:nt_, :], tv, AF.Exp), reads=[tmps], writes=[PTx])
                                k.op("act", lambda e: e.activation(PTx[0:1, nt_, :], BS[0:1, 128:132], AF.Exp, scale=0.125), reads=[BS], writes=[PTx])
                                oc = col6 * NQ + c0
                                for t in range(nt_ + 1):
                                    k.op("pe", lambda e: e.matmul(B6[0:65, oc:oc + 4], Vx[:, t, n, 0:65], PTx[:, t, :], start=(t == 0), stop=(t == nt_)),
                                         reads=[Vx, PTx], writes=[B6], inc=(t == nt_))
                    ot = k.sb(e3, "ot", [65, 3, NQ])
                    k.op("act", lambda e: e.copy(ot[:].rearrange("p a q -> p (a q)"), B6[0:65, 0:3 * NQ]), reads=[B6], writes=[ot])
                    k.op("dve", lambda e: e.tensor_scalar(ot[64:65, :, :], ot[64:65, :, :], 1e-30, None, ALU.max), reads=[ot], writes=[ot])
                    k.op("dve", lambda e: e.reciprocal(ot[64:65, :, :], ot[64:65, :, :]), reads=[ot], writes=[ot])
                    k.op("pe", lambda e: e.matmul(B0[0:64, 0:3 * NQ], ones32[64:65, 0:64], ot[64:65, :, :].rearrange("p a q -> p (a q)"), start=True, stop=True),
                         reads=[ot, c32], writes=[B0])
                    k.op("dve", lambda e: e.tensor_tensor(ot[0:64, :, :], ot[0:64, :, :], B0[0:64, 0:3 * NQ].rearrange("p (a q) -> p a q", a=3), ALU.mult), reads=[ot, B0], writes=[ot])
                    gbd = k.sb(e3, "gbd", [NS, 3, NS, 8])
                    for br in range(3):
                        k.op("dve", lambda e: e.tensor_tensor(gbd[:, br, :, :], gts[:].rearrange("s (h b) -> s h b", b=3)[:, :, br].unsqueeze(1).to_broadcast([NS, NS, 8]),
                                                              ident[0:NS, 0:NS].unsqueeze(2).to_broadcast([NS, NS, 8]), ALU.mult), reads=[gts, c32], writes=[gbd])
                    k.op("pe", lambda e: e.matmul(B0[0:64, 0:3 * NQ], ones32[0:NS, 0:64], gbd[:].rearrange("p a s h -> p (a s h)"), start=True, stop=True),
                         reads=[gbd, c32], writes=[B0])
                    k.op("dve", lambda e: e.tensor_tensor(ot[0:64, :, :], ot[0:64, :, :], B0[0:64, 0:3 * NQ].rearrange("p (a q) -> p a q", a=3), ALU.mult), reads=[ot, B0], writes=[ot])
                    k.op("dve", lambda e: e.tensor_tensor(ot[0:64, 0, :], ot[0:64, 0, :], ot[0:64, 1, :], ALU.add), reads=[ot], writes=[ot])
                    k.op("dve", lambda e: e.tensor_tensor(ot[0:64, 0, :], ot[0:64, 0, :], ot[0:64, 2, :], ALU.add), reads=[ot], writes=[ot])
                    onb = k.sb(e3, "onb", [64, 8, NS], BF16)
                    k.op("dve", lambda e: e.tensor_copy(onb[:], ot[0:64, 0, :].rearrange("p (s h) -> p h s", h=8)), reads=[ot], writes=[onb])
                    k.dma("pool", mixsT_d.t[0:512, :].rearrange("(h d) s -> d h s", d=64), onb[:], reads=[onb], writes=[mixsT_d])
            k.barrier()

        STOP = os.environ.get("DEV_STOP", "")
        for l in range(DEPTH):
            if STOP in ("W", "X0"):
                break
            xres_p = xp if l == 0 else x1_d
            xres_s = xs if l == 0 else xs1_d
            yout_p = x1_d if l == 0 else y_p
            yout_s = xs1_d if l == 0 else y_s
            last = (l == DEPTH - 1)
            if do_prompt:
                with ExitStack() as es:
                    cbp = k.sb(es, "cbp", [128, NCBP], BF16)
                    k.dma("sp", cbp[:], cbp_d[:], writes=[cbp])
                    cb64 = cb3 = cb8 = cbp
                    wq = k.sb(es, "wq", [128, 8, 512], BF16)
                    wkT = k.sb(es, "wkT", [128, 8, 256], BF16)
                    wtok = k.sb(es, "wtok", [128, 8, 792], BF16)
                    wsrc = wb_in.t[l].rearrange("(c p) n -> p c n", p=128)
                    for kc in range(8):
                        for n in range(2):
                            k.dma("sp", wq[:, kc, :].rearrange("p (c n d) -> p c n d", c=4, n=2)[:, :, n, :],
                                  wb_in.t[l, kc * 128:(kc + 1) * 128, n * 256:(n + 1) * 256].rearrange("p (c d) -> p c d", c=4),
                                  reads=[wb_in], writes=[wq])
                    k.dma("sp", wkT[:, :, 0:128], wsrc[:, :, C_KV + 256:C_KV + 384], reads=[wb_in], writes=[wkT])
                    k.dma("sp", wkT[:, :, 128:256], wsrc[:, :, C_WIN:C_WIN + 128], reads=[wb_in], writes=[wkT])
                    k.dma("sp", wtok[:], wsrc[:, :, C_KV:C_KV + 792], reads=[wb_in], writes=[wtok])
                    phis = k.sb(es, "phis", [128, 2, 128])
                    k.op("pool", lambda e: e.memset(phis[:], 0.0), writes=[phis])
                    for a in range(2):
                        for n in range(2):
                            k.dma("sp", phis[64 * n:64 * n + 64, a, 64 * n:64 * n + 64], nsa_phi[l, a], writes=[phis])
                    phib = k.sb(es, "phib", [128, 2, 128], BF16)
                    k.op("dve", lambda e: e.tensor_copy(phib[:], phis[:]), reads=[phis], writes=[phib])
                    pet = k.sb(es, "pet", [128, 2, 32])
                    for a in range(2):
                        for n in range(2):
                            k.dma("sp", pet[64 * n:64 * n + 64, a, :], nsa_pe[l, a].rearrange("r d -> d r"), writes=[pet],
                                  allow_slow_non_contiguous=True)
                    pem = k.sb(es, "pem", [128, 2])
                    k.op("dve", lambda e: e.tensor_reduce(pem[:], pet[:], AX.X, ALU.add), reads=[pet], writes=[pem])
                    k.op("dve", lambda e: e.tensor_scalar(pem[:], pem[:], 1.0 / 32, None, ALU.mult), reads=[pem], writes=[pem])
                    KTs = k.sb(es, "KTs", [128, T], BF16)
                    KTw = k.sb(es, "KTw", [128, T], BF16)
                    Vs = k.sb(es, "Vs", [128, NJ, 2, 68], BF16)
                    Vw = k.sb(es, "Vw", [128, NJ, 2, 68], BF16)
                    k.op("pool", lambda e: e.memset(Vs[:], 1.0), writes=[Vs])
                    k.op("pool", lambda e: e.memset(Vw[:], 1.0), writes=[Vw])
                    kcmT = k.sb(es, "kcmT", [128, 128], BF16)
                    vcmT = k.sb(es, "vcmT", [128, 128], BF16)
                    kcT = k.sb(es, "kcT", [128, 128], BF16)
                    k.op("pool", lambda e: e.memset(kcmT[:], 0.0), writes=[kcmT])
                    k.op("pool", lambda e: e.memset(vcmT[:], 0.0), writes=[vcmT])
                    k.op("pool", lambda e: e.memset(kcT[:], 0.0), writes=[kcT])
                    cmprhs = k.sb(es, "cmprhs", [128, 2, 68], BF16)
                    k.op("pool", lambda e: e.memset(cmprhs[:], 1.0), writes=[cmprhs])
                    PT0 = k.sb(es, "PT0", [128, NJ, 512], BF16)
                    PT = [PT0, PT0]
                    PTc = k.sb(es, "PTc", [128, 512], BF16)
                    xTt = [k.sb(es, "xTt%d" % i, [128, 8, 128], BF16) for i in range(2)]
                    kvf = [k.sb(es, "kvf%d" % i, [128, 792]) for i in range(2)]
                    cmpkv = k.sb(es, "cmpkv", [128, 256], BF16)
                    gates = k.sb(es, "gates", [128, 24])
                    qTz = [k.sb(es, "qTz%d" % n, [128, 4, 128], BF16) for n in range(2)]
                    for n in range(2):
                        k.op("pool", lambda e: e.memset(qTz[n][:], 0.0), writes=[qTz[n]])
                    selbT = [k.sb(es, "selbT%d" % n, [128, 4, 128], BF16) for n in range(2)]
                    for n in range(2):
                        k.op("pool", lambda e: e.memset(selbT[n][:], 0.0), writes=[selbT[n]])
                    sm = k.sb(es, "sm", [128, 64])
                    imp = k.sb(es, "imp", [128, 64])
                    tkw = k.sb(es, "tkw", [128, 64])
                    m8 = k.sb(es, "m8", [128, 16])
                    selb = k.sb(es, "selb", [128, 64])
                    ocmp = k.sb(es, "ocmp", [128, 8, 65])
                    rl = k.sb(es, "rl", [128, 8, 3])
                    fgt = k.sb(es, "fgt", [128, 8, 3])
                    onsa = k.sb(es, "onsa", [128, 512])
                    onT = k.sb(es, "onT", [128, 4, 128], BF16)
                    o_ = 0
                    kaux = cbp.t[:, o_:o_ + NJ * 128]
                    o_ += NJ * 128
                    kcaux = cbp.t[:, o_:o_ + NJ * 128]
                    o_ += NJ * 128
                    qaux = cbp.t[:, o_:o_ + 1024]
                    o_ += 1024
                    cmpsel = cbp.t[:, o_:o_ + NJ * 128]
                    o_ += NJ * 128
                    vispat = cbp.t[:, o_:o_ + 512]
                    o_ += 512
                    Epad = cbp.t[:, o_:o_ + T]
                    SC = (pb[0], pb[1])
                    ACC = {("s", 0): pb[2], ("s", 1): pb[3], ("w", 0): pb[4], ("w", 1): pb[5]}
                    MA, MB = pb[6], pb[7]
                    sc_i = [0]

                    def scores(n, lhs_aux, lhsK, mask, out_pt):
                        S = SC[sc_i[0] % 2]
                        sc_i[0] += 1
                        k.op("pe", lambda e: e.matmul(S[:, :], lhs_aux, qaux[:, n * 512:(n + 1) * 512],
                                                      start=True, stop=False), reads=[cb3], writes=[S], inc=False)
                        if mask is not None:
                            ml, mr, mrd = mask
                            k.op("pe", lambda e: e.matmul(S[:, :], ml, mr, start=False, stop=False), reads=mrd, writes=[S], inc=False)
                        for g in range(4):
                            k.op("pe", lambda e, g=g: e.matmul(S[:, g * 128:(g + 1) * 128], lhsK, qTz[n][:, g, :],
                                                             start=False, stop=(g == 3)),
                                 reads=[qTz[n], KTs, KTw, kcT], writes=[S], inc=(g == 3))
                        k.op("act", lambda e: e.activation(out_pt, S[:, :], AF.Exp, scale=0.125), reads=[S], writes=[PT[n], PTc])

                    for j in range(NJ):
                        xt = xTt[j % 2]
                        kv = kvf[j % 2]
                        k.dma("sp", xt[:], xT_d.t.rearrange("(c p) t -> p c t", p=128)[:, :, j * 128:(j + 1) * 128],
                              reads=[xT_d], writes=[xt])
                        for kc in range(8):
                            k.op("pe", lambda e, kc=kc: e.matmul(MA[:, :], xt[:, kc, :], wtok[:, kc, 0:512], start=(kc == 0), stop=(kc == 7)),
                                 reads=[xt, wtok], writes=[MA], inc=(kc == 7))
                        for kc in range(8):
                            k.op("pe", lambda e, kc=kc: e.matmul(MB[:, 0:280], xt[:, kc, :], wtok[:, kc, 512:792], start=(kc == 0), stop=(kc == 7)),
                                 reads=[xt, wtok], writes=[MB], inc=(kc == 7))
                        k.op("act", lambda e: e.copy(kv[:, 0:512], MA[:, :]), reads=[MA], writes=[kv])
                        k.op("dve", lambda e: e.tensor_copy(kv[:, 512:792], MB[:, 0:280]), reads=[MB], writes=[kv])
                        k.dma("pool", kv_p[l, j * 128:(j + 1) * 128, :], kv[:, 0:512], reads=[kv], writes=[kv_p])
                        if j >= NJ - NWT:
                            jj = j - (NJ - NWT)
                            k.dma("pool", win_p[l, jj * 128:(jj + 1) * 128, :], kv[:, 512:768], reads=[kv], writes=[win_p])
                        k.op("act", lambda e: e.activation(gates[:], kv[:, 768:792], AF.Sigmoid), reads=[kv], writes=[gates])
                        k.op("dve", lambda e: e.tensor_copy(Vs[:, j, :, 0:64], kv[:, 384:512].rearrange("p (n d) -> p n d", n=2)),
                             reads=[kv], writes=[Vs])
                        k.op("pool", lambda e: e.tensor_copy(Vw[:, j, :, 0:64], kv[:, 640:768].rearrange("p (n d) -> p n d", n=2)),
                             reads=[kv], writes=[Vw])
                        k.op("dve", lambda e: e.tensor_copy(cmpkv[:], kv[:, 0:256]), reads=[kv], writes=[cmpkv])
                        k.op("pe", lambda e: e.matmul(MA[:, 0:4], cmpkv[:, 0:128], avg4, start=True, stop=True), reads=[cmpkv, cb128], writes=[MA])
                        k.op("pe", lambda e: e.matmul(MA[:, 4:8], cmpkv[:, 128:256], avg4, start=True, stop=True), reads=[cmpkv, cb128], writes=[MA])
                        k.op("dve", lambda e: e.tensor_scalar(kcmT[:, 4 * j:4 * j + 4], MA[:, 0:4], pem[:, 0:1], None, ALU.add),
                             reads=[MA, pem], writes=[kcmT])
                        k.op("dve", lambda e: e.tensor_scalar(vcmT[:, 4 * j:4 * j + 4], MA[:, 4:8], pem[:, 1:2], None, ALU.add),
                             reads=[MA, pem], writes=[vcmT])
                        k.op("pe", lambda e: e.matmul(MB[:, 0:4], phib[:, 0, :], kcmT[:, 4 * j:4 * j + 4], start=True, stop=True),
                             reads=[phib, kcmT], writes=[MB])
                        k.op("dve", lambda e: e.tensor_copy(kcT[:, 4 * j:4 * j + 4], MB[:, 0:4]), reads=[MB], writes=[kcT])
                        k.op("pe", lambda e: e.matmul(MB[:, 128:256], vcmT[:, :], phib[:, 1, :], start=True, stop=True),
                             reads=[phib, vcmT], writes=[MB])
                        k.op("dve", lambda e: e.tensor_copy(cmprhs[:, :, 0:64], MB[:, 128:256].rearrange("p (n d) -> p n d", n=2)),
                             reads=[MB], writes=[cmprhs])
                        for c in range(4):
                            for kc in range(8):
                                k.op("pe", lambda e, c=c, kc=kc: e.matmul(MA[:, c * 128:(c + 1) * 128], wq[:, kc, c * 128:(c + 1) * 128], xt[:, kc, :],
                                                                       start=(kc == 0), stop=(kc == 7)),
                                     reads=[xt, wq], writes=[MA], inc=(kc == 7))
                        k.op("act", lambda e: e.copy(qTz[0][0:64, :, :], MA[0:64, :].rearrange("p (c t) -> p c t", c=4)), reads=[MA], writes=[qTz[0]])
                        k.op("dve", lambda e: e.tensor_copy(qTz[1][64:128, :, :], MA[64:128, :].rearrange("p (c t) -> p c t", c=4)), reads=[MA], writes=[qTz[1]])
                        for a, KT in ((0, KTs), (1, KTw)):
                            for kc in range(8):
                                k.op("pe", lambda e, a=a, kc=kc: e.matmul(MB[:, a * 128:(a + 1) * 128], wkT[:, kc, a * 128:(a + 1) * 128], xt[:, kc, :],
                                                                       start=(kc == 0), stop=(kc == 7)),
                                     reads=[xt, wkT], writes=[MB], inc=(kc == 7))
                        k.op("dve", lambda e: e.tensor_copy(KTs[:, j * 128:(j + 1) * 128], MB[:, 0:128]), reads=[MB], writes=[KTs])
                        k.op("dve", lambda e: e.tensor_copy(KTw[:, j * 128:(j + 1) * 128], MB[:, 128:256]), reads=[MB], writes=[KTw])
                        if os.environ.get("DEV_P1", "") == "proj":
                            continue
                        for n in range(2):
                            scores(n, kcaux[:, j * 128:(j + 1) * 128], kcT[:, :],
                                   (cmpsel[:, j * 128:(j + 1) * 128], vispat, [cb8]), PTc[:, :])
                            for g in range(4):
                                k.op("pe", lambda e, g=g: e.matmul(MA[:, g * 65:(g + 1) * 65], PTc[:, g * 128:(g + 1) * 128], cmprhs[:, n, 0:65],
                                                                 start=True, stop=True), reads=[PTc, cmprhs], writes=[MA])
                            for g in range(4):
                                k.op("pe", lambda e, g=g: e.matmul(MB[:, g * 64:(g + 1) * 64], PTc[:, g * 128:(g + 1) * 128], pool2,
                                                                 start=True, stop=True), reads=[PTc, cb128], writes=[MB])
                            k.op("act", lambda e: e.copy(ocmp[:, 4 * n:4 * n + 4, :], MA[:, 0:260].rearrange("p (g d) -> p g d", g=4)),
                                 reads=[MA], writes=[ocmp])
                            k.op("dve", lambda e: e.tensor_scalar(rl[:, 4 * n:4 * n + 4, 0], ocmp[:, 4 * n:4 * n + 4, 64], 1e-30, None, ALU.max),
                                 reads=[ocmp], writes=[rl])
                            k.op("dve", lambda e: e.reciprocal(rl[:, 4 * n:4 * n + 4, 0], rl[:, 4 * n:4 * n + 4, 0]), reads=[rl], writes=[rl])
                            if j >= 8:
                                k.op("dve", lambda e: e.tensor_scalar(imp[:], MB[:, 0:64], rl[:, 4 * n, 0:1], None, ALU.mult),
                                     reads=[MB, rl], writes=[imp])
                                for g in range(1, 4):
                                    k.op("dve", lambda e, g=g: e.scalar_tensor_tensor(imp[:], MB[:, g * 64:(g + 1) * 64], rl[:, 4 * n + g, 0:1], imp[:],
                                                                                     ALU.mult, ALU.add), reads=[MB, rl, imp], writes=[imp])
                                k.op("dve", lambda e: e.tensor_tensor(imp[:], imp[:], tkb[:, j * 64:(j + 1) * 64], ALU.add), reads=[imp, c32], writes=[imp])
                                k.op("dve", lambda e: e.max(out=m8[:, 0:8], in_=imp[:]), reads=[imp], writes=[m8])
                                k.op("dve", lambda e: e.match_replace(out=tkw[:], in_to_replace=m8[:, 0:8], in_values=imp[:], imm_value=-1e9),
                                     reads=[imp, m8], writes=[tkw])
                                k.op("dve", lambda e: e.max(out=m8[:, 8:16], in_=tkw[:]), reads=[tkw], writes=[m8])
                                k.op("dve", lambda e: e.tensor_scalar(selb[:], imp[:], m8[:, 15:16], None, ALU.is_ge), reads=[imp, m8], writes=[selb])
                                k.op("dve", lambda e: e.tensor_scalar(selb[:], selb[:], -1.0, -NEG, ALU.add, ALU.mult), reads=[selb], writes=[selb])
                                k.op("pe", lambda e: e.transpose(MB[0:64, 256:384], selb[:, :], ident), reads=[selb, c32], writes=[MB])
                                k.op("dve", lambda e: e.tensor_copy(selbT[n][0:64, :, :], MB[0:64, 256:384].unsqueeze(1).to_broadcast([64, 4, 128])),
                                     reads=[MB], writes=[selbT[n]])
                        if os.environ.get("DEV_P1", "") == "cmp":
                            continue
                        for br, KT, V, tiles in (("s", KTs, Vs, list(range(0, j + 1))), ("w", KTw, Vw, list(range(max(0, j - 4), j + 1)))):
                            if os.environ.get("DEV_P1", "") == "sonly" and br == "w":
                                continue
                            for n in range(2):
                                for t in tiles:
                                    if t == j:
                                        mask = (identb, causal4, [cb128])
                                    elif br == "w" and t == j - 4:
                                        mask = (identb, winedge4, [cb128])
                                    elif br == "s" and j >= 8:
                                        mask = (Epad[:, t * 128:(t + 1) * 128], selbT[n][:].rearrange("p g q -> p (g q)"), [cb64, selbT[n]])
                                    else:
                                        mask = None
                                    nm_ = os.environ.get("DEV_NOMASK", "")
                                    if nm_ == "1" or (nm_ == "2" and mask is not None and mask[2][0] is cb128) or (nm_ == "3" and mask is not None and mask[2][0] is cb64):
                                        mask = None
                                    scores(n, kaux[:, (j - t) * 128:(j - t + 1) * 128], KT[:, t * 128:(t + 1) * 128], mask, PT[n][:, t, :])
                                if os.environ.get("DEV_P1", "") == "sc":
                                    continue
                                A = ACC[(br, n)]
                                for g in range(4):
                                    for ti, t in enumerate(tiles):
                                        k.op("pe", lambda e, g=g, t=t, ti=ti: e.matmul(A[:, g * 65:(g + 1) * 65], PT[n][:, t, g * 128:(g + 1) * 128], V[:, t, n, 0:65],
                                                                                  start=(ti == 0), stop=(ti == len(tiles) - 1)),
                                             reads=[PT[n], V], writes=[A], inc=(ti == len(tiles) - 1))
                                if os.environ.get("DEV_P1", "") == "pv":
                                    continue
                                col = 1 if br == "s" else 2
                                k.op("dve", lambda e: e.reciprocal(rl[:, 4 * n:4 * n + 4, col],
                                                                   A[:, 0:260].rearrange("p (g d) -> p g d", g=4)[:, :, 64]),
                                     reads=[A], writes=[rl])
                        if os.environ.get("DEV_P1", "") in ("sc", "pv", "rc"):
                            continue
                        k.op("dve", lambda e: e.tensor_tensor(fgt[:], rl[:], gates[:].rearrange("p (h b) -> p h b", b=3), ALU.mult),
                             reads=[rl, gates], writes=[fgt])
                        for h in range(8):
                            n, g = h // 4, h % 4
                            o_ = onsa[:, h * 64:(h + 1) * 64]
                            k.op("dve", lambda e: e.tensor_scalar(o_, ocmp[:, h, 0:64], fgt[:, h, 0:1], None, ALU.mult), reads=[ocmp, fgt], writes=[onsa])
                            k.op("dve", lambda e: e.scalar_tensor_tensor(o_, ACC[("s", n)][:, g * 65:g * 65 + 64], fgt[:, h, 1:2], o_, ALU.mult, ALU.add),
                                 reads=[ACC[("s", n)], fgt, onsa], writes=[onsa])
                            k.op("dve", lambda e: e.scalar_tensor_tensor(o_, ACC[("w", n)][:, g * 65:g * 65 + 64], fgt[:, h, 2:3], o_, ALU.mult, ALU.add),
                                 reads=[ACC[("w", n)], fgt, onsa], writes=[onsa])
                        for c in range(4):
                            k.op("pe", lambda e, c=c: e.transpose(MA[:, c * 128:(c + 1) * 128], onsa[:, c * 128:(c + 1) * 128], ident),
                                 reads=[onsa, c32], writes=[MA])
                        k.op("act", lambda e: e.copy(onT[:], MA[:, :].rearrange("p (c t) -> p c t", c=4)), reads=[MA], writes=[onT])
                        k.dma("pool", mixT_d.t[0:512, :].rearrange("(c p) t -> p c t", p=128)[:, :, j * 128:(j + 1) * 128], onT[:],
                              reads=[onT], writes=[mixT_d])
                k.barrier()
            if STOP == "P1":
                break
            if do_prompt:
                gdn_prompt(l)
            if STOP == "P2":
                break
            if do_sample:
                sample_mixers(l)
            chain(l, xres_p, xres_s, yout_p, yout_s, last)
            if STOP == "P3":
                break
        k.finish()
    return nc


_W_NAMES = {"w_in": "w_in", "nsa_pe": "nsa_pe", "nsa_phi": "nsa_phi", "gdn_conv_w": "gconv_w", "gdn_A_log": "A_log",
            "gdn_dt_bias": "dt_bias", "gdn_norm_w": "gnorm_w", "w_out": "w_out", "ln_g": "ln_g", "ln_b": "ln_b",
            "ffn_w_up": "w_up", "ffn_conv_w": "fconv_w", "ffn_w_down": "w_down", "ple_w_proj": "w_proj", "ple_w_gate": "w_gate"}


def run_cores(inp, ncores, parts=("prompt", "sample")):
    f32 = np.float32
    x_prompt = np.asarray(inp["x_prompt"], f32)
    B, T, _ = x_prompt.shape
    x_sample = np.asarray(inp["x_sample"], f32)
    NSTOT = x_sample.shape[0]
    cache = np.asarray(inp["cache_nsa_kv"], f32)
    NPOOL = cache.shape[1]
    page_table = np.asarray(inp["page_table"], np.int32)
    NPG = page_table.shape[1]
    NS = NSTOT // ncores
    swin = np.asarray(inp["state_nsa_win"], f32)
    WB = swin.shape[2]
    nc = build(T, NS, NPOOL, NPG, WB, parts)
    consts = make_consts(T, NS, NPG)
    common = {v: np.ascontiguousarray(np.asarray(inp[kk], f32)) for kk, v in _W_NAMES.items()}
    common.update(consts)
    pool = np.ascontiguousarray(cache.reshape(DEPTH * NPOOL * 128, 512))
    p_prompt = np.asarray(inp["p_prompt"], f32)
    p_sample = np.asarray(inp["p_sample"], f32)
    sgdn = np.asarray(inp["state_gdn"], f32)
    sgc = np.asarray(inp["state_gdn_conv"], f32)
    sfc = np.asarray(inp["state_ffn_conv"], f32)
    per = ncores // B if ncores >= B else 1
    in_maps = []
    for c in range(ncores):
        b = min(c // per, B - 1)
        sl = slice(c * NS, (c + 1) * NS)
        m = dict(common)
        m.update({
            "xp": np.ascontiguousarray(x_prompt[b]),
            "pp": np.ascontiguousarray(p_prompt[:, b]),
            "xs": np.ascontiguousarray(x_sample[sl, 0]),
            "pps": np.ascontiguousarray(p_sample[:, sl, 0]),
            "pool": pool,
            "ptab": np.ascontiguousarray(page_table[sl].reshape(1, NS * NPG)),
            "st_win": np.ascontiguousarray(swin[:, sl].reshape(DEPTH, NS, WB, 256)),
            "st_gdn": np.ascontiguousarray(sgdn[:, sl]),
            "st_gconv": np.ascontiguousarray(sgc[:, sl]),
            "st_fconv": np.ascontiguousarray(sfc[:, sl]),
        })
        in_maps.append(m)
    if os.environ.get("DEV_TRACE", "") == "1":
        res = run_bass_kernel_spmd(nc, in_maps, core_ids=list(range(ncores)), trace=True)
        print("EXEC_TIME_NS", res.exec_time_ns)
    else:
        res = run_bass_kernel_spmd(nc, in_maps, core_ids=list(range(ncores)))
    R = res.results
    pc = [min(b * per, ncores - 1) for b in range(B)]
    y_prompt = np.stack([R[c]["y_p"] for c in pc])
    y_sample = np.concatenate([R[c]["y_s"] for c in range(ncores)])[:, None, :]
    kv_rows_prompt = np.stack([R[c]["kv_p"] for c in pc], axis=1).reshape(DEPTH, B, T, 4, 2, 64)
    kv_rows_sample = np.concatenate([R[c]["kv_s"] for c in range(ncores)], axis=1).reshape(DEPTH, NSTOT, 1, 4, 2, 64)
    win_prompt = np.stack([R[c]["win_p"] for c in pc], axis=1).reshape(DEPTH, B, -1, 2, 2, 64)
    win_sample = np.concatenate([R[c]["win_s"] for c in range(ncores)], axis=1).reshape(DEPTH, NSTOT, WB, 2, 2, 64)
    gdn_state_prompt = np.stack([R[c]["gst_p"] for c in pc], axis=1)
    gdn_state_sample = np.concatenate([R[c]["gst_s"] for c in range(ncores)], axis=1)
    gdn_conv_prompt = np.stack([R[c]["gcv_p"] for c in pc], axis=1)
    gdn_conv_sample = np.concatenate([R[c]["gcv_s"] for c in range(ncores)], axis=1)
    ffn_conv_prompt = np.stack([R[c]["fcv_p"] for c in pc], axis=1)
    ffn_conv_sample = np.concatenate([R[c]["fcv_s"] for c in range(ncores)], axis=1)
    return (y_prompt, y_sample, kv_rows_prompt, kv_rows_sample, win_prompt, win_sample, gdn_state_prompt, gdn_state_sample,
            gdn_conv_prompt, gdn_conv_sample, ffn_conv_prompt, ffn_conv_sample)


def kernel(**inputs):
    outs = run_cores(inputs, NCORES)
    return tuple(np.ascontiguousarray(o, dtype=np.float32) for o in outs)
```

```python
import os
import numpy as np
import ml_dtypes
from contextlib import ExitStack
import concourse.bass as bass
import concourse.mybir as mybir
from concourse.bass_utils import run_bass_kernel_spmd

F32 = mybir.dt.float32
BF16 = mybir.dt.bfloat16
I32 = mybir.dt.int32
F32R = mybir.dt.float32r
FASTF32 = os.environ.get("DEV_F32R", "0") == "1"
NOSELF = os.environ.get("DEV_NOSELF", "0") == "1"


def fr(ap):
    return ap


def f32v(ap):
    return ap.bitcast(F32) if FASTF32 else ap
AF = mybir.ActivationFunctionType
ALU = mybir.AluOpType
AX = mybir.AxisListType
bf = ml_dtypes.bfloat16

NEG = -60000.0
DM = 1024
INW = 3368
DFF = 2816
PLED = 256
C_KV, C_WIN, C_GATE, C_GQKV, C_GA, C_GB, C_GZ = 512, 1024, 1280, 1304, 2840, 2848, 2856
DEPTH = 2
ALPHA = float((2 * DEPTH) ** 0.25)
LN_EPS = 1e-5
RMS_EPS = 1e-6
NCORES = 8


class Buf:
    __slots__ = ("t", "w", "r", "name", "excl")

    def __init__(self, t, name="", excl=False):
        self.t = t
        self.w = {}
        self.r = {}
        self.name = name
        self.excl = excl

    def __getitem__(self, idx):
        return self.t[idx]


class K:
    NDMA = 32

    def __init__(self, nc):
        self.nc = nc
        self.eng = {"pe": nc.tensor, "act": nc.scalar, "dve": nc.vector, "pool": nc.gpsimd, "sp": nc.sync}
        self.sem = {}
        self.cnt = {}
        for e in self.eng:
            self.sem[e] = nc.alloc_semaphore("sem_" + e)
            self.cnt[e] = 0
        for j in range(self.NDMA):
            key = ("dma", j)
            self.sem[key] = nc.alloc_semaphore("sem_dma%d" % j)
            self.cnt[key] = 0
        self.dma_rr = 0
        self.seen = {e: {} for e in self.eng}
        self.nins = 0
        self.uid = 0

    def name(self, s):
        self.uid += 1
        return "%s_%d" % (s, self.uid)

    def sb(self, es, name, shape, dt=F32):
        t = es.enter_context(self.nc.sbuf_tensor(self.name(name), list(shape), dt))
        return Buf(t, name)

    def ps(self, name, shape, dt=F32):
        return Buf(self.nc.alloc_psum_tensor(self.name(name), list(shape), dt), name, excl=True)

    def dram(self, name, shape, dt=F32, kind="Internal"):
        return Buf(self.nc.dram_tensor(name, list(shape), dt, kind=kind).ap(), name)

    def _wait(self, e, deps):
        eng = self.eng[e]
        seen = self.seen[e]
        for key, v in deps.items():
            if v <= 0 or seen.get(key, 0) >= v:
                continue
            eng.wait_ge(self.sem[key], v)
            self.nins += 1
            seen[key] = v

    @staticmethod
    def _deps(reads, writes):
        deps = {}
        for b in reads:
            for key, v in b.w.items():
                if deps.get(key, 0) < v:
                    deps[key] = v
        for b in writes:
            for key, v in b.w.items():
                if deps.get(key, 0) < v:
                    deps[key] = v
            for key, v in b.r.items():
                if deps.get(key, 0) < v:
                    deps[key] = v
        return deps

    @staticmethod
    def _record(key, val, reads, writes):
        for b in reads:
            if b.r.get(key, 0) < val:
                b.r[key] = val
        for b in writes:
            b.w.clear()
            b.w[key] = val
            b.r.clear()

    def op(self, e, fn, reads=(), writes=(), inc=True):
        ex = [b for b in reads if b.excl]
        if ex:
            writes = list(writes) + ex
        deps = self._deps(reads, writes)
        if e == "pe" or NOSELF:
            deps.pop(e, None)
        if e in deps and deps[e] > self.cnt[e]:
            deps[e] = self.cnt[e]
        self._wait(e, deps)
        ins = fn(self.eng[e])
        self.nins += 1
        if inc:
            self.cnt[e] += 1
            ins.then_inc(self.sem[e], 1)
            val = self.cnt[e]
        else:
            val = self.cnt[e] + 1
        self._record(e, val, reads, writes)
        return ins

    def _dma_issue(self, q, reads, writes, fn):
        deps = self._deps(reads, writes)
        j = self.dma_rr
        self.dma_rr = (self.dma_rr + 1) % self.NDMA
        key = ("dma", j)
        if self.cnt[key] > 0 and deps.get(key, 0) < self.cnt[key]:
            deps[key] = self.cnt[key]
        if q in deps and deps[q] > self.cnt[q]:
            deps[q] = self.cnt[q]
        self._wait(q, deps)
        ins = fn(self.eng[q])
        self.nins += 1
        self.cnt[key] += 16
        ins.then_inc(self.sem[key], 16)
        self._record(key, self.cnt[key], reads, writes)
        return ins

    def dma(self, q, out, in_, reads=(), writes=(), **kw):
        return self._dma_issue(q, reads, writes, lambda e: e.dma_start(out=out, in_=in_, **kw))

    def gather(self, out, table, idx, reads=(), writes=()):
        return self._dma_issue(
            "pool", reads, writes,
            lambda e: e.indirect_dma_start(out=out, out_offset=None, in_=table,
                                           in_offset=bass.IndirectOffsetOnAxis(ap=idx, axis=0)))

    def barrier(self):
        full = dict(self.cnt)
        for e in self.eng:
            deps = {key: v for key, v in full.items() if key != e}
            self._wait(e, deps)

    def finish(self):
        deps = {key: v for key, v in self.cnt.items() if key != "sp"}
        self._wait("sp", deps)


def make_consts(T, NS=16, NPG=16):
    NJ = T // 128
    c = {}
    p = np.arange(128)
    ident = np.eye(128, dtype=np.float32)
    U = (p[:, None] <= p[None, :]).astype(np.float32)
    ones = np.ones((128, 128), np.float32)
    mask_incl = np.where(p[:, None] >= p[None, :], 0.0, -1e4).astype(np.float32)
    maskT_incl = np.where(p[None, :] >= p[:, None], 0.0, -1e4).astype(np.float32)
    strict01 = (p[:, None] > p[None, :]).astype(np.float32)
    tk = np.zeros((128, NJ, 64), np.float32)
    blk = np.arange(64)
    for j in range(NJ):
        cur = (128 * j + p) // 64
        fut = blk[None, :] > cur[:, None]
        forced = (blk[None, :] == 0) | (((cur[:, None] - blk[None, :]) < 2) & ~fut)
        tk[:, j, :] = np.where(fut, -1e4, np.where(forced, 1e4, 0.0))
    c["c32"] = np.concatenate([ident, U, ones, mask_incl, maskT_incl, strict01, tk.reshape(128, NJ * 64)], axis=1)
    causal = np.where(p[:, None] <= p[None, :], 0.0, NEG)
    winedge = np.where(p[:, None] > p[None, :], 0.0, NEG)
    pool2 = (p[:, None] // 2 == np.arange(64)[None, :]).astype(np.float32)
    avg = (p[:, None] // 32 == np.arange(4)[None, :]).astype(np.float32) / 32.0
    c["cb128"] = np.concatenate([np.tile(causal, (1, 4)), np.tile(winedge, (1, 4)), ident, pool2,
                                 avg, np.ones((128, 4))], axis=1).astype(bf)
    kp = np.arange(T)
    Eall = (kp[None, :] // 64 == np.arange(64)[:, None]).astype(np.float32)
    slopes = 2.0 ** (-np.arange(1, 9, dtype=np.float64))
    kauxrel = np.zeros((128, NJ, 128), np.float32)
    for d in range(NJ):
        kauxrel[0, d, :] = -d
        kauxrel[1, d, :] = p
        kauxrel[2, d, :] = 1.0
    cc = np.arange(128)
    kcauxrel = np.zeros((128, NJ, 128), np.float32)
    for j in range(NJ):
        kcauxrel[0, j, :] = cc / 4.0 - j
        kcauxrel[2, j, :] = 1.0
    qaux = np.zeros((128, 2, 4, 128), np.float32)
    for n in range(2):
        for g in range(4):
            sl = slopes[4 * n + g]
            qaux[0, n, g, :] = sl * 8 * 128
            qaux[1, n, g, :] = sl * 8
            qaux[2, n, g, :] = -8.0 * sl * p
    cmpsel = np.zeros((128, NJ, 128), np.float32)
    for j in range(NJ):
        for r in range(4):
            if 4 * j + r < 128:
                cmpsel[r, j, 4 * j + r] = 1.0
        cmpsel[4, j, 4 * j + 4:] = 1.0
    vis = np.zeros((128, 4, 128), np.float32)
    for r in range(4):
        vis[r, :, :] = np.where(32 * r + 31 <= p, 0.0, NEG)[None, :]
    vis[4] = NEG
    Epad = np.zeros((128, T), np.float32)
    Epad[0:64] = Eall
    c["cbp"] = np.concatenate([kauxrel.reshape(128, -1), kcauxrel.reshape(128, -1), qaux.reshape(128, 1024),
                               cmpsel.reshape(128, -1), vis.reshape(128, 512), Epad], axis=1).astype(bf)
    PAST = NPG * 128
    NWT = 4
    al_s = np.zeros((128, 2, NPG, 4), np.float32)
    al_w = np.zeros((128, 2, NWT, 4), np.float32)
    al_c = np.zeros((128, NS, 2, 4), np.float32)
    for n in range(2):
        for g in range(4):
            sl = slopes[4 * n + g]
            for t in range(NPG):
                al_s[:, n, t, g] = -sl * (PAST - (t * 128 + p))
            for t in range(NWT):
                dist = NWT * 128 - (t * 128 + p)
                al_w[:, n, t, g] = np.where(dist < NWT * 128, -sl * dist, -1e4)
            al_c[:, :, n, g] = (-sl * (PAST - (32 * p + 15.5)))[:, None]
    GS = np.zeros((128, 2 * NS), np.float32)
    GS[np.arange(NS * 8), np.arange(NS * 8) // 4] = 1.0
    NB33 = NPG * 2 + 1
    tkb = np.zeros((128, 40), np.float32)
    tkb[:, 0] = 1e4
    tkb[:, NB33 - 2] = 1e4
    tkb[:, NB33 - 1] = 1e4
    piota = np.stack([2.0 * p, 2.0 * p], axis=1).astype(np.float32)
    c["cs32"] = np.concatenate([al_s.reshape(128, -1), al_w.reshape(128, -1), al_c.reshape(128, -1), GS, tkb, piota], axis=1).astype(np.float32)
    OH = np.zeros((128, 2 * NS, 128), np.float32)
    for r in range(2 * NS):
        OH[r, r, :] = 1.0
    pool33 = np.zeros((128, NB33 + 1), np.float32)
    for cidx in range(NPG * 4):
        pool33[cidx, cidx // 2] = 1.0
    pool33[:, NB33] = 1.0
    c["csb"] = np.concatenate([OH.reshape(128, -1), pool33], axis=1).astype(bf)
    return c


def build(T, NS, NPOOL, NPG=16, WB=512, parts=("prompt", "sample")):
    NJ = T // 128
    PAST = NPG * 128
    NWT = WB // 128
    nc = bass.Bass("TRN2", target_bir_lowering=False)
    k = K(nc)
    do_prompt = "prompt" in parts
    do_sample = "sample" in parts

    def din(name, shape, dt=F32):
        return k.dram(name, shape, dt, kind="ExternalInput")

    def dout(name, shape, dt=F32):
        return k.dram(name, shape, dt, kind="ExternalOutput")

    xp = din("xp", [T, DM])
    pp = din("pp", [DEPTH, T, PLED])
    xs = din("xs", [NS, DM])
    pps = din("pps", [DEPTH, NS, PLED])
    pool = din("pool", [DEPTH * NPOOL * 128, 512])
    ptab = din("ptab", [1, NS * NPG], I32)
    st_win = din("st_win", [DEPTH, NS, WB, 256])
    st_gdn = din("st_gdn", [DEPTH, NS, 8, 64, 64])
    st_gconv = din("st_gconv", [DEPTH, NS, 3, 1536])
    st_fconv = din("st_fconv", [DEPTH, NS, 2, DFF])
    w_in = din("w_in", [DEPTH, DM, INW])
    nsa_pe = din("nsa_pe", [DEPTH, 2, 32, 64])
    nsa_phi = din("nsa_phi", [DEPTH, 2, 64, 64])
    gconv_w = din("gconv_w", [DEPTH, 4, 1536])
    A_log = din("A_log", [DEPTH, 8])
    dt_bias = din("dt_bias", [DEPTH, 8])
    gnorm_w = din("gnorm_w", [DEPTH, 64])
    w_out = din("w_out", [DEPTH, DM, DM])
    ln_g = din("ln_g", [DEPTH, 3, DM])
    ln_b = din("ln_b", [DEPTH, 3, DM])
    w_up = din("w_up", [DEPTH, DM, 2 * DFF])
    fconv_w = din("fconv_w", [DEPTH, 3, DFF])
    w_down = din("w_down", [DEPTH, DFF, DM])
    w_proj = din("w_proj", [DEPTH, PLED, DM])
    w_gate = din("w_gate", [DEPTH, DM, DM])
    NC32 = 6 * 128 + NJ * 64
    c32_d = din("c32", [128, NC32])
    NCB128 = 512 + 512 + 128 + 64 + 4 + 4
    cb128_d = din("cb128", [128, NCB128], BF16)
    NCS32 = 2 * NPG * 4 + 2 * NWT * 4 + NS * 8 + 2 * NS + 40 + 2
    cs32_d = din("cs32", [128, NCS32])
    NCSB = 2 * NS * 128 + NPG * 2 + 2
    csb_d = din("csb", [128, NCSB], BF16)
    NCBP = 3 * NJ * 128 + 1024 + 512 + T
    cbp_d = din("cbp", [128, NCBP], BF16)
    y_p = dout("y_p", [T, DM])
    y_s = dout("y_s", [NS, DM])
    kv_p = dout("kv_p", [DEPTH, T, 512])
    kv_s = dout("kv_s", [DEPTH, NS, 512])
    win_p = dout("win_p", [DEPTH, WB, 256])
    win_s = dout("win_s", [DEPTH, NS, WB, 256])
    gst_p = dout("gst_p", [DEPTH, 8, 64, 64])
    gst_s = dout("gst_s", [DEPTH, NS, 8, 64, 64])
    gcv_p = dout("gcv_p", [DEPTH, 3, 1536])
    gcv_s = dout("gcv_s", [DEPTH, NS, 3, 1536])
    fcv_p = dout("fcv_p", [DEPTH, 2, DFF])
    fcv_s = dout("fcv_s", [DEPTH, NS, 2, DFF])
    wb_in = k.dram("wb_in", [DEPTH, DM, INW], BF16)
    wb_out = k.dram("wb_out", [DEPTH, DM, DM], BF16)
    wb_up = k.dram("wb_up", [DEPTH, DM, 2 * DFF], BF16)
    wb_down = k.dram("wb_down", [DEPTH, DFF, DM], BF16)
    wb_gate = k.dram("wb_gate", [DEPTH, DM, DM], BF16)
    wb_proj = k.dram("wb_proj", [DEPTH, PLED, DM], BF16)
    xT_d = k.dram("xT_d", [DM, T], BF16)
    mixT_d = k.dram("mixT_d", [DM, T], BF16)
    x1_d = k.dram("x1_d", [T, DM], F32)
    xsT_d = k.dram("xsT_d", [DM, NS], BF16)
    mixsT_d = k.dram("mixsT_d", [DM, NS], BF16)
    xs1_d = k.dram("xs1_d", [NS, DM], F32)

    pb = [k.ps("pb%d" % i, [128, 512]) for i in range(8)]

    with ExitStack() as gs:
        c32 = k.sb(gs, "c32", [128, NC32])
        k.dma("sp", c32[:], c32_d[:], writes=[c32])
        cb128 = k.sb(gs, "cb128", [128, NCB128], BF16)
        k.dma("sp", cb128[:], cb128_d[:], writes=[cb128])
        ident = c32.t[:, 0:128]
        Umat = c32.t[:, 128:256]
        ones32 = c32.t[:, 256:384]
        mask_incl = c32.t[:, 384:512]
        maskT_incl = c32.t[:, 512:640]
        strict01 = c32.t[:, 640:768]
        tkb = c32.t[:, 768:768 + NJ * 64]
        causal4 = cb128.t[:, 0:512]
        winedge4 = cb128.t[:, 512:1024]
        identb = cb128.t[:, 1024:1152]
        pool2 = cb128.t[:, 1152:1216]
        avg4 = cb128.t[:, 1216:1220]
        epsc = k.sb(gs, "epsc", [128, 2])
        k.op("pool", lambda e: e.memset(epsc[:, 0:1], RMS_EPS), writes=[epsc])
        k.op("pool", lambda e: e.memset(epsc[:, 1:2], LN_EPS), writes=[epsc])

        cast_rr = [0]

        def cast(out, in_, reads, writes, psum=False):
            e = ("dve", "act")[cast_rr[0] % 2] if psum else ("dve", "act", "pool")[cast_rr[0] % 3]
            cast_rr[0] += 1
            if e == "act":
                k.op("act", lambda en: en.copy(out, in_), reads=reads, writes=writes)
            else:
                k.op(e, lambda en: en.tensor_copy(out, in_), reads=reads, writes=writes)

        with ExitStack() as es:
            stg = [k.sb(es, "wstg%d" % i, [128, 2 * DFF]) for i in range(2)]
            stgb = [k.sb(es, "wstgb%d" % i, [128, 2 * DFF], BF16) for i in range(2)]
            it = 0
            for l in range(DEPTH):
                for (src, dst, rows, cols) in ((w_in, wb_in, DM, INW), (w_out, wb_out, DM, DM), (w_up, wb_up, DM, 2 * DFF),
                                               (w_down, wb_down, DFF, DM), (w_gate, wb_gate, DM, DM), (w_proj, wb_proj, PLED, DM)):
                    for r0 in range(0, rows, 128):
                        s, sb_ = stg[it % 2], stgb[it % 2]
                        it += 1
                        k.dma("sp", s[:, 0:cols], src[l, r0:r0 + 128, :], writes=[s])
                        cast(sb_[:, 0:cols], s[:, 0:cols], [s], [sb_])
                        k.dma("pool", dst[l, r0:r0 + 128, :], sb_[:, 0:cols], reads=[sb_], writes=[dst])
        k.barrier()

        def transpose_to_xT(es_name, src_tile, ntok, dstT, col0, trp, trs):
            for half in range(2):
                for c4 in range(4):
                    c = half * 4 + c4
                    k.op("pe", lambda e, c=c, c4=c4: e.transpose(trp[:, c4 * 128:c4 * 128 + ntok],
                                                               src_tile[0:ntok, c * 128:(c + 1) * 128], ident[0:ntok, 0:ntok]),
                         reads=[src_tile, c32], writes=[trp])
                cast(trs[:, half * 4:(half + 1) * 4, 0:ntok],
                     trp[:, :].rearrange("p (c t) -> p c t", c=4)[:, :, 0:ntok], [trp], [trs], psum=True)
            k.dma("pool", dstT.t.rearrange("(c p) t -> p c t", p=128)[:, :, col0:col0 + ntok], trs[:, :, 0:ntok],
                  reads=[trs], writes=[dstT])

        with ExitStack() as es:
            xt_in = [k.sb(es, "xt_in%d" % i, [128, DM]) for i in range(2)]
            trs = [k.sb(es, "trs%d" % i, [128, 8, 128], BF16) for i in range(2)]
            if do_prompt:
                for j in range(NJ):
                    xt = xt_in[j % 2]
                    k.dma("sp", xt[:], xp[j * 128:(j + 1) * 128, :], writes=[xt])
                    transpose_to_xT("x0", xt, 128, xT_d, j * 128, pb[j % 2], trs[j % 2])
            if do_sample:
                xt = xt_in[0]
                k.dma("sp", xt[0:NS, :], xs[:, :], writes=[xt])
                transpose_to_xT("xs0", xt, NS, xsT_d, 0, pb[2], trs[0])
        k.barrier()

        def gdn_prompt(l):
            with ExitStack() as es:
                wsrc = wb_in.t[l].rearrange("(c p) n -> p c n", p=128)
                wg = k.sb(es, "wg", [128, 8, 8, 192], BF16)
                for blk in range(3):
                    for kc in range(8):
                        k.dma("sp", wg[:, kc, :, blk * 64:(blk + 1) * 64],
                              wb_in.t[l, kc * 128:(kc + 1) * 128, C_GQKV + blk * 512:C_GQKV + (blk + 1) * 512].rearrange("p (h d) -> p h d", h=8),
                              reads=[wb_in], writes=[wg])
                wab = k.sb(es, "wab", [128, 8, 16], BF16)
                k.dma("sp", wab[:], wsrc[:, :, C_GA:C_GA + 16], reads=[wb_in], writes=[wab])
                wz = k.sb(es, "wz", [128, 8, 512], BF16)
                k.dma("sp", wz[:], wsrc[:, :, C_GZ:C_GZ + 512], reads=[wb_in], writes=[wz])
                cw = k.sb(es, "cw", [64, 8, 3, 4])
                for h in range(8):
                    for blk in range(3):
                        c0 = blk * 512 + h * 64
                        k.dma("sp", cw[:, h, blk, :], gconv_w[l][:, c0:c0 + 64].rearrange("w d -> d w"), writes=[cw],
                              allow_slow_non_contiguous=True)
                dtb = k.sb(es, "dtb", [128, 8])
                k.dma("sp", dtb[:], dt_bias[l:l + 1, :].partition_broadcast(128), writes=[dtb])
                negA = k.sb(es, "negA", [128, 8])
                k.dma("sp", negA[:], A_log[l:l + 1, :].partition_broadcast(128), writes=[negA])
                k.op("act", lambda e: e.activation(negA[:], negA[:], AF.Exp), reads=[negA], writes=[negA])
                k.op("dve", lambda e: e.tensor_scalar(negA[:], negA[:], -1.0, None, ALU.mult), reads=[negA], writes=[negA])
                nw = k.sb(es, "nw", [128, 64])
                k.dma("sp", nw[:], gnorm_w[l:l + 1, :].partition_broadcast(128), writes=[nw])
                S = [k.sb(es, "S%d" % h, [128, 64]) for h in range(8)]
                carry = [k.sb(es, "carry%d" % h, [64, 3, 3]) for h in range(8)]
                for h in range(8):
                    k.op("pool", lambda e: e.memset(S[h][:], 0.0), writes=[S[h]])
                    k.op("pool", lambda e: e.memset(carry[h][:], 0.0), writes=[carry[h]])
                xTt = [k.sb(es, "gxTt%d" % i, [128, 8, 128], BF16) for i in range(2)]
                tmp8 = k.sb(es, "tmp8", [128, 8])
                gtok = k.sb(es, "gtok", [128, 8])
                btok = k.sb(es, "btok", [128, 8])
                gctok = k.sb(es, "gctok", [128, 8])
                ngctok = k.sb(es, "ngctok", [128, 8])
                egctok = k.sb(es, "egctok", [128, 8])
                nz = k.sb(es, "nz", [128, 8, 64])
                og = k.sb(es, "og", [128, 512])
                ogT = k.sb(es, "ogT", [128, 4, 128], BF16)
                NSLOT = int(os.environ.get("DEV_NSLOT", "8"))
                RG = []
                for s_ in range(NSLOT):
                    row = []
                    for r in range(4):
                        rb = Buf(pb[s_].t[:, r * 128:(r + 1) * 128], "R%d_%d" % (s_, r), excl=True)
                        rb.w = pb[s_].w
                        rb.r = pb[s_].r
                        row.append(rb)
                    RG.append(row)
                PAB, PZ, PTR = (pb[4], pb[5], pb[6]) if NSLOT == 4 else (pb[7], pb[6], pb[5])

                def mk(s):
                    d = {}
                    for nm, shp in (("ext", [64, 3, 131]), ("y", [64, 3, 128]), ("ys", [64, 3, 128]), ("sq", [64, 256]), ("rs", [64, 256]),
                                    ("qTf", [64, 128]), ("kTf", [64, 128]), ("qgT", [128, 128]), ("rhsX", [128, 128]), ("kend", [128, 64]),
                                    ("Ug", [128, 128]), ("X1", [128, 128]), ("dec", [128, 128]), ("X2", [128, 128]), ("decT", [128, 128]),
                                    ("egcB", [64, 128]), ("glc", [128, 4]), ("tN", [128, 128]), ("N", [128, 128]), ("NT", [128, 128]),
                                    ("P0", [128, 128]), ("P1", [128, 128]), ("Q1", [128, 128]), ("W0", [128, 128]), ("W1", [128, 128]),
                                    ("innerT", [128, 128]), ("val", [128, 64]), ("kcdT", [64, 128]), ("vn", [128, 64]), ("junk", [128, 64]),
                                    ("rstd", [128, 2])):
                        d[nm] = k.sb(es, "%s_%d" % (nm, s), shp, F32R if (FASTF32 and nm in ("N", "NT", "P0", "P1", "Q1", "W0", "W1")) else F32)
                    k.op("pool", lambda e: e.memset(d["qgT"][:], 0.0), writes=[d["qgT"]])
                    return d
                WK = [mk(s) for s in range(NSLOT)]
                id64 = ident[0:64, 0:64]
                ones64 = ones32[0:64, 0:64]

                def head_chain(h, i, slot, xt):
                    R = RG[slot]
                    w = WK[slot]
                    bank = pb[slot].t
                    ext, y, ys = w["ext"], w["y"], w["ys"]
                    for blk in range(3):
                        for kc in range(8):
                            k.op("pe", lambda e: e.matmul(R[blk][0:64, :], wg[:, kc, h, blk * 64:(blk + 1) * 64], xt[:, kc, :],
                                                          start=(kc == 0), stop=(kc == 7)), reads=[wg, xt], writes=[R[blk]], inc=(kc == 7))
                    yield
                    k.op("pool", lambda e: e.tensor_copy(ext[:, :, 0:3], carry[h][:]), reads=[carry[h]], writes=[ext])
                    k.op("act", lambda e: e.copy(ext[:, :, 3:131], bank[0:64, 0:384].rearrange("p (b t) -> p b t", b=3)),
                         reads=[R[0], R[1], R[2]], writes=[ext])
                    k.op("pool", lambda e: e.tensor_copy(carry[h][:], ext[:, :, 128:131]), reads=[ext], writes=[carry[h]])
                    yield
                    for blk in range(3):
                        eng = "dve"
                        k.op(eng, lambda e: e.tensor_scalar(y[:, blk, :], ext[:, blk, 0:128], cw[:, h, blk, 0:1], None, ALU.mult),
                             reads=[ext, cw], writes=[y])
                        for tap in range(1, 4):
                            k.op(eng, lambda e: e.scalar_tensor_tensor(y[:, blk, :], ext[:, blk, tap:tap + 128], cw[:, h, blk, tap:tap + 1], y[:, blk, :],
                                                                       ALU.mult, ALU.add), reads=[ext, cw, y], writes=[y])
                    yield
                    k.op("act", lambda e: e.activation(ys[:], y[:], AF.Silu), reads=[y], writes=[ys])
                    k.op("pool", lambda e: e.tensor_tensor(w["sq"][:], ys[:, 0:2, :].rearrange("p b t -> p (b t)"), ys[:, 0:2, :].rearrange("p b t -> p (b t)"), ALU.mult),
                         reads=[ys], writes=[w["sq"]])
                    yield
                    k.op("pe", lambda e: e.matmul(R[0][0:64, :], ones64, w["sq"][:, 0:128], start=True, stop=True), reads=[c32, w["sq"]], writes=[R[0]])
                    k.op("pe", lambda e: e.matmul(R[1][0:64, :], ones64, w["sq"][:, 128:256], start=True, stop=True), reads=[c32, w["sq"]], writes=[R[1]])
                    yield
                    sub = int(os.environ.get("DEV_SUB", "9"))
                    if sub >= 1:
                        k.op("act", lambda e: e.activation(w["rs"][:], bank[0:64, 0:256], AF.Sqrt, bias=epsc[0:64, 0:1], scale=1.0),
                             reads=[R[0], R[1], epsc], writes=[w["rs"]])
                    if sub >= 2:
                        k.op("dve", lambda e: e.reciprocal(w["rs"][:], w["rs"][:]), reads=[w["rs"]], writes=[w["rs"]])
                    if sub >= 3:
                        k.op("dve", lambda e: e.scalar_tensor_tensor(w["qTf"][:], ys[:, 0, :], 0.125, w["rs"][:, 0:128], ALU.mult, ALU.mult),
                             reads=[ys, w["rs"]], writes=[w["qTf"]])
                    if sub >= 4:
                        k.op("dve", lambda e: e.tensor_tensor(w["kTf"][:], ys[:, 1, :], w["rs"][:, 128:256], ALU.mult), reads=[ys, w["rs"]], writes=[w["kTf"]])
                    yield
                    k.op("dve", lambda e: e.tensor_scalar(w["Ug"][:], Umat, gtok[:, h:h + 1], None, ALU.mult), reads=[c32, gtok], writes=[w["Ug"]])
                    k.op("pe", lambda e: e.matmul(R[3][:, :], ones32, w["Ug"][:], start=True, stop=True), reads=[c32, w["Ug"]], writes=[R[3]])
                    k.op("pe", lambda e: e.transpose(R[2][:, 0:64], w["kTf"][:], id64), reads=[w["kTf"], c32], writes=[R[2]])
                    k.op("pe", lambda e: e.transpose(R[2][:, 64:128], ys[:, 2, :], id64), reads=[ys, c32], writes=[R[2]])
                    yield
                    sub8 = int(os.environ.get("DEV_SUB8", "99"))
                    if sub8 >= 1:
                        k.op("dve", lambda e: e.scalar_tensor_tensor(w["X1"][:], R[3][:, :], -1.0, mask_incl, ALU.mult, ALU.add), reads=[R[3], c32], writes=[w["X1"]])
                    if sub8 >= 2:
                        k.op("dve", lambda e: e.tensor_tensor(w["X2"][:], R[3][:, :], maskT_incl, ALU.add), reads=[R[3], c32], writes=[w["X2"]])
                    if sub8 >= 3:
                        k.op("act", lambda e: e.activation(w["egcB"][:], R[3][0:64, :], AF.Exp), reads=[R[3]], writes=[w["egcB"]])
                    if sub8 >= 4:
                        k.op("act", lambda e: e.copy(w["glc"][:, 0:1], R[3][:, 127:128]), reads=[R[3]], writes=[w["glc"]])
                    if sub8 >= 5:
                        k.op("act", lambda e: e.activation(w["dec"][:], w["X1"][:], AF.Exp, bias=gctok[:, h:h + 1], scale=1.0), reads=[w["X1"], gctok], writes=[w["dec"]])
                    if sub8 >= 6:
                        k.op("act", lambda e: e.activation(w["decT"][:], w["X2"][:], AF.Exp, bias=ngctok[:, h:h + 1], scale=1.0), reads=[w["X2"], ngctok], writes=[w["decT"]])
                    if sub8 >= 7:
                        k.op("act", lambda e: e.activation(w["glc"][:, 1:2], w["glc"][:, 0:1], AF.Exp), reads=[w["glc"]], writes=[w["glc"]])
                    if sub8 >= 8:
                        k.op("act", lambda e: e.activation(w["glc"][:, 2:3], ngctok[:, h:h + 1], AF.Exp, bias=w["glc"][:, 0:1], scale=1.0),
                             reads=[w["glc"], ngctok], writes=[w["glc"]])
                    yield
                    k.op("dve", lambda e: e.tensor_tensor(w["qgT"][0:64, :], w["qTf"][:], w["egcB"][:], ALU.mult), reads=[w["qTf"], w["egcB"]], writes=[w["qgT"]])
                    k.op("dve", lambda e: e.tensor_scalar(w["rhsX"][:, 0:64], R[2][:, 64:128], btok[:, h:h + 1], None, ALU.mult),
                         reads=[R[2], btok], writes=[w["rhsX"]])
                    k.op("dve", lambda e: e.tensor_scalar(w["rhsX"][:, 64:128], R[2][:, 0:64], btok[:, h:h + 1], egctok[:, h:h + 1], ALU.mult, ALU.mult),
                         reads=[R[2], btok, egctok], writes=[w["rhsX"]])
                    k.op("dve", lambda e: e.tensor_scalar(w["kend"][:], R[2][:, 0:64], w["glc"][:, 2:3], None, ALU.mult), reads=[R[2], w["glc"]], writes=[w["kend"]])
                    yield
                    k.op("pe", lambda e: e.matmul(R[0][:, :], w["kTf"][:], w["kTf"][:], start=True, stop=True), reads=[w["kTf"]], writes=[R[0]])
                    k.op("pe", lambda e: e.matmul(R[1][:, :], w["kTf"][:], w["qTf"][:], start=True, stop=True), reads=[w["kTf"], w["qTf"]], writes=[R[1]])
                    yield
                    k.op("dve", lambda e: e.tensor_tensor(w["tN"][:], R[0][:, :], w["dec"][:], ALU.mult), reads=[R[0], w["dec"]], writes=[w["tN"]])
                    k.op("dve", lambda e: e.scalar_tensor_tensor(w["N"][:], w["tN"][:], btok[:, h:h + 1], strict01, ALU.mult, ALU.mult),
                         reads=[w["tN"], btok, c32], writes=[w["N"]])
                    k.op("dve", lambda e: e.tensor_tensor(w["innerT"][:], R[1][:, :], w["decT"][:], ALU.mult), reads=[R[1], w["decT"]], writes=[w["innerT"]])
                    k.op("pe", lambda e: e.transpose(R[2][:, :], f32v(w["N"][:]), ident), reads=[w["N"], c32], writes=[R[2]])
                    yield
                    k.op("act", lambda e: e.copy(w["NT"][:], R[2][:, :]), reads=[R[2]], writes=[w["NT"]])
                    k.op("dve", lambda e: e.tensor_tensor(w["W0"][:], ident, R[2][:, :], ALU.subtract), reads=[c32, R[2]], writes=[w["W0"]])
                    yield
                    P, Q, Wc = w["N"], w["NT"], w["W0"]
                    Pn_l = [w["P0"], w["P1"]]
                    Qn_l = [w["Q1"], w["NT"]]
                    Wn_l = [w["W1"], w["W0"]]
                    for kk in range(1, 7):
                        Pn = Pn_l[kk % 2]
                        k.op("pe", lambda e: e.matmul(R[0][:, :], fr(Q[:]), fr(P[:]), start=True, stop=True), reads=[Q, P], writes=[R[0]])
                        if kk <= 5:
                            Qn = (w["NT"], w["Q1"])[kk % 2]
                            k.op("pe", lambda e: e.matmul(R[1][:, :], fr(P[:]), fr(Q[:]), start=True, stop=True), reads=[Q, P], writes=[R[1]])
                        yield
                        k.op("act", lambda e: e.copy(Pn[:], R[0][:, :]), reads=[R[0]], writes=[Pn])
                        if kk <= 5:
                            k.op("dve", lambda e: e.tensor_copy(Qn[:], R[1][:, :]), reads=[R[1]], writes=[Qn])
                        yield
                        Wn = Wn_l[(kk - 1) % 2]
                        k.op("pe", lambda e: e.matmul(R[2][:, :], fr(Pn[:]), fr(Wc[:]), start=True, stop=True), reads=[Pn, Wc], writes=[R[2]])
                        yield
                        k.op("dve", lambda e: e.tensor_tensor(Wn[:], f32v(Wc[:]), R[2][:, :], ALU.add), reads=[Wc, R[2]], writes=[Wn])
                        P, Wc = Pn, Wn
                        if kk <= 5:
                            Q = Qn
                        yield
                    k.op("pe", lambda e: e.matmul(R[3][:, 0:64], f32v(Wc[:]), w["rhsX"][:, 0:64], start=True, stop=True), reads=[Wc, w["rhsX"]], writes=[R[3]])
                    k.op("pe", lambda e: e.matmul(R[0][0:64, :], w["rhsX"][:, 64:128], f32v(Wc[:]), start=True, stop=True), reads=[Wc, w["rhsX"]], writes=[R[0]])
                    yield
                    k.op("act", lambda e: e.copy(w["val"][:], R[3][:, 0:64]), reads=[R[3]], writes=[w["val"]])
                    k.op("act", lambda e: e.copy(w["kcdT"][:], R[0][0:64, :]), reads=[R[0]], writes=[w["kcdT"]])
                    yield
                    k.op("pe", lambda e: e.matmul(R[1][:, 0:64], w["kcdT"][:], S[h][0:64, :], start=True, stop=True), reads=[w["kcdT"], S[h]], writes=[R[1]])
                    yield
                    k.op("dve", lambda e: e.tensor_tensor(w["vn"][:], w["val"][:], R[1][:, 0:64], ALU.subtract), reads=[w["val"], R[1]], writes=[w["vn"]])
                    yield
                    k.op("pe", lambda e: e.matmul(R[2][:, 0:64], w["qgT"][:], S[h][:], start=True, stop=False), reads=[w["qgT"], S[h]], writes=[R[2]], inc=False)
                    k.op("pe", lambda e: e.matmul(R[2][:, 0:64], w["innerT"][:], w["vn"][:], start=False, stop=True), reads=[w["innerT"], w["vn"]], writes=[R[2]])
                    k.op("pe", lambda e: e.matmul(R[3][0:64, 0:64], w["kend"][:], w["vn"][:], start=True, stop=True), reads=[w["kend"], w["vn"]], writes=[R[3]])
                    yield
                    k.op("dve", lambda e: e.scalar_tensor_tensor(S[h][0:64, :], S[h][0:64, :], w["glc"][0:64, 1:2], R[3][0:64, 0:64], ALU.mult, ALU.add),
                         reads=[S[h], w["glc"], R[3]], writes=[S[h]])
                    k.op("pool", lambda e: e.memset(w["rstd"][:, 0:1], 0.0), writes=[w["rstd"]])
                    k.op("act", lambda e: e.activation(w["junk"][:], R[2][:, 0:64], AF.Square, accum_out=w["rstd"][:, 0:1]), reads=[R[2]], writes=[w["junk"], w["rstd"]])
                    yield
                    k.op("act", lambda e: e.activation(w["rstd"][:, 1:2], w["rstd"][:, 0:1], AF.Sqrt, bias=epsc[:, 0:1], scale=1.0 / 64), reads=[w["rstd"], epsc], writes=[w["rstd"]])
                    k.op("dve", lambda e: e.reciprocal(w["rstd"][:, 1:2], w["rstd"][:, 1:2]), reads=[w["rstd"]], writes=[w["rstd"]])
                    k.op("dve", lambda e: e.scalar_tensor_tensor(og[:, h * 64:(h + 1) * 64], R[2][:, 0:64], w["rstd"][:, 1:2], nz[:, h, :], ALU.mult, ALU.mult),
                         reads=[R[2], w["rstd"], nz], writes=[og])
                    yield

                for i in range(NJ):
                    xt = xTt[i % 2]
                    k.dma("sp", xt[:], xT_d.t.rearrange("(c p) t -> p c t", p=128)[:, :, i * 128:(i + 1) * 128], reads=[xT_d], writes=[xt])
                    for kc in range(8):
                        k.op("pe", lambda e: e.matmul(PAB[:, 0:16], xt[:, kc, :], wab[:, kc, :], start=(kc == 0), stop=(kc == 7)),
                             reads=[xt, wab], writes=[PAB], inc=(kc == 7))
                    for kc in range(8):
                        k.op("pe", lambda e: e.matmul(PZ[:, :], xt[:, kc, :], wz[:, kc, :], start=(kc == 0), stop=(kc == 7)),
                             reads=[xt, wz], writes=[PZ], inc=(kc == 7))
                    k.op("dve", lambda e: e.tensor_tensor(tmp8[:], PAB[:, 0:8], dtb[:], ALU.add), reads=[PAB, dtb], writes=[tmp8])
                    k.op("act", lambda e: e.activation(tmp8[:], tmp8[:], AF.Exp), reads=[tmp8], writes=[tmp8])
                    k.op("act", lambda e: e.activation(tmp8[:], tmp8[:], AF.Ln, bias=1.0, scale=1.0), reads=[tmp8], writes=[tmp8])
                    k.op("dve", lambda e: e.tensor_tensor(gtok[:], tmp8[:], negA[:], ALU.mult), reads=[tmp8, negA], writes=[gtok])
                    k.op("act", lambda e: e.activation(btok[:], PAB[:, 8:16], AF.Sigmoid), reads=[PAB], writes=[btok])
                    k.op("pe", lambda e: e.matmul(PAB[:, 16:24], Umat, gtok[:], start=True, stop=True), reads=[c32, gtok], writes=[PAB])
                    k.op("act", lambda e: e.copy(gctok[:], PAB[:, 16:24]), reads=[PAB], writes=[gctok])
                    k.op("dve", lambda e: e.tensor_scalar(ngctok[:], PAB[:, 16:24], -1.0, None, ALU.mult), reads=[PAB], writes=[ngctok])
                    k.op("act", lambda e: e.activation(egctok[:], gctok[:], AF.Exp), reads=[gctok], writes=[egctok])
                    k.op("act", lambda e: e.activation(nz[:].rearrange("p h d -> p (h d)"), PZ[:, :], AF.Silu), reads=[PZ], writes=[nz])
                    k.op("pool", lambda e: e.tensor_tensor(nz[:], nz[:], nw[:].unsqueeze(1).to_broadcast([128, 8, 64]), ALU.mult), reads=[nz, nw], writes=[nz])
                    for wave in range(8 // NSLOT):
                        gens = [head_chain(wave * NSLOT + s, i, s, xt) for s in range(NSLOT)]
                        alive = list(gens)
                        gsteps = int(os.environ.get("DEV_GSTEPS", "1000"))
                        nst = 0
                        while alive and nst < gsteps:
                            nst += 1
                            nxt = []
                            for gdef in alive:
                                try:
                                    next(gdef)
                                    nxt.append(gdef)
                                except StopIteration:
                                    pass
                            alive = nxt
                    for c in range(4):
                        k.op("pe", lambda e: e.transpose(PTR[:, c * 128:(c + 1) * 128], og[:, c * 128:(c + 1) * 128], ident), reads=[og, c32], writes=[PTR])
                    k.op("act", lambda e: e.copy(ogT[:], PTR[:, :].rearrange("p (c t) -> p c t", c=4)), reads=[PTR], writes=[ogT])
                    k.dma("pool", mixT_d.t[512:1024, :].rearrange("(c p) t -> p c t", p=128)[:, :, i * 128:(i + 1) * 128], ogT[:],
                          reads=[ogT], writes=[mixT_d])
                for h in range(8):
                    k.dma("pool", gst_p[l, h], S[h][0:64, :], reads=[S[h]], writes=[gst_p])
                    for blk in range(3):
                        c0 = blk * 512 + h * 64
                        k.dma("pool", gcv_p[l][:, c0:c0 + 64].rearrange("w d -> d w"), carry[h][:, blk, :], reads=[carry[h]], writes=[gcv_p],
                              allow_slow_non_contiguous=True)
            k.barrier()

        def chain(l, xres_p, xres_s, yout_p, yout_s, last):
            TS = 256
            with ExitStack() as es:
                wo = k.sb(es, "wo", [128, 8, DM], BF16)
                k.dma("sp", wo[:], wb_out.t[l].rearrange("(c p) n -> p c n", p=128), reads=[wb_out], writes=[wo])
                wgt = k.sb(es, "wgt", [128, 8, DM], BF16)
                k.dma("sp", wgt[:], wb_gate.t[l].rearrange("(c p) n -> p c n", p=128), reads=[wb_gate], writes=[wgt])
                wpj = k.sb(es, "wpj", [128, 2, DM], BF16)
                k.dma("sp", wpj[:], wb_proj.t[l].rearrange("(c p) n -> p c n", p=128), reads=[wb_proj], writes=[wpj])
                wdn = k.sb(es, "wdn", [128, 22, DM], BF16)
                k.dma("sp", wdn[:], wb_down.t[l].rearrange("(c p) n -> p c n", p=128), reads=[wb_down], writes=[wdn])
                lng = k.sb(es, "lng", [128, 3, DM])
                lnb = k.sb(es, "lnb", [128, 3, DM])
                for i in range(3):
                    k.dma("sp", lng[:, i, :], ln_g[l, i:i + 1, :].partition_broadcast(128), writes=[lng])
                    k.dma("sp", lnb[:, i, :], ln_b[l, i:i + 1, :].partition_broadcast(128), writes=[lnb])
                fcw = k.sb(es, "fcw", [128, 22, 3])
                for tap in range(3):
                    k.dma("sp", fcw[:, :, tap], fconv_w[l, tap].rearrange("(c p) -> p c", p=128), writes=[fcw], allow_slow_non_contiguous=True)
                wup = [k.sb(es, "wup%d" % i, [128, 8, 256], BF16) for i in range(3)]
                mixT = k.sb(es, "mixT", [128, 8, TS], BF16)
                x1T = k.sb(es, "x1T", [128, 8, TS], BF16)
                x2T = k.sb(es, "x2T", [128, 8, 128], BF16)
                hid = k.sb(es, "hid", [128, 22, TS], BF16)
                ext = [k.sb(es, "fext%d" % i, [128, TS + 2]) for i in range(2)]
                hg = [k.sb(es, "hg%d" % i, [128, TS]) for i in range(2)]
                carry = k.sb(es, "fcarry", [128, 22, 2])
                k.op("pool", lambda e: e.memset(carry[:], 0.0), writes=[carry])
                xres = k.sb(es, "xres", [128, DM])
                x1 = [k.sb(es, "x1_%d" % i, [128, DM]) for i in range(2)]
                x2 = k.sb(es, "x2", [128, DM])
                x3 = xres
                rr = k.sb(es, "rr", [128, DM])
                sig = k.sb(es, "sig", [128, DM])
                junk = sig
                st = k.sb(es, "lnst", [128, 8])
                ptk = k.sb(es, "ptk", [128, PLED])
                pT = k.sb(es, "pT", [128, 2, 128], BF16)
                trs = k.sb(es, "ctrs", [128, 8, 128], BF16)
                sgo = k.sb(es, "sgo", [NS, DFF])
                sbT = k.sb(es, "sbT", [128, 22, 2, NS])
                sgT = k.sb(es, "sgT", [128, 22, NS])
                PY = (pb[0], pb[1])
                PU = (pb[2], pb[3], pb[4], pb[5])
                PT_, PM = pb[6], pb[7]

                def layer_norm(src, nt, idx, dst):
                    k.op("pool", lambda e: e.memset(st[:, 0:4], 0.0), writes=[st])
                    k.op("act", lambda e: e.activation(junk[0:nt, :], src[0:nt, :], AF.Identity, accum_out=st[0:nt, 0:1]), reads=[src], writes=[junk, st])
                    k.op("dve", lambda e: e.tensor_scalar(st[0:nt, 1:2], st[0:nt, 0:1], -1.0 / DM, None, ALU.mult), reads=[st], writes=[st])
                    k.op("act", lambda e: e.activation(junk[0:nt, :], src[0:nt, :], AF.Square, bias=st[0:nt, 1:2], scale=1.0, accum_out=st[0:nt, 2:3]),
                         reads=[src, st], writes=[junk, st])
                    k.op("act", lambda e: e.activation(st[0:nt, 3:4], st[0:nt, 2:3], AF.Sqrt, bias=epsc[0:nt, 1:2], scale=1.0 / DM), reads=[st, epsc], writes=[st])
                    k.op("dve", lambda e: e.reciprocal(st[0:nt, 3:4], st[0:nt, 3:4]), reads=[st], writes=[st])
                    k.op("dve", lambda e: e.tensor_scalar(dst[0:nt, :], src[0:nt, :], st[0:nt, 1:2], st[0:nt, 3:4], ALU.add, ALU.mult), reads=[src, st], writes=[dst])
                    k.op("dve", lambda e: e.tensor_tensor(dst[0:nt, :], dst[0:nt, :], lng[0:nt, idx, :], ALU.mult), reads=[dst, lng], writes=[dst])
                    k.op("pool", lambda e: e.tensor_tensor(dst[0:nt, :], dst[0:nt, :], lnb[0:nt, idx, :], ALU.add), reads=[dst, lnb], writes=[dst])

                def to_T(src, nt, dstT, c0):
                    for half in range(2):
                        for c4 in range(4):
                            c = half * 4 + c4
                            k.op("pe", lambda e: e.transpose(PT_[:, c4 * 128:c4 * 128 + nt], src[0:nt, c * 128:(c + 1) * 128], ident[0:nt, 0:nt]),
                                 reads=[src, c32], writes=[PT_])
                        k.op("act", lambda e: e.copy(dstT[:, half * 4:(half + 1) * 4, c0:c0 + nt],
                                                     PT_[:, :].rearrange("p (c t) -> p c t", c=4)[:, :, 0:nt]), reads=[PT_], writes=[dstT])

                def supertile(kind, tok0, tiles):
                    ntot = sum(nt for _, nt in tiles)
                    if kind == "p":
                        srcT, xr_d, p_d, y_d, nxtT = mixT_d, xres_p, pp, yout_p, xT_d
                    else:
                        srcT, xr_d, p_d, y_d, nxtT = mixsT_d, xres_s, pps, yout_s, xsT_d
                    k.dma("sp", mixT[:, :, 0:ntot], srcT.t.rearrange("(c p) t -> p c t", p=128)[:, :, tok0:tok0 + ntot], reads=[srcT], writes=[mixT])
                    for ti, (o0, nt) in enumerate(tiles):
                        k.dma("sp", xres[0:nt, :], xr_d[tok0 + o0:tok0 + o0 + nt, :], reads=[xr_d], writes=[xres])
                        for hb in range(2):
                            for kc in range(8):
                                k.op("pe", lambda e: e.matmul(PY[hb][0:nt, :], mixT[:, kc, o0:o0 + nt], wo[:, kc, hb * 512:(hb + 1) * 512],
                                                              start=(kc == 0), stop=(kc == 7)), reads=[mixT, wo], writes=[PY[hb]], inc=(kc == 7))
                        for hb in range(2):
                            k.op("dve", lambda e: e.scalar_tensor_tensor(rr[0:nt, hb * 512:(hb + 1) * 512], xres[0:nt, hb * 512:(hb + 1) * 512], ALPHA,
                                                                         PY[hb][0:nt, :], ALU.mult, ALU.add), reads=[xres, PY[hb]], writes=[rr])
                        layer_norm(rr, nt, 0, x1[ti])
                        to_T(x1[ti], nt, x1T, o0)
                    if kind == "s":
                        for r in range(2):
                            k.dma("sp", sgo[:], st_fconv[l, :, r, :], writes=[sgo])
                            for c in range(22):
                                k.op("pe", lambda e: e.matmul(PM[:, (c % 16) * NS:(c % 16 + 1) * NS], sgo[0:NS, c * 128:(c + 1) * 128], ident[0:NS, 0:NS], start=True, stop=True),
                                     reads=[sgo, c32], writes=[PM])
                                if c % 16 == 15 or c == 21:
                                    cs = (c // 16) * 16
                                    k.op("act", lambda e: e.copy(sbT[:, cs:c + 1, r, :], PM[:, 0:(c - cs + 1) * NS].rearrange("p (c s) -> p c s", s=NS)),
                                         reads=[PM], writes=[sbT])
                        k.dma("pool", fcv_s[l, :, 0, :], st_fconv[l, :, 1, :], reads=[st_fconv], writes=[fcv_s])
                    for c in range(22):
                        wu = wup[c % 3]
                        wsrc = wb_up.t[l].rearrange("(kc p) n -> p kc n", p=128)
                        k.dma("sp", wu[:, :, 0:128], wsrc[:, :, c * 128:(c + 1) * 128], reads=[wb_up], writes=[wu])
                        k.dma("sp", wu[:, :, 128:256], wsrc[:, :, DFF + c * 128:DFF + (c + 1) * 128], reads=[wb_up], writes=[wu])
                        PG, PV = PU[(c % 2) * 2], PU[(c % 2) * 2 + 1]
                        for kc in range(8):
                            k.op("pe", lambda e: e.matmul(PG[:, 0:ntot], wu[:, kc, 0:128], x1T[:, kc, 0:ntot], start=(kc == 0), stop=(kc == 7)),
                                 reads=[wu, x1T], writes=[PG], inc=(kc == 7))
                        for kc in range(8):
                            k.op("pe", lambda e: e.matmul(PV[:, 0:ntot], wu[:, kc, 128:256], x1T[:, kc, 0:ntot], start=(kc == 0), stop=(kc == 7)),
                                 reads=[wu, x1T], writes=[PV], inc=(kc == 7))
                        h_ = hg[c % 2]
                        if kind == "p":
                            ex = ext[c % 2]
                            k.op("pool", lambda e: e.tensor_copy(ex[:, 0:2], carry[:, c, :]), reads=[carry], writes=[ex])
                            k.op("act", lambda e: e.copy(ex[:, 2:2 + ntot], PG[:, 0:ntot]), reads=[PG], writes=[ex])
                            k.op("pool", lambda e: e.tensor_copy(carry[:, c, :], ex[:, ntot:ntot + 2]), reads=[ex], writes=[carry])
                            k.op("dve", lambda e: e.tensor_scalar(h_[:, 0:ntot], ex[:, 0:ntot], fcw[:, c, 0:1], None, ALU.mult), reads=[ex, fcw], writes=[h_])
                            for tap in (1, 2):
                                k.op("dve", lambda e: e.scalar_tensor_tensor(h_[:, 0:ntot], ex[:, tap:tap + ntot], fcw[:, c, tap:tap + 1], h_[:, 0:ntot],
                                                                             ALU.mult, ALU.add), reads=[ex, fcw, h_], writes=[h_])
                        else:
                            k.op("act", lambda e: e.copy(sgT[:, c, :], PG[:, 0:NS]), reads=[PG], writes=[sgT])
                            k.op("dve", lambda e: e.tensor_scalar(h_[:, 0:NS], sbT[:, c, 0, :], fcw[:, c, 0:1], None, ALU.mult), reads=[sbT, fcw], writes=[h_])
                            k.op("dve", lambda e: e.scalar_tensor_tensor(h_[:, 0:NS], sbT[:, c, 1, :], fcw[:, c, 1:2], h_[:, 0:NS], ALU.mult, ALU.add),
                                 reads=[sbT, fcw, h_], writes=[h_])
                            k.op("dve", lambda e: e.scalar_tensor_tensor(h_[:, 0:NS], sgT[:, c, :], fcw[:, c, 2:3], h_[:, 0:NS], ALU.mult, ALU.add),
                                 reads=[sgT, fcw, h_], writes=[h_])
                        k.op("act", lambda e: e.activation(h_[:, 0:ntot], h_[:, 0:ntot], AF.Gelu), reads=[h_], writes=[h_])
                        k.op("dve", lambda e: e.tensor_tensor(hid[:, c, 0:ntot], h_[:, 0:ntot], PV[:, 0:ntot], ALU.mult), reads=[h_, PV], writes=[hid])
                    if kind == "s":
                        for c in range(22):
                            k.op("pe", lambda e: e.transpose(PM[0:NS, (c % 4) * 128:(c % 4 + 1) * 128], sgT[:, c, :], ident), reads=[sgT, c32], writes=[PM])
                            if c % 4 == 3 or c == 21:
                                cs = (c // 4) * 4
                                k.op("act", lambda e: e.copy(sgo[:, cs * 128:(c + 1) * 128], PM[0:NS, 0:(c - cs + 1) * 128]), reads=[PM], writes=[sgo])
                        k.dma("pool", fcv_s[l, :, 1, :], sgo[:], reads=[sgo], writes=[fcv_s])
                    for ti, (o0, nt) in enumerate(tiles):
                        for c in range(22):
                            for hb in range(2):
                                k.op("pe", lambda e: e.matmul(PY[hb][0:nt, :], hid[:, c, o0:o0 + nt], wdn[:, c, hb * 512:(hb + 1) * 512],
                                                              start=(c == 0), stop=(c == 21)), reads=[hid, wdn], writes=[PY[hb]], inc=(c == 21))
                        for hb in range(2):
                            k.op("dve", lambda e: e.scalar_tensor_tensor(rr[0:nt, hb * 512:(hb + 1) * 512], x1[ti][0:nt, hb * 512:(hb + 1) * 512], ALPHA,
                                                                         PY[hb][0:nt, :], ALU.mult, ALU.add), reads=[x1[ti], PY[hb]], writes=[rr])
                        layer_norm(rr, nt, 1, x2)
                        to_T(x2, nt, x2T, 0)
                        k.dma("sp", ptk[0:nt, :], p_d[l, tok0 + o0:tok0 + o0 + nt, :], reads=[p_d], writes=[ptk])
                        for c in range(2):
                            k.op("pe", lambda e: e.transpose(PM[:, c * 128:c * 128 + nt], ptk[0:nt, c * 128:(c + 1) * 128], ident[0:nt, 0:nt]), reads=[ptk, c32], writes=[PM])
                        k.op("act", lambda e: e.copy(pT[:, :, 0:nt], PM[:, 0:256].rearrange("p (c t) -> p c t", c=2)[:, :, 0:nt]), reads=[PM], writes=[pT])
                        for hb in range(2):
                            for kc in range(8):
                                k.op("pe", lambda e: e.matmul(PY[hb][0:nt, :], x2T[:, kc, 0:nt], wgt[:, kc, hb * 512:(hb + 1) * 512],
                                                              start=(kc == 0), stop=(kc == 7)), reads=[x2T, wgt], writes=[PY[hb]], inc=(kc == 7))
                            k.op("act", lambda e: e.activation(sig[0:nt, hb * 512:(hb + 1) * 512], PY[hb][0:nt, :], AF.Sigmoid), reads=[PY[hb]], writes=[sig])
                        for hb in range(2):
                            for c in range(2):
                                k.op("pe", lambda e: e.matmul(PY[hb][0:nt, :], pT[:, c, 0:nt], wpj[:, c, hb * 512:(hb + 1) * 512],
                                                              start=(c == 0), stop=(c == 1)), reads=[pT, wpj], writes=[PY[hb]], inc=(c == 1))
                            k.op("dve", lambda e: e.tensor_tensor(sig[0:nt, hb * 512:(hb + 1) * 512], sig[0:nt, hb * 512:(hb + 1) * 512], PY[hb][0:nt, :], ALU.mult),
                                 reads=[sig, PY[hb]], writes=[sig])
                        k.op("dve", lambda e: e.scalar_tensor_tensor(rr[0:nt, :], x2[0:nt, :], ALPHA, sig[0:nt, :], ALU.mult, ALU.add), reads=[x2, sig], writes=[rr])
                        layer_norm(rr, nt, 2, x3)
                        k.dma("pool", y_d[tok0 + o0:tok0 + o0 + nt, :], x3[0:nt, :], reads=[x3], writes=[y_d])
                        if not last:
                            to_T(x3, nt, trs, 0)
                            k.dma("pool", nxtT.t.rearrange("(c p) t -> p c t", p=128)[:, :, tok0 + o0:tok0 + o0 + nt], trs[:, :, 0:nt], reads=[trs], writes=[nxtT])

                if do_prompt:
                    for s0 in range(0, T, TS):
                        supertile("p", s0, [(o, 128) for o in range(0, TS, 128)])
                    for tap in range(2):
                        k.dma("pool", fcv_p[l, tap].rearrange("(c p) -> p c", p=128), carry[:, :, tap], reads=[carry], writes=[fcv_p],
                              allow_slow_non_contiguous=True)
                if do_sample:
                    supertile("s", 0, [(0, NS)])
            k.barrier()

        def sample_mixers(l):
            NQ = NS * 8
            with ExitStack() as es:
                cs32 = k.sb(es, "cs32", [128, NCS32])
                k.dma("sp", cs32[:], cs32_d[:], writes=[cs32])
                csb = k.sb(es, "csb", [128, NCSB], BF16)
                k.dma("sp", csb[:], csb_d[:], writes=[csb])
                o_ = 0
                alibi_s = cs32.t[:, o_:o_ + 2 * NPG * 4]
                o_ += 2 * NPG * 4
                alibi_w = cs32.t[:, o_:o_ + 2 * NWT * 4]
                o_ += 2 * NWT * 4
                alibi_c = cs32.t[:, o_:o_ + NQ]
                o_ += NQ
                GS = cs32.t[:, o_:o_ + 2 * NS]
                o_ += 2 * NS
                tkb_s = cs32.t[:, o_:o_ + 40]
                o_ += 40
                piota = cs32.t[:, o_:o_ + 2]
                NB33 = NPG * 2 + 1
                OH = csb.t[0:2 * NS, 0:2 * NS * 128]
                pool33 = csb.t[0:NPG * 4, 2 * NS * 128:2 * NS * 128 + NB33 + 1]
                B0, B1, B2, B3, B4, B5, B6, B7 = pb
                xsT = k.sb(es, "xsT", [128, 8, NS], BF16)
                k.dma("sp", xsT[:], xsT_d.t.rearrange("(c p) s -> p c s", p=128), reads=[xsT_d], writes=[xsT])
                hs = k.sb(es, "hs", [NS, INW])
                wch = [k.sb(es, "wch%d" % i, [128, 8, 512], BF16) for i in range(2)]
                wsrc = wb_in.t[l].rearrange("(c p) n -> p c n", p=128)
                for ch in range(7):
                    c0 = ch * 512
                    cw_ = min(512, INW - c0)
                    wc = wch[ch % 2]
                    k.dma("sp", wc[:, :, 0:cw_], wsrc[:, :, c0:c0 + cw_], reads=[wb_in], writes=[wc])
                    for kc in range(8):
                        k.op("pe", lambda e: e.matmul(B0[0:NS, 0:cw_], xsT[:, kc, :], wc[:, kc, 0:cw_], start=(kc == 0), stop=(kc == 7)),
                             reads=[xsT, wc], writes=[B0], inc=(kc == 7))
                    k.op("act", lambda e: e.copy(hs[:, c0:c0 + cw_], B0[0:NS, 0:cw_]), reads=[B0], writes=[hs])
                k.dma("pool", kv_s[l], hs[:, C_KV:C_WIN], reads=[hs], writes=[kv_s])
                k.dma("pool", win_s[l, :, WB - 1, :], hs[:, C_WIN:C_GATE], reads=[hs], writes=[win_s])
                for s in range(NS):
                    k.dma("pool", win_s[l, s, 0:WB - 1, :], st_win[l, s, 1:WB, :], reads=[st_win], writes=[win_s])
                k.dma("pool", gcv_s[l, :, 2, :], hs[:, C_GQKV:C_GA], reads=[hs], writes=[gcv_s])
                for r in range(2):
                    k.dma("pool", gcv_s[l, :, r, :], st_gconv[l, :, r + 1, :], reads=[st_gconv], writes=[gcv_s])
                with ExitStack() as e2:
                    gbuf = k.sb(e2, "gbuf", [NS, 3, 1536])
                    k.dma("sp", gbuf[:], st_gconv[l], writes=[gbuf])
                    cwt = k.sb(e2, "cwt", [NS, 4, 1536])
                    for tap in range(4):
                        k.dma("sp", cwt[:, tap, :], gconv_w[l, tap:tap + 1, :].partition_broadcast(NS), writes=[cwt])
                    qkv = k.sb(e2, "qkv", [NS, 1536])
                    tq = k.sb(e2, "tq", [NS, 1536])
                    k.op("dve", lambda e: e.tensor_tensor(qkv[:], cwt[:, 3, :], hs[:, C_GQKV:C_GA], ALU.mult), reads=[cwt, hs], writes=[qkv])
                    for tap in range(3):
                        k.op("dve", lambda e: e.tensor_tensor(tq[:], cwt[:, tap, :], gbuf[:, tap, :], ALU.mult), reads=[cwt, gbuf], writes=[tq])
                        k.op("dve", lambda e: e.tensor_tensor(qkv[:], qkv[:], tq[:], ALU.add), reads=[qkv, tq], writes=[qkv])
                    k.op("act", lambda e: e.activation(qkv[:], qkv[:], AF.Silu), reads=[qkv], writes=[qkv])
                    ss = k.sb(e2, "ss", [NS, 16])
                    k.op("dve", lambda e: e.tensor_tensor(tq[:, 0:1024], qkv[:, 0:1024], qkv[:, 0:1024], ALU.mult), reads=[qkv], writes=[tq])
                    k.op("dve", lambda e: e.tensor_reduce(ss[:], tq[:, 0:1024].rearrange("s (a d) -> s a d", d=64), AX.X, ALU.add), reads=[tq], writes=[ss])
                    k.op("act", lambda e: e.activation(ss[:], ss[:], AF.Sqrt, bias=epsc[0:NS, 0:1], scale=1.0), reads=[ss, epsc], writes=[ss])
                    k.op("dve", lambda e: e.reciprocal(ss[:], ss[:]), reads=[ss], writes=[ss])
                    qkn = k.sb(e2, "qkn", [NS, 16, 64])
                    k.op("dve", lambda e: e.tensor_tensor(qkn[:], qkv[:, 0:1024].rearrange("s (a d) -> s a d", d=64), ss[:].unsqueeze(2).to_broadcast([NS, 16, 64]), ALU.mult),
                         reads=[qkv, ss], writes=[qkn])
                    k.op("dve", lambda e: e.tensor_scalar(qkn[:, 0:8, :], qkn[:, 0:8, :], 0.125, None, ALU.mult), reads=[qkn], writes=[qkn])
                    dtb = k.sb(e2, "sdtb", [NS, 8])
                    k.dma("sp", dtb[:], dt_bias[l:l + 1, :].partition_broadcast(NS), writes=[dtb])
                    negA = k.sb(e2, "snegA", [NS, 8])
                    k.dma("sp", negA[:], A_log[l:l + 1, :].partition_broadcast(NS), writes=[negA])
                    k.op("act", lambda e: e.activation(negA[:], negA[:], AF.Exp), reads=[negA], writes=[negA])
                    k.op("dve", lambda e: e.tensor_scalar(negA[:], negA[:], -1.0, None, ALU.mult), reads=[negA], writes=[negA])
                    nw = k.sb(e2, "snw", [NS, 64])
                    k.dma("sp", nw[:], gnorm_w[l:l + 1, :].partition_broadcast(NS), writes=[nw])
                    gea = k.sb(e2, "gea", [NS, 8])
                    gbe = k.sb(e2, "gbe", [NS, 8])
                    k.op("dve", lambda e: e.tensor_tensor(gea[:], hs[:, C_GA:C_GB], dtb[:], ALU.add), reads=[hs, dtb], writes=[gea])
                    k.op("act", lambda e: e.activation(gea[:], gea[:], AF.Exp), reads=[gea], writes=[gea])
                    k.op("act", lambda e: e.activation(gea[:], gea[:], AF.Ln, bias=1.0, scale=1.0), reads=[gea], writes=[gea])
                    k.op("dve", lambda e: e.tensor_tensor(gea[:], gea[:], negA[:], ALU.mult), reads=[gea, negA], writes=[gea])
                    k.op("act", lambda e: e.activation(gea[:], gea[:], AF.Exp), reads=[gea], writes=[gea])
                    k.op("act", lambda e: e.activation(gbe[:], hs[:, C_GB:C_GZ], AF.Sigmoid), reads=[hs], writes=[gbe])
                    kqT = k.sb(e2, "kqT", [64, 2, NS, 8])
                    for a in range(2):
                        for h in range(8):
                            k.op("pe", lambda e: e.matmul(B7[0:64, h * NS:(h + 1) * NS], qkn[:, (1 - a) * 8 + h, :], ident[0:NS, 0:NS], start=True, stop=True), reads=[qkn, c32], writes=[B7])
                        k.op("act", lambda e: e.copy(kqT[:, a, :, :].rearrange("p s h -> p h s"), B7[0:64, 0:8 * NS].rearrange("p (h s) -> p h s", h=8)),
                             reads=[B7], writes=[kqT])
                    bd = k.sb(e2, "bd", [NS, 2, NS, 8])
                    k.op("dve", lambda e: e.tensor_tensor(bd[:, 0, :, :], gea[:].unsqueeze(1).to_broadcast([NS, NS, 8]),
                                                          ident[0:NS, 0:NS].unsqueeze(2).to_broadcast([NS, NS, 8]), ALU.mult), reads=[gea, c32], writes=[bd])
                    k.op("dve", lambda e: e.tensor_tensor(bd[:, 1, :, :], gbe[:].unsqueeze(1).to_broadcast([NS, NS, 8]),
                                                          ident[0:NS, 0:NS].unsqueeze(2).to_broadcast([NS, NS, 8]), ALU.mult), reads=[gbe, c32], writes=[bd])
                    k.op("pe", lambda e: e.matmul(B7[0:64, 0:2 * NQ], ones32[0:NS, 0:64], bd[:].rearrange("p a s h -> p (a s h)"), start=True, stop=True),
                         reads=[bd, c32], writes=[B7])
                    eab = k.sb(e2, "eab", [64, 2, NS, 8])
                    k.op("act", lambda e: e.copy(eab[:].rearrange("p a s h -> p (a s h)"), B7[0:64, 0:2 * NQ]), reads=[B7], writes=[eab])
                    bk = k.sb(e2, "bk", [64, NS, 8])
                    k.op("dve", lambda e: e.tensor_tensor(bk[:], kqT[:, 0, :, :], eab[:, 1, :, :], ALU.mult), reads=[kqT, eab], writes=[bk])
                    if os.environ.get("DEV_DBG", "") == "1" and l == 0:
                        k.dma("pool", y_p[0:64, 0:256], kqT[:].rearrange("p a s h -> p (a s h)"), reads=[kqT], writes=[y_p])
                        k.dma("pool", y_p[64:128, 0:256], eab[:].rearrange("p a s h -> p (a s h)"), reads=[eab], writes=[y_p])
                        k.dma("pool", y_p[128:192, 0:128], bk[:].rearrange("p s h -> p (s h)"), reads=[bk], writes=[y_p])
                        k.dma("pool", y_p[192:208, 0:1024], qkn[:].rearrange("p a d -> p (a d)"), reads=[qkn], writes=[y_p])
                        k.dma("pool", y_p[208:224, 0:1024], qkv[:, 0:1024], reads=[qkv], writes=[y_p])
                        k.dma("pool", y_p[224:240, 0:16], ss[:], reads=[ss], writes=[y_p])
                    S0 = k.sb(e2, "S0", [64, NQ, 64])
                    k.dma("sp", S0[:], st_gdn[l].rearrange("s h a b -> a (s h) b"), writes=[S0])
                    otok = k.sb(e2, "otok", [NS, 512])
                    k.op("pool", lambda e: e.memset(otok[:], 0.0), writes=[otok])
                    tmpS = k.sb(e2, "tmpS", [64, 8, 64])
                    t1 = k.sb(e2, "t1", [64, 8, 64])
                    bdv = k.sb(e2, "bdv", [NS, 512])
                    for s in range(NS):
                        S0s = S0[:, s * 8:(s + 1) * 8, :]
                        k.op("dve", lambda e: e.tensor_tensor(tmpS[:], S0s, kqT[:, 0, s, :].unsqueeze(2).to_broadcast([64, 8, 64]), ALU.mult),
                             reads=[S0, kqT], writes=[tmpS])
                        k.op("pe", lambda e: e.matmul(B7[0:64, :], ones32[0:64, 0:64], tmpS[:].rearrange("p h d -> p (h d)"), start=True, stop=True),
                             reads=[tmpS, c32], writes=[B7])
                        k.op("dve", lambda e: e.tensor_tensor(t1[:], B7[0:64, :].rearrange("p (h d) -> p h d", h=8), bk[:, s, :].unsqueeze(2).to_broadcast([64, 8, 64]), ALU.mult),
                             reads=[B7, bk], writes=[t1])
                        k.op("dve", lambda e: e.tensor_tensor(t1[:], S0s, t1[:], ALU.subtract), reads=[S0, t1], writes=[t1])
                        k.op("dve", lambda e: e.tensor_tensor(t1[:], t1[:], eab[:, 0, s, :].unsqueeze(2).to_broadcast([64, 8, 64]), ALU.mult), reads=[t1, eab], writes=[t1])
                        k.op("dve", lambda e: e.tensor_scalar(bdv[:], qkv[:, 1024:1536], ident[0:NS, s:s + 1], None, ALU.mult), reads=[qkv, c32], writes=[bdv])
                        k.op("pe", lambda e: e.matmul(B7[0:64, :], ones32[0:NS, 0:64], bdv[:], start=True, stop=True), reads=[bdv, c32], writes=[B7])
                        k.op("dve", lambda e: e.tensor_tensor(tmpS[:], B7[0:64, :].rearrange("p (h d) -> p h d", h=8), bk[:, s, :].unsqueeze(2).to_broadcast([64, 8, 64]), ALU.mult),
                             reads=[B7, bk], writes=[tmpS])
                        k.op("dve", lambda e: e.tensor_tensor(S0s, t1[:], tmpS[:], ALU.add), reads=[t1, tmpS], writes=[S0])
                        k.op("dve", lambda e: e.tensor_tensor(tmpS[:], S0s, kqT[:, 1, s, :].unsqueeze(2).to_broadcast([64, 8, 64]), ALU.mult),
                             reads=[S0, kqT], writes=[tmpS])
                        k.op("pe", lambda e: e.matmul(B7[0:NS, :], ones32[0:64, 0:NS], tmpS[:].rearrange("p h d -> p (h d)"), start=True, stop=True),
                             reads=[tmpS, c32], writes=[B7])
                        k.op("dve", lambda e: e.scalar_tensor_tensor(otok[:], B7[0:NS, :], ident[0:NS, s:s + 1], otok[:], ALU.mult, ALU.add),
                             reads=[B7, c32, otok], writes=[otok])
                    k.dma("pool", gst_s[l].rearrange("s h a b -> a (s h) b"), S0[:], reads=[S0], writes=[gst_s])
                    ms = k.sb(e2, "gms", [NS, 8])
                    k.op("dve", lambda e: e.tensor_tensor(tq[:, 0:512], otok[:], otok[:], ALU.mult), reads=[otok], writes=[tq])
                    k.op("dve", lambda e: e.tensor_reduce(ms[:], tq[:, 0:512].rearrange("s (h d) -> s h d", d=64), AX.X, ALU.add), reads=[tq], writes=[ms])
                    k.op("act", lambda e: e.activation(ms[:], ms[:], AF.Sqrt, bias=epsc[0:NS, 0:1], scale=1.0 / 64), reads=[ms, epsc], writes=[ms])
                    k.op("dve", lambda e: e.reciprocal(ms[:], ms[:]), reads=[ms], writes=[ms])
                    zs = k.sb(e2, "szs", [NS, 8, 64])
                    k.op("act", lambda e: e.activation(zs[:].rearrange("s h d -> s (h d)"), hs[:, C_GZ:INW], AF.Silu), reads=[hs], writes=[zs])
                    k.op("dve", lambda e: e.tensor_tensor(zs[:], zs[:], nw[:].unsqueeze(1).to_broadcast([NS, 8, 64]), ALU.mult), reads=[zs, nw], writes=[zs])
                    k.op("dve", lambda e: e.tensor_tensor(zs[:], zs[:], ms[:].unsqueeze(2).to_broadcast([NS, 8, 64]), ALU.mult), reads=[zs, ms], writes=[zs])
                    k.op("dve", lambda e: e.tensor_tensor(otok[:], otok[:], zs[:].rearrange("s h d -> s (h d)"), ALU.mult), reads=[otok, zs], writes=[otok])
                    ogT = k.sb(e2, "sogT", [128, 4, NS], BF16)
                    for c in range(4):
                        k.op("pe", lambda e: e.matmul(B7[:, c * NS:(c + 1) * NS], otok[:, c * 128:(c + 1) * 128], ident[0:NS, 0:NS], start=True, stop=True), reads=[otok, c32], writes=[B7])
                    k.op("act", lambda e: e.copy(ogT[:], B7[:, 0:4 * NS].rearrange("p (c s) -> p c s", c=4)), reads=[B7], writes=[ogT])
                    k.dma("pool", mixsT_d.t[512:1024, :].rearrange("(c p) s -> p c s", p=128), ogT[:], reads=[ogT], writes=[mixsT_d])
                with ExitStack() as e3:
                    gts = k.sb(e3, "gts", [NS, 24])
                    k.op("act", lambda e: e.activation(gts[:], hs[:, C_GATE:C_GQKV], AF.Sigmoid), reads=[hs], writes=[gts])
                    qperm = k.sb(e3, "qperm", [NS, 4, 2, 64])
                    for n in range(2):
                        k.op("dve", lambda e: e.tensor_copy(qperm[:, :, n, :], hs[:, n * 256:(n + 1) * 256].rearrange("s (g d) -> s g d", g=4)), reads=[hs], writes=[qperm])
                    qz = [k.sb(e3, "qz%d" % n, [128, NS, 4], BF16) for n in range(2)]
                    for n in range(2):
                        k.op("pool", lambda e: e.memset(qz[n][:], 0.0), writes=[qz[n]])
                    for g in range(4):
                        k.op("pe", lambda e: e.matmul(B0[:, g * NS:(g + 1) * NS], qperm[:, g, :, :].rearrange("s n d -> s (n d)"), ident[0:NS, 0:NS], start=True, stop=True), reads=[qperm, c32], writes=[B0])
                    k.op("act", lambda e: e.copy(qz[0][0:64, :, :].rearrange("p s g -> p g s"), B0[0:64, 0:4 * NS].rearrange("p (g s) -> p g s", g=4)), reads=[B0], writes=[qz[0]])
                    k.op("act", lambda e: e.copy(qz[1][64:128, :, :].rearrange("p s g -> p g s"), B0[64:128, 0:4 * NS].rearrange("p (g s) -> p g s", g=4)), reads=[B0], writes=[qz[1]])
                    nK = k.sb(e3, "nK", [128, 2, NS], BF16)
                    k.op("pe", lambda e: e.matmul(B0[:, 0:NS], hs[:, C_KV + 256:C_KV + 384], ident[0:NS, 0:NS], start=True, stop=True), reads=[hs, c32], writes=[B0])
                    k.op("pe", lambda e: e.matmul(B0[:, NS:2 * NS], hs[:, C_WIN:C_WIN + 128], ident[0:NS, 0:NS], start=True, stop=True), reads=[hs, c32], writes=[B0])
                    k.op("act", lambda e: e.copy(nK[:].rearrange("p a s -> p (a s)"), B0[:, 0:2 * NS]), reads=[B0], writes=[nK])
                    pti = k.sb(e3, "pti", [128, NS * NPG], I32)
                    k.dma("sp", pti[:], ptab[:].partition_broadcast(128), writes=[pti])
                    ptf = k.sb(e3, "ptf", [128, NS * NPG])
                    k.op("dve", lambda e: e.tensor_copy(ptf[:], pti[:]), reads=[pti], writes=[ptf])
                    k.op("dve", lambda e: e.tensor_scalar(ptf[:], ptf[:], 256.0, piota[:, 0:1], ALU.mult, ALU.add), reads=[ptf, cs32], writes=[ptf])
                    k.op("dve", lambda e: e.tensor_scalar(ptf[:], ptf[:], float(2 * l * NPOOL * 128), None, ALU.add), reads=[ptf], writes=[ptf])
                    idxc = k.sb(e3, "idxc", [128, NS * NPG], I32)
                    idxs = k.sb(e3, "idxs", [128, NS * NPG], I32)
                    k.op("dve", lambda e: e.tensor_copy(idxc[:], ptf[:]), reads=[ptf], writes=[idxc])
                    k.op("dve", lambda e: e.tensor_scalar(ptf[:], ptf[:], 1.0, None, ALU.add), reads=[ptf], writes=[ptf])
                    k.op("dve", lambda e: e.tensor_copy(idxs[:], ptf[:]), reads=[ptf], writes=[idxs])
                    pool2v = pool.t.rearrange("r (two c) -> (r two) c", two=2)
                    phis = k.sb(e3, "sphis", [128, 2, 128])
                    k.op("pool", lambda e: e.memset(phis[:], 0.0), writes=[phis])
                    for a in range(2):
                        for n in range(2):
                            k.dma("sp", phis[64 * n:64 * n + 64, a, 64 * n:64 * n + 64], nsa_phi[l, a], writes=[phis])
                    phib = k.sb(e3, "sphib", [128, 2, 128], BF16)
                    k.op("dve", lambda e: e.tensor_copy(phib[:], phis[:]), reads=[phis], writes=[phib])
                    pet = k.sb(e3, "spet", [128, 2, 32])
                    for a in range(2):
                        for n in range(2):
                            k.dma("sp", pet[64 * n:64 * n + 64, a, :], nsa_pe[l, a].rearrange("r d -> d r"), writes=[pet], allow_slow_non_contiguous=True)
                    pem = k.sb(e3, "spem", [128, 2])
                    k.op("dve", lambda e: e.tensor_reduce(pem[:], pet[:], AX.X, ALU.add), reads=[pet], writes=[pem])
                    k.op("dve", lambda e: e.tensor_scalar(pem[:], pem[:], 1.0 / 32, None, ALU.mult), reads=[pem], writes=[pem])
                    NCB = NPG * 4
                    kcT_all = k.sb(e3, "kcT_all", [128, NS, NCB], BF16)
                    vc_all = k.sb(e3, "vc_all", [NCB, NS, 2, 68], BF16)
                    k.op("pool", lambda e: e.memset(vc_all[:], 1.0), writes=[vc_all])
                    pgt = [k.sb(e3, "pgt%d" % i, [128, 256]) for i in range(3)]
                    cmpkv = [k.sb(e3, "scmpkv%d" % i, [128, 256], BF16) for i in range(2)]
                    kvm = k.sb(e3, "kvm", [128, 2, NCB], BF16)
                    it = 0
                    MG = os.environ.get("DEV_MG", "0") == "1"
                    pgm = [k.sb(e3, "pgm%d" % i, [128, NPG, 256]) for i in range(2)] if MG else None
                    for s in range(NS):
                        if MG:
                            pm_ = pgm[s % 2]
                            k.gather(pm_[:], pool2v, idxc[:, s * NPG:(s + 1) * NPG], reads=[idxc], writes=[pm_])
                        for pg in range(NPG):
                            cb_ = cmpkv[it % 2]
                            if MG:
                                src_, srcb_ = pm_[:, pg, :], pm_
                            else:
                                pt_ = pgt[it % 3]
                                k.gather(pt_[:], pool2v, idxc[:, s * NPG + pg:s * NPG + pg + 1], reads=[idxc], writes=[pt_])
                                src_, srcb_ = pt_[:], pt_
                            it += 1
                            cast(cb_[:], src_, [srcb_], [cb_])
                            for a in range(2):
                                k.op("pe", lambda e: e.matmul(B1[:, a * NCB + pg * 4:a * NCB + pg * 4 + 4], cb_[:, a * 128:(a + 1) * 128], avg4, start=True, stop=True),
                                     reads=[cb_, cb128], writes=[B1])
                        for a in range(2):
                            k.op("dve", lambda e: e.tensor_scalar(kvm[:, a, :], B1[:, a * NCB:(a + 1) * NCB], pem[:, a:a + 1], None, ALU.add), reads=[B1, pem], writes=[kvm])
                        k.op("pe", lambda e: e.matmul(B2[:, 0:NCB], phib[:, 0, :], kvm[:, 0, :], start=True, stop=True), reads=[phib, kvm], writes=[B2])
                        k.op("act", lambda e: e.copy(kcT_all[:, s, :], B2[:, 0:NCB]), reads=[B2], writes=[kcT_all])
                        k.op("pe", lambda e: e.matmul(B2[0:NCB, 128:256], kvm[:, 1, :], phib[:, 1, :], start=True, stop=True), reads=[phib, kvm], writes=[B2])
                        k.op("act", lambda e: e.copy(vc_all[:, s, :, 0:64], B2[0:NCB, 128:256].rearrange("p (n d) -> p n d", n=2)), reads=[B2], writes=[vc_all])
                        for n in range(2):
                            k.op("pe", lambda e: e.matmul(B3[0:NCB, (s * 2 + n) * 4:(s * 2 + n) * 4 + 4], kcT_all[:, s, :], qz[n][:, s, :], start=True, stop=True),
                                 reads=[kcT_all, qz[n]], writes=[B3])
                    tmpc = k.sb(e3, "tmpc", [NCB, NQ])
                    PTc = k.sb(e3, "sPTc", [NCB, NQ], BF16)
                    k.op("dve", lambda e: e.scalar_tensor_tensor(tmpc[:], B3[0:NCB, 0:NQ], 0.125, alibi_c[0:NCB, :], ALU.mult, ALU.add), reads=[B3, cs32], writes=[tmpc])
                    k.op("act", lambda e: e.activation(PTc[:], tmpc[:], AF.Exp), reads=[tmpc], writes=[PTc])
                    k.op("pe", lambda e: e.matmul(B3[:, 256:256 + NB33 + 1], PTc[:], pool33, start=True, stop=True), reads=[PTc, csb], writes=[B3])
                    ul = k.sb(e3, "ul", [128, NB33 + 1])
                    k.op("act", lambda e: e.copy(ul[:], B3[:, 256:256 + NB33 + 1]), reads=[B3], writes=[ul])
                    k.op("dve", lambda e: e.tensor_scalar(ul[:, NB33:NB33 + 1], ul[:, NB33:NB33 + 1], 1e-30, None, ALU.max), reads=[ul], writes=[ul])
                    k.op("dve", lambda e: e.reciprocal(ul[:, NB33:NB33 + 1], ul[:, NB33:NB33 + 1]), reads=[ul], writes=[ul])
                    k.op("dve", lambda e: e.tensor_scalar(ul[:, 0:NB33], ul[:, 0:NB33], ul[:, NB33:NB33 + 1], None, ALU.mult), reads=[ul], writes=[ul])
                    k.op("pe", lambda e: e.matmul(B3[0:2 * NS, 320:320 + NB33], GS, ul[:, 0:NB33], start=True, stop=True), reads=[ul, cs32], writes=[B3])
                    sc = k.sb(e3, "ssc", [2 * NS, 40])
                    k.op("dve", lambda e: e.tensor_tensor(sc[:, 0:NB33], B3[0:2 * NS, 320:320 + NB33], tkb_s[0:2 * NS, 0:NB33], ALU.add), reads=[B3, cs32], writes=[sc])
                    m8 = k.sb(e3, "sm8", [2 * NS, 16])
                    tkw = k.sb(e3, "stkw", [2 * NS, NB33])
                    k.op("dve", lambda e: e.max(out=m8[:, 0:8], in_=sc[:, 0:NB33]), reads=[sc], writes=[m8])
                    k.op("dve", lambda e: e.match_replace(out=tkw[:], in_to_replace=m8[:, 0:8], in_values=sc[:, 0:NB33], imm_value=-1e9), reads=[sc, m8], writes=[tkw])
                    k.op("dve", lambda e: e.max(out=m8[:, 8:16], in_=tkw[:]), reads=[tkw], writes=[m8])
                    k.op("dve", lambda e: e.tensor_scalar(sc[:, 0:NB33], sc[:, 0:NB33], m8[:, 15:16], None, ALU.is_ge), reads=[sc, m8], writes=[sc])
                    k.op("dve", lambda e: e.tensor_scalar(sc[:, 0:NB33], sc[:, 0:NB33], -1.0, -NEG, ALU.add, ALU.mult), reads=[sc], writes=[sc])
                    selb16 = k.sb(e3, "selb16", [2 * NS, 40], BF16)
                    k.op("dve", lambda e: e.tensor_copy(selb16[:, 0:NB33], sc[:, 0:NB33]), reads=[sc], writes=[selb16])
                    for s in range(NS):
                        for n in range(2):
                            c0 = (s * 2 + n) * 4
                            k.op("pe", lambda e: e.matmul(B6[0:65, c0:c0 + 4], vc_all[:, s, n, 0:65], PTc[:, c0:c0 + 4], start=True, stop=True),
                                 reads=[vc_all, PTc], writes=[B6])
                    KT_s = [k.sb(e3, "KT_s%d" % i, [128, (NPG + 1) * 128], BF16) for i in range(2)]
                    Vs_s = [k.sb(e3, "Vs_s%d" % i, [128, NPG + 1, 2, 68], BF16) for i in range(2)]
                    KTw_s = [k.sb(e3, "KTw_s%d" % i, [128, (NWT + 1) * 128], BF16) for i in range(2)]
                    Vw_s = [k.sb(e3, "Vw_s%d" % i, [128, NWT + 1, 2, 68], BF16) for i in range(2)]
                    for i in range(2):
                        k.op("pool", lambda e: e.memset(Vs_s[i][:], 1.0), writes=[Vs_s[i]])
                        k.op("pool", lambda e: e.memset(Vw_s[i][:], 1.0), writes=[Vw_s[i]])
                        k.op("pool", lambda e: e.memset(Vs_s[i][:, NPG, :, :], 0.0), writes=[Vs_s[i]])
                        k.op("pool", lambda e: e.memset(Vw_s[i][:, NWT, :, :], 0.0), writes=[Vw_s[i]])
                    wt = [k.sb(e3, "wt%d" % i, [128, NWT, 256]) for i in range(2)]
                    vstg = k.sb(e3, "vstg", [1, 2, 128])
                    ones1 = k.sb(e3, "ones1", [1, 2, 2])
                    k.op("pool", lambda e: e.memset(ones1[:], 1.0), writes=[ones1])
                    selm = k.sb(e3, "selm", [128, NPG])
                    tmps = k.sb(e3, "tmps", [128, NPG + 1, 4])
                    PTs = k.sb(e3, "sPTs", [128, NPG + 1, 4], BF16)
                    PTw = k.sb(e3, "sPTw", [128, NWT + 1, 4], BF16)
                    k.op("pool", lambda e: e.memset(PTs[:], 0.0), writes=[PTs])
                    k.op("pool", lambda e: e.memset(PTw[:], 0.0), writes=[PTw])
                    for s in range(NS):
                        KT, V, KTw, Vw, wts = KT_s[s % 2], Vs_s[s % 2], KTw_s[s % 2], Vw_s[s % 2], wt[s % 2]
                        for pg in range(NPG):
                            pt_ = pgt[it % 3]
                            it += 1
                            k.gather(pt_[:], pool2v, idxs[:, s * NPG + pg:s * NPG + pg + 1], reads=[idxs], writes=[pt_])
                            k.op("pe", lambda e: e.matmul(B2[:, 0:128], pt_[:, 0:128], ident, start=True, stop=True), reads=[pt_, c32], writes=[B2])
                            cast(KT[:, pg * 128:(pg + 1) * 128], B2[:, 0:128], [B2], [KT], psum=True)
                            k.op("pool", lambda e: e.tensor_copy(V[:, pg, :, 0:64], pt_[:, 128:256].rearrange("p (n d) -> p n d", n=2)), reads=[pt_], writes=[V])
                        k.dma("sp", wts[:], st_win[l, s].rearrange("(t p) c -> p t c", p=128), writes=[wts])
                        for t in range(NWT):
                            k.op("pe", lambda e: e.matmul(B2[:, 128:256], wts[:, t, 0:128], ident, start=True, stop=True), reads=[wts, c32], writes=[B2])
                            cast(KTw[:, t * 128:(t + 1) * 128], B2[:, 128:256], [B2], [KTw], psum=True)
                            k.op("pool", lambda e: e.tensor_copy(Vw[:, t, :, 0:64], wts[:, t, 128:256].rearrange("p (n d) -> p n d", n=2)), reads=[wts], writes=[Vw])
                        k.op("dve", lambda e: e.tensor_copy(KT[:, NPG * 128:NPG * 128 + 1], nK[:, 0, s:s + 1]), reads=[nK], writes=[KT])
                        k.op("dve", lambda e: e.tensor_copy(KTw[:, NWT * 128:NWT * 128 + 1], nK[:, 1, s:s + 1]), reads=[nK], writes=[KTw])
                        k.dma("sp", vstg[0:1, 0, :], hs[s:s + 1, C_KV + 384:C_KV + 512], reads=[hs], writes=[vstg])
                        k.dma("sp", vstg[0:1, 1, :], hs[s:s + 1, C_WIN + 128:C_WIN + 256], reads=[hs], writes=[vstg])
                        k.op("dve", lambda e: e.tensor_copy(V[0:1, NPG, :, 0:64], vstg[0:1, 0, :].rearrange("p (n d) -> p n d", n=2)), reads=[vstg], writes=[V])
                        k.op("dve", lambda e: e.tensor_copy(Vw[0:1, NWT, :, 0:64], vstg[0:1, 1, :].rearrange("p (n d) -> p n d", n=2)), reads=[vstg], writes=[Vw])
                        k.op("dve", lambda e: e.tensor_copy(V[0:1, NPG, :, 64:65], ones1[0:1, :, 0:1]), reads=[ones1], writes=[V])
                        k.op("dve", lambda e: e.tensor_copy(Vw[0:1, NWT, :, 64:65], ones1[0:1, :, 0:1]), reads=[ones1], writes=[Vw])
                        for n in range(2):
                            r = s * 2 + n
                            c0 = r * 4
                            k.op("pe", lambda e: e.matmul(B2[:, 256:256 + NB33], OH[:, r * 128:(r + 1) * 128], selb16[:, 0:NB33], start=True, stop=True),
                                 reads=[csb, selb16], writes=[B2])
                            k.op("dve", lambda e: e.tensor_copy(selm[0:64, :], B2[0:64, 256:256 + 2 * NPG].rearrange("p (t two) -> p t two", two=2)[:, :, 0]), reads=[B2], writes=[selm])
                            k.op("dve", lambda e: e.tensor_copy(selm[64:128, :], B2[64:128, 256:256 + 2 * NPG].rearrange("p (t two) -> p t two", two=2)[:, :, 1]), reads=[B2], writes=[selm])
                            for (BS, K_, nt_, PTx, Vx, ali, col6) in ((B4, KT, NPG, PTs, V, alibi_s, 1), (B5, KTw, NWT, PTw, Vw, alibi_w, 2)):
                                for t in range(nt_):
                                    k.op("pe", lambda e: e.matmul(BS[:, t * 4:(t + 1) * 4], K_[:, t * 128:(t + 1) * 128], qz[n][:, s, :], start=True, stop=True),
                                         reads=[K_, qz[n]], writes=[BS])
                                k.op("pe", lambda e: e.matmul(BS[0:1, 128:132], K_[:, nt_ * 128:nt_ * 128 + 1], qz[n][:, s, :], start=True, stop=True),
                                     reads=[K_, qz[n]], writes=[BS])
                                tv = tmps[:, 0:nt_, :]
                                k.op("dve", lambda e: e.scalar_tensor_tensor(tv, BS[:, 0:nt_ * 4].rearrange("p (t g) -> p t g", g=4), 0.125,
                                                                             ali[:, n * nt_ * 4:(n + 1) * nt_ * 4].rearrange("p (t g) -> p t g", g=4), ALU.mult, ALU.add),
                                     reads=[BS, cs32], writes=[tmps])
                                if col6 == 1:
                                    k.op("dve", lambda e: e.tensor_tensor(tv, tv, selm[:].unsqueeze(2).to_broadcast([128, NPG, 4]), ALU.add), reads=[tmps, selm], writes=[tmps])
                                k.op("act", lambda e: e.activation(PTx[:, 0:nt_, :], tv, AF.Exp), reads=[tmps], writes=[PTx])
                                k.op("act", lambda e: e.activation(PTx[0:1, nt_, :], BS[0:1, 128:132], AF.Exp, scale=0.125), reads=[BS], writes=[PTx])
                                oc = col6 * NQ + c0
                                for t in range(nt_ + 1):
                                    k.op("pe", lambda e: e.matmul(B6[0:65, oc:oc + 4], Vx[:, t, n, 0:65], PTx[:, t, :], start=(t == 0), stop=(t == nt_)),
                                         reads=[Vx, PTx], writes=[B6], inc=(t == nt_))
                    ot = k.sb(e3, "ot", [65, 3, NQ])
                    k.op("act", lambda e: e.copy(ot[:].rearrange("p a q -> p (a q)"), B6[0:65, 0:3 * NQ]), reads=[B6], writes=[ot])
                    k.op("dve", lambda e: e.tensor_scalar(ot[64:65, :, :], ot[64:65, :, :], 1e-30, None, ALU.max), reads=[ot], writes=[ot])
                    k.op("dve", lambda e: e.reciprocal(ot[64:65, :, :], ot[64:65, :, :]), reads=[ot], writes=[ot])
                    k.op("pe", lambda e: e.matmul(B0[0:64, 0:3 * NQ], ones32[64:65, 0:64], ot[64:65, :, :].rearrange("p a q -> p (a q)"), start=True, stop=True),
                         reads=[ot, c32], writes=[B0])
                    k.op("dve", lambda e: e.tensor_tensor(ot[0:64, :, :], ot[0:64, :, :], B0[0:64, 0:3 * NQ].rearrange("p (a q) -> p a q", a=3), ALU.mult), reads=[ot, B0], writes=[ot])
                    gbd = k.sb(e3, "gbd", [NS, 3, NS, 8])
                    for br in range(3):
                        k.op("dve", lambda e: e.tensor_tensor(gbd[:, br, :, :], gts[:].rearrange("s (h b) -> s h b", b=3)[:, :, br].unsqueeze(1).to_broadcast([NS, NS, 8]),
                                                              ident[0:NS, 0:NS].unsqueeze(2).to_broadcast([NS, NS, 8]), ALU.mult), reads=[gts, c32], writes=[gbd])
                    k.op("pe", lambda e: e.matmul(B0[0:64, 0:3 * NQ], ones32[0:NS, 0:64], gbd[:].rearrange("p a s h -> p (a s h)"), start=True, stop=True),
                         reads=[gbd, c32], writes=[B0])
                    k.op("dve", lambda e: e.tensor_tensor(ot[0:64, :, :], ot[0:64, :, :], B0[0:64, 0:3 * NQ].rearrange("p (a q) -> p a q", a=3), ALU.mult), reads=[ot, B0], writes=[ot])
                    k.op("dve", lambda e: e.tensor_tensor(ot[0:64, 0, :], ot[0:64, 0, :], ot[0:64, 1, :], ALU.add), reads=[ot], writes=[ot])
                    k.op("dve", lambda e: e.tensor_tensor(ot[0:64, 0, :], ot[0:64, 0, :], ot[0:64, 2, :], ALU.add), reads=[ot], writes=[ot])
                    onb = k.sb(e3, "onb", [64, 8, NS], BF16)
                    k.op("dve", lambda e: e.tensor_copy(onb[:], ot[0:64, 0, :].rearrange("p (s h) -> p h s", h=8)), reads=[ot], writes=[onb])
                    k.dma("pool", mixsT_d.t[0:512, :].rearrange("(h d) s -> d h s", d=64), onb[:], reads=[onb], writes=[mixsT_d])
            k.barrier()

        STOP = os.environ.get("DEV_STOP", "")
        for l in range(DEPTH):
            if STOP in ("W", "X0"):
                break
            xres_p = xp if l == 0 else x1_d
            xres_s = xs if l == 0 else xs1_d
            yout_p = x1_d if l == 0 else y_p
            yout_s = xs1_d if l == 0 else y_s
            last = (l == DEPTH - 1)
            if do_prompt:
                with ExitStack() as es:
                    cbp = k.sb(es, "cbp", [128, NCBP], BF16)
                    k.dma("sp", cbp[:], cbp_d[:], writes=[cbp])
                    cb64 = cb3 = cb8 = cbp
                    wq = k.sb(es, "wq", [128, 8, 512], BF16)
                    wkT = k.sb(es, "wkT", [128, 8, 256], BF16)
                    wtok = k.sb(es, "wtok", [128, 8, 792], BF16)
                    wsrc = wb_in.t[l].rearrange("(c p) n -> p c n", p=128)
                    for kc in range(8):
                        for n in range(2):
                            k.dma("sp", wq[:, kc, :].rearrange("p (c n d) -> p c n d", c=4, n=2)[:, :, n, :],
                                  wb_in.t[l, kc * 128:(kc + 1) * 128, n * 256:(n + 1) * 256].rearrange("p (c d) -> p c d", c=4),
                                  reads=[wb_in], writes=[wq])
                    k.dma("sp", wkT[:, :, 0:128], wsrc[:, :, C_KV + 256:C_KV + 384], reads=[wb_in], writes=[wkT])
                    k.dma("sp", wkT[:, :, 128:256], wsrc[:, :, C_WIN:C_WIN + 128], reads=[wb_in], writes=[wkT])
                    k.dma("sp", wtok[:], wsrc[:, :, C_KV:C_KV + 792], reads=[wb_in], writes=[wtok])
                    phis = k.sb(es, "phis", [128, 2, 128])
                    k.op("pool", lambda e: e.memset(phis[:], 0.0), writes=[phis])
                    for a in range(2):
                        for n in range(2):
                            k.dma("sp", phis[64 * n:64 * n + 64, a, 64 * n:64 * n + 64], nsa_phi[l, a], writes=[phis])
                    phib = k.sb(es, "phib", [128, 2, 128], BF16)
                    k.op("dve", lambda e: e.tensor_copy(phib[:], phis[:]), reads=[phis], writes=[phib])
                    pet = k.sb(es, "pet", [128, 2, 32])
                    for a in range(2):
                        for n in range(2):
                            k.dma("sp", pet[64 * n:64 * n + 64, a, :], nsa_pe[l, a].rearrange("r d -> d r"), writes=[pet],
                                  allow_slow_non_contiguous=True)
                    pem = k.sb(es, "pem", [128, 2])
                    k.op("dve", lambda e: e.tensor_reduce(pem[:], pet[:], AX.X, ALU.add), reads=[pet], writes=[pem])
                    k.op("dve", lambda e: e.tensor_scalar(pem[:], pem[:], 1.0 / 32, None, ALU.mult), reads=[pem], writes=[pem])
                    KTs = k.sb(es, "KTs", [128, T], BF16)
                    KTw = k.sb(es, "KTw", [128, T], BF16)
                    Vs = k.sb(es, "Vs", [128, NJ, 2, 68], BF16)
                    Vw = k.sb(es, "Vw", [128, NJ, 2, 68], BF16)
                    k.op("pool", lambda e: e.memset(Vs[:], 1.0), writes=[Vs])
                    k.op("pool", lambda e: e.memset(Vw[:], 1.0), writes=[Vw])
                    kcmT = k.sb(es, "kcmT", [128, 128], BF16)
                    vcmT = k.sb(es, "vcmT", [128, 128], BF16)
                    kcT = k.sb(es, "kcT", [128, 128], BF16)
                    k.op("pool", lambda e: e.memset(kcmT[:], 0.0), writes=[kcmT])
                    k.op("pool", lambda e: e.memset(vcmT[:], 0.0), writes=[vcmT])
                    k.op("pool", lambda e: e.memset(kcT[:], 0.0), writes=[kcT])
                    cmprhs = k.sb(es, "cmprhs", [128, 2, 68], BF16)
                    k.op("pool", lambda e: e.memset(cmprhs[:], 1.0), writes=[cmprhs])
                    PT0 = k.sb(es, "PT0", [128, NJ, 512], BF16)
                    PT = [PT0, PT0]
                    PTc = k.sb(es, "PTc", [128, 512], BF16)
                    xTt = [k.sb(es, "xTt%d" % i, [128, 8, 128], BF16) for i in range(2)]
                    kvf = [k.sb(es, "kvf%d" % i, [128, 792]) for i in range(2)]
                    cmpkv = k.sb(es, "cmpkv", [128, 256], BF16)
                    gates = k.sb(es, "gates", [128, 24])
                    qTz = [k.sb(es, "qTz%d" % n, [128, 4, 128], BF16) for n in range(2)]
                    for n in range(2):
                        k.op("pool", lambda e: e.memset(qTz[n][:], 0.0), writes=[qTz[n]])
                    selbT = [k.sb(es, "selbT%d" % n, [128, 4, 128], BF16) for n in range(2)]
                    for n in range(2):
                        k.op("pool", lambda e: e.memset(selbT[n][:], 0.0), writes=[selbT[n]])
                    sm = k.sb(es, "sm", [128, 64])
                    imp = k.sb(es, "imp", [128, 64])
                    tkw = k.sb(es, "tkw", [128, 64])
                    m8 = k.sb(es, "m8", [128, 16])
                    selb = k.sb(es, "selb", [128, 64])
                    ocmp = k.sb(es, "ocmp", [128, 8, 65])
                    rl = k.sb(es, "rl", [128, 8, 3])
                    fgt = k.sb(es, "fgt", [128, 8, 3])
                    onsa = k.sb(es, "onsa", [128, 512])
                    onT = k.sb(es, "onT", [128, 4, 128], BF16)
                    o_ = 0
                    kaux = cbp.t[:, o_:o_ + NJ * 128]
                    o_ += NJ * 128
                    kcaux = cbp.t[:, o_:o_ + NJ * 128]
                    o_ += NJ * 128
                    qaux = cbp.t[:, o_:o_ + 1024]
                    o_ += 1024
                    cmpsel = cbp.t[:, o_:o_ + NJ * 128]
                    o_ += NJ * 128
                    vispat = cbp.t[:, o_:o_ + 512]
                    o_ += 512
                    Epad = cbp.t[:, o_:o_ + T]
                    SC = (pb[0], pb[1])
                    ACC = {("s", 0): pb[2], ("s", 1): pb[3], ("w", 0): pb[4], ("w", 1): pb[5]}
                    MA, MB = pb[6], pb[7]
                    sc_i = [0]

                    def scores(n, lhs_aux, lhsK, mask, out_pt):
                        S = SC[sc_i[0] % 2]
                        sc_i[0] += 1
                        k.op("pe", lambda e: e.matmul(S[:, :], lhs_aux, qaux[:, n * 512:(n + 1) * 512],
                                                      start=True, stop=False), reads=[cb3], writes=[S], inc=False)
                        if mask is not None:
                            ml, mr, mrd = mask
                            k.op("pe", lambda e: e.matmul(S[:, :], ml, mr, start=False, stop=False), reads=mrd, writes=[S], inc=False)
                        k.op("pe", lambda e: e.matmul(S[:, :], lhsK, qTz[n][:].rearrange("p g q -> p (g q)"), start=False, stop=True),
                             reads=[qTz[n], KTs, KTw, kcT], writes=[S])
                        k.op("act", lambda e: e.activation(out_pt, S[:, :], AF.Exp, scale=0.125), reads=[S], writes=[PT[n], PTc])

                    for j in range(NJ):
                        xt = xTt[j % 2]
                        kv = kvf[j % 2]
                        k.dma("sp", xt[:], xT_d.t.rearrange("(c p) t -> p c t", p=128)[:, :, j * 128:(j + 1) * 128],
                              reads=[xT_d], writes=[xt])
                        for kc in range(8):
                            k.op("pe", lambda e, kc=kc: e.matmul(MA[:, :], xt[:, kc, :], wtok[:, kc, 0:512], start=(kc == 0), stop=(kc == 7)),
                                 reads=[xt, wtok], writes=[MA], inc=(kc == 7))
                        for kc in range(8):
                            k.op("pe", lambda e, kc=kc: e.matmul(MB[:, 0:280], xt[:, kc, :], wtok[:, kc, 512:792], start=(kc == 0), stop=(kc == 7)),
                                 reads=[xt, wtok], writes=[MB], inc=(kc == 7))
                        k.op("act", lambda e: e.copy(kv[:, 0:512], MA[:, :]), reads=[MA], writes=[kv])
                        k.op("dve", lambda e: e.tensor_copy(kv[:, 512:792], MB[:, 0:280]), reads=[MB], writes=[kv])
                        k.dma("pool", kv_p[l, j * 128:(j + 1) * 128, :], kv[:, 0:512], reads=[kv], writes=[kv_p])
                        if j >= NJ - NWT:
                            jj = j - (NJ - NWT)
                            k.dma("pool", win_p[l, jj * 128:(jj + 1) * 128, :], kv[:, 512:768], reads=[kv], writes=[win_p])
                        k.op("act", lambda e: e.activation(gates[:], kv[:, 768:792], AF.Sigmoid), reads=[kv], writes=[gates])
                        k.op("dve", lambda e: e.tensor_copy(Vs[:, j, :, 0:64], kv[:, 384:512].rearrange("p (n d) -> p n d", n=2)),
                             reads=[kv], writes=[Vs])
                        k.op("pool", lambda e: e.tensor_copy(Vw[:, j, :, 0:64], kv[:, 640:768].rearrange("p (n d) -> p n d", n=2)),
                             reads=[kv], writes=[Vw])
                        k.op("dve", lambda e: e.tensor_copy(cmpkv[:], kv[:, 0:256]), reads=[kv], writes=[cmpkv])
                        k.op("pe", lambda e: e.matmul(MA[:, 0:4], cmpkv[:, 0:128], avg4, start=True, stop=True), reads=[cmpkv, cb128], writes=[MA])
                        k.op("pe", lambda e: e.matmul(MA[:, 4:8], cmpkv[:, 128:256], avg4, start=True, stop=True), reads=[cmpkv, cb128], writes=[MA])
                        k.op("dve", lambda e: e.tensor_scalar(kcmT[:, 4 * j:4 * j + 4], MA[:, 0:4], pem[:, 0:1], None, ALU.add),
                             reads=[MA, pem], writes=[kcmT])
                        k.op("dve", lambda e: e.tensor_scalar(vcmT[:, 4 * j:4 * j + 4], MA[:, 4:8], pem[:, 1:2], None, ALU.add),
                             reads=[MA, pem], writes=[vcmT])
                        k.op("pe", lambda e: e.matmul(MB[:, 0:4], phib[:, 0, :], kcmT[:, 4 * j:4 * j + 4], start=True, stop=True),
                             reads=[phib, kcmT], writes=[MB])
                        k.op("dve", lambda e: e.tensor_copy(kcT[:, 4 * j:4 * j + 4], MB[:, 0:4]), reads=[MB], writes=[kcT])
                        k.op("pe", lambda e: e.matmul(MB[:, 128:256], vcmT[:, :], phib[:, 1, :], start=True, stop=True),
                             reads=[phib, vcmT], writes=[MB])
                        k.op("dve", lambda e: e.tensor_copy(cmprhs[:, :, 0:64], MB[:, 128:256].rearrange("p (n d) -> p n d", n=2)),
                             reads=[MB], writes=[cmprhs])
                        for c in range(4):
                            for kc in range(8):
                                k.op("pe", lambda e, c=c, kc=kc: e.matmul(MA[:, c * 128:(c + 1) * 128], wq[:, kc, c * 128:(c + 1) * 128], xt[:, kc, :],
                                                                       start=(kc == 0), stop=(kc == 7)),
                                     reads=[xt, wq], writes=[MA], inc=(kc == 7))
                        k.op("act", lambda e: e.copy(qTz[0][0:64, :, :], MA[0:64, :].rearrange("p (c t) -> p c t", c=4)), reads=[MA], writes=[qTz[0]])
                        k.op("dve", lambda e: e.tensor_copy(qTz[1][64:128, :, :], MA[64:128, :].rearrange("p (c t) -> p c t", c=4)), reads=[MA], writes=[qTz[1]])
                        for a, KT in ((0, KTs), (1, KTw)):
                            for kc in range(8):
                                k.op("pe", lambda e, a=a, kc=kc: e.matmul(MB[:, a * 128:(a + 1) * 128], wkT[:, kc, a * 128:(a + 1) * 128], xt[:, kc, :],
                                                                       start=(kc == 0), stop=(kc == 7)),
                                     reads=[xt, wkT], writes=[MB], inc=(kc == 7))
                        k.op("dve", lambda e: e.tensor_copy(KTs[:, j * 128:(j + 1) * 128], MB[:, 0:128]), reads=[MB], writes=[KTs])
                        k.op("dve", lambda e: e.tensor_copy(KTw[:, j * 128:(j + 1) * 128], MB[:, 128:256]), reads=[MB], writes=[KTw])
                        if os.environ.get("DEV_P1", "") == "proj":
                            continue
                        for n in range(2):
                            scores(n, kcaux[:, j * 128:(j + 1) * 128], kcT[:, :],
                                   (cmpsel[:, j * 128:(j + 1) * 128], vispat, [cb8]), PTc[:, :])
                            for g in range(4):
                                k.op("pe", lambda e, g=g: e.matmul(MA[:, g * 65:(g + 1) * 65], PTc[:, g * 128:(g + 1) * 128], cmprhs[:, n, 0:65],
                                                                 start=True, stop=True), reads=[PTc, cmprhs], writes=[MA])
                            for g in range(4):
                                k.op("pe", lambda e, g=g: e.matmul(MB[:, g * 64:(g + 1) * 64], PTc[:, g * 128:(g + 1) * 128], pool2,
                                                                 start=True, stop=True), reads=[PTc, cb128], writes=[MB])
                            k.op("act", lambda e: e.copy(ocmp[:, 4 * n:4 * n + 4, :], MA[:, 0:260].rearrange("p (g d) -> p g d", g=4)),
                                 reads=[MA], writes=[ocmp])
                            k.op("dve", lambda e: e.tensor_scalar(rl[:, 4 * n:4 * n + 4, 0], ocmp[:, 4 * n:4 * n + 4, 64], 1e-30, None, ALU.max),
                                 reads=[ocmp], writes=[rl])
                            k.op("dve", lambda e: e.reciprocal(rl[:, 4 * n:4 * n + 4, 0], rl[:, 4 * n:4 * n + 4, 0]), reads=[rl], writes=[rl])
                            if j >= 8:
                                k.op("dve", lambda e: e.tensor_scalar(imp[:], MB[:, 0:64], rl[:, 4 * n, 0:1], None, ALU.mult),
                                     reads=[MB, rl], writes=[imp])
                                for g in range(1, 4):
                                    k.op("dve", lambda e, g=g: e.scalar_tensor_tensor(imp[:], MB[:, g * 64:(g + 1) * 64], rl[:, 4 * n + g, 0:1], imp[:],
                                                                                     ALU.mult, ALU.add), reads=[MB, rl, imp], writes=[imp])
                                k.op("dve", lambda e: e.tensor_tensor(imp[:], imp[:], tkb[:, j * 64:(j + 1) * 64], ALU.add), reads=[imp, c32], writes=[imp])
                                k.op("dve", lambda e: e.max(out=m8[:, 0:8], in_=imp[:]), reads=[imp], writes=[m8])
                                k.op("dve", lambda e: e.match_replace(out=tkw[:], in_to_replace=m8[:, 0:8], in_values=imp[:], imm_value=-1e9),
                                     reads=[imp, m8], writes=[tkw])
                                k.op("dve", lambda e: e.max(out=m8[:, 8:16], in_=tkw[:]), reads=[tkw], writes=[m8])
                                k.op("dve", lambda e: e.tensor_scalar(selb[:], imp[:], m8[:, 15:16], None, ALU.is_ge), reads=[imp, m8], writes=[selb])
                                k.op("dve", lambda e: e.tensor_scalar(selb[:], selb[:], -1.0, -NEG, ALU.add, ALU.mult), reads=[selb], writes=[selb])
                                k.op("pe", lambda e: e.transpose(MB[0:64, 256:384], selb[:, :], ident), reads=[selb, c32], writes=[MB])
                                k.op("dve", lambda e: e.tensor_copy(selbT[n][0:64, :, :], MB[0:64, 256:384].unsqueeze(1).to_broadcast([64, 4, 128])),
                                     reads=[MB], writes=[selbT[n]])
                        if os.environ.get("DEV_P1", "") == "cmp":
                            continue
                        for br, KT, V, tiles in (("s", KTs, Vs, list(range(0, j + 1))), ("w", KTw, Vw, list(range(max(0, j - 4), j + 1)))):
                            if os.environ.get("DEV_P1", "") == "sonly" and br == "w":
                                continue
                            for n in range(2):
                                for t in tiles:
                                    if t == j:
                                        mask = (identb, causal4, [cb128])
                                    elif br == "w" and t == j - 4:
                                        mask = (identb, winedge4, [cb128])
                                    elif br == "s" and j >= 8:
                                        mask = (Epad[:, t * 128:(t + 1) * 128], selbT[n][:].rearrange("p g q -> p (g q)"), [cb64, selbT[n]])
                                    else:
                                        mask = None
                                    nm_ = os.environ.get("DEV_NOMASK", "")
                                    if nm_ == "1" or (nm_ == "2" and mask is not None and mask[2][0] is cb128) or (nm_ == "3" and mask is not None and mask[2][0] is cb64):
                                        mask = None
                                    scores(n, kaux[:, (j - t) * 128:(j - t + 1) * 128], KT[:, t * 128:(t + 1) * 128], mask, PT[n][:, t, :])
                                if os.environ.get("DEV_P1", "") == "sc":
                                    continue
                                A = ACC[(br, n)]
                                for g in range(4):
                                    for ti, t in enumerate(tiles):
                                        k.op("pe", lambda e, g=g, t=t, ti=ti: e.matmul(A[:, g * 65:(g + 1) * 65], PT[n][:, t, g * 128:(g + 1) * 128], V[:, t, n, 0:65],
                                                                                  start=(ti == 0), stop=(ti == len(tiles) - 1)),
                                             reads=[PT[n], V], writes=[A], inc=(ti == len(tiles) - 1))
                                if os.environ.get("DEV_P1", "") == "pv":
                                    continue
                                col = 1 if br == "s" else 2
                                k.op("dve", lambda e: e.reciprocal(rl[:, 4 * n:4 * n + 4, col],
                                                                   A[:, 0:260].rearrange("p (g d) -> p g d", g=4)[:, :, 64]),
                                     reads=[A], writes=[rl])
                        if os.environ.get("DEV_P1", "") in ("sc", "pv", "rc"):
                            continue
                        k.op("dve", lambda e: e.tensor_tensor(fgt[:], rl[:], gates[:].rearrange("p (h b) -> p h b", b=3), ALU.mult),
                             reads=[rl, gates], writes=[fgt])
                        for h in range(8):
                            n, g = h // 4, h % 4
                            o_ = onsa[:, h * 64:(h + 1) * 64]
                            k.op("dve", lambda e: e.tensor_scalar(o_, ocmp[:, h, 0:64], fgt[:, h, 0:1], None, ALU.mult), reads=[ocmp, fgt], writes=[onsa])
                            k.op("dve", lambda e: e.scalar_tensor_tensor(o_, ACC[("s", n)][:, g * 65:g * 65 + 64], fgt[:, h, 1:2], o_, ALU.mult, ALU.add),
                                 reads=[ACC[("s", n)], fgt, onsa], writes=[onsa])
                            k.op("dve", lambda e: e.scalar_tensor_tensor(o_, ACC[("w", n)][:, g * 65:g * 65 + 64], fgt[:, h, 2:3], o_, ALU.mult, ALU.add),
                                 reads=[ACC[("w", n)], fgt, onsa], writes=[onsa])
                        for c in range(4):
                            k.op("pe", lambda e, c=c: e.transpose(MA[:, c * 128:(c + 1) * 128], onsa[:, c * 128:(c + 1) * 128], ident),
                                 reads=[onsa, c32], writes=[MA])
                        k.op("act", lambda e: e.copy(onT[:], MA[:, :].rearrange("p (c t) -> p c t", c=4)), reads=[MA], writes=[onT])
                        k.dma("pool", mixT_d.t[0:512, :].rearrange("(c p) t -> p c t", p=128)[:, :, j * 128:(j + 1) * 128], onT[:],
                              reads=[onT], writes=[mixT_d])
                k.barrier()
            if STOP == "P1":
                break
            if do_prompt:
                gdn_prompt(l)
            if STOP == "P2":
                break
            if do_sample:
                sample_mixers(l)
            chain(l, xres_p, xres_s, yout_p, yout_s, last)
            if STOP == "P3":
                break
        k.finish()
    return nc


_W_NAMES = {"w_in": "w_in", "nsa_pe": "nsa_pe", "nsa_phi": "nsa_phi", "gdn_conv_w": "gconv_w", "gdn_A_log": "A_log",
            "gdn_dt_bias": "dt_bias", "gdn_norm_w": "gnorm_w", "w_out": "w_out", "ln_g": "ln_g", "ln_b": "ln_b",
            "ffn_w_up": "w_up", "ffn_conv_w": "fconv_w", "ffn_w_down": "w_down", "ple_w_proj": "w_proj", "ple_w_gate": "w_gate"}


def run_cores(inp, ncores, parts=("prompt", "sample")):
    f32 = np.float32
    x_prompt = np.asarray(inp["x_prompt"], f32)
    B, T, _ = x_prompt.shape
    x_sample = np.asarray(inp["x_sample"], f32)
    NSTOT = x_sample.shape[0]
    cache = np.asarray(inp["cache_nsa_kv"], f32)
    NPOOL = cache.shape[1]
    page_table = np.asarray(inp["page_table"], np.int32)
    NPG = page_table.shape[1]
    NS = NSTOT // ncores
    swin = np.asarray(inp["state_nsa_win"], f32)
    WB = swin.shape[2]
    nc = build(T, NS, NPOOL, NPG, WB, parts)
    consts = make_consts(T, NS, NPG)
    common = {v: np.ascontiguousarray(np.asarray(inp[kk], f32)) for kk, v in _W_NAMES.items()}
    common.update(consts)
    pool = np.ascontiguousarray(cache.reshape(DEPTH * NPOOL * 128, 512))
    p_prompt = np.asarray(inp["p_prompt"], f32)
    p_sample = np.asarray(inp["p_sample"], f32)
    sgdn = np.asarray(inp["state_gdn"], f32)
    sgc = np.asarray(inp["state_gdn_conv"], f32)
    sfc = np.asarray(inp["state_ffn_conv"], f32)
    per = ncores // B if ncores >= B else 1
    in_maps = []
    for c in range(ncores):
        b = min(c // per, B - 1)
        sl = slice(c * NS, (c + 1) * NS)
        m = dict(common)
        m.update({
            "xp": np.ascontiguousarray(x_prompt[b]),
            "pp": np.ascontiguousarray(p_prompt[:, b]),
            "xs": np.ascontiguousarray(x_sample[sl, 0]),
            "pps": np.ascontiguousarray(p_sample[:, sl, 0]),
            "pool": pool,
            "ptab": np.ascontiguousarray(page_table[sl].reshape(1, NS * NPG)),
            "st_win": np.ascontiguousarray(swin[:, sl].reshape(DEPTH, NS, WB, 256)),
            "st_gdn": np.ascontiguousarray(sgdn[:, sl]),
            "st_gconv": np.ascontiguousarray(sgc[:, sl]),
            "st_fconv": np.ascontiguousarray(sfc[:, sl]),
        })
        in_maps.append(m)
    if os.environ.get("DEV_TRACE", "") == "1":
        res = run_bass_kernel_spmd(nc, in_maps, core_ids=list(range(ncores)), trace=True)
        print("EXEC_TIME_NS", res.exec_time_ns)
    else:
        res = run_bass_kernel_spmd(nc, in_maps, core_ids=list(range(ncores)))
    R = res.results
    pc = [min(b * per, ncores - 1) for b in range(B)]
    y_prompt = np.stack([R[c]["y_p"] for c in pc])
    y_sample = np.concatenate([R[c]["y_s"] for c in range(ncores)])[:, None, :]
    kv_rows_prompt = np.stack([R[c]["kv_p"] for c in pc], axis=1).reshape(DEPTH, B, T, 4, 2, 64)
    kv_rows_sample = np.concatenate([R[c]["kv_s"] for c in range(ncores)], axis=1).reshape(DEPTH, NSTOT, 1, 4, 2, 64)
    win_prompt = np.stack([R[c]["win_p"] for c in pc], axis=1).reshape(DEPTH, B, -1, 2, 2, 64)
    win_sample = np.concatenate([R[c]["win_s"] for c in range(ncores)], axis=1).reshape(DEPTH, NSTOT, WB, 2, 2, 64)
    gdn_state_prompt = np.stack([R[c]["gst_p"] for c in pc], axis=1)
    gdn_state_sample = np.concatenate([R[c]["gst_s"] for c in range(ncores)], axis=1)
    gdn_conv_prompt = np.stack([R[c]["gcv_p"] for c in pc], axis=1)
    gdn_conv_sample = np.concatenate([R[c]["gcv_s"] for c in range(ncores)], axis=1)
    ffn_conv_prompt = np.stack([R[c]["fcv_p"] for c in pc], axis=1)
    ffn_conv_sample = np.concatenate([R[c]["fcv_s"] for c in range(ncores)], axis=1)
    return (y_prompt, y_sample, kv_rows_prompt, kv_rows_sample, win_prompt, win_sample, gdn_state_prompt, gdn_state_sample,
            gdn_conv_prompt, gdn_conv_sample, ffn_conv_prompt, ffn_conv_sample)


def kernel(**inputs):
    outs = run_cores(inputs, NCORES)
    return tuple(np.ascontiguousarray(o, dtype=np.float32) for o in outs)
```

```python
import os
import numpy as np
import ml_dtypes
from contextlib import ExitStack
import concourse.bass as bass
import concourse.mybir as mybir
from concourse.bass_utils import run_bass_kernel_spmd

F32 = mybir.dt.float32
BF16 = mybir.dt.bfloat16
I32 = mybir.dt.int32
F32R = mybir.dt.float32r
FASTF32 = os.environ.get("DEV_F32R", "0") == "1"
NOSELF = os.environ.get("DEV_NOSELF", "0") == "1"


def fr(ap):
    return ap


def f32v(ap):
    return ap.bitcast(F32) if FASTF32 else ap
AF = mybir.ActivationFunctionType
ALU = mybir.AluOpType
AX = mybir.AxisListType
bf = ml_dtypes.bfloat16

NEG = -60000.0
DM = 1024
INW = 3368
DFF = 2816
PLED = 256
C_KV, C_WIN, C_GATE, C_GQKV, C_GA, C_GB, C_GZ = 512, 1024, 1280, 1304, 2840, 2848, 2856
DEPTH = 2
ALPHA = float((2 * DEPTH) ** 0.25)
LN_EPS = 1e-5
RMS_EPS = 1e-6
NCORES = 8


class Buf:
    __slots__ = ("t", "w", "r", "name", "excl")

    def __init__(self, t, name="", excl=False):
        self.t = t
        self.w = {}
        self.r = {}
        self.name = name
        self.excl = excl

    def __getitem__(self, idx):
        return self.t[idx]


class K:
    NDMA = 32

    def __init__(self, nc):
        self.nc = nc
        self.eng = {"pe": nc.tensor, "act": nc.scalar, "dve": nc.vector, "pool": nc.gpsimd, "sp": nc.sync}
        self.sem = {}
        self.cnt = {}
        for e in self.eng:
            self.sem[e] = nc.alloc_semaphore("sem_" + e)
            self.cnt[e] = 0
        for j in range(self.NDMA):
            key = ("dma", j)
            self.sem[key] = nc.alloc_semaphore("sem_dma%d" % j)
            self.cnt[key] = 0
        self.dma_rr = 0
        self.seen = {e: {} for e in self.eng}
        self.nins = 0
        self.uid = 0

    def name(self, s):
        self.uid += 1
        return "%s_%d" % (s, self.uid)

    def sb(self, es, name, shape, dt=F32):
        t = es.enter_context(self.nc.sbuf_tensor(self.name(name), list(shape), dt))
        return Buf(t, name)

    def ps(self, name, shape, dt=F32):
        return Buf(self.nc.alloc_psum_tensor(self.name(name), list(shape), dt), name, excl=True)

    def dram(self, name, shape, dt=F32, kind="Internal"):
        return Buf(self.nc.dram_tensor(name, list(shape), dt, kind=kind).ap(), name)

    def _wait(self, e, deps):
        eng = self.eng[e]
        seen = self.seen[e]
        for key, v in deps.items():
            if v <= 0 or seen.get(key, 0) >= v:
                continue
            eng.wait_ge(self.sem[key], v)
            self.nins += 1
            seen[key] = v

    @staticmethod
    def _deps(reads, writes):
        deps = {}
        for b in reads:
            for key, v in b.w.items():
                if deps.get(key, 0) < v:
                    deps[key] = v
        for b in writes:
            for key, v in b.w.items():
                if deps.get(key, 0) < v:
                    deps[key] = v
            for key, v in b.r.items():
                if deps.get(key, 0) < v:
                    deps[key] = v
        return deps

    @staticmethod
    def _record(key, val, reads, writes):
        for b in reads:
            if b.r.get(key, 0) < val:
                b.r[key] = val
        for b in writes:
            b.w.clear()
            b.w[key] = val
            b.r.clear()

    def op(self, e, fn, reads=(), writes=(), inc=True):
        ex = [b for b in reads if b.excl]
        if ex:
            writes = list(writes) + ex
        deps = self._deps(reads, writes)
        if e == "pe" or NOSELF:
            deps.pop(e, None)
        if e in deps and deps[e] > self.cnt[e]:
            deps[e] = self.cnt[e]
        self._wait(e, deps)
        ins = fn(self.eng[e])
        self.nins += 1
        if inc:
            self.cnt[e] += 1
            ins.then_inc(self.sem[e], 1)
            val = self.cnt[e]
        else:
            val = self.cnt[e] + 1
        self._record(e, val, reads, writes)
        return ins

    def _dma_issue(self, q, reads, writes, fn):
        deps = self._deps(reads, writes)
        j = self.dma_rr
        self.dma_rr = (self.dma_rr + 1) % self.NDMA
        key = ("dma", j)
        if self.cnt[key] > 0 and deps.get(key, 0) < self.cnt[key]:
            deps[key] = self.cnt[key]
        if q in deps and deps[q] > self.cnt[q]:
            deps[q] = self.cnt[q]
        self._wait(q, deps)
        ins = fn(self.eng[q])
        self.nins += 1
        self.cnt[key] += 16
        ins.then_inc(self.sem[key], 16)
        self._record(key, self.cnt[key], reads, writes)
        return ins

    def dma(self, q, out, in_, reads=(), writes=(), **kw):
        return self._dma_issue(q, reads, writes, lambda e: e.dma_start(out=out, in_=in_, **kw))

    def gather(self, out, table, idx, reads=(), writes=()):
        return self._dma_issue(
            "pool", reads, writes,
            lambda e: e.indirect_dma_start(out=out, out_offset=None, in_=table,
                                           in_offset=bass.IndirectOffsetOnAxis(ap=idx, axis=0)))

    def barrier(self):
        full = dict(self.cnt)
        for e in self.eng:
            deps = {key: v for key, v in full.items() if key != e}
            self._wait(e, deps)

    def finish(self):
        deps = {key: v for key, v in self.cnt.items() if key != "sp"}
        self._wait("sp", deps)


def make_consts(T, NS=16, NPG=16):
    NJ = T // 128
    c = {}
    p = np.arange(128)
    ident = np.eye(128, dtype=np.float32)
    U = (p[:, None] <= p[None, :]).astype(np.float32)
    ones = np.ones((128, 128), np.float32)
    mask_incl = np.where(p[:, None] >= p[None, :], 0.0, -1e4).astype(np.float32)
    maskT_incl = np.where(p[None, :] >= p[:, None], 0.0, -1e4).astype(np.float32)
    strict01 = (p[:, None] > p[None, :]).astype(np.float32)
    tk = np.zeros((128, NJ, 64), np.float32)
    blk = np.arange(64)
    for j in range(NJ):
        cur = (128 * j + p) // 64
        fut = blk[None, :] > cur[:, None]
        forced = (blk[None, :] == 0) | (((cur[:, None] - blk[None, :]) < 2) & ~fut)
        tk[:, j, :] = np.where(fut, -1e4, np.where(forced, 1e4, 0.0))
    c["c32"] = np.concatenate([ident, U, ones, mask_incl, maskT_incl, strict01, tk.reshape(128, NJ * 64)], axis=1)
    causal = np.where(p[:, None] <= p[None, :], 0.0, NEG)
    winedge = np.where(p[:, None] > p[None, :], 0.0, NEG)
    pool2 = (p[:, None] // 2 == np.arange(64)[None, :]).astype(np.float32)
    avg = (p[:, None] // 32 == np.arange(4)[None, :]).astype(np.float32) / 32.0
    c["cb128"] = np.concatenate([np.tile(causal, (1, 4)), np.tile(winedge, (1, 4)), ident, pool2,
                                 avg, np.ones((128, 4))], axis=1).astype(bf)
    kp = np.arange(T)
    Eall = (kp[None, :] // 64 == np.arange(64)[:, None]).astype(np.float32)
    slopes = 2.0 ** (-np.arange(1, 9, dtype=np.float64))
    kauxrel = np.zeros((128, NJ, 128), np.float32)
    for d in range(NJ):
        kauxrel[0, d, :] = -d
        kauxrel[1, d, :] = p
        kauxrel[2, d, :] = 1.0
    cc = np.arange(128)
    kcauxrel = np.zeros((128, NJ, 128), np.float32)
    for j in range(NJ):
        kcauxrel[0, j, :] = cc / 4.0 - j
        kcauxrel[2, j, :] = 1.0
    qaux = np.zeros((128, 2, 4, 128), np.float32)
    for n in range(2):
        for g in range(4):
            sl = slopes[4 * n + g]
            qaux[0, n, g, :] = sl * 8 * 128
            qaux[1, n, g, :] = sl * 8
            qaux[2, n, g, :] = -8.0 * sl * p
    cmpsel = np.zeros((128, NJ, 128), np.float32)
    for j in range(NJ):
        for r in range(4):
            if 4 * j + r < 128:
                cmpsel[r, j, 4 * j + r] = 1.0
        cmpsel[4, j, 4 * j + 4:] = 1.0
    vis = np.zeros((128, 4, 128), np.float32)
    for r in range(4):
        vis[r, :, :] = np.where(32 * r + 31 <= p, 0.0, NEG)[None, :]
    vis[4] = NEG
    Epad = np.zeros((128, T), np.float32)
    Epad[0:64] = Eall
    c["cbp"] = np.concatenate([kauxrel.reshape(128, -1), kcauxrel.reshape(128, -1), qaux.reshape(128, 1024),
                               cmpsel.reshape(128, -1), vis.reshape(128, 512), Epad], axis=1).astype(bf)
    PAST = NPG * 128
    NWT = 4
    al_s = np.zeros((128, 2, NPG, 4), np.float32)
    al_w = np.zeros((128, 2, NWT, 4), np.float32)
    al_c = np.zeros((128, NS, 2, 4), np.float32)
    for n in range(2):
        for g in range(4):
            sl = slopes[4 * n + g]
            for t in range(NPG):
                al_s[:, n, t, g] = -sl * (PAST - (t * 128 + p))
            for t in range(NWT):
                dist = NWT * 128 - (t * 128 + p)
                al_w[:, n, t, g] = np.where(dist < NWT * 128, -sl * dist, -1e4)
            al_c[:, :, n, g] = (-sl * (PAST - (32 * p + 15.5)))[:, None]
    GS = np.zeros((128, 2 * NS), np.float32)
    GS[np.arange(NS * 8), np.arange(NS * 8) // 4] = 1.0
    NB33 = NPG * 2 + 1
    tkb = np.zeros((128, 40), np.float32)
    tkb[:, 0] = 1e4
    tkb[:, NB33 - 2] = 1e4
    tkb[:, NB33 - 1] = 1e4
    piota = np.stack([2.0 * p, 2.0 * p], axis=1).astype(np.float32)
    c["cs32"] = np.concatenate([al_s.reshape(128, -1), al_w.reshape(128, -1), al_c.reshape(128, -1), GS, tkb, piota], axis=1).astype(np.float32)
    OH = np.zeros((128, 2 * NS, 128), np.float32)
    for r in range(2 * NS):
        OH[r, r, :] = 1.0
    pool33 = np.zeros((128, NB33 + 1), np.float32)
    for cidx in range(NPG * 4):
        pool33[cidx, cidx // 2] = 1.0
    pool33[:, NB33] = 1.0
    c["csb"] = np.concatenate([OH.reshape(128, -1), pool33], axis=1).astype(bf)
    return c


def build(T, NS, NPOOL, NPG=16, WB=512, parts=("prompt", "sample")):
    NJ = T // 128
    PAST = NPG * 128
    NWT = WB // 128
    nc = bass.Bass("TRN2", target_bir_lowering=False)
    k = K(nc)
    do_prompt = "prompt" in parts
    do_sample = "sample" in parts

    def din(name, shape, dt=F32):
        return k.dram(name, shape, dt, kind="ExternalInput")

    def dout(name, shape, dt=F32):
        return k.dram(name, shape, dt, kind="ExternalOutput")

    xp = din("xp", [T, DM])
    pp = din("pp", [DEPTH, T, PLED])
    xs = din("xs", [NS, DM])
    pps = din("pps", [DEPTH, NS, PLED])
    pool = din("pool", [DEPTH * NPOOL * 128, 512])
    ptab = din("ptab", [1, NS * NPG], I32)
    st_win = din("st_win", [DEPTH, NS, WB, 256])
    st_gdn = din("st_gdn", [DEPTH, NS, 8, 64, 64])
    st_gconv = din("st_gconv", [DEPTH, NS, 3, 1536])
    st_fconv = din("st_fconv", [DEPTH, NS, 2, DFF])
    w_in = din("w_in", [DEPTH, DM, INW])
    nsa_pe = din("nsa_pe", [DEPTH, 2, 32, 64])
    nsa_phi = din("nsa_phi", [DEPTH, 2, 64, 64])
    gconv_w = din("gconv_w", [DEPTH, 4, 1536])
    A_log = din("A_log", [DEPTH, 8])
    dt_bias = din("dt_bias", [DEPTH, 8])
    gnorm_w = din("gnorm_w", [DEPTH, 64])
    w_out = din("w_out", [DEPTH, DM, DM])
    ln_g = din("ln_g", [DEPTH, 3, DM])
    ln_b = din("ln_b", [DEPTH, 3, DM])
    w_up = din("w_up", [DEPTH, DM, 2 * DFF])
    fconv_w = din("fconv_w", [DEPTH, 3, DFF])
    w_down = din("w_down", [DEPTH, DFF, DM])
    w_proj = din("w_proj", [DEPTH, PLED, DM])
    w_gate = din("w_gate", [DEPTH, DM, DM])
    NC32 = 6 * 128 + NJ * 64
    c32_d = din("c32", [128, NC32])
    NCB128 = 512 + 512 + 128 + 64 + 4 + 4
    cb128_d = din("cb128", [128, NCB128], BF16)
    NCS32 = 2 * NPG * 4 + 2 * NWT * 4 + NS * 8 + 2 * NS + 40 + 2
    cs32_d = din("cs32", [128, NCS32])
    NCSB = 2 * NS * 128 + NPG * 2 + 2
    csb_d = din("csb", [128, NCSB], BF16)
    NCBP = 3 * NJ * 128 + 1024 + 512 + T
    cbp_d = din("cbp", [128, NCBP], BF16)
    y_p = dout("y_p", [T, DM])
    y_s = dout("y_s", [NS, DM])
    kv_p = dout("kv_p", [DEPTH, T, 512])
    kv_s = dout("kv_s", [DEPTH, NS, 512])
    win_p = dout("win_p", [DEPTH, WB, 256])
    win_s = dout("win_s", [DEPTH, NS, WB, 256])
    gst_p = dout("gst_p", [DEPTH, 8, 64, 64])
    gst_s = dout("gst_s", [DEPTH, NS, 8, 64, 64])
    gcv_p = dout("gcv_p", [DEPTH, 3, 1536])
    gcv_s = dout("gcv_s", [DEPTH, NS, 3, 1536])
    fcv_p = dout("fcv_p", [DEPTH, 2, DFF])
    fcv_s = dout("fcv_s", [DEPTH, NS, 2, DFF])
    wb_in = k.dram("wb_in", [DEPTH, DM, INW], BF16)
    wb_out = k.dram("wb_out", [DEPTH, DM, DM], BF16)
    wb_up = k.dram("wb_up", [DEPTH, DM, 2 * DFF], BF16)
    wb_down = k.dram("wb_down", [DEPTH, DFF, DM], BF16)
    wb_gate = k.dram("wb_gate", [DEPTH, DM, DM], BF16)
    wb_proj = k.dram("wb_proj", [DEPTH, PLED, DM], BF16)
    xT_d = k.dram("xT_d", [DM, T], BF16)
    mixT_d = k.dram("mixT_d", [DM, T], BF16)
    x1_d = k.dram("x1_d", [T, DM], F32)
    xsT_d = k.dram("xsT_d", [DM, NS], BF16)
    mixsT_d = k.dram("mixsT_d", [DM, NS], BF16)
    xs1_d = k.dram("xs1_d", [NS, DM], F32)

    pb = [k.ps("pb%d" % i, [128, 512]) for i in range(8)]

    with ExitStack() as gs:
        c32 = k.sb(gs, "c32", [128, NC32])
        k.dma("sp", c32[:], c32_d[:], writes=[c32])
        cb128 = k.sb(gs, "cb128", [128, NCB128], BF16)
        k.dma("sp", cb128[:], cb128_d[:], writes=[cb128])
        ident = c32.t[:, 0:128]
        Umat = c32.t[:, 128:256]
        ones32 = c32.t[:, 256:384]
        mask_incl = c32.t[:, 384:512]
        maskT_incl = c32.t[:, 512:640]
        strict01 = c32.t[:, 640:768]
        tkb = c32.t[:, 768:768 + NJ * 64]
        causal4 = cb128.t[:, 0:512]
        winedge4 = cb128.t[:, 512:1024]
        identb = cb128.t[:, 1024:1152]
        pool2 = cb128.t[:, 1152:1216]
        avg4 = cb128.t[:, 1216:1220]
        epsc = k.sb(gs, "epsc", [128, 2])
        k.op("pool", lambda e: e.memset(epsc[:, 0:1], RMS_EPS), writes=[epsc])
        k.op("pool", lambda e: e.memset(epsc[:, 1:2], LN_EPS), writes=[epsc])

        cast_rr = [0]

        def cast(out, in_, reads, writes, psum=False):
            e = ("dve", "act")[cast_rr[0] % 2] if psum else ("dve", "act", "pool")[cast_rr[0] % 3]
            cast_rr[0] += 1
            if e == "act":
                k.op("act", lambda en: en.copy(out, in_), reads=reads, writes=writes)
            else:
                k.op(e, lambda en: en.tensor_copy(out, in_), reads=reads, writes=writes)

        with ExitStack() as es:
            stg = [k.sb(es, "wstg%d" % i, [128, 2 * DFF]) for i in range(2)]
            stgb = [k.sb(es, "wstgb%d" % i, [128, 2 * DFF], BF16) for i in range(2)]
            it = 0
            for l in range(DEPTH):
                for (src, dst, rows, cols) in ((w_in, wb_in, DM, INW), (w_out, wb_out, DM, DM), (w_up, wb_up, DM, 2 * DFF),
                                               (w_down, wb_down, DFF, DM), (w_gate, wb_gate, DM, DM), (w_proj, wb_proj, PLED, DM)):
                    for r0 in range(0, rows, 128):
                        s, sb_ = stg[it % 2], stgb[it % 2]
                        it += 1
                        k.dma("sp", s[:, 0:cols], src[l, r0:r0 + 128, :], writes=[s])
                        cast(sb_[:, 0:cols], s[:, 0:cols], [s], [sb_])
                        k.dma("pool", dst[l, r0:r0 + 128, :], sb_[:, 0:cols], reads=[sb_], writes=[dst])
        k.barrier()

        def transpose_to_xT(es_name, src_tile, ntok, dstT, col0, trp, trs):
            for half in range(2):
                for c4 in range(4):
                    c = half * 4 + c4
                    k.op("pe", lambda e, c=c, c4=c4: e.transpose(trp[:, c4 * 128:c4 * 128 + ntok],
                                                               src_tile[0:ntok, c * 128:(c + 1) * 128], ident[0:ntok, 0:ntok]),
                         reads=[src_tile, c32], writes=[trp])
                cast(trs[:, half * 4:(half + 1) * 4, 0:ntok],
                     trp[:, :].rearrange("p (c t) -> p c t", c=4)[:, :, 0:ntok], [trp], [trs], psum=True)
            k.dma("pool", dstT.t.rearrange("(c p) t -> p c t", p=128)[:, :, col0:col0 + ntok], trs[:, :, 0:ntok],
                  reads=[trs], writes=[dstT])

        with ExitStack() as es:
            xt_in = [k.sb(es, "xt_in%d" % i, [128, DM]) for i in range(2)]
            trs = [k.sb(es, "trs%d" % i, [128, 8, 128], BF16) for i in range(2)]
            if do_prompt:
                for j in range(NJ):
                    xt = xt_in[j % 2]
                    k.dma("sp", xt[:], xp[j * 128:(j + 1) * 128, :], writes=[xt])
                    transpose_to_xT("x0", xt, 128, xT_d, j * 128, pb[j % 2], trs[j % 2])
            if do_sample:
                xt = xt_in[0]
                k.dma("sp", xt[0:NS, :], xs[:, :], writes=[xt])
                transpose_to_xT("xs0", xt, NS, xsT_d, 0, pb[2], trs[0])
        k.barrier()

        def gdn_prompt(l):
            with ExitStack() as es:
                wsrc = wb_in.t[l].rearrange("(c p) n -> p c n", p=128)
                wg = k.sb(es, "wg", [128, 8, 8, 192], BF16)
                for blk in range(3):
                    for kc in range(8):
                        k.dma("sp", wg[:, kc, :, blk * 64:(blk + 1) * 64],
                              wb_in.t[l, kc * 128:(kc + 1) * 128, C_GQKV + blk * 512:C_GQKV + (blk + 1) * 512].rearrange("p (h d) -> p h d", h=8),
                              reads=[wb_in], writes=[wg])
                wab = k.sb(es, "wab", [128, 8, 16], BF16)
                k.dma("sp", wab[:], wsrc[:, :, C_GA:C_GA + 16], reads=[wb_in], writes=[wab])
                wz = k.sb(es, "wz", [128, 8, 512], BF16)
                k.dma("sp", wz[:], wsrc[:, :, C_GZ:C_GZ + 512], reads=[wb_in], writes=[wz])
                cw = k.sb(es, "cw", [64, 8, 3, 4])
                for h in range(8):
                    for blk in range(3):
                        c0 = blk * 512 + h * 64
                        k.dma("sp", cw[:, h, blk, :], gconv_w[l][:, c0:c0 + 64].rearrange("w d -> d w"), writes=[cw],
                              allow_slow_non_contiguous=True)
                dtb = k.sb(es, "dtb", [128, 8])
                k.dma("sp", dtb[:], dt_bias[l:l + 1, :].partition_broadcast(128), writes=[dtb])
                negA = k.sb(es, "negA", [128, 8])
                k.dma("sp", negA[:], A_log[l:l + 1, :].partition_broadcast(128), writes=[negA])
                k.op("act", lambda e: e.activation(negA[:], negA[:], AF.Exp), reads=[negA], writes=[negA])
                k.op("dve", lambda e: e.tensor_scalar(negA[:], negA[:], -1.0, None, ALU.mult), reads=[negA], writes=[negA])
                nw = k.sb(es, "nw", [128, 64])
                k.dma("sp", nw[:], gnorm_w[l:l + 1, :].partition_broadcast(128), writes=[nw])
                S = [k.sb(es, "S%d" % h, [128, 64]) for h in range(8)]
                carry = [k.sb(es, "carry%d" % h, [64, 3, 3]) for h in range(8)]
                for h in range(8):
                    k.op("pool", lambda e: e.memset(S[h][:], 0.0), writes=[S[h]])
                    k.op("pool", lambda e: e.memset(carry[h][:], 0.0), writes=[carry[h]])
                xTt = [k.sb(es, "gxTt%d" % i, [128, 8, 128], BF16) for i in range(2)]
                tmp8 = k.sb(es, "tmp8", [128, 8])
                gtok = k.sb(es, "gtok", [128, 8])
                btok = k.sb(es, "btok", [128, 8])
                gctok = k.sb(es, "gctok", [128, 8])
                ngctok = k.sb(es, "ngctok", [128, 8])
                egctok = k.sb(es, "egctok", [128, 8])
                nz = k.sb(es, "nz", [128, 8, 64])
                og = k.sb(es, "og", [128, 512])
                ogT = k.sb(es, "ogT", [128, 4, 128], BF16)
                NSLOT = int(os.environ.get("DEV_NSLOT", "8"))
                RG = []
                for s_ in range(NSLOT):
                    row = []
                    for r in range(4):
                        rb = Buf(pb[s_].t[:, r * 128:(r + 1) * 128], "R%d_%d" % (s_, r), excl=True)
                        rb.w = pb[s_].w
                        rb.r = pb[s_].r
                        row.append(rb)
                    RG.append(row)
                PAB, PZ, PTR = (pb[4], pb[5], pb[6]) if NSLOT == 4 else (pb[7], pb[6], pb[5])

                def mk(s):
                    d = {}
                    for nm, shp in (("ext", [64, 3, 131]), ("y", [64, 3, 128]), ("ys", [64, 3, 128]), ("sq", [64, 256]), ("rs", [64, 256]),
                                    ("qTf", [64, 128]), ("kTf", [64, 128]), ("qgT", [128, 128]), ("rhsX", [128, 128]), ("kend", [128, 64]),
                                    ("Ug", [128, 128]), ("X1", [128, 128]), ("dec", [128, 128]), ("X2", [128, 128]), ("decT", [128, 128]),
                                    ("egcB", [64, 128]), ("glc", [128, 4]), ("tN", [128, 128]), ("N", [128, 128]), ("NT", [128, 128]),
                                    ("P0", [128, 128]), ("P1", [128, 128]), ("Q1", [128, 128]), ("W0", [128, 128]), ("W1", [128, 128]),
                                    ("innerT", [128, 128]), ("val", [128, 64]), ("kcdT", [64, 128]), ("vn", [128, 64]), ("junk", [128, 64]),
                                    ("rstd", [128, 2])):
                        d[nm] = k.sb(es, "%s_%d" % (nm, s), shp, F32R if (FASTF32 and nm in ("N", "NT", "P0", "P1", "Q1", "W0", "W1")) else F32)
                    k.op("pool", lambda e: e.memset(d["qgT"][:], 0.0), writes=[d["qgT"]])
                    return d
                WK = [mk(s) for s in range(NSLOT)]
                id64 = ident[0:64, 0:64]
                ones64 = ones32[0:64, 0:64]

                def head_chain(h, i, slot, xt):
                    R = RG[slot]
                    w = WK[slot]
                    bank = pb[slot].t
                    ext, y, ys = w["ext"], w["y"], w["ys"]
                    for blk in range(3):
                        for kc in range(8):
                            k.op("pe", lambda e: e.matmul(R[blk][0:64, :], wg[:, kc, h, blk * 64:(blk + 1) * 64], xt[:, kc, :],
                                                          start=(kc == 0), stop=(kc == 7)), reads=[wg, xt], writes=[R[blk]], inc=(kc == 7))
                    yield
                    k.op("pool", lambda e: e.tensor_copy(ext[:, :, 0:3], carry[h][:]), reads=[carry[h]], writes=[ext])
                    k.op("act", lambda e: e.copy(ext[:, :, 3:131], bank[0:64, 0:384].rearrange("p (b t) -> p b t", b=3)),
                         reads=[R[0], R[1], R[2]], writes=[ext])
                    k.op("pool", lambda e: e.tensor_copy(carry[h][:], ext[:, :, 128:131]), reads=[ext], writes=[carry[h]])
                    yield
                    for blk in range(3):
                        eng = "dve"
                        k.op(eng, lambda e: e.tensor_scalar(y[:, blk, :], ext[:, blk, 0:128], cw[:, h, blk, 0:1], None, ALU.mult),
                             reads=[ext, cw], writes=[y])
                        for tap in range(1, 4):
                            k.op(eng, lambda e: e.scalar_tensor_tensor(y[:, blk, :], ext[:, blk, tap:tap + 128], cw[:, h, blk, tap:tap + 1], y[:, blk, :],
                                                                       ALU.mult, ALU.add), reads=[ext, cw, y], writes=[y])
                    yield
                    k.op("act", lambda e: e.activation(ys[:], y[:], AF.Silu), reads=[y], writes=[ys])
                    k.op("pool", lambda e: e.tensor_tensor(w["sq"][:], ys[:, 0:2, :].rearrange("p b t -> p (b t)"), ys[:, 0:2, :].rearrange("p b t -> p (b t)"), ALU.mult),
                         reads=[ys], writes=[w["sq"]])
                    yield
                    k.op("pe", lambda e: e.matmul(R[0][0:64, :], ones64, w["sq"][:, 0:128], start=True, stop=True), reads=[c32, w["sq"]], writes=[R[0]])
                    k.op("pe", lambda e: e.matmul(R[1][0:64, :], ones64, w["sq"][:, 128:256], start=True, stop=True), reads=[c32, w["sq"]], writes=[R[1]])
                    yield
                    sub = int(os.environ.get("DEV_SUB", "9"))
                    if sub >= 1:
                        k.op("act", lambda e: e.activation(w["rs"][:], bank[0:64, 0:256], AF.Sqrt, bias=epsc[0:64, 0:1], scale=1.0),
                             reads=[R[0], R[1], epsc], writes=[w["rs"]])
                    if sub >= 2:
                        k.op("dve", lambda e: e.reciprocal(w["rs"][:], w["rs"][:]), reads=[w["rs"]], writes=[w["rs"]])
                    if sub >= 3:
                        k.op("dve", lambda e: e.scalar_tensor_tensor(w["qTf"][:], ys[:, 0, :], 0.125, w["rs"][:, 0:128], ALU.mult, ALU.mult),
                             reads=[ys, w["rs"]], writes=[w["qTf"]])
                    if sub >= 4:
                        k.op("dve", lambda e: e.tensor_tensor(w["kTf"][:], ys[:, 1, :], w["rs"][:, 128:256], ALU.mult), reads=[ys, w["rs"]], writes=[w["kTf"]])
                    yield
                    k.op("dve", lambda e: e.tensor_scalar(w["Ug"][:], Umat, gtok[:, h:h + 1], None, ALU.mult), reads=[c32, gtok], writes=[w["Ug"]])
                    k.op("pe", lambda e: e.matmul(R[3][:, :], ones32, w["Ug"][:], start=True, stop=True), reads=[c32, w["Ug"]], writes=[R[3]])
                    k.op("pe", lambda e: e.transpose(R[2][:, 0:64], w["kTf"][:], id64), reads=[w["kTf"], c32], writes=[R[2]])
                    k.op("pe", lambda e: e.transpose(R[2][:, 64:128], ys[:, 2, :], id64), reads=[ys, c32], writes=[R[2]])
                    yield
                    sub8 = int(os.environ.get("DEV_SUB8", "99"))
                    if sub8 >= 1:
                        k.op("dve", lambda e: e.scalar_tensor_tensor(w["X1"][:], R[3][:, :], -1.0, mask_incl, ALU.mult, ALU.add), reads=[R[3], c32], writes=[w["X1"]])
                    if sub8 >= 2:
                        k.op("dve", lambda e: e.tensor_tensor(w["X2"][:], R[3][:, :], maskT_incl, ALU.add), reads=[R[3], c32], writes=[w["X2"]])
                    if sub8 >= 3:
                        k.op("act", lambda e: e.activation(w["egcB"][:], R[3][0:64, :], AF.Exp), reads=[R[3]], writes=[w["egcB"]])
                    if sub8 >= 4:
                        k.op("act", lambda e: e.copy(w["glc"][:, 0:1], R[3][:, 127:128]), reads=[R[3]], writes=[w["glc"]])
                    if sub8 >= 5:
                        k.op("act", lambda e: e.activation(w["dec"][:], w["X1"][:], AF.Exp, bias=gctok[:, h:h + 1], scale=1.0), reads=[w["X1"], gctok], writes=[w["dec"]])
                    if sub8 >= 6:
                        k.op("act", lambda e: e.activation(w["decT"][:], w["X2"][:], AF.Exp, bias=ngctok[:, h:h + 1], scale=1.0), reads=[w["X2"], ngctok], writes=[w["decT"]])
                    if sub8 >= 7:
                        k.op("act", lambda e: e.activation(w["glc"][:, 1:2], w["glc"][:, 0:1], AF.Exp), reads=[w["glc"]], writes=[w["glc"]])
                    if sub8 >= 8:
                        k.op("act", lambda e: e.activation(w["glc"][:, 2:3], ngctok[:, h:h + 1], AF.Exp, bias=w["glc"][:, 0:1], scale=1.0),
                             reads=[w["glc"], ngctok], writes=[w["glc"]])
                    yield
                    k.op("dve", lambda e: e.tensor_tensor(w["qgT"][0:64, :], w["qTf"][:], w["egcB"][:], ALU.mult), reads=[w["qTf"], w["egcB"]], writes=[w["qgT"]])
                    k.op("dve", lambda e: e.tensor_scalar(w["rhsX"][:, 0:64], R[2][:, 64:128], btok[:, h:h + 1], None, ALU.mult),
                         reads=[R[2], btok], writes=[w["rhsX"]])
                    k.op("dve", lambda e: e.tensor_scalar(w["rhsX"][:, 64:128], R[2][:, 0:64], btok[:, h:h + 1], egctok[:, h:h + 1], ALU.mult, ALU.mult),
                         reads=[R[2], btok, egctok], writes=[w["rhsX"]])
                    k.op("dve", lambda e: e.tensor_scalar(w["kend"][:], R[2][:, 0:64], w["glc"][:, 2:3], None, ALU.mult), reads=[R[2], w["glc"]], writes=[w["kend"]])
                    yield
                    k.op("pe", lambda e: e.matmul(R[0][:, :], w["kTf"][:], w["kTf"][:], start=True, stop=True), reads=[w["kTf"]], writes=[R[0]])
                    k.op("pe", lambda e: e.matmul(R[1][:, :], w["kTf"][:], w["qTf"][:], start=True, stop=True), reads=[w["kTf"], w["qTf"]], writes=[R[1]])
                    yield
                    k.op("dve", lambda e: e.tensor_tensor(w["tN"][:], R[0][:, :], w["dec"][:], ALU.mult), reads=[R[0], w["dec"]], writes=[w["tN"]])
                    k.op("dve", lambda e: e.scalar_tensor_tensor(w["N"][:], w["tN"][:], btok[:, h:h + 1], strict01, ALU.mult, ALU.mult),
                         reads=[w["tN"], btok, c32], writes=[w["N"]])
                    k.op("dve", lambda e: e.tensor_tensor(w["innerT"][:], R[1][:, :], w["decT"][:], ALU.mult), reads=[R[1], w["decT"]], writes=[w["innerT"]])
                    k.op("pe", lambda e: e.transpose(R[2][:, :], f32v(w["N"][:]), ident), reads=[w["N"], c32], writes=[R[2]])
                    yield
                    k.op("act", lambda e: e.copy(w["NT"][:], R[2][:, :]), reads=[R[2]], writes=[w["NT"]])
                    k.op("dve", lambda e: e.tensor_tensor(w["W0"][:], ident, R[2][:, :], ALU.subtract), reads=[c32, R[2]], writes=[w["W0"]])
                    yield
                    P, Q, Wc = w["N"], w["NT"], w["W0"]
                    Pn_l = [w["P0"], w["P1"]]
                    Qn_l = [w["Q1"], w["NT"]]
                    Wn_l = [w["W1"], w["W0"]]
                    for kk in range(1, 7):
                        Pn = Pn_l[kk % 2]
                        k.op("pe", lambda e: e.matmul(R[0][:, :], fr(Q[:]), fr(P[:]), start=True, stop=True), reads=[Q, P], writes=[R[0]])
                        if kk <= 5:
                            Qn = (w["NT"], w["Q1"])[kk % 2]
                            k.op("pe", lambda e: e.matmul(R[1][:, :], fr(P[:]), fr(Q[:]), start=True, stop=True), reads=[Q, P], writes=[R[1]])
                        yield
                        k.op("act", lambda e: e.copy(Pn[:], R[0][:, :]), reads=[R[0]], writes=[Pn])
                        if kk <= 5:
                            k.op("dve", lambda e: e.tensor_copy(Qn[:], R[1][:, :]), reads=[R[1]], writes=[Qn])
                        yield
                        Wn = Wn_l[(kk - 1) % 2]
                        k.op("pe", lambda e: e.matmul(R[2][:, :], fr(Pn[:]), fr(Wc[:]), start=True, stop=True), reads=[Pn, Wc], writes=[R[2]])
                        yield
                        k.op("dve", lambda e: e.tensor_tensor(Wn[:], f32v(Wc[:]), R[2][:, :], ALU.add), reads=[Wc, R[2]], writes=[Wn])
                        P, Wc = Pn, Wn
                        if kk <= 5:
                            Q = Qn
                        yield
                    k.op("pe", lambda e: e.matmul(R[3][:, 0:64], f32v(Wc[:]), w["rhsX"][:, 0:64], start=True, stop=True), reads=[Wc, w["rhsX"]], writes=[R[3]])
                    k.op("pe", lambda e: e.matmul(R[0][0:64, :], w["rhsX"][:, 64:128], f32v(Wc[:]), start=True, stop=True), reads=[Wc, w["rhsX"]], writes=[R[0]])
                    yield
                    k.op("act", lambda e: e.copy(w["val"][:], R[3][:, 0:64]), reads=[R[3]], writes=[w["val"]])
                    k.op("act", lambda e: e.copy(w["kcdT"][:], R[0][0:64, :]), reads=[R[0]], writes=[w["kcdT"]])
                    yield
                    k.op("pe", lambda e: e.matmul(R[1][:, 0:64], w["kcdT"][:], S[h][0:64, :], start=True, stop=True), reads=[w["kcdT"], S[h]], writes=[R[1]])
                    yield
                    k.op("dve", lambda e: e.tensor_tensor(w["vn"][:], w["val"][:], R[1][:, 0:64], ALU.subtract), reads=[w["val"], R[1]], writes=[w["vn"]])
                    yield
                    k.op("pe", lambda e: e.matmul(R[2][:, 0:64], w["qgT"][:], S[h][:], start=True, stop=False), reads=[w["qgT"], S[h]], writes=[R[2]], inc=False)
                    k.op("pe", lambda e: e.matmul(R[2][:, 0:64], w["innerT"][:], w["vn"][:], start=False, stop=True), reads=[w["innerT"], w["vn"]], writes=[R[2]])
                    k.op("pe", lambda e: e.matmul(R[3][0:64, 0:64], w["kend"][:], w["vn"][:], start=True, stop=True), reads=[w["kend"], w["vn"]], writes=[R[3]])
                    yield
                    k.op("dve", lambda e: e.scalar_tensor_tensor(S[h][0:64, :], S[h][0:64, :], w["glc"][0:64, 1:2], R[3][0:64, 0:64], ALU.mult, ALU.add),
                         reads=[S[h], w["glc"], R[3]], writes=[S[h]])
                    k.op("pool", lambda e: e.memset(w["rstd"][:, 0:1], 0.0), writes=[w["rstd"]])
                    k.op("act", lambda e: e.activation(w["junk"][:], R[2][:, 0:64], AF.Square, accum_out=w["rstd"][:, 0:1]), reads=[R[2]], writes=[w["junk"], w["rstd"]])
                    yield
                    k.op("act", lambda e: e.activation(w["rstd"][:, 1:2], w["rstd"][:, 0:1], AF.Sqrt, bias=epsc[:, 0:1], scale=1.0 / 64), reads=[w["rstd"], epsc], writes=[w["rstd"]])
                    k.op("dve", lambda e: e.reciprocal(w["rstd"][:, 1:2], w["rstd"][:, 1:2]), reads=[w["rstd"]], writes=[w["rstd"]])
                    k.op("dve", lambda e: e.scalar_tensor_tensor(og[:, h * 64:(h + 1) * 64], R[2][:, 0:64], w["rstd"][:, 1:2], nz[:, h, :], ALU.mult, ALU.mult),
                         reads=[R[2], w["rstd"], nz], writes=[og])
                    yield

                for i in range(NJ):
                    xt = xTt[i % 2]
                    k.dma("sp", xt[:], xT_d.t.rearrange("(c p) t -> p c t", p=128)[:, :, i * 128:(i + 1) * 128], reads=[xT_d], writes=[xt])
                    for kc in range(8):
                        k.op("pe", lambda e: e.matmul(PAB[:, 0:16], xt[:, kc, :], wab[:, kc, :], start=(kc == 0), stop=(kc == 7)),
                             reads=[xt, wab], writes=[PAB], inc=(kc == 7))
                    for kc in range(8):
                        k.op("pe", lambda e: e.matmul(PZ[:, :], xt[:, kc, :], wz[:, kc, :], start=(kc == 0), stop=(kc == 7)),
                             reads=[xt, wz], writes=[PZ], inc=(kc == 7))
                    k.op("dve", lambda e: e.tensor_tensor(tmp8[:], PAB[:, 0:8], dtb[:], ALU.add), reads=[PAB, dtb], writes=[tmp8])
                    k.op("act", lambda e: e.activation(tmp8[:], tmp8[:], AF.Exp), reads=[tmp8], writes=[tmp8])
                    k.op("act", lambda e: e.activation(tmp8[:], tmp8[:], AF.Ln, bias=1.0, scale=1.0), reads=[tmp8], writes=[tmp8])
                    k.op("dve", lambda e: e.tensor_tensor(gtok[:], tmp8[:], negA[:], ALU.mult), reads=[tmp8, negA], writes=[gtok])
                    k.op("act", lambda e: e.activation(btok[:], PAB[:, 8:16], AF.Sigmoid), reads=[PAB], writes=[btok])
                    k.op("pe", lambda e: e.matmul(PAB[:, 16:24], Umat, gtok[:], start=True, stop=True), reads=[c32, gtok], writes=[PAB])
                    k.op("act", lambda e: e.copy(gctok[:], PAB[:, 16:24]), reads=[PAB], writes=[gctok])
                    k.op("dve", lambda e: e.tensor_scalar(ngctok[:], PAB[:, 16:24], -1.0, None, ALU.mult), reads=[PAB], writes=[ngctok])
                    k.op("act", lambda e: e.activation(egctok[:], gctok[:], AF.Exp), reads=[gctok], writes=[egctok])
                    k.op("act", lambda e: e.activation(nz[:].rearrange("p h d -> p (h d)"), PZ[:, :], AF.Silu), reads=[PZ], writes=[nz])
                    k.op("pool", lambda e: e.tensor_tensor(nz[:], nz[:], nw[:].unsqueeze(1).to_broadcast([128, 8, 64]), ALU.mult), reads=[nz, nw], writes=[nz])
                    for wave in range(8 // NSLOT):
                        gens = [head_chain(wave * NSLOT + s, i, s, xt) for s in range(NSLOT)]
                        alive = list(gens)
                        gsteps = int(os.environ.get("DEV_GSTEPS", "1000"))
                        nst = 0
                        while alive and nst < gsteps:
                            nst += 1
                            nxt = []
                            for gdef in alive:
                                try:
                                    next(gdef)
                                    nxt.append(gdef)
                                except StopIteration:
                                    pass
                            alive = nxt
                    for c in range(4):
                        k.op("pe", lambda e: e.transpose(PTR[:, c * 128:(c + 1) * 128], og[:, c * 128:(c + 1) * 128], ident), reads=[og, c32], writes=[PTR])
                    k.op("act", lambda e: e.copy(ogT[:], PTR[:, :].rearrange("p (c t) -> p c t", c=4)), reads=[PTR], writes=[ogT])
                    k.dma("pool", mixT_d.t[512:1024, :].rearrange("(c p) t -> p c t", p=128)[:, :, i * 128:(i + 1) * 128], ogT[:],
                          reads=[ogT], writes=[mixT_d])
                for h in range(8):
                    k.dma("pool", gst_p[l, h], S[h][0:64, :], reads=[S[h]], writes=[gst_p])
                    for blk in range(3):
                        c0 = blk * 512 + h * 64
                        k.dma("pool", gcv_p[l][:, c0:c0 + 64].rearrange("w d -> d w"), carry[h][:, blk, :], reads=[carry[h]], writes=[gcv_p],
                              allow_slow_non_contiguous=True)
            k.barrier()

        def chain(l, xres_p, xres_s, yout_p, yout_s, last):
            TS = 256
            with ExitStack() as es:
                wo = k.sb(es, "wo", [128, 8, DM], BF16)
                k.dma("sp", wo[:], wb_out.t[l].rearrange("(c p) n -> p c n", p=128), reads=[wb_out], writes=[wo])
                wgt = k.sb(es, "wgt", [128, 8, DM], BF16)
                k.dma("sp", wgt[:], wb_gate.t[l].rearrange("(c p) n -> p c n", p=128), reads=[wb_gate], writes=[wgt])
                wpj = k.sb(es, "wpj", [128, 2, DM], BF16)
                k.dma("sp", wpj[:], wb_proj.t[l].rearrange("(c p) n -> p c n", p=128), reads=[wb_proj], writes=[wpj])
                wdn = k.sb(es, "wdn", [128, 22, DM], BF16)
                k.dma("sp", wdn[:], wb_down.t[l].rearrange("(c p) n -> p c n", p=128), reads=[wb_down], writes=[wdn])
                lng = k.sb(es, "lng", [128, 3, DM])
                lnb = k.sb(es, "lnb", [128, 3, DM])
                for i in range(3):
                    k.dma("sp", lng[:, i, :], ln_g[l, i:i + 1, :].partition_broadcast(128), writes=[lng])
                    k.dma("sp", lnb[:, i, :], ln_b[l, i:i + 1, :].partition_broadcast(128), writes=[lnb])
                fcw = k.sb(es, "fcw", [128, 22, 3])
                for tap in range(3):
                    k.dma("sp", fcw[:, :, tap], fconv_w[l, tap].rearrange("(c p) -> p c", p=128), writes=[fcw], allow_slow_non_contiguous=True)
                wup = [k.sb(es, "wup%d" % i, [128, 8, 256], BF16) for i in range(3)]
                mixT = k.sb(es, "mixT", [128, 8, TS], BF16)
                x1T = k.sb(es, "x1T", [128, 8, TS], BF16)
                x2T = k.sb(es, "x2T", [128, 8, 128], BF16)
                hid = k.sb(es, "hid", [128, 22, TS], BF16)
                ext = [k.sb(es, "fext%d" % i, [128, TS + 2]) for i in range(2)]
                hg = [k.sb(es, "hg%d" % i, [128, TS]) for i in range(2)]
                carry = k.sb(es, "fcarry", [128, 22, 2])
                k.op("pool", lambda e: e.memset(carry[:], 0.0), writes=[carry])
                xres = k.sb(es, "xres", [128, DM])
                x1 = [k.sb(es, "x1_%d" % i, [128, DM]) for i in range(2)]
                x2 = k.sb(es, "x2", [128, DM])
                x3 = xres
                rr = k.sb(es, "rr", [128, DM])
                sig = k.sb(es, "sig", [128, DM])
                junk = sig
                st = k.sb(es, "lnst", [128, 8])
                ptk = k.sb(es, "ptk", [128, PLED])
                pT = k.sb(es, "pT", [128, 2, 128], BF16)
                trs = k.sb(es, "ctrs", [128, 8, 128], BF16)
                sgo = k.sb(es, "sgo", [NS, DFF])
                sbT = k.sb(es, "sbT", [128, 22, 2, NS])
                sgT = k.sb(es, "sgT", [128, 22, NS])
                PY = (pb[0], pb[1])
                PU = (pb[2], pb[3], pb[4], pb[5])
                PT_, PM = pb[6], pb[7]

                def layer_norm(src, nt, idx, dst):
                    k.op("pool", lambda e: e.memset(st[:, 0:4], 0.0), writes=[st])
                    k.op("act", lambda e: e.activation(junk[0:nt, :], src[0:nt, :], AF.Identity, accum_out=st[0:nt, 0:1]), reads=[src], writes=[junk, st])
                    k.op("dve", lambda e: e.tensor_scalar(st[0:nt, 1:2], st[0:nt, 0:1], -1.0 / DM, None, ALU.mult), reads=[st], writes=[st])
                    k.op("act", lambda e: e.activation(junk[0:nt, :], src[0:nt, :], AF.Square, bias=st[0:nt, 1:2], scale=1.0, accum_out=st[0:nt, 2:3]),
                         reads=[src, st], writes=[junk, st])
                    k.op("act", lambda e: e.activation(st[0:nt, 3:4], st[0:nt, 2:3], AF.Sqrt, bias=epsc[0:nt, 1:2], scale=1.0 / DM), reads=[st, epsc], writes=[st])
                    k.op("dve", lambda e: e.reciprocal(st[0:nt, 3:4], st[0:nt, 3:4]), reads=[st], writes=[st])
                    k.op("dve", lambda e: e.tensor_scalar(dst[0:nt, :], src[0:nt, :], st[0:nt, 1:2], st[0:nt, 3:4], ALU.add, ALU.mult), reads=[src, st], writes=[dst])
                    k.op("dve", lambda e: e.tensor_tensor(dst[0:nt, :], dst[0:nt, :], lng[0:nt, idx, :], ALU.mult), reads=[dst, lng], writes=[dst])
                    k.op("pool", lambda e: e.tensor_tensor(dst[0:nt, :], dst[0:nt, :], lnb[0:nt, idx, :], ALU.add), reads=[dst, lnb], writes=[dst])

                def to_T(src, nt, dstT, c0):
                    for half in range(2):
                        for c4 in range(4):
                            c = half * 4 + c4
                            k.op("pe", lambda e: e.transpose(PT_[:, c4 * 128:c4 * 128 + nt], src[0:nt, c * 128:(c + 1) * 128], ident[0:nt, 0:nt]),
                                 reads=[src, c32], writes=[PT_])
                        k.op("act", lambda e: e.copy(dstT[:, half * 4:(half + 1) * 4, c0:c0 + nt],
                                                     PT_[:, :].rearrange("p (c t) -> p c t", c=4)[:, :, 0:nt]), reads=[PT_], writes=[dstT])

                def supertile(kind, tok0, tiles):
                    ntot = sum(nt for _, nt in tiles)
                    if kind == "p":
                        srcT, xr_d, p_d, y_d, nxtT = mixT_d, xres_p, pp, yout_p, xT_d
                    else:
                        srcT, xr_d, p_d, y_d, nxtT = mixsT_d, xres_s, pps, yout_s, xsT_d
                    k.dma("sp", mixT[:, :, 0:ntot], srcT.t.rearrange("(c p) t -> p c t", p=128)[:, :, tok0:tok0 + ntot], reads=[srcT], writes=[mixT])
                    for ti, (o0, nt) in enumerate(tiles):
                        k.dma("sp", xres[0:nt, :], xr_d[tok0 + o0:tok0 + o0 + nt, :], reads=[xr_d], writes=[xres])
                        for hb in range(2):
                            for kc in range(8):
                                k.op("pe", lambda e: e.matmul(PY[hb][0:nt, :], mixT[:, kc, o0:o0 + nt], wo[:, kc, hb * 512:(hb + 1) * 512],
                                                              start=(kc == 0), stop=(kc == 7)), reads=[mixT, wo], writes=[PY[hb]], inc=(kc == 7))
                        for hb in range(2):
                            k.op("dve", lambda e: e.scalar_tensor_tensor(rr[0:nt, hb * 512:(hb + 1) * 512], xres[0:nt, hb * 512:(hb + 1) * 512], ALPHA,
                                                                         PY[hb][0:nt, :], ALU.mult, ALU.add), reads=[xres, PY[hb]], writes=[rr])
                        layer_norm(rr, nt, 0, x1[ti])
                        to_T(x1[ti], nt, x1T, o0)
                    if kind == "s":
                        for r in range(2):
                            k.dma("sp", sgo[:], st_fconv[l, :, r, :], writes=[sgo])
                            for c in range(22):
                                k.op("pe", lambda e: e.matmul(PM[:, (c % 16) * NS:(c % 16 + 1) * NS], sgo[0:NS, c * 128:(c + 1) * 128], ident[0:NS, 0:NS], start=True, stop=True),
                                     reads=[sgo, c32], writes=[PM])
                                if c % 16 == 15 or c == 21:
                                    cs = (c // 16) * 16
                                    k.op("act", lambda e: e.copy(sbT[:, cs:c + 1, r, :], PM[:, 0:(c - cs + 1) * NS].rearrange("p (c s) -> p c s", s=NS)),
                                         reads=[PM], writes=[sbT])
                        k.dma("pool", fcv_s[l, :, 0, :], st_fconv[l, :, 1, :], reads=[st_fconv], writes=[fcv_s])
                    for c in range(22):
                        wu = wup[c % 3]
                        wsrc = wb_up.t[l].rearrange("(kc p) n -> p kc n", p=128)
                        k.dma("sp", wu[:, :, 0:128], wsrc[:, :, c * 128:(c + 1) * 128], reads=[wb_up], writes=[wu])
                        k.dma("sp", wu[:, :, 128:256], wsrc[:, :, DFF + c * 128:DFF + (c + 1) * 128], reads=[wb_up], writes=[wu])
                        PG, PV = PU[(c % 2) * 2], PU[(c % 2) * 2 + 1]
                        for kc in range(8):
                            k.op("pe", lambda e: e.matmul(PG[:, 0:ntot], wu[:, kc, 0:128], x1T[:, kc, 0:ntot], start=(kc == 0), stop=(kc == 7)),
                                 reads=[wu, x1T], writes=[PG], inc=(kc == 7))
                        for kc in range(8):
                            k.op("pe", lambda e: e.matmul(PV[:, 0:ntot], wu[:, kc, 128:256], x1T[:, kc, 0:ntot], start=(kc == 0), stop=(kc == 7)),
                                 reads=[wu, x1T], writes=[PV], inc=(kc == 7))
                        h_ = hg[c % 2]
                        if kind == "p":
                            ex = ext[c % 2]
                            k.op("pool", lambda e: e.tensor_copy(ex[:, 0:2], carry[:, c, :]), reads=[carry], writes=[ex])
                            k.op("act", lambda e: e.copy(ex[:, 2:2 + ntot], PG[:, 0:ntot]), reads=[PG], writes=[ex])
                            k.op("pool", lambda e: e.tensor_copy(carry[:, c, :], ex[:, ntot:ntot + 2]), reads=[ex], writes=[carry])
                            k.op("dve", lambda e: e.tensor_scalar(h_[:, 0:ntot], ex[:, 0:ntot], fcw[:, c, 0:1], None, ALU.mult), reads=[ex, fcw], writes=[h_])
                            for tap in (1, 2):
                                k.op("dve", lambda e: e.scalar_tensor_tensor(h_[:, 0:ntot], ex[:, tap:tap + ntot], fcw[:, c, tap:tap + 1], h_[:, 0:ntot],
                                                                             ALU.mult, ALU.add), reads=[ex, fcw, h_], writes=[h_])
                        else:
                            k.op("act", lambda e: e.copy(sgT[:, c, :], PG[:, 0:NS]), reads=[PG], writes=[sgT])
                            k.op("dve", lambda e: e.tensor_scalar(h_[:, 0:NS], sbT[:, c, 0, :], fcw[:, c, 0:1], None, ALU.mult), reads=[sbT, fcw], writes=[h_])
                            k.op("dve", lambda e: e.scalar_tensor_tensor(h_[:, 0:NS], sbT[:, c, 1, :], fcw[:, c, 1:2], h_[:, 0:NS], ALU.mult, ALU.add),
                                 reads=[sbT, fcw, h_], writes=[h_])
                            k.op("dve", lambda e: e.scalar_tensor_tensor(h_[:, 0:NS], sgT[:, c, :], fcw[:, c, 2:3], h_[:, 0:NS], ALU.mult, ALU.add),
                                 reads=[sgT, fcw, h_], writes=[h_])
                        k.op("act", lambda e: e.activation(h_[:, 0:ntot], h_[:, 0:ntot], AF.Gelu), reads=[h_], writes=[h_])
                        k.op("dve", lambda e: e.tensor_tensor(hid[:, c, 0:ntot], h_[:, 0:ntot], PV[:, 0:ntot], ALU.mult), reads=[h_, PV], writes=[hid])
                    if kind == "s":
                        for c in range(22):
                            k.op("pe", lambda e: e.transpose(PM[0:NS, (c % 4) * 128:(c % 4 + 1) * 128], sgT[:, c, :], ident), reads=[sgT, c32], writes=[PM])
                            if c % 4 == 3 or c == 21:
                                cs = (c // 4) * 4
                                k.op("act", lambda e: e.copy(sgo[:, cs * 128:(c + 1) * 128], PM[0:NS, 0:(c - cs + 1) * 128]), reads=[PM], writes=[sgo])
                        k.dma("pool", fcv_s[l, :, 1, :], sgo[:], reads=[sgo], writes=[fcv_s])
                    for ti, (o0, nt) in enumerate(tiles):
                        for c in range(22):
                            for hb in range(2):
                                k.op("pe", lambda e: e.matmul(PY[hb][0:nt, :], hid[:, c, o0:o0 + nt], wdn[:, c, hb * 512:(hb + 1) * 512],
                                                              start=(c == 0), stop=(c == 21)), reads=[hid, wdn], writes=[PY[hb]], inc=(c == 21))
                        for hb in range(2):
                            k.op("dve", lambda e: e.scalar_tensor_tensor(rr[0:nt, hb * 512:(hb + 1) * 512], x1[ti][0:nt, hb * 512:(hb + 1) * 512], ALPHA,
                                                                         PY[hb][0:nt, :], ALU.mult, ALU.add), reads=[x1[ti], PY[hb]], writes=[rr])
                        layer_norm(rr, nt, 1, x2)
                        to_T(x2, nt, x2T, 0)
                        k.dma("sp", ptk[0:nt, :], p_d[l, tok0 + o0:tok0 + o0 + nt, :], reads=[p_d], writes=[ptk])
                        for c in range(2):
                            k.op("pe", lambda e: e.transpose(PM[:, c * 128:c * 128 + nt], ptk[0:nt, c * 128:(c + 1) * 128], ident[0:nt, 0:nt]), reads=[ptk, c32], writes=[PM])
                        k.op("act", lambda e: e.copy(pT[:, :, 0:nt], PM[:, 0:256].rearrange("p (c t) -> p c t", c=2)[:, :, 0:nt]), reads=[PM], writes=[pT])
                        for hb in range(2):
                            for kc in range(8):
                                k.op("pe", lambda e: e.matmul(PY[hb][0:nt, :], x2T[:, kc, 0:nt], wgt[:, kc, hb * 512:(hb + 1) * 512],
                                                              start=(kc == 0), stop=(kc == 7)), reads=[x2T, wgt], writes=[PY[hb]], inc=(kc == 7))
                            k.op("act", lambda e: e.activation(sig[0:nt, hb * 512:(hb + 1) * 512], PY[hb][0:nt, :], AF.Sigmoid), reads=[PY[hb]], writes=[sig])
                        for hb in range(2):
                            for c in range(2):
                                k.op("pe", lambda e: e.matmul(PY[hb][0:nt, :], pT[:, c, 0:nt], wpj[:, c, hb * 512:(hb + 1) * 512],
                                                              start=(c == 0), stop=(c == 1)), reads=[pT, wpj], writes=[PY[hb]], inc=(c == 1))
                            k.op("dve", lambda e: e.tensor_tensor(sig[0:nt, hb * 512:(hb + 1) * 512], sig[0:nt, hb * 512:(hb + 1) * 512], PY[hb][0:nt, :], ALU.mult),
                                 reads=[sig, PY[hb]], writes=[sig])
                        k.op("dve", lambda e: e.scalar_tensor_tensor(rr[0:nt, :], x2[0:nt, :], ALPHA, sig[0:nt, :], ALU.mult, ALU.add), reads=[x2, sig], writes=[rr])
                        layer_norm(rr, nt, 2, x3)
                        k.dma("pool", y_d[tok0 + o0:tok0 + o0 + nt, :], x3[0:nt, :], reads=[x3], writes=[y_d])
                        if not last:
                            to_T(x3, nt, trs, 0)
                            k.dma("pool", nxtT.t.rearrange("(c p) t -> p c t", p=128)[:, :, tok0 + o0:tok0 + o0 + nt], trs[:, :, 0:nt], reads=[trs], writes=[nxtT])

                if do_prompt:
                    for s0 in range(0, T, TS):
                        supertile("p", s0, [(o, 128) for o in range(0, TS, 128)])
                    for tap in range(2):
                        k.dma("pool", fcv_p[l, tap].rearrange("(c p) -> p c", p=128), carry[:, :, tap], reads=[carry], writes=[fcv_p],
                              allow_slow_non_contiguous=True)
                if do_sample:
                    supertile("s", 0, [(0, NS)])
            k.barrier()

        def sample_mixers(l):
            NQ = NS * 8
            with ExitStack() as es:
                cs32 = k.sb(es, "cs32", [128, NCS32])
                k.dma("sp", cs32[:], cs32_d[:], writes=[cs32])
                csb = k.sb(es, "csb", [128, NCSB], BF16)
                k.dma("sp", csb[:], csb_d[:], writes=[csb])
                o_ = 0
                alibi_s = cs32.t[:, o_:o_ + 2 * NPG * 4]
                o_ += 2 * NPG * 4
                alibi_w = cs32.t[:, o_:o_ + 2 * NWT * 4]
                o_ += 2 * NWT * 4
                alibi_c = cs32.t[:, o_:o_ + NQ]
                o_ += NQ
                GS = cs32.t[:, o_:o_ + 2 * NS]
                o_ += 2 * NS
                tkb_s = cs32.t[:, o_:o_ + 40]
                o_ += 40
                piota = cs32.t[:, o_:o_ + 2]
                NB33 = NPG * 2 + 1
                OH = csb.t[0:2 * NS, 0:2 * NS * 128]
                pool33 = csb.t[0:NPG * 4, 2 * NS * 128:2 * NS * 128 + NB33 + 1]
                B0, B1, B2, B3, B4, B5, B6, B7 = pb
                xsT = k.sb(es, "xsT", [128, 8, NS], BF16)
                k.dma("sp", xsT[:], xsT_d.t.rearrange("(c p) s -> p c s", p=128), reads=[xsT_d], writes=[xsT])
                hs = k.sb(es, "hs", [NS, INW])
                wch = [k.sb(es, "wch%d" % i, [128, 8, 512], BF16) for i in range(2)]
                wsrc = wb_in.t[l].rearrange("(c p) n -> p c n", p=128)
                for ch in range(7):
                    c0 = ch * 512
                    cw_ = min(512, INW - c0)
                    wc = wch[ch % 2]
                    k.dma("sp", wc[:, :, 0:cw_], wsrc[:, :, c0:c0 + cw_], reads=[wb_in], writes=[wc])
                    for kc in range(8):
                        k.op("pe", lambda e: e.matmul(B0[0:NS, 0:cw_], xsT[:, kc, :], wc[:, kc, 0:cw_], start=(kc == 0), stop=(kc == 7)),
                             reads=[xsT, wc], writes=[B0], inc=(kc == 7))
                    k.op("act", lambda e: e.copy(hs[:, c0:c0 + cw_], B0[0:NS, 0:cw_]), reads=[B0], writes=[hs])
                k.dma("pool", kv_s[l], hs[:, C_KV:C_WIN], reads=[hs], writes=[kv_s])
                k.dma("pool", win_s[l, :, WB - 1, :], hs[:, C_WIN:C_GATE], reads=[hs], writes=[win_s])
                for s in range(NS):
                    k.dma("pool", win_s[l, s, 0:WB - 1, :], st_win[l, s, 1:WB, :], reads=[st_win], writes=[win_s])
                k.dma("pool", gcv_s[l, :, 2, :], hs[:, C_GQKV:C_GA], reads=[hs], writes=[gcv_s])
                for r in range(2):
                    k.dma("pool", gcv_s[l, :, r, :], st_gconv[l, :, r + 1, :], reads=[st_gconv], writes=[gcv_s])
                with ExitStack() as e2:
                    gbuf = k.sb(e2, "gbuf", [NS, 3, 1536])
                    k.dma("sp", gbuf[:], st_gconv[l], writes=[gbuf])
                    cwt = k.sb(e2, "cwt", [NS, 4, 1536])
                    for tap in range(4):
                        k.dma("sp", cwt[:, tap, :], gconv_w[l, tap:tap + 1, :].partition_broadcast(NS), writes=[cwt])
                    qkv = k.sb(e2, "qkv", [NS, 1536])
                    tq = k.sb(e2, "tq", [NS, 1536])
                    k.op("dve", lambda e: e.tensor_tensor(qkv[:], cwt[:, 3, :], hs[:, C_GQKV:C_GA], ALU.mult), reads=[cwt, hs], writes=[qkv])
                    for tap in range(3):
                        k.op("dve", lambda e: e.tensor_tensor(tq[:], cwt[:, tap, :], gbuf[:, tap, :], ALU.mult), reads=[cwt, gbuf], writes=[tq])
                        k.op("dve", lambda e: e.tensor_tensor(qkv[:], qkv[:], tq[:], ALU.add), reads=[qkv, tq], writes=[qkv])
                    k.op("act", lambda e: e.activation(qkv[:], qkv[:], AF.Silu), reads=[qkv], writes=[qkv])
                    ss = k.sb(e2, "ss", [NS, 16])
                    k.op("dve", lambda e: e.tensor_tensor(tq[:, 0:1024], qkv[:, 0:1024], qkv[:, 0:1024], ALU.mult), reads=[qkv], writes=[tq])
                    k.op("dve", lambda e: e.tensor_reduce(ss[:], tq[:, 0:1024].rearrange("s (a d) -> s a d", d=64), AX.X, ALU.add), reads=[tq], writes=[ss])
                    k.op("act", lambda e: e.activation(ss[:], ss[:], AF.Sqrt, bias=epsc[0:NS, 0:1], scale=1.0), reads=[ss, epsc], writes=[ss])
                    k.op("dve", lambda e: e.reciprocal(ss[:], ss[:]), reads=[ss], writes=[ss])
                    qkn = k.sb(e2, "qkn", [NS, 16, 64])
                    k.op("dve", lambda e: e.tensor_tensor(qkn[:], qkv[:, 0:1024].rearrange("s (a d) -> s a d", d=64), ss[:].unsqueeze(2).to_broadcast([NS, 16, 64]), ALU.mult),
                         reads=[qkv, ss], writes=[qkn])
                    k.op("dve", lambda e: e.tensor_scalar(qkn[:, 0:8, :], qkn[:, 0:8, :], 0.125, None, ALU.mult), reads=[qkn], writes=[qkn])
                    dtb = k.sb(e2, "sdtb", [NS, 8])
                    k.dma("sp", dtb[:], dt_bias[l:l + 1, :].partition_broadcast(NS), writes=[dtb])
                    negA = k.sb(e2, "snegA", [NS, 8])
                    k.dma("sp", negA[:], A_log[l:l + 1, :].partition_broadcast(NS), writes=[negA])
                    k.op("act", lambda e: e.activation(negA[:], negA[:], AF.Exp), reads=[negA], writes=[negA])
                    k.op("dve", lambda e: e.tensor_scalar(negA[:], negA[:], -1.0, None, ALU.mult), reads=[negA], writes=[negA])
                    nw = k.sb(e2, "snw", [NS, 64])
                    k.dma("sp", nw[:], gnorm_w[l:l + 1, :].partition_broadcast(NS), writes=[nw])
                    gea = k.sb(e2, "gea", [NS, 8])
                    gbe = k.sb(e2, "gbe", [NS, 8])
                    k.op("dve", lambda e: e.tensor_tensor(gea[:], hs[:, C_GA:C_GB], dtb[:], ALU.add), reads=[hs, dtb], writes=[gea])
                    k.op("act", lambda e: e.activation(gea[:], gea[:], AF.Exp), reads=[gea], writes=[gea])
                    k.op("act", lambda e: e.activation(gea[:], gea[:], AF.Ln, bias=1.0, scale=1.0), reads=[gea], writes=[gea])
                    k.op("dve", lambda e: e.tensor_tensor(gea[:], gea[:], negA[:], ALU.mult), reads=[gea, negA], writes=[gea])
                    k.op("act", lambda e: e.activation(gea[:], gea[:], AF.Exp), reads=[gea], writes=[gea])
                    k.op("act", lambda e: e.activation(gbe[:], hs[:, C_GB:C_GZ], AF.Sigmoid), reads=[hs], writes=[gbe])
                    kqT = k.sb(e2, "kqT", [64, 2, NS, 8])
                    for a in range(2):
                        for h in range(8):
                            k.op("pe", lambda e: e.matmul(B7[0:64, h * NS:(h + 1) * NS], qkn[:, (1 - a) * 8 + h, :], ident[0:NS, 0:NS], start=True, stop=True), reads=[qkn, c32], writes=[B7])
                        k.op("act", lambda e: e.copy(kqT[:, a, :, :].rearrange("p s h -> p h s"), B7[0:64, 0:8 * NS].rearrange("p (h s) -> p h s", h=8)),
                             reads=[B7], writes=[kqT])
                    bd = k.sb(e2, "bd", [NS, 2, NS, 8])
                    k.op("dve", lambda e: e.tensor_tensor(bd[:, 0, :, :], gea[:].unsqueeze(1).to_broadcast([NS, NS, 8]),
                                                          ident[0:NS, 0:NS].unsqueeze(2).to_broadcast([NS, NS, 8]), ALU.mult), reads=[gea, c32], writes=[bd])
                    k.op("dve", lambda e: e.tensor_tensor(bd[:, 1, :, :], gbe[:].unsqueeze(1).to_broadcast([NS, NS, 8]),
                                                          ident[0:NS, 0:NS].unsqueeze(2).to_broadcast([NS, NS, 8]), ALU.mult), reads=[gbe, c32], writes=[bd])
                    k.op("pe", lambda e: e.matmul(B7[0:64, 0:2 * NQ], ones32[0:NS, 0:64], bd[:].rearrange("p a s h -> p (a s h)"), start=True, stop=True),
                         reads=[bd, c32], writes=[B7])
                    eab = k.sb(e2, "eab", [64, 2, NS, 8])
                    k.op("act", lambda e: e.copy(eab[:].rearrange("p a s h -> p (a s h)"), B7[0:64, 0:2 * NQ]), reads=[B7], writes=[eab])
                    bk = k.sb(e2, "bk", [64, NS, 8])
                    k.op("dve", lambda e: e.tensor_tensor(bk[:], kqT[:, 0, :, :], eab[:, 1, :, :], ALU.mult), reads=[kqT, eab], writes=[bk])
                    if os.environ.get("DEV_DBG", "") == "1" and l == 0:
                        k.dma("pool", y_p[0:64, 0:256], kqT[:].rearrange("p a s h -> p (a s h)"), reads=[kqT], writes=[y_p])
                        k.dma("pool", y_p[64:128, 0:256], eab[:].rearrange("p a s h -> p (a s h)"), reads=[eab], writes=[y_p])
                        k.dma("pool", y_p[128:192, 0:128], bk[:].rearrange("p s h -> p (s h)"), reads=[bk], writes=[y_p])
                        k.dma("pool", y_p[192:208, 0:1024], qkn[:].rearrange("p a d -> p (a d)"), reads=[qkn], writes=[y_p])
                        k.dma("pool", y_p[208:224, 0:1024], qkv[:, 0:1024], reads=[qkv], writes=[y_p])
                        k.dma("pool", y_p[224:240, 0:16], ss[:], reads=[ss], writes=[y_p])
                    S0 = k.sb(e2, "S0", [64, NQ, 64])
                    k.dma("sp", S0[:], st_gdn[l].rearrange("s h a b -> a (s h) b"), writes=[S0])
                    otok = k.sb(e2, "otok", [NS, 512])
                    k.op("pool", lambda e: e.memset(otok[:], 0.0), writes=[otok])
                    tmpS = k.sb(e2, "tmpS", [64, 8, 64])
                    t1 = k.sb(e2, "t1", [64, 8, 64])
                    bdv = k.sb(e2, "bdv", [NS, 512])
                    for s in range(NS):
                        S0s = S0[:, s * 8:(s + 1) * 8, :]
                        k.op("dve", lambda e: e.tensor_tensor(tmpS[:], S0s, kqT[:, 0, s, :].unsqueeze(2).to_broadcast([64, 8, 64]), ALU.mult),
                             reads=[S0, kqT], writes=[tmpS])
                        k.op("pe", lambda e: e.matmul(B7[0:64, :], ones32[0:64, 0:64], tmpS[:].rearrange("p h d -> p (h d)"), start=True, stop=True),
                             reads=[tmpS, c32], writes=[B7])
                        k.op("dve", lambda e: e.tensor_tensor(t1[:], B7[0:64, :].rearrange("p (h d) -> p h d", h=8), bk[:, s, :].unsqueeze(2).to_broadcast([64, 8, 64]), ALU.mult),
                             reads=[B7, bk], writes=[t1])
                        k.op("dve", lambda e: e.tensor_tensor(t1[:], S0s, t1[:], ALU.subtract), reads=[S0, t1], writes=[t1])
                        k.op("dve", lambda e: e.tensor_tensor(t1[:], t1[:], eab[:, 0, s, :].unsqueeze(2).to_broadcast([64, 8, 64]), ALU.mult), reads=[t1, eab], writes=[t1])
                        k.op("dve", lambda e: e.tensor_scalar(bdv[:], qkv[:, 1024:1536], ident[0:NS, s:s + 1], None, ALU.mult), reads=[qkv, c32], writes=[bdv])
                        k.op("pe", lambda e: e.matmul(B7[0:64, :], ones32[0:NS, 0:64], bdv[:], start=True, stop=True), reads=[bdv, c32], writes=[B7])
                        k.op("dve", lambda e: e.tensor_tensor(tmpS[:], B7[0:64, :].rearrange("p (h d) -> p h d", h=8), bk[:, s, :].unsqueeze(2).to_broadcast([64, 8, 64]), ALU.mult),
                             reads=[B7, bk], writes=[tmpS])
                        k.op("dve", lambda e: e.tensor_tensor(S0s, t1[:], tmpS[:], ALU.add), reads=[t1, tmpS], writes=[S0])
                        k.op("dve", lambda e: e.tensor_tensor(tmpS[:], S0s, kqT[:, 1, s, :].unsqueeze(2).to_broadcast([64, 8, 64]), ALU.mult),
                             reads=[S0, kqT], writes=[tmpS])
                        k.op("pe", lambda e: e.matmul(B7[0:NS, :], ones32[0:64, 0:NS], tmpS[:].rearrange("p h d -> p (h d)"), start=True, stop=True),
                             reads=[tmpS, c32], writes=[B7])
                        k.op("dve", lambda e: e.scalar_tensor_tensor(otok[:], B7[0:NS, :], ident[0:NS, s:s + 1], otok[:], ALU.mult, ALU.add),
                             reads=[B7, c32, otok], writes=[otok])
                    k.dma("pool", gst_s[l].rearrange("s h a b -> a (s h) b"), S0[:], reads=[S0], writes=[gst_s])
                    ms = k.sb(e2, "gms", [NS, 8])
                    k.op("dve", lambda e: e.tensor_tensor(tq[:, 0:512], otok[:], otok[:], ALU.mult), reads=[otok], writes=[tq])
                    k.op("dve", lambda e: e.tensor_reduce(ms[:], tq[:, 0:512].rearrange("s (h d) -> s h d", d=64), AX.X, ALU.add), reads=[tq], writes=[ms])
                    k.op("act", lambda e: e.activation(ms[:], ms[:], AF.Sqrt, bias=epsc[0:NS, 0:1], scale=1.0 / 64), reads=[ms, epsc], writes=[ms])
                    k.op("dve", lambda e: e.reciprocal(ms[:], ms[:]), reads=[ms], writes=[ms])
                    zs = k.sb(e2, "szs", [NS, 8, 64])
                    k.op("act", lambda e: e.activation(zs[:].rearrange("s h d -> s (h d)"), hs[:, C_GZ:INW], AF.Silu), reads=[hs], writes=[zs])
                    k.op("dve", lambda e: e.tensor_tensor(zs[:], zs[:], nw[:].unsqueeze(1).to_broadcast([NS, 8, 64]), ALU.mult), reads=[zs, nw], writes=[zs])
                    k.op("dve", lambda e: e.tensor_tensor(zs[:], zs[:], ms[:].unsqueeze(2).to_broadcast([NS, 8, 64]), ALU.mult), reads=[zs, ms], writes=[zs])
                    k.op("dve", lambda e: e.tensor_tensor(otok[:], otok[:], zs[:].rearrange("s h d -> s (h d)"), ALU.mult), reads=[otok, zs], writes=[otok])
                    ogT = k.sb(e2, "sogT", [128, 4, NS], BF16)
                    for c in range(4):
                        k.op("pe", lambda e: e.matmul(B7[:, c * NS:(c + 1) * NS], otok[:, c * 128:(c + 1) * 128], ident[0:NS, 0:NS], start=True, stop=True), reads=[otok, c32], writes=[B7])
                    k.op("act", lambda e: e.copy(ogT[:], B7[:, 0:4 * NS].rearrange("p (c s) -> p c s", c=4)), reads=[B7], writes=[ogT])
                    k.dma("pool", mixsT_d.t[512:1024, :].rearrange("(c p) s -> p c s", p=128), ogT[:], reads=[ogT], writes=[mixsT_d])
                with ExitStack() as e3:
                    gts = k.sb(e3, "gts", [NS, 24])
                    k.op("act", lambda e: e.activation(gts[:], hs[:, C_GATE:C_GQKV], AF.Sigmoid), reads=[hs], writes=[gts])
                    qperm = k.sb(e3, "qperm", [NS, 4, 2, 64])
                    for n in range(2):
                        k.op("dve", lambda e: e.tensor_copy(qperm[:, :, n, :], hs[:, n * 256:(n + 1) * 256].rearrange("s (g d) -> s g d", g=4)), reads=[hs], writes=[qperm])
                    qz = [k.sb(e3, "qz%d" % n, [128, NS, 4], BF16) for n in range(2)]
                    for n in range(2):
                        k.op("pool", lambda e: e.memset(qz[n][:], 0.0), writes=[qz[n]])
                    for g in range(4):
                        k.op("pe", lambda e: e.matmul(B0[:, g * NS:(g + 1) * NS], qperm[:, g, :, :].rearrange("s n d -> s (n d)"), ident[0:NS, 0:NS], start=True, stop=True), reads=[qperm, c32], writes=[B0])
                    k.op("act", lambda e: e.copy(qz[0][0:64, :, :].rearrange("p s g -> p g s"), B0[0:64, 0:4 * NS].rearrange("p (g s) -> p g s", g=4)), reads=[B0], writes=[qz[0]])
                    k.op("act", lambda e: e.copy(qz[1][64:128, :, :].rearrange("p s g -> p g s"), B0[64:128, 0:4 * NS].rearrange("p (g s) -> p g s", g=4)), reads=[B0], writes=[qz[1]])
                    nK = k.sb(e3, "nK", [128, 2, NS], BF16)
                    k.op("pe", lambda e: e.matmul(B0[:, 0:NS], hs[:, C_KV + 256:C_KV + 384], ident[0:NS, 0:NS], start=True, stop=True), reads=[hs, c32], writes=[B0])
                    k.op("pe", lambda e: e.matmul(B0[:, NS:2 * NS], hs[:, C_WIN:C_WIN + 128], ident[0:NS, 0:NS], start=True, stop=True), reads=[hs, c32], writes=[B0])
                    k.op("act", lambda e: e.copy(nK[:].rearrange("p a s -> p (a s)"), B0[:, 0:2 * NS]), reads=[B0], writes=[nK])
                    pti = k.sb(e3, "pti", [128, NS * NPG], I32)
                    k.dma("sp", pti[:], ptab[:].partition_broadcast(128), writes=[pti])
                    ptf = k.sb(e3, "ptf", [128, NS * NPG])
                    k.op("dve", lambda e: e.tensor_copy(ptf[:], pti[:]), reads=[pti], writes=[ptf])
                    k.op("dve", lambda e: e.tensor_scalar(ptf[:], ptf[:], 256.0, piota[:, 0:1], ALU.mult, ALU.add), reads=[ptf, cs32], writes=[ptf])
                    k.op("dve", lambda e: e.tensor_scalar(ptf[:], ptf[:], float(2 * l * NPOOL * 128), None, ALU.add), reads=[ptf], writes=[ptf])
                    idxc = k.sb(e3, "idxc", [128, NS * NPG], I32)
                    idxs = k.sb(e3, "idxs", [128, NS * NPG], I32)
                    k.op("dve", lambda e: e.tensor_copy(idxc[:], ptf[:]), reads=[ptf], writes=[idxc])
                    k.op("dve", lambda e: e.tensor_scalar(ptf[:], ptf[:], 1.0, None, ALU.add), reads=[ptf], writes=[ptf])
                    k.op("dve", lambda e: e.tensor_copy(idxs[:], ptf[:]), reads=[ptf], writes=[idxs])
                    pool2v = pool.t.rearrange("r (two c) -> (r two) c", two=2)
                    phis = k.sb(e3, "sphis", [128, 2, 128])
                    k.op("pool", lambda e: e.memset(phis[:], 0.0), writes=[phis])
                    for a in range(2):
                        for n in range(2):
                            k.dma("sp", phis[64 * n:64 * n + 64, a, 64 * n:64 * n + 64], nsa_phi[l, a], writes=[phis])
                    phib = k.sb(e3, "sphib", [128, 2, 128], BF16)
                    k.op("dve", lambda e: e.tensor_copy(phib[:], phis[:]), reads=[phis], writes=[phib])
                    pet = k.sb(e3, "spet", [128, 2, 32])
                    for a in range(2):
                        for n in range(2):
                            k.dma("sp", pet[64 * n:64 * n + 64, a, :], nsa_pe[l, a].rearrange("r d -> d r"), writes=[pet], allow_slow_non_contiguous=True)
                    pem = k.sb(e3, "spem", [128, 2])
                    k.op("dve", lambda e: e.tensor_reduce(pem[:], pet[:], AX.X, ALU.add), reads=[pet], writes=[pem])
                    k.op("dve", lambda e: e.tensor_scalar(pem[:], pem[:], 1.0 / 32, None, ALU.mult), reads=[pem], writes=[pem])
                    NCB = NPG * 4
                    kcT_all = k.sb(e3, "kcT_all", [128, NS, NCB], BF16)
                    vc_all = k.sb(e3, "vc_all", [NCB, NS, 2, 68], BF16)
                    k.op("pool", lambda e: e.memset(vc_all[:], 1.0), writes=[vc_all])
                    pgt = [k.sb(e3, "pgt%d" % i, [128, 256]) for i in range(3)]
                    cmpkv = [k.sb(e3, "scmpkv%d" % i, [128, 256], BF16) for i in range(2)]
                    kvm = k.sb(e3, "kvm", [128, 2, NCB], BF16)
                    it = 0
                    MG = os.environ.get("DEV_MG", "0") == "1"
                    pgm = [k.sb(e3, "pgm%d" % i, [128, NPG, 256]) for i in range(2)] if MG else None
                    for s in range(NS):
                        if MG:
                            pm_ = pgm[s % 2]
                            k.gather(pm_[:], pool2v, idxc[:, s * NPG:(s + 1) * NPG], reads=[idxc], writes=[pm_])
                        for pg in range(NPG):
                            cb_ = cmpkv[it % 2]
                            if MG:
                                src_, srcb_ = pm_[:, pg, :], pm_
                            else:
                                pt_ = pgt[it % 3]
                                k.gather(pt_[:], pool2v, idxc[:, s * NPG + pg:s * NPG + pg + 1], reads=[idxc], writes=[pt_])
                                src_, srcb_ = pt_[:], pt_
                            it += 1
                            cast(cb_[:], src_, [srcb_], [cb_], psum=True)
                            for a in range(2):
                                k.op("pe", lambda e: e.matmul(B1[:, a * NCB + pg * 4:a * NCB + pg * 4 + 4], cb_[:, a * 128:(a + 1) * 128], avg4, start=True, stop=True),
                                     reads=[cb_, cb128], writes=[B1])
                        for a in range(2):
                            k.op("dve", lambda e: e.tensor_scalar(kvm[:, a, :], B1[:, a * NCB:(a + 1) * NCB], pem[:, a:a + 1], None, ALU.add), reads=[B1, pem], writes=[kvm])
                        k.op("pe", lambda e: e.matmul(B2[:, 0:NCB], phib[:, 0, :], kvm[:, 0, :], start=True, stop=True), reads=[phib, kvm], writes=[B2])
                        k.op("act", lambda e: e.copy(kcT_all[:, s, :], B2[:, 0:NCB]), reads=[B2], writes=[kcT_all])
                        k.op("pe", lambda e: e.matmul(B2[0:NCB, 128:256], kvm[:, 1, :], phib[:, 1, :], start=True, stop=True), reads=[phib, kvm], writes=[B2])
                        k.op("act", lambda e: e.copy(vc_all[:, s, :, 0:64], B2[0:NCB, 128:256].rearrange("p (n d) -> p n d", n=2)), reads=[B2], writes=[vc_all])
                        for n in range(2):
                            k.op("pe", lambda e: e.matmul(B3[0:NCB, (s * 2 + n) * 4:(s * 2 + n) * 4 + 4], kcT_all[:, s, :], qz[n][:, s, :], start=True, stop=True),
                                 reads=[kcT_all, qz[n]], writes=[B3])
                    tmpc = k.sb(e3, "tmpc", [NCB, NQ])
                    PTc = k.sb(e3, "sPTc", [NCB, NQ], BF16)
                    k.op("dve", lambda e: e.scalar_tensor_tensor(tmpc[:], B3[0:NCB, 0:NQ], 0.125, alibi_c[0:NCB, :], ALU.mult, ALU.add), reads=[B3, cs32], writes=[tmpc])
                    k.op("act", lambda e: e.activation(PTc[:], tmpc[:], AF.Exp), reads=[tmpc], writes=[PTc])
                    k.op("pe", lambda e: e.matmul(B3[:, 256:256 + NB33 + 1], PTc[:], pool33, start=True, stop=True), reads=[PTc, csb], writes=[B3])
                    ul = k.sb(e3, "ul", [128, NB33 + 1])
                    k.op("act", lambda e: e.copy(ul[:], B3[:, 256:256 + NB33 + 1]), reads=[B3], writes=[ul])
                    k.op("dve", lambda e: e.tensor_scalar(ul[:, NB33:NB33 + 1], ul[:, NB33:NB33 + 1], 1e-30, None, ALU.max), reads=[ul], writes=[ul])
                    k.op("dve", lambda e: e.reciprocal(ul[:, NB33:NB33 + 1], ul[:, NB33:NB33 + 1]), reads=[ul], writes=[ul])
                    k.op("dve", lambda e: e.tensor_scalar(ul[:, 0:NB33], ul[:, 0:NB33], ul[:, NB33:NB33 + 1], None, ALU.mult), reads=[ul], writes=[ul])
                    k.op("pe", lambda e: e.matmul(B3[0:2 * NS, 320:320 + NB33], GS, ul[:, 0:NB33], start=True, stop=True), reads=[ul, cs32], writes=[B3])
                    sc = k.sb(e3, "ssc", [2 * NS, 40])
                    k.op("dve", lambda e: e.tensor_tensor(sc[:, 0:NB33], B3[0:2 * NS, 320:320 + NB33], tkb_s[0:2 * NS, 0:NB33], ALU.add), reads=[B3, cs32], writes=[sc])
                    m8 = k.sb(e3, "sm8", [2 * NS, 16])
                    tkw = k.sb(e3, "stkw", [2 * NS, NB33])
                    k.op("dve", lambda e: e.max(out=m8[:, 0:8], in_=sc[:, 0:NB33]), reads=[sc], writes=[m8])
                    k.op("dve", lambda e: e.match_replace(out=tkw[:], in_to_replace=m8[:, 0:8], in_values=sc[:, 0:NB33], imm_value=-1e9), reads=[sc, m8], writes=[tkw])
                    k.op("dve", lambda e: e.max(out=m8[:, 8:16], in_=tkw[:]), reads=[tkw], writes=[m8])
                    k.op("dve", lambda e: e.tensor_scalar(sc[:, 0:NB33], sc[:, 0:NB33], m8[:, 15:16], None, ALU.is_ge), reads=[sc, m8], writes=[sc])
                    k.op("dve", lambda e: e.tensor_scalar(sc[:, 0:NB33], sc[:, 0:NB33], -1.0, -NEG, ALU.add, ALU.mult), reads=[sc], writes=[sc])
                    selb16 = k.sb(e3, "selb16", [2 * NS, 40], BF16)
                    k.op("dve", lambda e: e.tensor_copy(selb16[:, 0:NB33], sc[:, 0:NB33]), reads=[sc], writes=[selb16])
                    for s in range(NS):
                        for n in range(2):
                            c0 = (s * 2 + n) * 4
                            k.op("pe", lambda e: e.matmul(B6[0:65, c0:c0 + 4], vc_all[:, s, n, 0:65], PTc[:, c0:c0 + 4], start=True, stop=True),
                                 reads=[vc_all, PTc], writes=[B6])
                    KT_s = [k.sb(e3, "KT_s%d" % i, [128, (NPG + 1) * 128], BF16) for i in range(2)]
                    Vs_s = [k.sb(e3, "Vs_s%d" % i, [128, NPG + 1, 2, 68], BF16) for i in range(2)]
                    KTw_s = [k.sb(e3, "KTw_s%d" % i, [128, (NWT + 1) * 128], BF16) for i in range(2)]
                    Vw_s = [k.sb(e3, "Vw_s%d" % i, [128, NWT + 1, 2, 68], BF16) for i in range(2)]
                    for i in range(2):
                        k.op("pool", lambda e: e.memset(Vs_s[i][:], 1.0), writes=[Vs_s[i]])
                        k.op("pool", lambda e: e.memset(Vw_s[i][:], 1.0), writes=[Vw_s[i]])
                        k.op("pool", lambda e: e.memset(Vs_s[i][:, NPG, :, :], 0.0), writes=[Vs_s[i]])
                        k.op("pool", lambda e: e.memset(Vw_s[i][:, NWT, :, :], 0.0), writes=[Vw_s[i]])
                    wt = [k.sb(e3, "wt%d" % i, [128, NWT, 256]) for i in range(2)]
                    vstg = k.sb(e3, "vstg", [1, 2, 128])
                    ones1 = k.sb(e3, "ones1", [1, 2, 2])
                    k.op("pool", lambda e: e.memset(ones1[:], 1.0), writes=[ones1])
                    selm = k.sb(e3, "selm", [128, NPG])
                    tmps = k.sb(e3, "tmps", [128, NPG + 1, 4])
                    PTs = k.sb(e3, "sPTs", [128, NPG + 1, 4], BF16)
                    PTw = k.sb(e3, "sPTw", [128, NWT + 1, 4], BF16)
                    k.op("pool", lambda e: e.memset(PTs[:], 0.0), writes=[PTs])
                    k.op("pool", lambda e: e.memset(PTw[:], 0.0), writes=[PTw])
                    for s in range(NS):
                        KT, V, KTw, Vw, wts = KT_s[s % 2], Vs_s[s % 2], KTw_s[s % 2], Vw_s[s % 2], wt[s % 2]
                        for pg in range(NPG):
                            pt_ = pgt[it % 3]
                            it += 1
                            k.gather(pt_[:], pool2v, idxs[:, s * NPG + pg:s * NPG + pg + 1], reads=[idxs], writes=[pt_])
                            k.op("pe", lambda e: e.matmul(B2[:, 0:128], pt_[:, 0:128], ident, start=True, stop=True), reads=[pt_, c32], writes=[B2])
                            cast(KT[:, pg * 128:(pg + 1) * 128], B2[:, 0:128], [B2], [KT], psum=True)
                            cast(V[:, pg, :, 0:64], pt_[:, 128:256].rearrange("p (n d) -> p n d", n=2), [pt_], [V], psum=True)
                        k.dma("sp", wts[:], st_win[l, s].rearrange("(t p) c -> p t c", p=128), writes=[wts])
                        for t in range(NWT):
                            k.op("pe", lambda e: e.matmul(B2[:, 128:256], wts[:, t, 0:128], ident, start=True, stop=True), reads=[wts, c32], writes=[B2])
                            cast(KTw[:, t * 128:(t + 1) * 128], B2[:, 128:256], [B2], [KTw], psum=True)
                            cast(Vw[:, t, :, 0:64], wts[:, t, 128:256].rearrange("p (n d) -> p n d", n=2), [wts], [Vw], psum=True)
                        k.op("dve", lambda e: e.tensor_copy(KT[:, NPG * 128:NPG * 128 + 1], nK[:, 0, s:s + 1]), reads=[nK], writes=[KT])
                        k.op("dve", lambda e: e.tensor_copy(KTw[:, NWT * 128:NWT * 128 + 1], nK[:, 1, s:s + 1]), reads=[nK], writes=[KTw])
                        k.dma("sp", vstg[0:1, 0, :], hs[s:s + 1, C_KV + 384:C_KV + 512], reads=[hs], writes=[vstg])
                        k.dma("sp", vstg[0:1, 1, :], hs[s:s + 1, C_WIN + 128:C_WIN + 256], reads=[hs], writes=[vstg])
                        k.op("dve", lambda e: e.tensor_copy(V[0:1, NPG, :, 0:64], vstg[0:1, 0, :].rearrange("p (n d) -> p n d", n=2)), reads=[vstg], writes=[V])
                        k.op("dve", lambda e: e.tensor_copy(Vw[0:1, NWT, :, 0:64], vstg[0:1, 1, :].rearrange("p (n d) -> p n d", n=2)), reads=[vstg], writes=[Vw])
                        k.op("dve", lambda e: e.tensor_copy(V[0:1, NPG, :, 64:65], ones1[0:1, :, 0:1]), reads=[ones1], writes=[V])
                        k.op("dve", lambda e: e.tensor_copy(Vw[0:1, NWT, :, 64:65], ones1[0:1, :, 0:1]), reads=[ones1], writes=[Vw])
                        for n in range(2):
                            r = s * 2 + n
                            c0 = r * 4
                            k.op("pe", lambda e: e.matmul(B2[:, 256:256 + NB33], OH[:, r * 128:(r + 1) * 128], selb16[:, 0:NB33], start=True, stop=True),
                                 reads=[csb, selb16], writes=[B2])
                            k.op("dve", lambda e: e.tensor_copy(selm[0:64, :], B2[0:64, 256:256 + 2 * NPG].rearrange("p (t two) -> p t two", two=2)[:, :, 0]), reads=[B2], writes=[selm])
                            k.op("dve", lambda e: e.tensor_copy(selm[64:128, :], B2[64:128, 256:256 + 2 * NPG].rearrange("p (t two) -> p t two", two=2)[:, :, 1]), reads=[B2], writes=[selm])
                            for (BS, K_, nt_, PTx, Vx, ali, col6) in ((B4, KT, NPG, PTs, V, alibi_s, 1), (B5, KTw, NWT, PTw, Vw, alibi_w, 2)):
                                for t in range(nt_):
                                    k.op("pe", lambda e: e.matmul(BS[:, t * 4:(t + 1) * 4], K_[:, t * 128:(t + 1) * 128], qz[n][:, s, :], start=True, stop=True),
                                         reads=[K_, qz[n]], writes=[BS])
                                k.op("pe", lambda e: e.matmul(BS[0:1, 128:132], K_[:, nt_ * 128:nt_ * 128 + 1], qz[n][:, s, :], start=True, stop=True),
                                     reads=[K_, qz[n]], writes=[BS])
                                tv = tmps[:, 0:nt_, :]
                                k.op("dve", lambda e: e.scalar_tensor_tensor(tv, BS[:, 0:nt_ * 4].rearrange("p (t g) -> p t g", g=4), 0.125,
                                                                             ali[:, n * nt_ * 4:(n + 1) * nt_ * 4].rearrange("p (t g) -> p t g", g=4), ALU.mult, ALU.add),
                                     reads=[BS, cs32], writes=[tmps])
                                if col6 == 1:
                                    k.op("dve", lambda e: e.tensor_tensor(tv, tv, selm[:].unsqueeze(2).to_broadcast([128, NPG, 4]), ALU.add), reads=[tmps, selm], writes=[tmps])
                                k.op("act", lambda e: e.activation(PTx[:, 0:nt_, :], tv, AF.Exp), reads=[tmps], writes=[PTx])
                                k.op("act", lambda e: e.activation(PTx[0:1, nt_, :], BS[0:1, 128:132], AF.Exp, scale=0.125), reads=[BS], writes=[PTx])
                                oc = col6 * NQ + c0
                                for t in range(nt_ + 1):
                                    k.op("pe", lambda e: e.matmul(B6[0:65, oc:oc + 4], Vx[:, t, n, 0:65], PTx[:, t, :], start=(t == 0), stop=(t == nt_)),
                                         reads=[Vx, PTx], writes=[B6], inc=(t == nt_))
                    ot = k.sb(e3, "ot", [65, 3, NQ])
                    k.op("act", lambda e: e.copy(ot[:].rearrange("p a q -> p (a q)"), B6[0:65, 0:3 * NQ]), reads=[B6], writes=[ot])
                    k.op("dve", lambda e: e.tensor_scalar(ot[64:65, :, :], ot[64:65, :, :], 1e-30, None, ALU.max), reads=[ot], writes=[ot])
                    k.op("dve", lambda e: e.reciprocal(ot[64:65, :, :], ot[64:65, :, :]), reads=[ot], writes=[ot])
                    k.op("pe", lambda e: e.matmul(B0[0:64, 0:3 * NQ], ones32[64:65, 0:64], ot[64:65, :, :].rearrange("p a q -> p (a q)"), start=True, stop=True),
                         reads=[ot, c32], writes=[B0])
                    k.op("dve", lambda e: e.tensor_tensor(ot[0:64, :, :], ot[0:64, :, :], B0[0:64, 0:3 * NQ].rearrange("p (a q) -> p a q", a=3), ALU.mult), reads=[ot, B0], writes=[ot])
                    gbd = k.sb(e3, "gbd", [NS, 3, NS, 8])
                    for br in range(3):
                        k.op("dve", lambda e: e.tensor_tensor(gbd[:, br, :, :], gts[:].rearrange("s (h b) -> s h b", b=3)[:, :, br].unsqueeze(1).to_broadcast([NS, NS, 8]),
                                                              ident[0:NS, 0:NS].unsqueeze(2).to_broadcast([NS, NS, 8]), ALU.mult), reads=[gts, c32], writes=[gbd])
                    k.op("pe", lambda e: e.matmul(B0[0:64, 0:3 * NQ], ones32[0:NS, 0:64], gbd[:].rearrange("p a s h -> p (a s h)"), start=True, stop=True),
                         reads=[gbd, c32], writes=[B0])
                    k.op("dve", lambda e: e.tensor_tensor(ot[0:64, :, :], ot[0:64, :, :], B0[0:64, 0:3 * NQ].rearrange("p (a q) -> p a q", a=3), ALU.mult), reads=[ot, B0], writes=[ot])
                    k.op("dve", lambda e: e.tensor_tensor(ot[0:64, 0, :], ot[0:64, 0, :], ot[0:64, 1, :], ALU.add), reads=[ot], writes=[ot])
                    k.op("dve", lambda e: e.tensor_tensor(ot[0:64, 0, :], ot[0:64, 0, :], ot[0:64, 2, :], ALU.add), reads=[ot], writes=[ot])
                    onb = k.sb(e3, "onb", [64, 8, NS], BF16)
                    k.op("dve", lambda e: e.tensor_copy(onb[:], ot[0:64, 0, :].rearrange("p (s h) -> p h s", h=8)), reads=[ot], writes=[onb])
                    k.dma("pool", mixsT_d.t[0:512, :].rearrange("(h d) s -> d h s", d=64), onb[:], reads=[onb], writes=[mixsT_d])
            k.barrier()

        STOP = os.environ.get("DEV_STOP", "")
        for l in range(DEPTH):
            if STOP in ("W", "X0"):
                break
            xres_p = xp if l == 0 else x1_d
            xres_s = xs if l == 0 else xs1_d
            yout_p = x1_d if l == 0 else y_p
            yout_s = xs1_d if l == 0 else y_s
            last = (l == DEPTH - 1)
            if do_prompt:
                with ExitStack() as es:
                    cbp = k.sb(es, "cbp", [128, NCBP], BF16)
                    k.dma("sp", cbp[:], cbp_d[:], writes=[cbp])
                    cb64 = cb3 = cb8 = cbp
                    wq = k.sb(es, "wq", [128, 8, 512], BF16)
                    wkT = k.sb(es, "wkT", [128, 8, 256], BF16)
                    wtok = k.sb(es, "wtok", [128, 8, 792], BF16)
                    wsrc = wb_in.t[l].rearrange("(c p) n -> p c n", p=128)
                    for kc in range(8):
                        for n in range(2):
                            k.dma("sp", wq[:, kc, :].rearrange("p (c n d) -> p c n d", c=4, n=2)[:, :, n, :],
                                  wb_in.t[l, kc * 128:(kc + 1) * 128, n * 256:(n + 1) * 256].rearrange("p (c d) -> p c d", c=4),
                                  reads=[wb_in], writes=[wq])
                    k.dma("sp", wkT[:, :, 0:128], wsrc[:, :, C_KV + 256:C_KV + 384], reads=[wb_in], writes=[wkT])
                    k.dma("sp", wkT[:, :, 128:256], wsrc[:, :, C_WIN:C_WIN + 128], reads=[wb_in], writes=[wkT])
                    k.dma("sp", wtok[:], wsrc[:, :, C_KV:C_KV + 792], reads=[wb_in], writes=[wtok])
                    phis = k.sb(es, "phis", [128, 2, 128])
                    k.op("pool", lambda e: e.memset(phis[:], 0.0), writes=[phis])
                    for a in range(2):
                        for n in range(2):
                            k.dma("sp", phis[64 * n:64 * n + 64, a, 64 * n:64 * n + 64], nsa_phi[l, a], writes=[phis])
                    phib = k.sb(es, "phib", [128, 2, 128], BF16)
                    k.op("dve", lambda e: e.tensor_copy(phib[:], phis[:]), reads=[phis], writes=[phib])
                    pet = k.sb(es, "pet", [128, 2, 32])
                    for a in range(2):
                        for n in range(2):
                            k.dma("sp", pet[64 * n:64 * n + 64, a, :], nsa_pe[l, a].rearrange("r d -> d r"), writes=[pet],
                                  allow_slow_non_contiguous=True)
                    pem = k.sb(es, "pem", [128, 2])
                    k.op("dve", lambda e: e.tensor_reduce(pem[:], pet[:], AX.X, ALU.add), reads=[pet], writes=[pem])
                    k.op("dve", lambda e: e.tensor_scalar(pem[:], pem[:], 1.0 / 32, None, ALU.mult), reads=[pem], writes=[pem])
                    KTs = k.sb(es, "KTs", [128, T], BF16)
                    KTw = k.sb(es, "KTw", [128, T], BF16)
                    Vs = k.sb(es, "Vs", [128, NJ, 2, 68], BF16)
                    Vw = k.sb(es, "Vw", [128, NJ, 2, 68], BF16)
                    k.op("pool", lambda e: e.memset(Vs[:], 1.0), writes=[Vs])
                    k.op("pool", lambda e: e.memset(Vw[:], 1.0), writes=[Vw])
                    kcmT = k.sb(es, "kcmT", [128, 128], BF16)
                    vcmT = k.sb(es, "vcmT", [128, 128], BF16)
                    kcT = k.sb(es, "kcT", [128, 128], BF16)
                    k.op("pool", lambda e: e.memset(kcmT[:], 0.0), writes=[kcmT])
                    k.op("pool", lambda e: e.memset(vcmT[:], 0.0), writes=[vcmT])
                    k.op("pool", lambda e: e.memset(kcT[:], 0.0), writes=[kcT])
                    cmprhs = k.sb(es, "cmprhs", [128, 2, 68], BF16)
                    k.op("pool", lambda e: e.memset(cmprhs[:], 1.0), writes=[cmprhs])
                    PT0 = k.sb(es, "PT0", [128, NJ, 512], BF16)
                    PT = [PT0, PT0]
                    PTc = k.sb(es, "PTc", [128, 512], BF16)
                    xTt = [k.sb(es, "xTt%d" % i, [128, 8, 128], BF16) for i in range(2)]
                    kvf = [k.sb(es, "kvf%d" % i, [128, 792]) for i in range(2)]
                    cmpkv = k.sb(es, "cmpkv", [128, 256], BF16)
                    gates = k.sb(es, "gates", [128, 24])
                    qTz = [k.sb(es, "qTz%d" % n, [128, 4, 128], BF16) for n in range(2)]
                    for n in range(2):
                        k.op("pool", lambda e: e.memset(qTz[n][:], 0.0), writes=[qTz[n]])
                    selbT = [k.sb(es, "selbT%d" % n, [128, 4, 128], BF16) for n in range(2)]
                    for n in range(2):
                        k.op("pool", lambda e: e.memset(selbT[n][:], 0.0), writes=[selbT[n]])
                    sm = k.sb(es, "sm", [128, 64])
                    imp = k.sb(es, "imp", [128, 64])
                    tkw = k.sb(es, "tkw", [128, 64])
                    m8 = k.sb(es, "m8", [128, 16])
                    selb = k.sb(es, "selb", [128, 64])
                    ocmp = k.sb(es, "ocmp", [128, 8, 65])
                    rl = k.sb(es, "rl", [128, 8, 3])
                    fgt = k.sb(es, "fgt", [128, 8, 3])
                    onsa = k.sb(es, "onsa", [128, 512])
                    onT = k.sb(es, "onT", [128, 4, 128], BF16)
                    o_ = 0
                    kaux = cbp.t[:, o_:o_ + NJ * 128]
                    o_ += NJ * 128
                    kcaux = cbp.t[:, o_:o_ + NJ * 128]
                    o_ += NJ * 128
                    qaux = cbp.t[:, o_:o_ + 1024]
                    o_ += 1024
                    cmpsel = cbp.t[:, o_:o_ + NJ * 128]
                    o_ += NJ * 128
                    vispat = cbp.t[:, o_:o_ + 512]
                    o_ += 512
                    Epad = cbp.t[:, o_:o_ + T]
                    SC = (pb[0], pb[1])
                    ACC = {("s", 0): pb[2], ("s", 1): pb[3], ("w", 0): pb[4], ("w", 1): pb[5]}
                    MA, MB = pb[6], pb[7]
                    sc_i = [0]

                    def scores(n, lhs_aux, lhsK, mask, out_pt):
                        S = SC[sc_i[0] % 2]
                        sc_i[0] += 1
                        k.op("pe", lambda e: e.matmul(S[:, :], lhs_aux, qaux[:, n * 512:(n + 1) * 512],
                                                      start=True, stop=False), reads=[cb3], writes=[S], inc=False)
                        if mask is not None:
                            ml, mr, mrd = mask
                            k.op("pe", lambda e: e.matmul(S[:, :], ml, mr, start=False, stop=False), reads=mrd, writes=[S], inc=False)
                        for g in range(4):
                            k.op("pe", lambda e, g=g: e.matmul(S[:, g * 128:(g + 1) * 128], lhsK, qTz[n][:, g, :],
                                                             start=False, stop=(g == 3)),
                                 reads=[qTz[n], KTs, KTw, kcT], writes=[S], inc=(g == 3))
                        k.op("act", lambda e: e.activation(out_pt, S[:, :], AF.Exp, scale=0.125), reads=[S], writes=[PT[n], PTc])

                    for j in range(NJ):
                        xt = xTt[j % 2]
                        kv = kvf[j % 2]
                        k.dma("sp", xt[:], xT_d.t.rearrange("(c p) t -> p c t", p=128)[:, :, j * 128:(j + 1) * 128],
                              reads=[xT_d], writes=[xt])
                        for kc in range(8):
                            k.op("pe", lambda e, kc=kc: e.matmul(MA[:, :], xt[:, kc, :], wtok[:, kc, 0:512], start=(kc == 0), stop=(kc == 7)),
                                 reads=[xt, wtok], writes=[MA], inc=(kc == 7))
                        for kc in range(8):
                            k.op("pe", lambda e, kc=kc: e.matmul(MB[:, 0:280], xt[:, kc, :], wtok[:, kc, 512:792], start=(kc == 0), stop=(kc == 7)),
                                 reads=[xt, wtok], writes=[MB], inc=(kc == 7))
                        k.op("act", lambda e: e.copy(kv[:, 0:512], MA[:, :]), reads=[MA], writes=[kv])
                        k.op("dve", lambda e: e.tensor_copy(kv[:, 512:792], MB[:, 0:280]), reads=[MB], writes=[kv])
                        k.dma("pool", kv_p[l, j * 128:(j + 1) * 128, :], kv[:, 0:512], reads=[kv], writes=[kv_p])
                        if j >= NJ - NWT:
                            jj = j - (NJ - NWT)
                            k.dma("pool", win_p[l, jj * 128:(jj + 1) * 128, :], kv[:, 512:768], reads=[kv], writes=[win_p])
                        k.op("act", lambda e: e.activation(gates[:], kv[:, 768:792], AF.Sigmoid), reads=[kv], writes=[gates])
                        k.op("dve", lambda e: e.tensor_copy(Vs[:, j, :, 0:64], kv[:, 384:512].rearrange("p (n d) -> p n d", n=2)),
                             reads=[kv], writes=[Vs])
                        k.op("pool", lambda e: e.tensor_copy(Vw[:, j, :, 0:64], kv[:, 640:768].rearrange("p (n d) -> p n d", n=2)),
                             reads=[kv], writes=[Vw])
                        k.op("dve", lambda e: e.tensor_copy(cmpkv[:], kv[:, 0:256]), reads=[kv], writes=[cmpkv])
                        k.op("pe", lambda e: e.matmul(MA[:, 0:4], cmpkv[:, 0:128], avg4, start=True, stop=True), reads=[cmpkv, cb128], writes=[MA])
                        k.op("pe", lambda e: e.matmul(MA[:, 4:8], cmpkv[:, 128:256], avg4, start=True, stop=True), reads=[cmpkv, cb128], writes=[MA])
                        k.op("dve", lambda e: e.tensor_scalar(kcmT[:, 4 * j:4 * j + 4], MA[:, 0:4], pem[:, 0:1], None, ALU.add),
                             reads=[MA, pem], writes=[kcmT])
                        k.op("dve", lambda e: e.tensor_scalar(vcmT[:, 4 * j:4 * j + 4], MA[:, 4:8], pem[:, 1:2], None, ALU.add),
                             reads=[MA, pem], writes=[vcmT])
                        k.op("pe", lambda e: e.matmul(MB[:, 0:4], phib[:, 0, :], kcmT[:, 4 * j:4 * j + 4], start=True, stop=True),
                             reads=[phib, kcmT], writes=[MB])
                        k.op("dve", lambda e: e.tensor_copy(kcT[:, 4 * j:4 * j + 4], MB[:, 0:4]), reads=[MB], writes=[kcT])
                        k.op("pe", lambda e: e.matmul(MB[:, 128:256], vcmT[:, :], phib[:, 1, :], start=True, stop=True),
                             reads=[phib, vcmT], writes=[MB])
                        k.op("dve", lambda e: e.tensor_copy(cmprhs[:, :, 0:64], MB[:, 128:256].rearrange("p (n d) -> p n d", n=2)),
                             reads=[MB], writes=[cmprhs])
                        for c in range(4):
                            for kc in range(8):
                                k.op("pe", lambda e, c=c, kc=kc: e.matmul(MA[:, c * 128:(c + 1) * 128], wq[:, kc, c * 128:(c + 1) * 128], xt[:, kc, :],
                                                                       start=(kc == 0), stop=(kc == 7)),
                                     reads=[xt, wq], writes=[MA], inc=(kc == 7))
                        k.op("act", lambda e: e.copy(qTz[0][0:64, :, :], MA[0:64, :].rearrange("p (c t) -> p c t", c=4)), reads=[MA], writes=[qTz[0]])
                        k.op("dve", lambda e: e.tensor_copy(qTz[1][64:128, :, :], MA[64:128, :].rearrange("p (c t) -> p c t", c=4)), reads=[MA], writes=[qTz[1]])
                        for a, KT in ((0, KTs), (1, KTw)):
                            for kc in range(8):
                                k.op("pe", lambda e, a=a, kc=kc: e.matmul(MB[:, a * 128:(a + 1) * 128], wkT[:, kc, a * 128:(a + 1) * 128], xt[:, kc, :],
                                                                       start=(kc == 0), stop=(kc == 7)),
                                     reads=[xt, wkT], writes=[MB], inc=(kc == 7))
                        k.op("dve", lambda e: e.tensor_copy(KTs[:, j * 128:(j + 1) * 128], MB[:, 0:128]), reads=[MB], writes=[KTs])
                        k.op("dve", lambda e: e.tensor_copy(KTw[:, j * 128:(j + 1) * 128], MB[:, 128:256]), reads=[MB], writes=[KTw])
                        if os.environ.get("DEV_P1", "") == "proj":
                            continue
                        for n in range(2):
                            scores(n, kcaux[:, j * 128:(j + 1) * 128], kcT[:, :],
                                   (cmpsel[:, j * 128:(j + 1) * 128], vispat, [cb8]), PTc[:, :])
                            for g in range(4):
                                k.op("pe", lambda e, g=g: e.matmul(MA[:, g * 65:(g + 1) * 65], PTc[:, g * 128:(g + 1) * 128], cmprhs[:, n, 0:65],
                                                                 start=True, stop=True), reads=[PTc, cmprhs], writes=[MA])
                            for g in range(4):
                                k.op("pe", lambda e, g=g: e.matmul(MB[:, g * 64:(g + 1) * 64], PTc[:, g * 128:(g + 1) * 128], pool2,
                                                                 start=True, stop=True), reads=[PTc, cb128], writes=[MB])
                            k.op("act", lambda e: e.copy(ocmp[:, 4 * n:4 * n + 4, :], MA[:, 0:260].rearrange("p (g d) -> p g d", g=4)),
                                 reads=[MA], writes=[ocmp])
                            k.op("dve", lambda e: e.tensor_scalar(rl[:, 4 * n:4 * n + 4, 0], ocmp[:, 4 * n:4 * n + 4, 64], 1e-30, None, ALU.max),
                                 reads=[ocmp], writes=[rl])
                            k.op("dve", lambda e: e.reciprocal(rl[:, 4 * n:4 * n + 4, 0], rl[:, 4 * n:4 * n + 4, 0]), reads=[rl], writes=[rl])
                            if j >= 8:
                                k.op("dve", lambda e: e.tensor_scalar(imp[:], MB[:, 0:64], rl[:, 4 * n, 0:1], None, ALU.mult),
                                     reads=[MB, rl], writes=[imp])
                                for g in range(1, 4):
                                    k.op("dve", lambda e, g=g: e.scalar_tensor_tensor(imp[:], MB[:, g * 64:(g + 1) * 64], rl[:, 4 * n + g, 0:1], imp[:],
                                                                                     ALU.mult, ALU.add), reads=[MB, rl, imp], writes=[imp])
                                k.op("dve", lambda e: e.tensor_tensor(imp[:], imp[:], tkb[:, j * 64:(j + 1) * 64], ALU.add), reads=[imp, c32], writes=[imp])
                                k.op("dve", lambda e: e.max(out=m8[:, 0:8], in_=imp[:]), reads=[imp], writes=[m8])
                                k.op("dve", lambda e: e.match_replace(out=tkw[:], in_to_replace=m8[:, 0:8], in_values=imp[:], imm_value=-1e9),
                                     reads=[imp, m8], writes=[tkw])
                                k.op("dve", lambda e: e.max(out=m8[:, 8:16], in_=tkw[:]), reads=[tkw], writes=[m8])
                                k.op("dve", lambda e: e.tensor_scalar(selb[:], imp[:], m8[:, 15:16], None, ALU.is_ge), reads=[imp, m8], writes=[selb])
                                k.op("dve", lambda e: e.tensor_scalar(selb[:], selb[:], -1.0, -NEG, ALU.add, ALU.mult), reads=[selb], writes=[selb])
                                k.op("pe", lambda e: e.transpose(MB[0:64, 256:384], selb[:, :], ident), reads=[selb, c32], writes=[MB])
                                k.op("dve", lambda e: e.tensor_copy(selbT[n][0:64, :, :], MB[0:64, 256:384].unsqueeze(1).to_broadcast([64, 4, 128])),
                                     reads=[MB], writes=[selbT[n]])
                        if os.environ.get("DEV_P1", "") == "cmp":
                            continue
                        for br, KT, V, tiles in (("s", KTs, Vs, list(range(0, j + 1))), ("w", KTw, Vw, list(range(max(0, j - 4), j + 1)))):
                            if os.environ.get("DEV_P1", "") == "sonly" and br == "w":
                                continue
                            for n in range(2):
                                for t in tiles:
                                    if t == j:
                                        mask = (identb, causal4, [cb128])
                                    elif br == "w" and t == j - 4:
                                        mask = (identb, winedge4, [cb128])
                                    elif br == "s" and j >= 8:
                                        mask = (Epad[:, t * 128:(t + 1) * 128], selbT[n][:].rearrange("p g q -> p (g q)"), [cb64, selbT[n]])
                                    else:
                                        mask = None
                                    nm_ = os.environ.get("DEV_NOMASK", "")
                                    if nm_ == "1" or (nm_ == "2" and mask is not None and mask[2][0] is cb128) or (nm_ == "3" and mask is not None and mask[2][0] is cb64):
                                        mask = None
                                    scores(n, kaux[:, (j - t) * 128:(j - t + 1) * 128], KT[:, t * 128:(t + 1) * 128], mask, PT[n][:, t, :])
                                if os.environ.get("DEV_P1", "") == "sc":
                                    continue
                                A = ACC[(br, n)]
                                for g in range(4):
                                    for ti, t in enumerate(tiles):
                                        k.op("pe", lambda e, g=g, t=t, ti=ti: e.matmul(A[:, g * 65:(g + 1) * 65], PT[n][:, t, g * 128:(g + 1) * 128], V[:, t, n, 0:65],
                                                                                  start=(ti == 0), stop=(ti == len(tiles) - 1)),
                                             reads=[PT[n], V], writes=[A], inc=(ti == len(tiles) - 1))
                                if os.environ.get("DEV_P1", "") == "pv":
                                    continue
                                col = 1 if br == "s" else 2
                                k.op("dve", lambda e: e.reciprocal(rl[:, 4 * n:4 * n + 4, col],
                                                                   A[:, 0:260].rearrange("p (g d) -> p g d", g=4)[:, :, 64]),
                                     reads=[A], writes=[rl])
                        if os.environ.get("DEV_P1", "") in ("sc", "pv", "rc"):
                            continue
                        k.op("dve", lambda e: e.tensor_tensor(fgt[:], rl[:], gates[:].rearrange("p (h b) -> p h b", b=3), ALU.mult),
                             reads=[rl, gates], writes=[fgt])
                        for h in range(8):
                            n, g = h // 4, h % 4
                            o_ = onsa[:, h * 64:(h + 1) * 64]
                            k.op("dve", lambda e: e.tensor_scalar(o_, ocmp[:, h, 0:64], fgt[:, h, 0:1], None, ALU.mult), reads=[ocmp, fgt], writes=[onsa])
                            k.op("dve", lambda e: e.scalar_tensor_tensor(o_, ACC[("s", n)][:, g * 65:g * 65 + 64], fgt[:, h, 1:2], o_, ALU.mult, ALU.add),
                                 reads=[ACC[("s", n)], fgt, onsa], writes=[onsa])
                            k.op("dve", lambda e: e.scalar_tensor_tensor(o_, ACC[("w", n)][:, g * 65:g * 65 + 64], fgt[:, h, 2:3], o_, ALU.mult, ALU.add),
                                 reads=[ACC[("w", n)], fgt, onsa], writes=[onsa])
                        for c in range(4):
                            k.op("pe", lambda e, c=c: e.transpose(MA[:, c * 128:(c + 1) * 128], onsa[:, c * 128:(c + 1) * 128], ident),
                                 reads=[onsa, c32], writes=[MA])
                        k.op("act", lambda e: e.copy(onT[:], MA[:, :].rearrange("p (c t) -> p c t", c=4)), reads=[MA], writes=[onT])
                        k.dma("pool", mixT_d.t[0:512, :].rearrange("(c p) t -> p c t", p=128)[:, :, j * 128:(j + 1) * 128], onT[:],
                              reads=[onT], writes=[mixT_d])
                k.barrier()
            if STOP == "P1":
                break
            if do_prompt:
                gdn_prompt(l)
            if STOP == "P2":
                break
            if do_sample:
                sample_mixers(l)
            chain(l, xres_p, xres_s, yout_p, yout_s, last)
            if STOP == "P3":
                break
        k.finish()
    return nc


_W_NAMES = {"w_in": "w_in", "nsa_pe": "nsa_pe", "nsa_phi": "nsa_phi", "gdn_conv_w": "gconv_w", "gdn_A_log": "A_log",
            "gdn_dt_bias": "dt_bias", "gdn_norm_w": "gnorm_w", "w_out": "w_out", "ln_g": "ln_g", "ln_b": "ln_b",
            "ffn_w_up": "w_up", "ffn_conv_w": "fconv_w", "ffn_w_down": "w_down", "ple_w_proj": "w_proj", "ple_w_gate": "w_gate"}


def run_cores(inp, ncores, parts=("prompt", "sample")):
    f32 = np.float32
    x_prompt = np.asarray(inp["x_prompt"], f32)
    B, T, _ = x_prompt.shape
    x_sample = np.asarray(inp["x_sample"], f32)
    NSTOT = x_sample.shape[0]
    cache = np.asarray(inp["cache_nsa_kv"], f32)
    NPOOL = cache.shape[1]
    page_table = np.asarray(inp["page_table"], np.int32)
    NPG = page_table.shape[1]
    NS = NSTOT // ncores
    swin = np.asarray(inp["state_nsa_win"], f32)
    WB = swin.shape[2]
    nc = build(T, NS, NPOOL, NPG, WB, parts)
    consts = make_consts(T, NS, NPG)
    common = {v: np.ascontiguousarray(np.asarray(inp[kk], f32)) for kk, v in _W_NAMES.items()}
    common.update(consts)
    pool = np.ascontiguousarray(cache.reshape(DEPTH * NPOOL * 128, 512))
    p_prompt = np.asarray(inp["p_prompt"], f32)
    p_sample = np.asarray(inp["p_sample"], f32)
    sgdn = np.asarray(inp["state_gdn"], f32)
    sgc = np.asarray(inp["state_gdn_conv"], f32)
    sfc = np.asarray(inp["state_ffn_conv"], f32)
    per = ncores // B if ncores >= B else 1
    in_maps = []
    for c in range(ncores):
        b = min(c // per, B - 1)
        sl = slice(c * NS, (c + 1) * NS)
        m = dict(common)
        m.update({
            "xp": np.ascontiguousarray(x_prompt[b]),
            "pp": np.ascontiguousarray(p_prompt[:, b]),
            "xs": np.ascontiguousarray(x_sample[sl, 0]),
            "pps": np.ascontiguousarray(p_sample[:, sl, 0]),
            "pool": pool,
            "ptab": np.ascontiguousarray(page_table[sl].reshape(1, NS * NPG)),
            "st_win": np.ascontiguousarray(swin[:, sl].reshape(DEPTH, NS, WB, 256)),
            "st_gdn": np.ascontiguousarray(sgdn[:, sl]),
            "st_gconv": np.ascontiguousarray(sgc[:, sl]),
            "st_fconv": np.ascontiguousarray(sfc[:, sl]),
        })
        in_maps.append(m)
    if os.environ.get("DEV_TRACE", "") == "1":
        res = run_bass_kernel_spmd(nc, in_maps, core_ids=list(range(ncores)), trace=True)
        print("EXEC_TIME_NS", res.exec_time_ns)
    else:
        res = run_bass_kernel_spmd(nc, in_maps, core_ids=list(range(ncores)))
    R = res.results
    pc = [min(b * per, ncores - 1) for b in range(B)]
    y_prompt = np.stack([R[c]["y_p"] for c in pc])
    y_sample = np.concatenate([R[c]["y_s"] for c in range(ncores)])[:, None, :]
    kv_rows_prompt = np.stack([R[c]["kv_p"] for c in pc], axis=1).reshape(DEPTH, B, T, 4, 2, 64)
    kv_rows_sample = np.concatenate([R[c]["kv_s"] for c in range(ncores)], axis=1).reshape(DEPTH, NSTOT, 1, 4, 2, 64)
    win_prompt = np.stack([R[c]["win_p"] for c in pc], axis=1).reshape(DEPTH, B, -1, 2, 2, 64)
    win_sample = np.concatenate([R[c]["win_s"] for c in range(ncores)], axis=1).reshape(DEPTH, NSTOT, WB, 2, 2, 64)
    gdn_state_prompt = np.stack([R[c]["gst_p"] for c in pc], axis=1)
    gdn_state_sample = np.concatenate([R[c]["gst_s"] for c in range(ncores)], axis=1)
    gdn_conv_prompt = np.stack([R[c]["gcv_p"] for c in pc], axis=1)
    gdn_conv_sample = np.concatenate([R[c]["gcv_s"] for c in range(ncores)], axis=1)
    ffn_conv_prompt = np.stack([R[c]["fcv_p"] for c in pc], axis=1)
    ffn_conv_sample = np.concatenate([R[c]["fcv_s"] for c in range(ncores)], axis=1)
    return (y_prompt, y_sample, kv_rows_prompt, kv_rows_sample, win_prompt, win_sample, gdn_state_prompt, gdn_state_sample,
            gdn_conv_prompt, gdn_conv_sample, ffn_conv_prompt, ffn_conv_sample)


def kernel(**inputs):
    outs = run_cores(inputs, NCORES)
    return tuple(np.ascontiguousarray(o, dtype=np.float32) for o in outs)
```
